# Optimizing a Trainium2 kernel written in Bass

```python
import math
import jax
import jax.numpy as jnp
from jax import lax
import numpy as np

D_MODEL = 1024
BATCH = 4
SEQ = 4096
DEPTH = 2

D_MIX = D_MODEL
ATT_QK_DIM = 64
ATT_V_DIM = 2 * ATT_QK_DIM
ATT_WIDTH = D_MIX // 2
ATT_HEADS = ATT_WIDTH // ATT_V_DIM
HG_DK = 64
HG_DV = 64
HG_WIDTH = D_MIX // 4
HG_HEADS = HG_WIDTH // HG_DV
DN_DK = 64
DN_DV = 64
DN_WIDTH = D_MIX - ATT_WIDTH - HG_WIDTH
DN_HEADS = DN_WIDTH // DN_DV
DN_CONV = 4
DN_CONV_CH = 2 * DN_HEADS * DN_DK + DN_HEADS * DN_DV
FFN_CONV = 3
D_FF = ((8 * D_MODEL // 3 + 255) // 256) * 256
QUERY_BLOCK = 128
HG_CHUNK = 16
DN_CHUNK = 64
EPS = 1e-6
MASK_NEG = -1e30
F_FLOOR = 1e-30

IN_SIZES = (
    ATT_HEADS * 2 * ATT_QK_DIM,
    ATT_HEADS * 2 * ATT_QK_DIM,
    ATT_HEADS * ATT_V_DIM,
    HG_HEADS * HG_DK,
    HG_HEADS * HG_DK,
    HG_HEADS * HG_DV,
    HG_WIDTH,
    DN_HEADS * DN_DK,
    DN_HEADS * DN_DK,
    DN_HEADS * DN_DV,
    DN_WIDTH,
    DN_HEADS,
    DN_HEADS,
)
IN_OFFSETS = tuple(int(o) for o in np.cumsum(IN_SIZES)[:-1])
D_IN = int(sum(IN_SIZES))

kernel_name = "hybrid_diffattn_hgrn2_gdn_convffn"


def rmsnorm(x, w):
    xf = x.astype(jnp.float32)
    y = xf * lax.rsqrt(jnp.mean(xf * xf, axis=-1, keepdims=True) + EPS)
    return (y * w.astype(jnp.float32)).astype(x.dtype)


def l2norm(x):
    return x * lax.rsqrt(jnp.sum(x * x, axis=-1, keepdims=True) + EPS)


def causal_depthwise_conv(x, w):
    K, C = w.shape
    return lax.conv_general_dilated(
        x, w[:, None, :].astype(x.dtype), window_strides=(1,), padding=[(K - 1, 0)],
        dimension_numbers=("NWC", "WIO", "NWC"), feature_group_count=C)


def diff_attention(q, k, v, lam, subln_w, lambda_init):
    B, T = q.shape[:2]
    nb = T // QUERY_BLOCK
    scale = ATT_QK_DIM ** -0.5
    qb = q.reshape(B, nb, QUERY_BLOCK, ATT_HEADS, 2, ATT_QK_DIM).transpose(1, 0, 3, 4, 2, 5)
    kt = k.transpose(0, 2, 3, 1, 4)
    vt = v.transpose(0, 2, 1, 3)
    kpos = jnp.arange(T)

    def block(args):
        qi, i = args
        s = jnp.einsum("bhmqd,bhmkd->bhmqk", qi, kt).astype(jnp.float32) * scale
        qpos = i * QUERY_BLOCK + jnp.arange(QUERY_BLOCK)
        s = jnp.where(kpos[None, :] <= qpos[:, None], s, MASK_NEG)
        p = jax.nn.softmax(s, axis=-1)
        pd = p[:, :, 0] - lam * p[:, :, 1]
        return jnp.einsum("bhqk,bhkv->bhqv", pd.astype(vt.dtype), vt)

    o = lax.map(block, (qb, jnp.arange(nb)))
    o = o.transpose(1, 0, 3, 2, 4).reshape(B, T, ATT_HEADS, ATT_V_DIM)
    o = rmsnorm(o, subln_w) * (1.0 - lambda_init)
    return o.reshape(B, T, ATT_WIDTH)


def chunked_gla(q, k, v, log_f):
    B, H, T, dk = q.shape
    dv = v.shape[-1]
    C = HG_CHUNK
    N = T // C
    q, k, log_f = (t.reshape(B, H, N, C, dk) for t in (q, k, log_f))
    v = v.reshape(B, H, N, C, dv)
    b = jnp.cumsum(log_f, axis=3)
    causal = jnp.tril(jnp.ones((C, C), dtype=bool))
    rel = jnp.where(causal[:, :, None], b[:, :, :, :, None, :] - b[:, :, :, None, :, :], MASK_NEG)
    a = jnp.einsum("bhntd,bhnsd,bhntsd->bhnts", q, k, jnp.exp(rel))
    o_intra = jnp.einsum("bhnts,bhnsv->bhntv", a, v)
    b_last = b[:, :, :, -1]
    kv = jnp.einsum("bhncd,bhncv->bhndv", k * jnp.exp(b_last[:, :, :, None] - b), v)

    def step(S, xs):
        decay, kv_n = xs
        return decay[..., None] * S + kv_n, S

    _, S_prev = lax.scan(step, jnp.zeros((B, H, dk, dv), jnp.float32),
                         (jnp.exp(b_last).transpose(2, 0, 1, 3), kv.transpose(2, 0, 1, 3, 4)))
    o_inter = jnp.einsum("bhncd,nbhdv->bhncv", q * jnp.exp(b), S_prev)
    return (o_intra + o_inter).reshape(B, H, T, dv)


def hgrn2_mixer(q, f_pre, i, g, lb, norm_w):
    B, T, _ = q.shape
    dt = q.dtype
    qf = jax.nn.silu(q.astype(jnp.float32))
    lbf = lb.astype(jnp.float32)
    fp = f_pre.astype(jnp.float32)
    f = lbf + (1.0 - lbf) * jax.nn.sigmoid(fp)
    log_f = jnp.log(jnp.maximum(f, F_FLOOR))
    kf = (1.0 - lbf) * jax.nn.sigmoid(-fp)

    def heads(t, d):
        return t.reshape(B, T, HG_HEADS, d).transpose(0, 2, 1, 3)

    o = chunked_gla(heads(qf, HG_DK), heads(kf, HG_DK), heads(i.astype(jnp.float32), HG_DV), heads(log_f, HG_DK))
    o = o.transpose(0, 2, 1, 3)
    o = rmsnorm(o, norm_w) * jax.nn.silu(g.astype(jnp.float32).reshape(B, T, HG_HEADS, HG_DV))
    return o.reshape(B, T, HG_WIDTH).astype(dt)


def chunked_gated_delta(q, k, v, g, beta):
    B, H, T, dk = q.shape
    dv = v.shape[-1]
    C = DN_CHUNK
    N = T // C
    q = q.reshape(B, H, N, C, dk)
    k = k.reshape(B, H, N, C, dk)
    v = v.reshape(B, H, N, C, dv)
    g = g.reshape(B, H, N, C)
    beta = beta.reshape(B, H, N, C)
    gc = jnp.cumsum(g, axis=-1)
    incl = jnp.tril(jnp.ones((C, C), dtype=bool))
    strict = jnp.tril(jnp.ones((C, C), dtype=bool), k=-1)
    decay = jnp.exp(jnp.where(incl, gc[..., :, None] - gc[..., None, :], MASK_NEG))
    kb = k * beta[..., None]
    a_kk = jnp.where(strict, jnp.einsum("bhncd,bhnsd->bhncs", kb, k) * decay, 0.0)
    rhs = jnp.concatenate([v * beta[..., None], kb * jnp.exp(gc)[..., None]], axis=-1)
    sol = lax.linalg.triangular_solve(a_kk + jnp.eye(C, dtype=jnp.float32), rhs,
                                      left_side=True, lower=True, unit_diagonal=True)
    u, w = sol[..., :dv], sol[..., dv:]
    a_qk = jnp.einsum("bhncd,bhnsd->bhncs", q, k) * decay
    q_dec = q * jnp.exp(gc)[..., None]
    k_dec = k * jnp.exp(gc[..., -1:] - gc)[..., None]
    g_last = jnp.exp(gc[..., -1])

    def step(S, xs):
        u_n, w_n, aqk_n, qd_n, kd_n, gl_n = xs
        v_new = u_n - jnp.einsum("bhcd,bhdv->bhcv", w_n, S)
        o = jnp.einsum("bhcd,bhdv->bhcv", qd_n, S) + jnp.einsum("bhcs,bhsv->bhcv", aqk_n, v_new)
        S = gl_n[..., None, None] * S + jnp.einsum("bhcd,bhcv->bhdv", kd_n, v_new)
        return S, o

    xs = tuple(jnp.moveaxis(t, 2, 0) for t in (u, w, a_qk, q_dec, k_dec, g_last))
    _, o = lax.scan(step, jnp.zeros((B, H, dk, dv), jnp.float32), xs)
    return jnp.moveaxis(o, 0, 2).reshape(B, H, T, dv)


def gated_deltanet_mixer(q, k, v, z, a, b, conv_w, A_log, dt_bias, norm_w):
    B, T, _ = q.shape
    dt = q.dtype
    qkv = jax.nn.silu(causal_depthwise_conv(jnp.concatenate([q, k, v], axis=-1), conv_w).astype(jnp.float32))
    qc, kc, vc = jnp.split(qkv, [DN_HEADS * DN_DK, 2 * DN_HEADS * DN_DK], axis=-1)

    def heads(t, d):
        return t.reshape(B, T, DN_HEADS, d).transpose(0, 2, 1, 3)

    qh = l2norm(heads(qc, DN_DK)) * (DN_DK ** -0.5)
    kh = l2norm(heads(kc, DN_DK))
    vh = heads(vc, DN_DV)
    g = -jnp.exp(A_log.astype(jnp.float32)) * jax.nn.softplus(a.astype(jnp.float32) + dt_bias.astype(jnp.float32))
    beta = jax.nn.sigmoid(b.astype(jnp.float32))
    o = chunked_gated_delta(qh, kh, vh, g.transpose(0, 2, 1), beta.transpose(0, 2, 1))
    o = o.transpose(0, 2, 1, 3)
    o = rmsnorm(o, norm_w) * jax.nn.silu(z.astype(jnp.float32).reshape(B, T, DN_HEADS, DN_DV))
    return o.reshape(B, T, DN_WIDTH).astype(dt)


def setup_inputs(seed: int = 0) -> dict:
    key = jax.random.key(seed)
    ks = jax.random.split(key, 20)
    f32 = jnp.float32

    def normal(k, shape, scale):
        return jax.random.normal(k, shape, f32) * scale

    dt0 = jnp.exp(jax.random.uniform(ks[9], (DEPTH, DN_HEADS), f32, math.log(1e-3), math.log(1e-1)))
    return {
        "x": normal(ks[0], (BATCH, SEQ, D_MODEL), 1.0),
        "attn_norm_w": 1.0 + normal(ks[1], (DEPTH, D_MODEL), 0.02),
        "w_in": normal(ks[2], (DEPTH, D_MODEL, D_IN), D_MODEL ** -0.5),
        "diff_lambda": normal(ks[3], (DEPTH, 4, ATT_QK_DIM), 0.1),
        "diff_subln_w": 1.0 + normal(ks[4], (DEPTH, ATT_V_DIM), 0.02),
        "hgrn_lb_logits": normal(ks[5], (DEPTH, HG_HEADS * HG_DK), 0.1),
        "hgrn_norm_w": 1.0 + normal(ks[6], (DEPTH, HG_DV), 0.02),
        "dn_conv_w": normal(ks[7], (DEPTH, DN_CONV, DN_CONV_CH), DN_CONV ** -0.5),
        "dn_A_log": jnp.log(jax.random.uniform(ks[8], (DEPTH, DN_HEADS), f32, 1.0, 16.0)),
        "dn_dt_bias": dt0 + jnp.log(-jnp.expm1(-dt0)),
        "dn_norm_w": 1.0 + normal(ks[10], (DEPTH, DN_DV), 0.02),
        "w_out": normal(ks[11], (DEPTH, D_MIX, D_MODEL), D_MIX ** -0.5),
        "ffn_norm_w": 1.0 + normal(ks[12], (DEPTH, D_MODEL), 0.02),
        "ffn_w_up": normal(ks[13], (DEPTH, D_MODEL, 2 * D_FF), D_MODEL ** -0.5),
        "ffn_conv_w": normal(ks[14], (DEPTH, FFN_CONV, 2 * D_FF), FFN_CONV ** -0.5),
        "ffn_conv_b": normal(ks[15], (DEPTH, 2 * D_FF), 0.02),
        "ffn_w_down": normal(ks[16], (DEPTH, D_FF, D_MODEL), D_FF ** -0.5),
        "final_norm_w": 1.0 + normal(ks[17], (D_MODEL,), 0.02),
    }


def reference(x, attn_norm_w, w_in, diff_lambda, diff_subln_w, hgrn_lb_logits, hgrn_norm_w,
              dn_conv_w, dn_A_log, dn_dt_bias, dn_norm_w, w_out, ffn_norm_w, ffn_w_up,
              ffn_conv_w, ffn_conv_b, ffn_w_down, final_norm_w):
    lb_soft = jax.nn.softmax(hgrn_lb_logits.astype(jnp.float32), axis=0)
    lower_bounds = jnp.cumsum(lb_soft, axis=0) - lb_soft[0]
    B, T, _ = x.shape
    h = x
    for l in range(DEPTH):
        u = rmsnorm(h, attn_norm_w[l])
        proj = jnp.einsum("btd,de->bte", u, w_in[l])
        (aq, ak, av, hq, hf, hi, hg, dq, dk, dv, dz, da, db) = jnp.split(proj, IN_OFFSETS, axis=-1)

        lambda_init = 0.8 - 0.6 * math.exp(-0.3 * l)
        lam_p = diff_lambda[l].astype(jnp.float32)
        lam = jnp.exp(jnp.sum(lam_p[0] * lam_p[1])) - jnp.exp(jnp.sum(lam_p[2] * lam_p[3])) + lambda_init
        att_o = diff_attention(aq.reshape(B, T, ATT_HEADS, 2, ATT_QK_DIM),
                               ak.reshape(B, T, ATT_HEADS, 2, ATT_QK_DIM),
                               av.reshape(B, T, ATT_HEADS, ATT_V_DIM),
                               lam, diff_subln_w[l], lambda_init)
        hg_o = hgrn2_mixer(hq, hf, hi, hg, lower_bounds[l], hgrn_norm_w[l])
        dn_o = gated_deltanet_mixer(dq, dk, dv, dz, da, db, dn_conv_w[l], dn_A_log[l], dn_dt_bias[l], dn_norm_w[l])
        mix = jnp.concatenate([att_o, hg_o, dn_o], axis=-1)
        h = h + jnp.einsum("bte,ed->btd", mix, w_out[l])

        u = rmsnorm(h, ffn_norm_w[l])
        up = jnp.einsum("btd,df->btf", u, ffn_w_up[l])
        up = causal_depthwise_conv(up, ffn_conv_w[l]) + ffn_conv_b[l].astype(up.dtype)
        gate, val = jnp.split(up, 2, axis=-1)
        h = h + jnp.einsum("btf,fd->btd", jax.nn.silu(gate) * val, ffn_w_down[l])
    return rmsnorm(h, final_norm_w)
```

```python
import numpy as np
import math
from contextlib import ExitStack
import concourse.bass as bass
import concourse.mybir as mybir
from concourse.bass_utils import run_bass_kernel_spmd

F32 = mybir.dt.float32
BF16 = mybir.dt.bfloat16
AF = mybir.ActivationFunctionType
ALU = mybir.AluOpType
AX = mybir.AxisListType

T = 4096
D = 1024
NT = 32
DIN = 3592
DFF = 2816
EPS = 1e-6


class Reg:
    __slots__ = ("name", "lw", "rd", "dsem", "dcnt")

    def __init__(self, name):
        self.name = name
        self.lw = None
        self.rd = {}
        self.dsem = None
        self.dcnt = 0


class Sched:
    ENG = ("pe", "act", "dve", "pool", "sp")

    def __init__(self, nc, stack):
        self.nc = nc
        self.stack = stack
        self.sem = {e: stack.enter_context(nc.semaphore("prog_" + e)) for e in self.ENG}
        self.cnt = {e: 0 for e in self.ENG}
        self.ops = {e: [] for e in self.ENG}
        self.waited = {e: {} for e in self.ENG}
        self.dsems = []
        self.regs = []
        self.nins = 0
        self.free_dsems = []

    def reg(self, name):
        r = Reg(name)
        self.regs.append(r)
        return r

    def _need(self, eng, dep, waits):
        if dep is None:
            return
        kind, who, cnt = dep
        if kind == "e" and who == "pe" and eng == "pe":
            return
        if kind == "e":
            key = ("e", who)
            semh = self.sem[who]
        else:
            key = ("d", id(who))
            semh = who.dsem
        if self.waited[eng].get(key, 0) >= cnt:
            return
        cur = waits.get(key)
        if cur is None or cur[1] < cnt:
            waits[key] = (semh, cnt)

    def _deps(self, eng, reads, writes, pe_accum=False):
        waits = {}
        for r in reads:
            self._need(eng, r.lw, waits)
        for r in writes:
            if not (pe_accum and r.lw is not None and r.lw[0] == "e" and r.lw[1] == "pe" and eng == "pe"):
                self._need(eng, r.lw, waits)
            for d in r.rd.values():
                self._need(eng, d, waits)
        for key, (semh, cnt) in waits.items():
            self.waited[eng][key] = cnt
        return list(waits.values())

    def op(self, eng, fn, reads=(), writes=(), pe_accum=False):
        waits = self._deps(eng, reads, writes, pe_accum)
        self.cnt[eng] += 1
        c = self.cnt[eng]
        sem = self.sem[eng]

        def emit(h, waits=waits, fn=fn, sem=sem):
            for (s, v) in waits:
                h.wait_ge(s, v)
            fn(h).then_inc(sem, 1)
        self.ops[eng].append(emit)
        self.nins += 1
        dep = ("e", eng, c)
        for r in writes:
            r.lw = dep
            r.rd = {}
        for r in reads:
            r.rd[("e", eng)] = dep
        return dep

    def dma(self, q, out_ap, in_ap, reads=(), writes=(), sbreg=None, **kw):
        assert sbreg is not None
        if sbreg.dsem is None:
            if self.free_dsems:
                sbreg.dsem, sbreg.dcnt = self.free_dsems.pop()
            else:
                sbreg.dsem = self.stack.enter_context(self.nc.semaphore("d%d" % len(self.dsems)))
                sbreg.dcnt = 0
            self.dsems.append(sbreg)
        waits = self._deps(q, reads, writes)
        sbreg.dcnt += 16
        c = sbreg.dcnt
        semh = sbreg.dsem

        def emit(h, waits=waits, semh=semh, out_ap=out_ap, in_ap=in_ap, kw=kw):
            for (s, v) in waits:
                h.wait_ge(s, v)
            h.dma_start(out=out_ap, in_=in_ap, **kw).then_inc(semh, 16)
        self.ops[q].append(emit)
        self.nins += 1
        dep = ("d", sbreg, c)
        for r in writes:
            r.lw = dep
            r.rd = {}
        for r in reads:
            r.rd[("d", id(sbreg))] = dep
        return dep

    def barrier(self, release=True):
        waits = []
        for e in self.ENG:
            if e != "sp" and self.cnt[e] > self.waited["sp"].get(("e", e), 0):
                waits.append((self.sem[e], self.cnt[e]))
                self.waited["sp"][("e", e)] = self.cnt[e]
        for r in self.dsems:
            key = ("d", id(r))
            if r.dcnt > self.waited["sp"].get(key, 0):
                waits.append((r.dsem, r.dcnt))
        self.cnt["sp"] += 1
        c = self.cnt["sp"]
        sem = self.sem["sp"]

        def emit(h, waits=waits, sem=sem):
            for (s, v) in waits:
                h.wait_ge(s, v)
            h.nop().then_inc(sem, 1)
        self.ops["sp"].append(emit)
        for e in self.ENG:
            if e == "sp":
                continue

            def emit2(h, sem=sem, c=c):
                h.wait_ge(sem, c)
            self.ops[e].append(emit2)
            self.waited[e][("e", "sp")] = c
            for e2 in self.ENG:
                if e2 != "sp":
                    self.waited[e][("e", e2)] = max(self.waited[e].get(("e", e2), 0), self.cnt[e2])
        for e2 in self.ENG:
            if e2 != "sp":
                self.waited["sp"][("e", e2)] = self.cnt[e2]
        for r in self.regs:
            r.lw = None
            r.rd = {}
        if release:
            for r in self.dsems:
                self.free_dsems.append((r.dsem, r.dcnt))
                r.dsem = None
                r.dcnt = 0
            self.dsems = []
            for e in self.ENG:
                self.waited[e] = {k: v for k, v in self.waited[e].items() if k[0] == "e"}
            self.regs = []

    def finish(self):
        nc = self.nc
        ops = self.ops
        with nc.Block() as block:
            @block.tensor
            def _(h):
                for f in ops["pe"]:
                    f(h)

            @block.scalar
            def _(h):
                for f in ops["act"]:
                    f(h)

            @block.vector
            def _(h):
                for f in ops["dve"]:
                    f(h)

            @block.gpsimd
            def _(h):
                for f in ops["pool"]:
                    f(h)

            @block.sync
            def _(h):
                for f in ops["sp"]:
                    f(h)


class Arena:
    def __init__(self, S, tens, nf32):
        self.S = S
        self.t = tens
        self.n = nf32
        self.off = 0

    def reset(self):
        self.off = 0

    def _view(self, ap, shape):
        if len(shape) == 2:
            return ap
        if len(shape) == 3:
            return ap.rearrange("p (a b) -> p a b", a=shape[1])
        if len(shape) == 4:
            return ap.rearrange("p (a b c) -> p a b c", a=shape[1], b=shape[2])
        raise ValueError

    def f32(self, name, shape, nreg=None):
        n = int(np.prod(shape[1:]))
        assert self.off + n <= self.n, (name, self.off, n, self.n)
        ap = self.t[0:shape[0], self.off:self.off + n]
        self.off += n
        v = self._view(ap, shape)
        if nreg is None:
            return v, self.S.reg(name)
        return v, [self.S.reg("%s%d" % (name, i)) for i in range(nreg)]

    def bf16(self, name, shape, nreg=None):
        n = int(np.prod(shape[1:]))
        n32 = (n + 1) // 2
        assert self.off + n32 <= self.n, (name, self.off, n32, self.n)
        ap = self.t[0:shape[0], self.off:self.off + n32].bitcast(BF16)[:, 0:n]
        self.off += n32
        v = self._view(ap, shape)
        if nreg is None:
            return v, self.S.reg(name)
        return v, [self.S.reg("%s%d" % (name, i)) for i in range(nreg)]


def run_interleaved(gens, width, stagger):
    gens = list(gens)
    active = []
    nxt = 0
    while nxt < len(gens) or active:
        if nxt < len(gens) and len(active) < width and (not active or active[-1][1] >= stagger):
            active.append([gens[nxt], 0])
            nxt += 1
        for item in list(active):
            try:
                next(item[0])
                item[1] += 1
            except StopIteration:
                active.remove(item)

PARAM_SHAPES = {
    "attn_norm_w": [2, 1024], "w_in": [2, 1024, 3592], "diff_lambda": [2, 4, 64], "diff_subln_w": [2, 128],
    "hgrn_lb_logits": [2, 256], "hgrn_norm_w": [2, 64], "dn_conv_w": [2, 4, 768], "dn_A_log": [2, 4],
    "dn_dt_bias": [2, 4], "dn_norm_w": [2, 64], "w_out": [2, 1024, 1024], "ffn_norm_w": [2, 1024],
    "ffn_w_up": [2, 1024, 5632], "ffn_conv_w": [2, 3, 5632], "ffn_conv_b": [2, 5632], "ffn_w_down": [2, 2816, 1024],
    "final_norm_w": [1024],
}
FM_COLS = [i * 128 for i in range(8)] + [1536, 1664, 1792, 1920] + [2560 + i * 128 for i in range(6)]
TM_BLOCKS = [(1024, 512, 0), (1792, 512, 512), (2304, 256, 1024), (3328, 264, 1280)]
PTM_W = 1544


def make_consts():
    c = {}
    c["ident"] = np.eye(128, dtype=np.float32)
    s = np.arange(128)[:, None]
    t = np.arange(128)[None, :]
    c["tri_incl"] = (s <= t).astype(np.float32)
    same16 = (s // 16 == t // 16)
    c["L16"] = (same16 & (s <= t)).astype(np.float32)
    c["SU16"] = (same16 & (s > t)).astype(np.float32)
    c["blk16"] = (np.arange(128)[:, None] // 16 == np.arange(8)[None, :]).astype(np.float32)
    s6 = np.arange(64)[:, None]
    t6 = np.arange(64)[None, :]
    c["L64"] = (s6 <= t6).astype(np.float32)
    c["SL64"] = (s6 < t6).astype(np.float32)
    c["LT64"] = (s6 >= t6).astype(np.float32)
    c["SLT64"] = (s6 > t6).astype(np.float32)
    c["ones64"] = np.ones((64, 64), np.float32)
    oh = np.zeros((4, 4, 64), np.float32)
    for h in range(4):
        oh[h, h, :] = 1.0
    c["onehot"] = oh.reshape(4, 256)
    return c


class K:
    pass


def build_program(debug=False, stop_after=None, nlayers=2, only=None):
    nc = bass.Bass("TRN2", target_bir_lowering=False)
    k = K()
    k.nc = nc
    k.debug = debug
    k.x = nc.dram_tensor("x", [T, D], F32, kind="ExternalInput").ap()
    k.p = {n: nc.dram_tensor(n, s, F32, kind="ExternalInput").ap() for n, s in PARAM_SHAPES.items()}
    consts = make_consts()
    k.c = {n: nc.dram_tensor("c_" + n, list(v.shape), F32, kind="ExternalInput").ap() for n, v in consts.items()}
    k.out = nc.dram_tensor("out", [T, D], F32, kind="ExternalOutput").ap()
    skind = "ExternalOutput" if debug else "Internal"
    k.mix_in = nc.dram_tensor("mix_in", [T, D], F32, kind="ExternalInput").ap() if debug == "mixin" else None

    def scratch(name, shape, dt):
        return nc.dram_tensor(name, shape, dt, kind=skind).ap()
    k.H = scratch("s_H", [T, D], F32)
    k.PFMb = scratch("s_PFMb", [1024, T], BF16)
    k.PFMf = scratch("s_PFMf", [1280, T], F32)
    k.PTM = scratch("s_PTM", [T, PTM_W], F32)
    k.MIX = scratch("s_MIX", [T, D], F32)
    k.ACTS = scratch("s_ACTS", [DFF, T], BF16)
    k.GQT = scratch("s_GQT", [256, T], F32)
    k.GKT = scratch("s_GKT", [256, T], F32)
    k.GKtm = scratch("s_GKtm", [T, 256], F32)
    k.GVtm = scratch("s_GVtm", [T, 256], F32)
    k.HK = scratch("s_HK", [T, 256], BF16)
    k.GO = scratch("s_GO", [T, 256], F32)

    with ExitStack() as st:
        S = Sched(nc, st)
        k.S = S
        ARENA_N = 51000
        arena_t = st.enter_context(nc.sbuf_tensor("arena", [128, ARENA_N], F32))
        k.A = Arena(S, arena_t, ARENA_N)
        cst_t = st.enter_context(nc.sbuf_tensor("cst", [128, 2048], F32))
        k.CA = Arena(S, cst_t, 2048)
        k.banks = [st.enter_context(nc.psum_tensor("bank%d" % i, [128, 512], F32)) for i in range(8)]
        setup_consts(k)
        src = k.x
        for l in range(nlayers):
            if only is None or "inproj" in only:
                phase_inproj(k, l, src)
            if stop_after == ("inproj", l):
                break
            if only is None or "attn" in only:
                phase_attn(k, l)
            if stop_after == ("attn", l):
                break
            if only is None or "hgrn" in only:
                phase_hgrn(k, l)
            if stop_after == ("hgrn", l):
                break
            if only is None or "gdn" in only:
                phase_gdn(k, l)
            if stop_after == ("gdn", l):
                break
            if only is None or "ffn" in only:
                phase_outproj_ffn(k, l, src, last=(l == nlayers - 1))
            if stop_after == ("ffn", l):
                break
            src = k.H
        S.barrier()
        S.finish()
    return nc


def setup_consts(k):
    S, CA = k.S, k.CA
    k.ident, k.identr = CA.f32("ident", [128, 128])
    S.dma("sp", k.ident, k.c["ident"], writes=[k.identr], sbreg=k.identr)
    k.identb, k.identbr = CA.bf16("identb", [128, 128])
    S.op("dve", lambda h: h.tensor_copy(k.identb, k.ident), reads=[k.identr], writes=[k.identbr])
    k.trib, k.tribr = CA.bf16("trib", [128, 128])
    tmp, tmpr = CA.f32("tmpc", [128, 128])
    S.dma("sp", tmp, k.c["tri_incl"], writes=[tmpr], sbreg=tmpr)
    S.op("dve", lambda h: h.tensor_copy(k.trib, tmp), reads=[tmpr], writes=[k.tribr])
    for n in ("L16", "SU16"):
        ap, r = CA.f32(n, [128, 128])
        S.dma("sp", ap, k.c[n], writes=[r], sbreg=r)
        setattr(k, n, ap)
        setattr(k, n + "r", r)
    k.blk16, k.blk16r = CA.f32("blk16", [128, 8])
    S.dma("sp", k.blk16, k.c["blk16"], writes=[k.blk16r], sbreg=k.blk16r)
    for n in ("L64", "SL64", "LT64", "SLT64", "ones64"):
        ap, r = CA.f32(n, [64, 64])
        S.dma("sp", ap, k.c[n], writes=[r], sbreg=r)
        setattr(k, n, ap)
        setattr(k, n + "r", r)
    k.onehot, k.onehotr = CA.f32("onehot", [4, 4, 64])
    S.dma("sp", k.onehot, k.c["onehot"].rearrange("a (b c) -> a b c", b=4), writes=[k.onehotr], sbreg=k.onehotr)
    k.ssF, k.ssFr = CA.f32("ssF", [128, NT])
    k.ssA, k.ssAr = CA.f32("ssA", [128, NT])
    k.epst, k.epsr = CA.f32("epst", [128, 1])
    S.op("pool", lambda h: h.memset(k.epst, EPS), writes=[k.epsr])
    S.barrier(release=False)
    k.creg = [k.identr, k.identbr, k.tribr, k.L16r, k.SU16r, k.blk16r, k.L64r, k.SL64r, k.LT64r, k.SLT64r, k.ones64r, k.onehotr, k.epsr]


def rstd_from_ss(k, rs, rsr, ss, ssr, n, eng_small="dve"):
    S = k.S
    S.op("dve", lambda h: h.tensor_scalar(rs, ss, 1.0 / n, EPS, ALU.mult, ALU.add), reads=[ssr], writes=[rsr])
    S.op("act", lambda h: h.activation(rs, rs, AF.Ln), reads=[rsr], writes=[rsr])
    S.op("act", lambda h: h.activation(rs, rs, AF.Exp, scale=-0.5), reads=[rsr], writes=[rsr])


def norm_transpose_pass(k, src, uT, uTr, hb, ub, ss_name, do_norm=True, ss_pre=None):
    S, A = k.S, k.A
    rs, rsr = A.f32(ss_name + "_rs", [128, NT])
    if do_norm and ss_pre is not None:
        rstd_from_ss(k, rs, rsr, ss_pre[0], ss_pre[1], D)
    elif do_norm:
        ss, ssr = A.f32(ss_name, [128, NT])
        junk, junkr = A.f32(ss_name + "_junk", [128, D])
        S.op("dve", lambda h: h.memset(ss, 0.0), writes=[ssr])
        for i in range(NT):
            ht, htr = hb[i % len(hb)]
            S.dma("sp", ht, src[i * 128:(i + 1) * 128, :], writes=[htr], sbreg=htr)
            S.op("act", lambda h, ht=ht, i=i: h.activation(junk, ht, AF.Square, accum_out=ss[:, i:i + 1]),
                 reads=[htr, ssr], writes=[junkr, ssr])
        rstd_from_ss(k, rs, rsr, ss, ssr, D)

    def block(tb):
        for i in range(tb * 4, tb * 4 + 4):
            ht, htr = hb[i % len(hb)]
            S.dma("sp", ht, src[i * 128:(i + 1) * 128, :], writes=[htr], sbreg=htr)
            u, ur = ub[i % len(ub)]
            if do_norm:
                S.op("dve", lambda h, ht=ht, u=u, i=i: h.tensor_scalar(u, ht, rs[:, i:i + 1], None, ALU.mult),
                     reads=[htr, rsr], writes=[ur])
            else:
                S.op("dve", lambda h, ht=ht, u=u: h.tensor_copy(u, ht), reads=[htr], writes=[ur])
            bank = k.banks[i % 2]
            bankr = k.bankr[i % 2]
            pt = bank[:, :].bitcast(BF16)
            for c in range(8):
                S.op("pe", lambda h, c=c, u=u, pt=pt: h.transpose(pt[:, c * 128:(c + 1) * 128], u[:, c * 128:(c + 1) * 128], k.identb),
                     reads=[ur, k.identbr], writes=[bankr], pe_accum=True)
            eng = "act" if i % 2 == 0 else "dve"
            if eng == "act":
                S.op("act", lambda h, pt=pt, i=i: h.copy(uT[:, :, i * 128:(i + 1) * 128], pt.rearrange("p (c t) -> p c t", c=8)),
                     reads=[bankr], writes=[uTr[i]])
            else:
                S.op("dve", lambda h, pt=pt, i=i: h.tensor_copy(uT[:, :, i * 128:(i + 1) * 128], pt.rearrange("p (c t) -> p c t", c=8)),
                     reads=[bankr], writes=[uTr[i]])
    return block


def phase_inproj(k, l, src):
    S, A = k.S, k.A
    S.barrier()
    A.reset()
    k.bankr = [S.reg("bank%d" % i) for i in range(8)]
    uT, uTr = A.bf16("uT", [128, 8, T], nreg=NT)
    Wb, Wbr = A.bf16("Wb", [128, 8, DIN], nreg=15)
    nw, nwr = A.f32("nw", [128, 8])
    S.dma("sp", nw, k.p["attn_norm_w"][l].rearrange("(c p) -> p c", p=128), writes=[nwr], sbreg=nwr, allow_slow_non_contiguous=True)
    wst = [A.f32("wst%d" % i, [128, 8, 256]) for i in range(2)]
    for bi, c0 in enumerate(range(0, DIN, 256)):
        cw = min(256, DIN - c0)
        wt, wtr = wst[bi % 2]
        S.dma("sp", wt[:, :, :cw], k.p["w_in"][l, :, c0:c0 + cw].rearrange("(c p) e -> p c e", p=128), writes=[wtr], sbreg=wtr)
        for c in range(8):
            if c % 2 == 0:
                S.op("dve", lambda h, c=c, wt=wt, c0=c0, cw=cw: h.tensor_scalar(Wb[:, c, c0:c0 + cw], wt[:, c, :cw], nw[:, c:c + 1], None, ALU.mult),
                     reads=[wtr, nwr], writes=[Wbr[bi]])
            else:
                S.op("act", lambda h, c=c, wt=wt, c0=c0, cw=cw: h.activation(Wb[:, c, c0:c0 + cw], wt[:, c, :cw], AF.Copy, scale=nw[:, c:c + 1]),
                     reads=[wtr, nwr], writes=[Wbr[bi]])
    hb = [A.f32("hb%d" % i, [128, D]) for i in range(3)]
    ub = [A.bf16("ub%d" % i, [128, D]) for i in range(2)]
    block = norm_transpose_pass(k, src, uT, uTr, hb, ub, "ssA", ss_pre=((k.ssA, k.ssAr) if l > 0 else None))
    fms = [A.f32("fms%d" % i, [128, 512]) for i in range(4)]
    tms = [A.f32("tms%d" % i, [128, PTM_W]) for i in range(2)]
    cnt = {"acc": 0, "fm": 0, "tm": 0, "ev": 0}

    def wreg(c0, cw):
        return [Wbr[b] for b in range(c0 // 256, (c0 + cw - 1) // 256 + 1)]

    def mm_block(tb):
        for ci, co in enumerate(FM_COLS):
            bank = k.banks[2 + cnt["acc"] % 4]
            bankr = k.bankr[2 + cnt["acc"] % 4]
            cnt["acc"] += 1
            for c in range(8):
                S.op("pe", lambda h, c=c, co=co, bank=bank: h.matmul(bank[:, :], Wb[:, c, co:co + 128], uT[:, c, tb * 512:(tb + 1) * 512], start=(c == 0), stop=(c == 7)),
                     reads=wreg(co, 128) + uTr[tb * 4:tb * 4 + 4], writes=[bankr], pe_accum=(c > 0))
            st_, str_ = fms[cnt["fm"] % 4]
            cnt["fm"] += 1
            if ci < 8:
                stv = st_[:, 0:256].bitcast(BF16)
                dst = k.PFMb[ci * 128:(ci + 1) * 128, tb * 512:(tb + 1) * 512]
            else:
                stv = st_
                dst = k.PFMf[(ci - 8) * 128:(ci - 7) * 128, tb * 512:(tb + 1) * 512]
            eng = "act" if cnt["ev"] % 2 == 0 else "dve"
            cnt["ev"] += 1
            if eng == "act":
                S.op("act", lambda h, stv=stv, bank=bank: h.copy(stv, bank[:, :]), reads=[bankr], writes=[str_])
            else:
                S.op("dve", lambda h, stv=stv, bank=bank: h.tensor_copy(stv, bank[:, :]), reads=[bankr], writes=[str_])
            S.dma("pool", dst, stv, reads=[str_], sbreg=str_)
        for i in range(tb * 4, tb * 4 + 4):
            st_, str_ = tms[cnt["tm"] % 2]
            cnt["tm"] += 1
            for (sc, w, dc) in TM_BLOCKS:
                bank = k.banks[2 + cnt["acc"] % 4]
                bankr = k.bankr[2 + cnt["acc"] % 4]
                cnt["acc"] += 1
                for c in range(8):
                    S.op("pe", lambda h, c=c, sc=sc, w=w, bank=bank, i=i: h.matmul(bank[:, :w], uT[:, c, i * 128:(i + 1) * 128], Wb[:, c, sc:sc + w], start=(c == 0), stop=(c == 7)),
                         reads=wreg(sc, w) + [uTr[i]], writes=[bankr], pe_accum=(c > 0))
                eng = "act" if cnt["ev"] % 2 == 0 else "dve"
                cnt["ev"] += 1
                if eng == "act":
                    S.op("act", lambda h, st_=st_, bank=bank, w=w, dc=dc: h.copy(st_[:, dc:dc + w], bank[:, :w]), reads=[bankr], writes=[str_])
                else:
                    S.op("dve", lambda h, st_=st_, bank=bank, w=w, dc=dc: h.tensor_copy(st_[:, dc:dc + w], bank[:, :w]), reads=[bankr], writes=[str_])
            S.dma("pool", k.PTM[i * 128:(i + 1) * 128, :], st_, reads=[str_], sbreg=str_)

    block(0)
    for tb in range(8):
        if tb + 1 < 8:
            block(tb + 1)
        mm_block(tb)

def phase_attn(k, l):
    S, A = k.S, k.A
    S.barrier()
    A.reset()
    k.bankr = [S.reg("bank%d" % i) for i in range(8)]
    lambda_init = 0.8 - 0.6 * math.exp(-0.3 * l)
    lp, lpr = A.f32("lp", [128, 4, 64])
    S.dma("sp", lp, k.p["diff_lambda"][l].rearrange("a b -> (a b)").partition_broadcast(128).rearrange("p (a b) -> p a b", a=4),
          writes=[lpr], sbreg=lpr)
    pr, prr = A.f32("lpp", [128, 2, 64])
    S.op("dve", lambda h: h.tensor_tensor(pr[:, 0, :], lp[:, 0, :], lp[:, 1, :], ALU.mult), reads=[lpr], writes=[prr])
    S.op("dve", lambda h: h.tensor_tensor(pr[:, 1, :], lp[:, 2, :], lp[:, 3, :], ALU.mult), reads=[lpr], writes=[prr])
    ls, lsr = A.f32("ls", [128, 2])
    S.op("dve", lambda h: h.tensor_reduce(ls, pr, AX.X, ALU.add), reads=[prr], writes=[lsr])
    S.op("act", lambda h: h.activation(ls, ls, AF.Exp), reads=[lsr], writes=[lsr])
    lam, lamr = A.f32("lam", [128, 1])
    S.op("dve", lambda h: h.tensor_tensor(lam, ls[:, 0:1], ls[:, 1:2], ALU.subtract), reads=[lsr], writes=[lamr])
    S.op("dve", lambda h: h.tensor_scalar(lam, lam, lambda_init, None, ALU.add), reads=[lamr], writes=[lamr])
    wsub, wsubr = A.f32("wsub", [128, 128])
    S.dma("sp", wsub, k.p["diff_subln_w"][l].partition_broadcast(128), writes=[wsubr], sbreg=wsubr)
    S.op("dve", lambda h: h.tensor_scalar(wsub, wsub, 1.0 - lambda_init, None, ALU.mult), reads=[wsubr], writes=[wsubr])

    kTb = [A.bf16("kT%d" % i, [128, T]) for i in range(2)]
    qzb = [[A.bf16("qz%d_%d" % (i, m), [128, T]) for m in range(2)] for i in range(2)]
    for i in range(2):
        S.op("pool", lambda h, i=i: h.memset(qzb[i][0][0][64:128, :], 0.0), writes=[qzb[i][0][1]])
        S.op("pool", lambda h, i=i: h.memset(qzb[i][1][0][0:64, :], 0.0), writes=[qzb[i][1][1]])
    Vf, Vfr = A.f32("Vf", [128, NT, 128])
    Vab = [A.bf16("Va%d" % i, [128, NT, 129]) for i in range(2)]
    PTs = [A.bf16("PT%d" % i, [128, 512]) for i in range(4)]
    Oacc = [A.f32("Oacc%d" % i, [128, 4, 129]) for i in range(2)]
    ot, otr = A.f32("ot", [128, 4, 128])
    t1, t1r = A.f32("t1", [128, 4, 128])
    outb = [A.f32("outb%d" % i, [128, 4, 128]) for i in range(2)]
    r01, r01r = A.f32("r01", [128, 2, 4])
    ssq, ssqr = A.f32("ssq", [128, 4])
    rsq, rsqr = A.f32("rsq", [128, 4])
    cnt = {"s": 0, "pt": 0, "po": 0, "ob": 0}

    def load_head(hh):
        kT, kTr = kTb[hh % 2]
        Va, Var = Vab[hh % 2]
        for m in range(2):
            qz, qzr = qzb[hh % 2][m]
            S.dma("sp", qz[m * 64:(m + 1) * 64, :], k.PFMb[hh * 128 + m * 64:hh * 128 + (m + 1) * 64, :], writes=[qzr], sbreg=qzr)
        S.dma("sp", kT, k.PFMb[512 + hh * 128:512 + (hh + 1) * 128, :], writes=[kTr], sbreg=kTr)
        S.dma("sp", Vf, k.PTM[:, hh * 128:(hh + 1) * 128].rearrange("(n p) e -> p n e", p=128), writes=[Vfr], sbreg=Vfr)
        S.op("pool", lambda h: h.tensor_copy(Va[:, :, 0:128], Vf), reads=[Vfr], writes=[Var])
        S.op("pool", lambda h: h.memset(Va[:, :, 128:129], 1.0), writes=[Var])

    steps = []

    def make_head(hh):
        kT, kTr = kTb[hh % 2]
        Va, Var = Vab[hh % 2]
        for Q in range(8):
            for m in range(2):
                qT, qTr = qzb[hh % 2][m]
                pbanks = [k.banks[2 + c] for c in range(4)]
                pbankr = [k.bankr[2 + c] for c in range(4)]
                first = [True, True, True, True]
                nj = 4 * Q + 4
                for j in range(nj):
                    i = j - 4 * Q
                    i0 = max(i, 0)
                    ncols = 512 - i0 * 128
                    qs = Q * 512 + i0 * 128
                    st = {}

                    def s_stage(st=st, j=j, i=i, ncols=ncols, qs=qs, m=m, kT=kT, qT=qT, kTr=kTr, qTr=qTr, hh=hh, Q=Q):
                        sbi = (0, 1, 6, 7)[cnt["s"] % 4]
                        sb = k.banks[sbi]
                        sbr = k.bankr[sbi]
                        cnt["s"] += 1
                        S.op("pe", lambda h: h.matmul(sb[:, :ncols], kT[:, j * 128:(j + 1) * 128], qT[:, qs:qs + ncols], start=True, stop=True),
                             reads=[kTr, qTr], writes=[sbr])
                        PT, PTr = PTs[cnt["pt"] % 4]
                        cnt["pt"] += 1
                        S.op("act", lambda h: h.activation(PT[:, :ncols], sb[:, :ncols], AF.Exp, scale=0.125), reads=[sbr], writes=[PTr])
                        if i >= 0:
                            S.op("dve", lambda h: h.tensor_tensor(PT[:, 0:128], PT[:, 0:128], k.trib, ALU.mult), reads=[PTr, k.tribr], writes=[PTr])
                        st["PT"] = (PT, PTr)

                    def pv_stage(st=st, j=j, i0=i0, Q=Q, m=m, Va=Va, Var=Var, pbanks=pbanks, pbankr=pbankr, first=first, nj=nj, hh=hh):
                        PT, PTr = st["PT"]
                        if Q == 0 and m == 0 and j == 3 and hh + 1 < 4:
                            load_head(hh + 1)
                        for c in range(i0, 4):
                            pb = pbanks[c]
                            S.op("pe", lambda h, pb=pb, c=c: h.matmul(pb[:, 0:129], PT[:, (c - i0) * 128:(c - i0 + 1) * 128], Va[:, j, :], start=(j == 0), stop=(j == 4 * Q + c)),
                                 reads=[PTr, Var], writes=[pbankr[c]], pe_accum=not first[c])
                            first[c] = False
                        if j == nj - 1:
                            Oa, Oar = Oacc[m]
                            for c in range(4):
                                S.op("dve", lambda h, c=c: h.tensor_copy(Oa[:, c, :], pbanks[c][:, 0:129]), reads=[pbankr[c]], writes=[Oar])
                            if m == 1:
                                finalize(hh, Q)
                    steps.append((s_stage, pv_stage))

    def finalize(hh, Q):
        O0, O0r = Oacc[0]
        O1, O1r = Oacc[1]
        S.op("dve", lambda h: h.reciprocal(r01[:, 0, :], O0[:, :, 128]), reads=[O0r], writes=[r01r])
        S.op("dve", lambda h: h.reciprocal(r01[:, 1, :], O1[:, :, 128]), reads=[O1r], writes=[r01r])
        S.op("dve", lambda h: h.tensor_scalar(r01[:, 1, :], r01[:, 1, :], lam[:, 0:1], None, ALU.mult), reads=[r01r, lamr], writes=[r01r])
        S.op("pool", lambda h: h.tensor_tensor(t1, O1[:, :, 0:128], r01[:, 1, :].unsqueeze(2).to_broadcast([128, 4, 128]), ALU.mult),
             reads=[O1r, r01r], writes=[t1r])
        S.op("dve", lambda h: h.tensor_tensor(ot, O0[:, :, 0:128], r01[:, 0, :].unsqueeze(2).to_broadcast([128, 4, 128]), ALU.mult),
             reads=[O0r, r01r], writes=[otr])
        S.op("dve", lambda h: h.tensor_tensor(ot, ot, t1, ALU.subtract), reads=[otr, t1r], writes=[otr])
        S.op("pool", lambda h: h.tensor_tensor(t1, ot, ot, ALU.mult), reads=[otr], writes=[t1r])
        S.op("dve", lambda h: h.tensor_reduce(ssq, t1, AX.X, ALU.add), reads=[t1r], writes=[ssqr])
        rstd_from_ss(k, rsq, rsqr, ssq, ssqr, 128)
        ob, obr = outb[cnt["ob"] % 2]
        cnt["ob"] += 1
        S.op("dve", lambda h: h.tensor_tensor(ot, ot, rsq.unsqueeze(2).to_broadcast([128, 4, 128]), ALU.mult), reads=[otr, rsqr], writes=[otr])
        S.op("pool", lambda h: h.tensor_tensor(ob, ot, wsub.unsqueeze(1).to_broadcast([128, 4, 128]), ALU.mult), reads=[otr, wsubr], writes=[obr])
        S.dma("pool", k.MIX[Q * 512:(Q + 1) * 512, hh * 128:(hh + 1) * 128].rearrange("(c p) e -> p c e", p=128), ob, reads=[obr], sbreg=obr)

    load_head(0)
    for hh in range(4):
        make_head(hh)
    LOOK = 3
    for idx in range(min(LOOK, len(steps))):
        steps[idx][0]()
    for idx in range(len(steps)):
        if idx + LOOK < len(steps):
            steps[idx + LOOK][0]()
        steps[idx][1]()

def phase_hgrn(k, l):
    S, A = k.S, k.A
    S.barrier()
    A.reset()
    k.bankr = [S.reg("bank%d" % i) for i in range(8)]
    NCH = T // 16
    r16, r16r = A.f32("r16", [128, T])
    S.op("pool", lambda h: h.memset(r16, 1.0), writes=[r16r])
    S.op("pool", lambda h: h.memset(r16.rearrange("p (n c) -> p n c", c=16)[:, :, 0:1], 0.0), writes=[r16r])
    lbf, lbfr = A.f32("lbf", [128, 2])
    omf, omfr = A.f32("omf", [128, 2])
    lbt, lbtr = A.f32("lbt", [128, 256])
    omt, omtr = A.f32("omt", [128, 256])
    if l == 0:
        S.op("pool", lambda h: h.memset(lbf, 0.0), writes=[lbfr])
        S.op("pool", lambda h: h.memset(lbt, 0.0), writes=[lbtr])
    else:
        lg, lgr = A.f32("lg", [128, 2, 2])
        for li in range(2):
            S.dma("sp", lg[:, li, :], k.p["hgrn_lb_logits"][li].rearrange("(c p) -> p c", p=128), writes=[lgr], sbreg=lgr, allow_slow_non_contiguous=True)
        S.op("dve", lambda h: h.tensor_tensor(lbf, lg[:, 1, :], lg[:, 0, :], ALU.subtract), reads=[lgr], writes=[lbfr])
        S.op("act", lambda h: h.activation(lbf, lbf, AF.Sigmoid), reads=[lbfr], writes=[lbfr])
        lg2, lg2r = A.f32("lg2", [128, 2, 256])
        S.dma("sp", lg2, k.p["hgrn_lb_logits"].rearrange("a b -> (a b)").partition_broadcast(128).rearrange("p (a b) -> p a b", a=2), writes=[lg2r], sbreg=lg2r)
        S.op("dve", lambda h: h.tensor_tensor(lbt, lg2[:, 1, :], lg2[:, 0, :], ALU.subtract), reads=[lg2r], writes=[lbtr])
        S.op("act", lambda h: h.activation(lbt, lbt, AF.Sigmoid), reads=[lbtr], writes=[lbtr])
    S.op("dve", lambda h: h.tensor_scalar(omf, lbf, -1.0, 1.0, ALU.mult, ALU.add), reads=[lbfr], writes=[omfr])
    S.op("dve", lambda h: h.tensor_scalar(omt, lbt, -1.0, 1.0, ALU.mult, ALU.add), reads=[lbtr], writes=[omtr])
    hnw, hnwr = A.f32("hnw", [128, 64])
    S.dma("sp", hnw, k.p["hgrn_norm_w"][l].partition_broadcast(128), writes=[hnwr], sbreg=hnwr)

    mark0 = A.off
    Kall, Kallr = A.f32("Kall", [128, NT, 256], nreg=4)
    Lall, Lallr = A.f32("Lall", [128, NT, 256], nreg=4)
    for q in range(4):
        sl = slice(q * 8, (q + 1) * 8)
        Kq = Kall[:, sl, :]
        Lq = Lall[:, sl, :]
        S.dma("sp", Kq, k.PTM[q * 1024:(q + 1) * 1024, 512:768].rearrange("(n p) e -> p n e", p=128), writes=[Kallr[q]], sbreg=Kallr[q])
        S.op("act", lambda h, Kq=Kq: h.activation(Kq, Kq, AF.Sigmoid), reads=[Kallr[q]], writes=[Kallr[q]])
    for q in range(4):
        sl = slice(q * 8, (q + 1) * 8)
        Kq = Kall[:, sl, :]
        Lq = Lall[:, sl, :]
        S.op("dve", lambda h, Kq=Kq: h.tensor_tensor(Kq, Kq, omt.unsqueeze(1).to_broadcast([128, 8, 256]), ALU.mult), reads=[Kallr[q], omtr], writes=[Kallr[q]])
        S.op("dve", lambda h, Kq=Kq: h.tensor_tensor(Kq, Kq, lbt.unsqueeze(1).to_broadcast([128, 8, 256]), ALU.add), reads=[Kallr[q], lbtr], writes=[Kallr[q]])
        S.op("dve", lambda h, Kq=Kq, Lq=Lq: h.tensor_scalar(Lq, Kq, 1e-30, None, ALU.max), reads=[Kallr[q]], writes=[Lallr[q]])
        S.op("act", lambda h, Lq=Lq: h.activation(Lq, Lq, AF.Ln), reads=[Lallr[q]], writes=[Lallr[q]])
    for q in range(4):
        for ii in range(8):
            i = q * 8 + ii
            bank = k.banks[i % 4]
            bankr = k.bankr[i % 4]
            S.op("pe", lambda h, bank=bank, i=i: h.matmul(bank[:, 0:256], k.SU16, Lall[:, i, :], start=True, stop=True), reads=[Lallr[q], k.SU16r], writes=[bankr])
            S.op("act", lambda h, bank=bank, i=i: h.activation(Lall[:, i, :], bank[:, 0:256], AF.Exp), reads=[bankr], writes=[Lallr[q]])
        sl = slice(q * 8, (q + 1) * 8)
        Kq = Kall[:, sl, :]
        Lq = Lall[:, sl, :]
        S.op("dve", lambda h, Kq=Kq: h.tensor_scalar(Kq, Kq, -1.0, 1.0, ALU.mult, ALU.add), reads=[Kallr[q]], writes=[Kallr[q]])
        Kbq = Lq.rearrange("p n e -> p (n e)")[:, 0:1024].bitcast(BF16).rearrange("p (n e) -> p n e", n=8)
        S.op("dve", lambda h, Kq=Kq, Lq=Lq: h.tensor_tensor(Kq, Kq, Lq, ALU.mult), reads=[Kallr[q], Lallr[q]], writes=[Kallr[q]])
        S.op("pool", lambda h, Kq=Kq, Kbq=Kbq: h.tensor_copy(Kbq, Kq), reads=[Kallr[q]], writes=[Lallr[q]])
        S.dma("pool", k.HK[q * 1024:(q + 1) * 1024, :].rearrange("(n p) e -> p n e", p=128), Kbq, reads=[Lallr[q]], sbreg=Lallr[q])
    S.barrier(release=False)
    A.off = mark0

    QtT, QtTr = A.f32("QtT", [128, T])
    KtT, KtTr = A.f32("KtT", [128, T])
    Dn, Dnr = A.f32("Dn", [128, NCH])
    U, Ur = A.f32("U", [128, 64, NCH])
    Ur2 = S.reg("Uscan")
    Oall, Oallr = A.f32("Oall", [128, NT, 128])
    ssh, sshr = A.f32("ssh", [128, NT, 2])
    rsh, rshr = A.f32("rsh", [128, NT, 2])
    vex = [A.bf16("vex%d" % i, [128, 8, 64]) for i in range(4)]
    Qb, Qbr = A.bf16("Qb", [128, T])
    Kb, Kbr = A.bf16("Kb", [128, T])
    ats = [A.bf16("ats%d" % i, [128, 128]) for i in range(4)]
    ots = [A.f32("ots%d" % i, [64, 128]) for i in range(4)]
    ktl = [A.bf16("ktl%d" % i, [128, 128]) for i in range(2)]
    vtl = [A.f32("vtl%d" % i, [128, 128]) for i in range(3)]
    vbl = [A.bf16("vbl%d" % i, [128, 128]) for i in range(3)]
    mark = A.off
    fpT, fpTr = A.f32("fpT", [128, T])
    BT, BTr = A.f32("BT", [128, T])
    for c in range(2):
        S.dma("sp", fpT, k.PFMf[256 + c * 128:256 + (c + 1) * 128, :], writes=[fpTr], sbreg=fpTr)
        S.dma("sp", QtT, k.PFMf[c * 128:(c + 1) * 128, :], writes=[QtTr], sbreg=QtTr)
        S.op("act", lambda h: h.activation(fpT, fpT, AF.Sigmoid), reads=[fpTr], writes=[fpTr])
        S.op("dve", lambda h, c=c: h.tensor_scalar(fpT, fpT, omf[:, c:c + 1], lbf[:, c:c + 1], ALU.mult, ALU.add), reads=[fpTr, omfr, lbfr], writes=[fpTr])
        S.op("pool", lambda h: h.tensor_scalar(KtT, fpT, -1.0, 1.0, ALU.mult, ALU.add), reads=[fpTr], writes=[KtTr])
        S.op("dve", lambda h: h.tensor_scalar(fpT, fpT, 1e-30, None, ALU.max), reads=[fpTr], writes=[fpTr])
        S.op("act", lambda h: h.activation(fpT, fpT, AF.Ln), reads=[fpTr], writes=[fpTr])
        S.op("dve", lambda h: h.tensor_tensor_scan(BT, r16, fpT, 0.0, ALU.mult, ALU.add), reads=[fpTr, r16r], writes=[BTr])
        S.op("act", lambda h: h.activation(Dn, BT.rearrange("p (n c) -> p n c", c=16)[:, :, 15], AF.Exp), reads=[BTr], writes=[Dnr])
        S.op("act", lambda h: h.activation(fpT, BT, AF.Exp), reads=[BTr], writes=[fpTr])
        S.op("act", lambda h: h.activation(BT, BT, AF.Exp, scale=-1.0), reads=[BTr], writes=[BTr])
        S.op("dve", lambda h: h.tensor_tensor(KtT, KtT, BT, ALU.mult), reads=[KtTr, BTr], writes=[KtTr])
        S.op("act", lambda h: h.activation(QtT, QtT, AF.Silu), reads=[QtTr], writes=[QtTr])
        S.op("dve", lambda h: h.tensor_tensor(QtT, QtT, fpT, ALU.mult), reads=[QtTr, fpTr], writes=[QtTr])
        S.op("pool", lambda h: h.tensor_copy(Qb, QtT), reads=[QtTr], writes=[Qbr])
        S.op("pool", lambda h: h.tensor_copy(Kb, KtT), reads=[KtTr], writes=[Kbr])
        def u_unit(i, c=c):
            u = i % 2
            kt, ktr = ktl[i % 2]
            vt, vtr = vtl[i % 3]
            S.dma("sp", kt, k.HK[i * 128:(i + 1) * 128, c * 128:(c + 1) * 128], writes=[ktr], sbreg=ktr)
            S.dma("sp", vt, k.PTM[i * 128:(i + 1) * 128, 768 + c * 128:768 + (c + 1) * 128], writes=[vtr], sbreg=vtr)
            for hl in range(2):
                vx, vxr = vex[u * 2 + hl]
                eng = "dve" if hl == 0 else "pool"
                S.op(eng, lambda h, vx=vx, hl=hl: h.tensor_tensor(
                    vx, vt[:, hl * 64:(hl + 1) * 64].unsqueeze(1).to_broadcast([128, 8, 64]),
                    k.blk16.unsqueeze(2).to_broadcast([128, 8, 64]), ALU.mult), reads=[vtr, k.blk16r], writes=[vxr])
            yield
            for hl in range(2):
                vx, vxr = vex[u * 2 + hl]
                bank = k.banks[u * 2 + hl]
                bankr = k.bankr[u * 2 + hl]
                S.op("pe", lambda h, bank=bank, vx=vx: h.matmul(bank[:, :], kt, vx.rearrange("p n v -> p (n v)"), start=True, stop=True),
                     reads=[ktr, vxr], writes=[bankr])
            yield
            for hl in range(2):
                bank = k.banks[u * 2 + hl]
                bankr = k.bankr[u * 2 + hl]
                p0 = hl * 64
                if hl == 0:
                    S.op("act", lambda h, bank=bank, p0=p0: h.copy(U[p0:p0 + 64, :, i * 8:(i + 1) * 8], bank[p0:p0 + 64, :].rearrange("d (n v) -> d v n", n=8)), reads=[bankr], writes=[Ur])
                else:
                    S.op("dve", lambda h, bank=bank, p0=p0: h.tensor_copy(U[p0:p0 + 64, :, i * 8:(i + 1) * 8], bank[p0:p0 + 64, :].rearrange("d (n v) -> d v n", n=8)), reads=[bankr], writes=[Ur])
        run_interleaved([u_unit(i) for i in range(NT)], 2, 1)
        for v in range(64):
            S.op("dve", lambda h, v=v: h.tensor_tensor_scan(U[:, v, :], Dn, U[:, v, :], 0.0, ALU.mult, ALU.add), reads=[Ur, Dnr], writes=[Ur2] if v else [Ur, Ur2])
        Ub = k.A.t[0:128, mark:mark + 8192].bitcast(BF16).rearrange("p (v n) -> p v n", n=NCH)
        Ubr = S.reg("Ub")
        S.op("dve", lambda h: h.tensor_copy(Ub, U), reads=[Ur, Ur2, fpTr, BTr, QtTr, KtTr, Qbr, Kbr], writes=[Ubr, fpTr, BTr])
        def o_unit(i, c=c):
            u = i % 2
            vt, vtr = vtl[i % 3]
            vb, vbr = vbl[i % 3]
            S.dma("sp", vt, k.PTM[i * 128:(i + 1) * 128, 768 + c * 128:768 + (c + 1) * 128], writes=[vtr], sbreg=vtr)
            S.op("pool", lambda h: h.tensor_copy(vb, vt), reads=[vtr], writes=[vbr])
            for hl in range(2):
                p0 = hl * 64
                ba = k.banks[u * 2 + hl]
                bar = k.bankr[u * 2 + hl]
                S.op("pe", lambda h, ba=ba, p0=p0: h.matmul(ba[:, 0:128], Kb[p0:p0 + 64, i * 128:(i + 1) * 128], Qb[p0:p0 + 64, i * 128:(i + 1) * 128], start=True, stop=True),
                     reads=[Kbr, Qbr], writes=[bar])
            yield
            for hl in range(2):
                ba = k.banks[u * 2 + hl]
                bar = k.bankr[u * 2 + hl]
                at, atr = ats[u * 2 + hl]
                S.op("dve", lambda h, at=at, ba=ba: h.tensor_tensor(at, ba[:, 0:128], k.L16, ALU.mult), reads=[bar, k.L16r], writes=[atr])
            yield
            for hl in range(2):
                p0 = hl * 64
                at, atr = ats[u * 2 + hl]
                bo = k.banks[4 + u * 2 + hl]
                bor = k.bankr[4 + u * 2 + hl]
                S.op("pe", lambda h, bo=bo, at=at, p0=p0: h.matmul(bo[0:64, 0:128], vb[:, p0:p0 + 64], at, start=True, stop=False),
                     reads=[vbr, atr], writes=[bor])
                for n in range(8):
                    g = i * 8 + n
                    if g == 0:
                        continue
                    S.op("pe", lambda h, bo=bo, g=g, n=n, p0=p0: h.matmul(bo[0:64, n * 16:(n + 1) * 16], Ub[p0:p0 + 64, :, g - 1], Qb[p0:p0 + 64, g * 16:(g + 1) * 16], start=False, stop=(n == 7)),
                         reads=[Ubr, fpTr, BTr, Qbr], writes=[bor], pe_accum=True)
            yield
            for hl in range(2):
                bo = k.banks[4 + u * 2 + hl]
                bor = k.bankr[4 + u * 2 + hl]
                ot, otr = ots[u * 2 + hl]
                S.op("act", lambda h, ot=ot, bo=bo: h.copy(ot, bo[0:64, 0:128]), reads=[bor], writes=[otr])
            yield
            for hl in range(2):
                ot, otr = ots[u * 2 + hl]
                bt = k.banks[u * 2 + hl]
                btr = k.bankr[u * 2 + hl]
                S.op("pe", lambda h, bt=bt, ot=ot: h.transpose(bt[:, 0:64], ot, k.ident[0:64, 0:64]), reads=[otr, k.identr], writes=[btr])
            yield
            for hl in range(2):
                p0 = hl * 64
                bt = k.banks[u * 2 + hl]
                btr = k.bankr[u * 2 + hl]
                S.op("dve", lambda h, bt=bt, p0=p0: h.tensor_copy(Oall[:, i, p0:p0 + 64], bt[:, 0:64]), reads=[btr], writes=[Oallr])
        run_interleaved([o_unit(i) for i in range(NT)], 2, 3)
        Osq, Osqr = fpT.rearrange("p (n e) -> p n e", n=NT), fpTr
        Gall, Gallr = BT.rearrange("p (n e) -> p n e", n=NT), BTr
        S.dma("sp", Gall, k.PTM[:, 1024 + c * 128:1024 + (c + 1) * 128].rearrange("(n p) e -> p n e", p=128), writes=[Gallr], sbreg=Gallr)
        S.op("act", lambda h: h.activation(Gall, Gall, AF.Silu), reads=[Gallr], writes=[Gallr])
        S.op("pool", lambda h: h.tensor_tensor(Osq, Oall, Oall, ALU.mult), reads=[Oallr], writes=[Osqr])
        S.op("dve", lambda h: h.tensor_reduce(ssh.rearrange("p n g -> p (n g)"), Osq.rearrange("p n (g v) -> p (n g) v", g=2), AX.X, ALU.add), reads=[Osqr], writes=[sshr])
        rstd_from_ss(k, rsh.rearrange("p n g -> p (n g)"), rshr, ssh.rearrange("p n g -> p (n g)"), sshr, 64)
        S.op("dve", lambda h: h.tensor_tensor(Oall.rearrange("p n (g v) -> p (n g) v", g=2), Oall.rearrange("p n (g v) -> p (n g) v", g=2),
                                              rsh.rearrange("p n g -> p (n g)").unsqueeze(2).to_broadcast([128, NT * 2, 64]), ALU.mult), reads=[Oallr, rshr], writes=[Oallr])
        S.op("pool", lambda h: h.tensor_tensor(Oall.rearrange("p n (g v) -> p (n g) v", g=2), Oall.rearrange("p n (g v) -> p (n g) v", g=2),
                                               hnw.unsqueeze(1).to_broadcast([128, NT * 2, 64]), ALU.mult), reads=[Oallr, hnwr], writes=[Oallr])
        S.op("dve", lambda h: h.tensor_tensor(Oall, Oall, Gall, ALU.mult), reads=[Oallr, Gallr], writes=[Oallr])
        S.dma("pool", k.MIX[:, 512 + c * 128:512 + (c + 1) * 128].rearrange("(n p) e -> p n e", p=128), Oall, reads=[Oallr], sbreg=Oallr)

def phase_gdn(k, l):
    S, A = k.S, k.A
    S.barrier()
    A.reset()
    k.bankr = [S.reg("bank%d" % i) for i in range(8)]
    cwd, cwdr = A.f32("cwd", [128, 4, 6])
    for j in range(4):
        S.dma("sp", cwd[:, j, :], k.p["dn_conv_w"][l, j].rearrange("(c p) -> p c", p=128), writes=[cwdr], sbreg=cwdr, allow_slow_non_contiguous=True)
    bo, bor = A.f32("bo128", [128, 128])
    S.op("pool", lambda h: h.memset(bo, 0.0), writes=[bor])
    S.op("pool", lambda h: h.memset(bo[0:64, 0:64], 1.0), writes=[bor])
    S.op("pool", lambda h: h.memset(bo[64:128, 64:128], 1.0), writes=[bor])
    Xb = [A.f32("Xc%d" % i, [128, T + 3]) for i in range(2)]
    Yb = [A.f32("Yc%d" % i, [128, T]) for i in range(2)]
    sqb = [A.f32("sqc%d" % i, [128, T]) for i in range(2)]
    tms = [A.f32("tmsg%d" % i, [128, NT, 128]) for i in range(2)]
    for i in range(2):
        S.op("pool", lambda h, i=i: h.memset(Xb[i][0][:, 0:3], 0.0), writes=[Xb[i][1]])
    for c in range(6):
        g1_chunk(k, c, Xb[c % 2], Yb[c % 2], sqb[c % 2], tms, cwd, cwdr, bo, bor)
    if False:
        X, Xr = Xb[0]; Y, Yr = Yb[0]; sq, sqr = sqb[0]; c = 0
        pass

    S.barrier()
    A.reset()
    k.bankr = [S.reg("bank%d" % i) for i in range(8)]
    NC = 64
    HP = 2
    ab, abr = A.f32("ab", [128, NC, 4])
    alog, alogr = A.f32("alog", [128, HP])
    dtb, dtbr = A.f32("dtb", [128, HP])
    for hb in range(2):
        ph = slice(hb * 64, (hb + 1) * 64)
        for which in range(2):
            c0 = 1536 + which * 4 + hb
            for hp in range(HP):
                S.dma("sp", ab[ph, :, which * 2 + hp:which * 2 + hp + 1], k.PTM[:, c0 + 2 * hp:c0 + 2 * hp + 1].rearrange("(n p) e -> p n e", p=64),
                      writes=[abr], sbreg=abr, allow_slow_non_contiguous=True)
        S.dma("sp", alog[ph, :], k.p["dn_A_log"][l].rearrange("(hp hb) -> hb hp", hb=2)[hb].partition_broadcast(64), writes=[alogr], sbreg=alogr, allow_slow_non_contiguous=True)
        S.dma("sp", dtb[ph, :], k.p["dn_dt_bias"][l].rearrange("(hp hb) -> hb hp", hb=2)[hb].partition_broadcast(64), writes=[dtbr], sbreg=dtbr, allow_slow_non_contiguous=True)
    S.op("act", lambda h: h.activation(alog, alog, AF.Exp), reads=[alogr], writes=[alogr])
    S.op("dve", lambda h: h.tensor_scalar(alog, alog, -1.0, None, ALU.mult), reads=[alogr], writes=[alogr])
    L64b, L64br = A.f32("L64b", [128, 128])
    on64b, on64br = A.f32("on64b", [128, 128])
    S.op("pool", lambda h: h.memset(L64b, 0.0), writes=[L64br])
    S.op("pool", lambda h: h.memset(on64b, 0.0), writes=[on64br])
    mk = {}
    for nm in ("L64", "SLT64"):
        mk[nm] = A.f32(nm + "x2", [128, 64])
    idx2, idx2r = A.f32("idx2", [128, 64])
    for hb in range(2):
        ph = slice(hb * 64, (hb + 1) * 64)
        S.dma("sp", L64b[ph, hb * 64:(hb + 1) * 64], k.c["L64"], writes=[L64br], sbreg=L64br)
        S.dma("sp", on64b[ph, hb * 64:(hb + 1) * 64], k.c["ones64"], writes=[on64br], sbreg=on64br)
        for nm in ("L64", "SLT64"):
            S.dma("sp", mk[nm][0][ph, :], k.c[nm], writes=[mk[nm][1]], sbreg=mk[nm][1])
        S.dma("sp", idx2[ph, :], k.c["ident"][0:64, 0:64], writes=[idx2r], sbreg=idx2r)
    L64x, L64xr = mk["L64"]
    SLT64x, SLT64xr = mk["SLT64"]

    def sc(name):
        return A.f32(name, [128, NC, HP])
    g, gr = sc("g")
    beta, betar = sc("beta")
    t0, t0r = sc("t0")
    t1, t1r = sc("t1")
    gc, gcr = sc("gc")
    gl, glr = sc("gl")
    egc, egcr = sc("egc")
    ekd, ekdr = sc("ekd")
    egl, eglr = sc("egl")
    nbeta, nbetar = sc("nbeta")
    begc, begcr = sc("begc")
    S.op("dve", lambda h: h.tensor_tensor(t0, ab[:, :, 0:2], dtb.unsqueeze(1).to_broadcast([128, NC, HP]), ALU.add), reads=[abr, dtbr], writes=[t0r])
    S.op("dve", lambda h: h.tensor_scalar(t1, t0, -1.0, None, ALU.mult), reads=[t0r], writes=[t1r])
    S.op("dve", lambda h: h.tensor_tensor(t1, t1, t0, ALU.max), reads=[t0r, t1r], writes=[t1r])
    S.op("act", lambda h: h.activation(t1, t1, AF.Exp, scale=-1.0), reads=[t1r], writes=[t1r])
    S.op("dve", lambda h: h.tensor_scalar(t1, t1, 1.0, None, ALU.add), reads=[t1r], writes=[t1r])
    S.op("act", lambda h: h.activation(t1, t1, AF.Ln), reads=[t1r], writes=[t1r])
    S.op("dve", lambda h: h.tensor_scalar(t0, t0, 0.0, None, ALU.max), reads=[t0r], writes=[t0r])
    S.op("dve", lambda h: h.tensor_tensor(t0, t0, t1, ALU.add), reads=[t0r, t1r], writes=[t0r])
    S.op("dve", lambda h: h.tensor_tensor(g, t0, alog.unsqueeze(1).to_broadcast([128, NC, HP]), ALU.mult), reads=[t0r, alogr], writes=[gr])
    S.op("act", lambda h: h.activation(beta, ab[:, :, 2:4], AF.Sigmoid), reads=[abr], writes=[betar])
    S.op("pe", lambda h: h.matmul(k.banks[0][:, 0:NC * HP], L64b, g.rearrange("p n h -> p (n h)"), start=True, stop=True), reads=[gr, L64br], writes=[k.bankr[0]])
    S.op("pe", lambda h: h.matmul(k.banks[1][:, 0:NC * HP], on64b, g.rearrange("p n h -> p (n h)"), start=True, stop=True), reads=[gr, on64br], writes=[k.bankr[1]])
    S.op("dve", lambda h: h.tensor_copy(gc.rearrange("p n h -> p (n h)"), k.banks[0][:, 0:NC * HP]), reads=[k.bankr[0]], writes=[gcr])
    S.op("dve", lambda h: h.tensor_copy(gl.rearrange("p n h -> p (n h)"), k.banks[1][:, 0:NC * HP]), reads=[k.bankr[1]], writes=[glr])
    S.op("act", lambda h: h.activation(egc, gc, AF.Exp), reads=[gcr], writes=[egcr])
    S.op("act", lambda h: h.activation(egl, gl, AF.Exp), reads=[glr], writes=[eglr])
    S.op("dve", lambda h: h.tensor_tensor(ekd, gl, gc, ALU.subtract), reads=[glr, gcr], writes=[ekdr])
    S.op("act", lambda h: h.activation(ekd, ekd, AF.Exp), reads=[ekdr], writes=[ekdr])
    S.op("dve", lambda h: h.tensor_scalar(nbeta, beta, -1.0, None, ALU.mult), reads=[betar], writes=[nbetar])
    S.op("dve", lambda h: h.tensor_tensor(begc, beta, egc, ALU.mult), reads=[betar, egcr], writes=[begcr])

    CB = 4
    NSL = CB * HP
    NSET = 2
    NB = 8 // CB
    qTg = [A.f32("qTg%d" % i, [128, HP, 512]) for i in range(2)]
    kTg = [A.f32("kTg%d" % i, [128, HP, 512]) for i in range(2)]
    ktg = [A.f32("ktg%d" % i, [128, 8, HP, 64]) for i in range(2)]
    vtg = [A.f32("vtg%d" % i, [128, 8, HP, 64]) for i in range(2)]
    og = [A.f32("og%d" % i, [128, 8, HP, 64]) for i in range(2)]
    qTb = [A.bf16("qTb%d" % i, [128, HP, 512]) for i in range(2)]
    kTb = [A.bf16("kTb%d" % i, [128, HP, 512]) for i in range(2)]

    def bt(name):
        return A.f32(name, [128, NSL, 64])

    def btb(name):
        return A.bf16(name, [128, NSL, 64])
    sets = []
    for si in range(NSET):
        d = {}
        for nm in ("raw", "mx", "mn", "tA", "tB", "ut", "gB"):
            d[nm] = bt(nm + str(si))
        for nm in ("Mt0", "Mt1", "Nt0", "Nt1", "Tt", "aqk", "Ru", "Rw", "kdec", "wT"):
            d[nm] = btb(nm + str(si))
        sets.append(d)
    Sst, Sstr = A.f32("Sst", [128, HP, 64])
    Ssb, Ssbr = A.bf16("Ssb", [128, HP, 64])
    vnew, vnewr = A.bf16("vnew", [128, HP, 64])
    otmp, otmpr = A.f32("otmp", [128, HP, 64])
    S.op("pool", lambda h: h.memset(Sst, 0.0), writes=[Sstr])
    S.op("pool", lambda h: h.memset(Ssb, 0.0), writes=[Ssbr])
    PH = [slice(0, 64), slice(64, 128)]

    def bc_s(ap2):
        return ap2.unsqueeze(2).to_broadcast([128, NSL, 64])

    def mask_b(m):
        return m.unsqueeze(1).to_broadcast([128, NSL, 64])

    def pv(bank):
        return bank[:, :].rearrange("p (a b) -> p a b", b=64)

    def pv2(bank):
        return bank[:, 0:HP * 64].rearrange("p (a b) -> p a b", b=64)

    def group_load(grp):
        qTt, qTtr = qTg[grp % 2]
        kTt, kTtr = kTg[grp % 2]
        ktt, kttr = ktg[grp % 2]
        vtt, vttr = vtg[grp % 2]
        qTtb, qTtbr = qTb[grp % 2]
        kTtb, kTtbr = kTb[grp % 2]
        tsl = slice(grp * 512, (grp + 1) * 512)
        S.dma("sp", qTt, k.GQT[:, tsl].rearrange("(hp p) t -> p hp t", p=128), writes=[qTtr], sbreg=qTtr)
        S.dma("sp", kTt, k.GKT[:, tsl].rearrange("(hp p) t -> p hp t", p=128), writes=[kTtr], sbreg=kTtr)
        S.op("pool", lambda h: h.tensor_copy(qTtb, qTt), reads=[qTtr], writes=[qTtbr])
        S.op("pool", lambda h: h.tensor_copy(kTtb, kTt), reads=[kTtr], writes=[kTtbr])
        for hb in range(2):
            for hp in range(HP):
                hcol = slice((2 * hp + hb) * 64, (2 * hp + hb + 1) * 64)
                S.dma("sp", ktt[PH[hb], :, hp, :], k.GKtm[tsl, hcol].rearrange("(n p) e -> p n e", p=64), writes=[kttr], sbreg=kttr)
                S.dma("sp", vtt[PH[hb], :, hp, :], k.GVtm[tsl, hcol].rearrange("(n p) e -> p n e", p=64), writes=[vttr], sbreg=vttr)

    def do_batch(gb):
        grp, b = divmod(gb, NB)
        if gb == 0:
            group_load(0)
            group_load(1)
        d = sets[gb % NSET]
        bs = (gb % NSET) * 4
        Bk = [k.banks[bs + i] for i in range(4)]
        Br = [k.bankr[bs + i] for i in range(4)]
        qTt, qTtr = qTb[grp % 2]
        kTt, kTtr = kTb[grp % 2]
        ktt, kttr = ktg[grp % 2]
        vtt, vttr = vtg[grp % 2]
        ogt, ogtr = og[grp % 2]
        raw, rawr = d["raw"]; mx, mxr = d["mx"]; mn, mnr = d["mn"]; tA, tAr = d["tA"]; tB, tBr = d["tB"]; gB, gBr = d["gB"]
        Mt = [d["Mt0"], d["Mt1"]]; Nt = [d["Nt0"], d["Nt1"]]
        Tt, Ttr = d["Tt"]; aqk, aqkr = d["aqk"]; Ru, Rur = d["Ru"]; Rw, Rwr = d["Rw"]; kdec, kdecr = d["kdec"]
        ut, utr = d["ut"]; wT, wTr = d["wT"]
        n0 = grp * 8 + b * CB
        ln0 = b * CB
        sc_sl = slice(n0, n0 + CB)

        def sview(ap):
            return ap[:, sc_sl, :].rearrange("p n h -> p (n h)")
        S.op("pool", lambda h: h.tensor_copy(gB, bc_s(sview(g))), reads=[gr], writes=[gBr])
        for cb in range(CB):
            tl = slice((ln0 + cb) * 64, (ln0 + cb + 1) * 64)
            for hp in range(HP):
                sl_ = cb * HP + hp
                col = slice(sl_ * 64, (sl_ + 1) * 64)
                for hb in range(2):
                    ph = PH[hb]
                    S.op("pe", lambda h, hp=hp, tl=tl, col=col, ph=ph: h.matmul(Bk[0][ph, col], kTt[ph, hp, tl], kTt[ph, hp, tl], start=True, stop=True), reads=[kTtr], writes=[Br[0]])
                    S.op("pe", lambda h, sl_=sl_, col=col, ph=ph: h.matmul(Bk[1][ph, col], gB[ph, sl_, :], L64x[ph, :], start=True, stop=True), reads=[gBr, L64xr], writes=[Br[1]])
                    S.op("pe", lambda h, hp=hp, tl=tl, col=col, ph=ph: h.matmul(Bk[2][ph, col], kTt[ph, hp, tl], qTt[ph, hp, tl], start=True, stop=True), reads=[kTtr, qTtr], writes=[Br[2]])
        yield
        S.op("dve", lambda h: h.tensor_tensor(raw, pv(Bk[1]), bc_s(sview(gc)), ALU.subtract), reads=[Br[1], gcr], writes=[rawr])
        yield
        S.op("dve", lambda h: h.tensor_scalar(mx, raw, 0.0, None, ALU.max), reads=[rawr], writes=[mxr])
        S.op("pool", lambda h: h.tensor_scalar(mn, raw, 0.0, None, ALU.min), reads=[rawr], writes=[mnr])
        yield
        S.op("act", lambda h: h.activation(mx, mx, AF.Exp, scale=-1.0), reads=[mxr], writes=[mxr])
        S.op("act", lambda h: h.activation(mn, mn, AF.Exp), reads=[mnr], writes=[mnr])
        vt3 = vtt[:, ln0:ln0 + CB, :, :].rearrange("p c h v -> p (c h) v")
        kt3 = ktt[:, ln0:ln0 + CB, :, :].rearrange("p c h v -> p (c h) v")
        S.op("pool", lambda h: h.tensor_tensor(Ru, vt3, bc_s(sview(beta)), ALU.mult), reads=[vttr, betar], writes=[Rur])
        S.op("pool", lambda h: h.tensor_tensor(Rw, kt3, bc_s(sview(begc)), ALU.mult), reads=[kttr, begcr], writes=[Rwr])
        S.op("pool", lambda h: h.tensor_tensor(kdec, kt3, bc_s(sview(ekd)), ALU.mult), reads=[kttr, ekdr], writes=[kdecr])
        yield
        S.op("dve", lambda h: h.tensor_tensor(tA, mx, mask_b(SLT64x), ALU.mult), reads=[mxr, SLT64xr], writes=[tAr])
        S.op("pool", lambda h: h.tensor_tensor(tB, mn, mask_b(L64x), ALU.mult), reads=[mnr, L64xr], writes=[tBr])
        yield
        S.op("dve", lambda h: h.tensor_tensor(tA, tA, bc_s(sview(nbeta)), ALU.mult), reads=[tAr, nbetar], writes=[tAr])
        yield
        M0, M0r = Mt[0]
        S.op("dve", lambda h: h.tensor_tensor(M0, pv(Bk[0]), tA, ALU.mult), reads=[Br[0], tAr], writes=[M0r])
        S.op("dve", lambda h: h.tensor_tensor(aqk, pv(Bk[2]), tB, ALU.mult), reads=[Br[2], tBr], writes=[aqkr])
        yield
        B3b = Bk[3][:, 0:256].bitcast(BF16)
        for e in range(NSL):
            for hb in range(2):
                ph = PH[hb]
                S.op("pe", lambda h, e=e, ph=ph: h.transpose(B3b[ph, e * 64:(e + 1) * 64], M0[ph, e, :], k.identb[ph, ph]), reads=[M0r, k.identbr], writes=[Br[3]])
        yield
        N0, N0r = Nt[0]
        S.op("act", lambda h: h.copy(N0, B3b.rearrange("p (a b) -> p a b", b=64)), reads=[Br[3]], writes=[N0r])
        yield
        S.op("dve", lambda h: h.tensor_tensor(Tt, N0, mask_b(idx2), ALU.add), reads=[N0r, idx2r], writes=[Ttr])
        for j in range(1, 6):
            Mp, Mpr = Mt[(j - 1) % 2]
            Np, Npr = Nt[(j - 1) % 2]
            Mc, Mcr = Mt[j % 2]
            Nc_, Ncr = Nt[j % 2]
            for e in range(NSL):
                col = slice(e * 64, (e + 1) * 64)
                for hb in range(2):
                    ph = PH[hb]
                    S.op("pe", lambda h, e=e, col=col, Np=Np, Mp=Mp, ph=ph: h.matmul(Bk[0][ph, col], Np[ph, e, :], Mp[ph, e, :], start=True, stop=True), reads=[Npr, Mpr], writes=[Br[0]])
            if j < 5:
                for e in range(NSL):
                    col = slice(e * 64, (e + 1) * 64)
                    for hb in range(2):
                        ph = PH[hb]
                        S.op("pe", lambda h, e=e, col=col, Np=Np, Mp=Mp, ph=ph: h.matmul(Bk[1][ph, col], Mp[ph, e, :], Np[ph, e, :], start=True, stop=True), reads=[Npr, Mpr], writes=[Br[1]])
            yield
            S.op("act", lambda h, Mc=Mc: h.copy(Mc, pv(Bk[0])), reads=[Br[0]], writes=[Mcr])
            if j < 5:
                S.op("act", lambda h, Nc_=Nc_: h.copy(Nc_, pv(Bk[1])), reads=[Br[1]], writes=[Ncr])
            yield
            for e in range(NSL):
                col = slice(e * 64, (e + 1) * 64)
                for hb in range(2):
                    ph = PH[hb]
                    S.op("pe", lambda h, e=e, col=col, Mc=Mc, ph=ph: h.matmul(Bk[2][ph, col], Mc[ph, e, :], Tt[ph, e, :], start=True, stop=True), reads=[Mcr, Ttr], writes=[Br[2]])
            yield
            S.op("dve", lambda h: h.tensor_tensor(Tt, Tt, pv(Bk[2]), ALU.add), reads=[Br[2], Ttr], writes=[Ttr])
            yield
        for e in range(NSL):
            col = slice(e * 64, (e + 1) * 64)
            for hb in range(2):
                ph = PH[hb]
                S.op("pe", lambda h, e=e, col=col, ph=ph: h.matmul(Bk[0][ph, col], Tt[ph, e, :], Ru[ph, e, :], start=True, stop=True), reads=[Ttr, Rur], writes=[Br[0]])
                S.op("pe", lambda h, e=e, col=col, ph=ph: h.matmul(Bk[1][ph, col], Rw[ph, e, :], Tt[ph, e, :], start=True, stop=True), reads=[Ttr, Rwr], writes=[Br[1]])
        yield
        S.op("act", lambda h: h.copy(ut, pv(Bk[0])), reads=[Br[0]], writes=[utr])
        S.op("act", lambda h: h.copy(wT, pv(Bk[1])), reads=[Br[1]], writes=[wTr])
        yield
        for cb in range(CB):
            n = n0 + cb
            tl = slice((ln0 + cb) * 64, (ln0 + cb + 1) * 64)
            es = slice(cb * HP, cb * HP + HP)
            for hp in range(HP):
                col = slice(hp * 64, (hp + 1) * 64)
                for hb in range(2):
                    ph = PH[hb]
                    S.op("pe", lambda h, hp=hp, col=col, cb=cb, ph=ph: h.matmul(Bk[3][ph, col], wT[ph, cb * HP + hp, :], Ssb[ph, hp, :], start=True, stop=True), reads=[wTr, Ssbr], writes=[Br[3]])
                    S.op("pe", lambda h, hp=hp, col=col, tl=tl, ph=ph: h.matmul(Bk[0][ph, col], qTt[ph, hp, tl], Ssb[ph, hp, :], start=True, stop=True), reads=[qTtr, Ssbr], writes=[Br[0]])
            yield
            S.op("dve", lambda h, es=es: h.tensor_tensor(vnew, ut[:, es, :], pv2(Bk[3]), ALU.subtract), reads=[utr, Br[3]], writes=[vnewr])
            yield
            for hp in range(HP):
                col = slice(hp * 64, (hp + 1) * 64)
                for hb in range(2):
                    ph = PH[hb]
                    S.op("pe", lambda h, hp=hp, col=col, cb=cb, ph=ph: h.matmul(Bk[2][ph, col], kdec[ph, cb * HP + hp, :], vnew[ph, hp, :], start=True, stop=True), reads=[kdecr, vnewr], writes=[Br[2]])
                    S.op("pe", lambda h, hp=hp, col=col, cb=cb, ph=ph: h.matmul(Bk[1][ph, col], aqk[ph, cb * HP + hp, :], vnew[ph, hp, :], start=True, stop=True), reads=[aqkr, vnewr], writes=[Br[1]])
            S.op("pool", lambda h, n=n: h.tensor_tensor(Sst, Sst, egl[:, n, :].unsqueeze(2).to_broadcast([128, HP, 64]), ALU.mult), reads=[Sstr, eglr], writes=[Sstr])
            yield
            S.op("dve", lambda h: h.tensor_tensor(Sst, Sst, pv2(Bk[2]), ALU.add), reads=[Br[2], Sstr], writes=[Sstr])
            S.op("act", lambda h: h.copy(Ssb, Sst), reads=[Sstr], writes=[Ssbr])
            S.op("dve", lambda h, n=n: h.tensor_tensor(otmp, pv2(Bk[0]), egc[:, n, :].unsqueeze(2).to_broadcast([128, HP, 64]), ALU.mult), reads=[Br[0], egcr], writes=[otmpr])
            S.op("dve", lambda h, cb=cb: h.tensor_tensor(ogt[:, ln0 + cb, :, :], otmp, pv2(Bk[1]), ALU.add), reads=[Br[1], otmpr], writes=[ogtr])
            yield
        if b == NB - 1:
            tsl = slice(grp * 512, (grp + 1) * 512)
            for hb in range(2):
                for hp in range(HP):
                    hcol = slice((2 * hp + hb) * 64, (2 * hp + hb + 1) * 64)
                    S.dma("pool", k.GO[tsl, hcol].rearrange("(n p) e -> p n e", p=64), ogt[PH[hb], :, hp, :], reads=[ogtr], sbreg=ogtr)
            if grp + 2 < NC // 8:
                group_load(grp + 2)

    run_interleaved([do_batch(gb) for gb in range(NC // CB)], NSET, 26)

    S.barrier()
    A.reset()
    Oall, Oallr = A.f32("OallG", [128, NT, 256])
    Zall, Zallr = A.f32("ZallG", [128, NT, 256])
    Osq, Osqr = A.f32("OsqG", [128, NT, 256])
    ssg, ssgr = A.f32("ssg", [128, NT * 4])
    rsg, rsgr = A.f32("rsg", [128, NT * 4])
    dnw, dnwr = A.f32("dnw", [128, 64])
    S.dma("sp", dnw, k.p["dn_norm_w"][l].partition_broadcast(128), writes=[dnwr], sbreg=dnwr)
    S.dma("sp", Oall, k.GO.rearrange("(n p) e -> p n e", p=128), writes=[Oallr], sbreg=Oallr)
    S.dma("sp", Zall, k.PTM[:, 1280:1536].rearrange("(n p) e -> p n e", p=128), writes=[Zallr], sbreg=Zallr)
    S.op("act", lambda h: h.activation(Osq, Oall, AF.Square), reads=[Oallr], writes=[Osqr])
    S.op("act", lambda h: h.activation(Zall, Zall, AF.Silu), reads=[Zallr], writes=[Zallr])
    S.op("pool", lambda h: h.tensor_tensor(Zall.rearrange("p n (g v) -> p (n g) v", g=4), Zall.rearrange("p n (g v) -> p (n g) v", g=4),
                                           dnw.unsqueeze(1).to_broadcast([128, NT * 4, 64]), ALU.mult), reads=[Zallr, dnwr], writes=[Zallr])
    S.op("dve", lambda h: h.tensor_reduce(ssg, Osq.rearrange("p n (g v) -> p (n g) v", g=4), AX.X, ALU.add), reads=[Osqr], writes=[ssgr])
    rstd_from_ss(k, rsg, rsgr, ssg, ssgr, 64)
    O3 = Oall.rearrange("p n (g v) -> p (n g) v", g=4)
    S.op("dve", lambda h: h.tensor_tensor(O3, O3, rsg.unsqueeze(2).to_broadcast([128, NT * 4, 64]), ALU.mult), reads=[Oallr, rsgr], writes=[Oallr])
    S.op("dve", lambda h: h.tensor_tensor(Oall, Oall, Zall, ALU.mult), reads=[Oallr, Zallr], writes=[Oallr])
    S.dma("pool", k.MIX[:, 768:1024].rearrange("(n p) e -> p n e", p=128), Oall, reads=[Oallr], sbreg=Oallr)


def g1_chunk(k, c, Xt, Yt, sqt, tms, cwd, cwdr, bo, bor):
    S = k.S
    X, Xr = Xt
    Y, Yr = Yt
    sq, sqr = sqt
    if True:
        S.dma("sp", X[:, 3:T + 3], k.PFMf[512 + c * 128:512 + (c + 1) * 128, :], writes=[Xr], sbreg=Xr)
        S.op("dve", lambda h, c=c: h.tensor_scalar(Y, X[:, 3:T + 3], cwd[:, 3, c:c + 1], None, ALU.mult), reads=[Xr, cwdr], writes=[Yr])
        S.op("dve", lambda h, c=c: h.scalar_tensor_tensor(Y, X[:, 2:T + 2], cwd[:, 2, c:c + 1], Y, ALU.mult, ALU.add), reads=[Xr, cwdr, Yr], writes=[Yr])
        S.op("dve", lambda h, c=c: h.scalar_tensor_tensor(Y, X[:, 1:T + 1], cwd[:, 1, c:c + 1], Y, ALU.mult, ALU.add), reads=[Xr, cwdr, Yr], writes=[Yr])
        S.op("dve", lambda h, c=c: h.scalar_tensor_tensor(Y, X[:, 0:T], cwd[:, 0, c:c + 1], Y, ALU.mult, ALU.add), reads=[Xr, cwdr, Yr], writes=[Yr])
        S.op("act", lambda h: h.activation(Y, Y, AF.Silu), reads=[Yr], writes=[Yr])
        if c < 4:
            S.op("act", lambda h: h.activation(sq, Y, AF.Square), reads=[Yr], writes=[sqr])
            for tb in range(8):
                bank = k.banks[tb % 2]
                bankr = k.bankr[tb % 2]
                S.op("pe", lambda h, bank=bank, tb=tb: h.matmul(bank[:, :], bo, sq[:, tb * 512:(tb + 1) * 512], start=True, stop=True), reads=[bor, sqr], writes=[bankr])
                S.op("act", lambda h, bank=bank, tb=tb: h.activation(X[:, 3 + tb * 512:3 + (tb + 1) * 512], bank[:, :], AF.Ln, bias=k.epst[:, 0:1]), reads=[bankr, k.epsr], writes=[Xr])
            S.op("act", lambda h: h.activation(X[:, 3:T + 3], X[:, 3:T + 3], AF.Exp, scale=-0.5), reads=[Xr], writes=[Xr])
            if c < 2:
                S.op("dve", lambda h: h.scalar_tensor_tensor(Y, Y, 0.125, X[:, 3:T + 3], ALU.mult, ALU.mult), reads=[Yr, Xr], writes=[Yr])
                S.dma("pool", k.GQT[c * 128:(c + 1) * 128, :], Y, reads=[Yr], sbreg=Yr)
            else:
                S.op("dve", lambda h: h.tensor_tensor(Y, Y, X[:, 3:T + 3], ALU.mult), reads=[Yr, Xr], writes=[Yr])
                S.dma("pool", k.GKT[(c - 2) * 128:(c - 1) * 128, :], Y, reads=[Yr], sbreg=Yr)
        if c >= 2:
            st_, str_ = tms[c % 2]
            for i in range(NT):
                bank = k.banks[2 + i % 4]
                bankr = k.bankr[2 + i % 4]
                S.op("pe", lambda h, bank=bank, i=i: h.transpose(bank[:, 0:128], Y[:, i * 128:(i + 1) * 128], k.ident), reads=[Yr, k.identr], writes=[bankr])
                if i % 2 == 0:
                    S.op("act", lambda h, bank=bank, i=i, st_=st_: h.copy(st_[:, i, :], bank[:, 0:128]), reads=[bankr], writes=[str_])
                else:
                    S.op("dve", lambda h, bank=bank, i=i, st_=st_: h.tensor_copy(st_[:, i, :], bank[:, 0:128]), reads=[bankr], writes=[str_])
            dst = k.GKtm if c < 4 else k.GVtm
            cc = (c - 2) % 2
            S.dma("pool", dst[:, cc * 128:(cc + 1) * 128].rearrange("(n p) e -> p n e", p=128), st_, reads=[str_], sbreg=str_)


def phase_outproj_ffn(k, l, src, last):
    S, A = k.S, k.A
    MIXsrc = k.mix_in if getattr(k, "mix_in", None) is not None else k.MIX
    S.barrier()
    A.reset()
    k.bankr = [S.reg("bank%d" % i) for i in range(8)]
    u2T, u2Tr = A.bf16("u2T", [128, 8, T], nreg=NT)
    mixT, mixTr = A.bf16("mixT", [128, 8, T], nreg=NT)
    Wob, Wobr = A.bf16("Wob", [128, 8, D], nreg=8)
    wst = [A.f32("wst%d" % i, [128, D]) for i in range(2)]
    for c in range(8):
        wt, wtr = wst[c % 2]
        S.dma("sp", wt, k.p["w_out"][l, c * 128:(c + 1) * 128, :], writes=[wtr], sbreg=wtr)
        S.op("pool", lambda h, wt=wt, c=c: h.tensor_copy(Wob[:, c, :], wt), reads=[wtr], writes=[Wobr[c]])
    hb = [A.f32("hb%d" % i, [128, D]) for i in range(3)]
    ub = [A.bf16("ub%d" % i, [128, D]) for i in range(2)]
    block = norm_transpose_pass(k, MIXsrc, mixT, mixTr, hb, ub, "ssO", do_norm=False)
    hres = [A.f32("hres%d" % i, [128, D]) for i in range(2)]
    hnew = [A.f32("hnew%d" % i, [128, D]) for i in range(2)]
    cnt = {"acc": 0}
    junkO, junkOr = A.f32("junkO", [128, D])
    S.op("pool", lambda h: h.memset(k.ssF, 0.0), writes=[k.ssFr])
    ub2 = [A.bf16("ub2_%d" % i, [128, D]) for i in range(2)]
    rsO, rsOr = A.f32("rsO", [128, NT])
    pend = []

    def emit_u2T(item):
        i, u2, u2r = item
        bank = k.banks[6 + i % 2]
        bankr = k.bankr[6 + i % 2]
        pt = bank[:, :].bitcast(BF16)
        for c in range(8):
            S.op("pe", lambda h, c=c, u2=u2, pt=pt: h.transpose(pt[:, c * 128:(c + 1) * 128], u2[:, c * 128:(c + 1) * 128], k.identb),
                 reads=[u2r, k.identbr], writes=[bankr], pe_accum=True)
        if i % 2 == 0:
            S.op("act", lambda h, pt=pt, i=i: h.copy(u2T[:, :, i * 128:(i + 1) * 128], pt.rearrange("p (c t) -> p c t", c=8)), reads=[bankr], writes=[u2Tr[i]])
        else:
            S.op("dve", lambda h, pt=pt, i=i: h.tensor_copy(u2T[:, :, i * 128:(i + 1) * 128], pt.rearrange("p (c t) -> p c t", c=8)), reads=[bankr], writes=[u2Tr[i]])

    def oproj(tb):
        for i in range(tb * 4, tb * 4 + 4):
            hr, hrr = hres[i % 2]
            hn, hnr = hnew[i % 2]
            S.dma("sp", hr, src[i * 128:(i + 1) * 128, :], writes=[hrr], sbreg=hrr)
            for half in range(2):
                bank = k.banks[2 + cnt["acc"] % 4]
                bankr = k.bankr[2 + cnt["acc"] % 4]
                cnt["acc"] += 1
                for c in range(8):
                    S.op("pe", lambda h, c=c, bank=bank, i=i, half=half: h.matmul(bank[:, :], mixT[:, c, i * 128:(i + 1) * 128], Wob[:, c, half * 512:(half + 1) * 512], start=(c == 0), stop=(c == 7)),
                         reads=[mixTr[i], Wobr[c]], writes=[bankr], pe_accum=(c > 0))
                S.op("dve", lambda h, hn=hn, hr=hr, bank=bank, half=half: h.tensor_tensor(hn[:, half * 512:(half + 1) * 512], bank[:, :], hr[:, half * 512:(half + 1) * 512], ALU.add),
                     reads=[bankr, hrr], writes=[hnr])
            S.op("act", lambda h, hn=hn, i=i: h.activation(junkO, hn, AF.Square, accum_out=k.ssF[:, i:i + 1]), reads=[hnr, k.ssFr], writes=[junkOr, k.ssFr])
            S.dma("pool", k.H[i * 128:(i + 1) * 128, :], hn, reads=[hnr], sbreg=hnr)
            S.op("dve", lambda h, i=i: h.tensor_scalar(rsO[:, i:i + 1], k.ssF[:, i:i + 1], 1.0 / D, EPS, ALU.mult, ALU.add), reads=[k.ssFr], writes=[rsOr])
            S.op("act", lambda h, i=i: h.activation(rsO[:, i:i + 1], rsO[:, i:i + 1], AF.Ln), reads=[rsOr], writes=[rsOr])
            S.op("act", lambda h, i=i: h.activation(rsO[:, i:i + 1], rsO[:, i:i + 1], AF.Exp, scale=-0.5), reads=[rsOr], writes=[rsOr])
            u2, u2r = ub2[i % 2]
            S.op("dve", lambda h, hn=hn, u2=u2, i=i: h.tensor_scalar(u2, hn, rsO[:, i:i + 1], None, ALU.mult), reads=[hnr, rsOr], writes=[u2r])
            pend.append((i, u2, u2r))
            if len(pend) > 1:
                emit_u2T(pend.pop(0))

    block(0)
    for tb in range(8):
        if tb + 1 < 8:
            block(tb + 1)
        oproj(tb)
    while pend:
        emit_u2T(pend.pop(0))

    S.barrier()
    A.reset()
    k.bankr = [S.reg("bank%d" % i) for i in range(8)]
    uT, uTr = A.bf16("u2T", [128, 8, T], nreg=NT)
    nw, nwr = A.f32("nw2", [128, 8])
    S.dma("sp", nw, k.p["ffn_norm_w"][l].rearrange("(c p) -> p c", p=128), writes=[nwr], sbreg=nwr, allow_slow_non_contiguous=True)
    cw, cwr = A.f32("cw", [128, 3, 44])
    cb, cbr = A.f32("cb", [128, 44])
    for j in range(3):
        S.dma("sp", cw[:, j, :], k.p["ffn_conv_w"][l, j].rearrange("(c p) -> p c", p=128), writes=[cwr], sbreg=cwr, allow_slow_non_contiguous=True)
    S.dma("sp", cb, k.p["ffn_conv_b"][l].rearrange("(c p) -> p c", p=128), writes=[cbr], sbreg=cbr, allow_slow_non_contiguous=True)
    wf = [[A.f32("wf%d_%d" % (i, j), [128, 8, 128]) for j in range(2)] for i in range(2)]
    wb = [[A.bf16("wb%d_%d" % (i, j), [128, 8, 128]) for j in range(2)] for i in range(2)]
    Xs = [[A.f32("X%d_%d" % (i, j), [128, T + 2]) for j in range(2)] for i in range(2)]
    Y = [A.f32("Y%d" % j, [128, T]) for j in range(2)]
    aout = [A.bf16("aout%d" % i, [128, T]) for i in range(1)]
    for i in range(2):
        for j in range(2):
            S.op("pool", lambda h, i=i, j=j: h.memset(Xs[i][j][0][:, 0:2], 0.0), writes=[Xs[i][j][1]])
    cnt = {"acc": 0, "ev": 0}
    def load_w(f):
        for j in range(2):
            wt, wtr = wf[f % 2][j]
            wbt, wbtr = wb[f % 2][j]
            col = j * DFF + f * 128
            S.dma("sp", wt, k.p["ffn_w_up"][l, :, col:col + 128].rearrange("(c p) e -> p c e", p=128), writes=[wtr], sbreg=wtr)
            S.op("pool", lambda h, wt=wt, wbt=wbt: h.tensor_tensor(wbt, wt, nw.unsqueeze(2).to_broadcast([128, 8, 128]), ALU.mult), reads=[wtr, nwr], writes=[wbtr])

    load_w(0)
    for f in range(22):
        X = Xs[f % 2]
        if f + 1 < 22:
            load_w(f + 1)
        for tb in range(8):
            for j in range(2):
                wbt, wbtr = wb[f % 2][j]
                bank = k.banks[cnt["acc"] % 4]
                bankr = k.bankr[cnt["acc"] % 4]
                cnt["acc"] += 1
                for c in range(8):
                    S.op("pe", lambda h, c=c, bank=bank, wbt=wbt, tb=tb: h.matmul(bank[:, :], wbt[:, c, :], uT[:, c, tb * 512:(tb + 1) * 512], start=(c == 0), stop=(c == 7)),
                         reads=[wbtr] + uTr[tb * 4:tb * 4 + 4], writes=[bankr], pe_accum=(c > 0))
                Xj, Xjr = X[j]
                S.op("act", lambda h, Xj=Xj, bank=bank, tb=tb: h.copy(Xj[:, 2 + tb * 512:2 + (tb + 1) * 512], bank[:, :]), reads=[bankr], writes=[Xjr])
        for j in range(2):
            Xj, Xjr = X[j]
            Yj, Yjr = Y[j]
            ch = j * 22 + f
            S.op("dve", lambda h, Xj=Xj, Yj=Yj, ch=ch: h.tensor_scalar(Yj, Xj[:, 2:T + 2], cw[:, 2, ch:ch + 1], cb[:, ch:ch + 1], ALU.mult, ALU.add),
                 reads=[Xjr, cwr, cbr], writes=[Yjr])
            S.op("dve", lambda h, Xj=Xj, Yj=Yj, ch=ch: h.scalar_tensor_tensor(Yj, Xj[:, 1:T + 1], cw[:, 1, ch:ch + 1], Yj, ALU.mult, ALU.add),
                 reads=[Xjr, cwr, Yjr], writes=[Yjr])
            S.op("dve", lambda h, Xj=Xj, Yj=Yj, ch=ch: h.scalar_tensor_tensor(Yj, Xj[:, 0:T], cw[:, 0, ch:ch + 1], Yj, ALU.mult, ALU.add),
                 reads=[Xjr, cwr, Yjr], writes=[Yjr])
        S.op("act", lambda h: h.activation(Y[0][0], Y[0][0], AF.Silu), reads=[Y[0][1]], writes=[Y[0][1]])
        ao, aor = aout[0]
        S.op("dve", lambda h, ao=ao: h.tensor_tensor(ao, Y[0][0], Y[1][0], ALU.mult), reads=[Y[0][1], Y[1][1]], writes=[aor])
        S.dma("pool", k.ACTS[f * 128:(f + 1) * 128, :], ao, reads=[aor], sbreg=aor)

    S.barrier()
    A.reset()
    k.bankr = [S.reg("bank%d" % i) for i in range(8)]
    Wdb, Wdbr = A.bf16("Wdb", [128, 22, D], nreg=22)
    wst = [A.f32("wst%d" % i, [128, D]) for i in range(2)]
    for f in range(22):
        wt, wtr = wst[f % 2]
        S.dma("sp", wt, k.p["ffn_w_down"][l, f * 128:(f + 1) * 128, :], writes=[wtr], sbreg=wtr)
        S.op("pool" if f % 2 else "dve", lambda h, wt=wt, f=f: h.tensor_copy(Wdb[:, f, :], wt), reads=[wtr], writes=[Wdbr[f]])
    acts = [A.bf16("acts%d" % i, [128, 22, 512]) for i in range(2)]
    hres = [A.f32("hres%d" % i, [128, D]) for i in range(2)]
    hnew = [A.f32("hnew%d" % i, [128, D]) for i in range(2)]
    if not last:
        junk2, junk2r = A.f32("junk2", [128, D])
        S.op("pool", lambda h: h.memset(k.ssA, 0.0), writes=[k.ssAr])
    if last:
        fnw, fnwr = A.f32("fnw", [128, D])
        S.dma("sp", fnw, k.p["final_norm_w"].partition_broadcast(128), writes=[fnwr], sbreg=fnwr)
        junk, junkr = A.f32("junkF", [128, D])
        ssf, ssfr = A.f32("ssf", [128, NT])
        rsf, rsfr = A.f32("rsf", [128, NT])
        S.op("dve", lambda h: h.memset(ssf, 0.0), writes=[ssfr])
    cnt = {"acc": 0}
    for tb in range(8):
        ac, acr = acts[tb % 2]
        S.dma("sp", ac, k.ACTS[:, tb * 512:(tb + 1) * 512].rearrange("(f p) t -> p f t", p=128), writes=[acr], sbreg=acr)
        for ti in range(4):
            i = tb * 4 + ti
            hr, hrr = hres[i % 2]
            hn, hnr = hnew[i % 2]
            S.dma("sp", hr, k.H[i * 128:(i + 1) * 128, :], writes=[hrr], sbreg=hrr)
            for half in range(2):
                bank = k.banks[cnt["acc"] % 4]
                bankr = k.bankr[cnt["acc"] % 4]
                cnt["acc"] += 1
                for f in range(22):
                    S.op("pe", lambda h, f=f, bank=bank, ac=ac, ti=ti, half=half: h.matmul(bank[:, :], ac[:, f, ti * 128:(ti + 1) * 128], Wdb[:, f, half * 512:(half + 1) * 512], start=(f == 0), stop=(f == 21)),
                         reads=[acr, Wdbr[f]], writes=[bankr], pe_accum=(f > 0))
                S.op("dve", lambda h, hn=hn, hr=hr, bank=bank, half=half: h.tensor_tensor(hn[:, half * 512:(half + 1) * 512], bank[:, :], hr[:, half * 512:(half + 1) * 512], ALU.add),
                     reads=[bankr, hrr], writes=[hnr])
            if not last:
                S.op("act", lambda h, hn=hn, i=i: h.activation(junk2, hn, AF.Square, accum_out=k.ssA[:, i:i + 1]), reads=[hnr, k.ssAr], writes=[junk2r, k.ssAr])
                S.dma("pool", k.H[i * 128:(i + 1) * 128, :], hn, reads=[hnr], sbreg=hnr)
            else:
                S.op("act", lambda h, hn=hn, i=i: h.activation(junk, hn, AF.Square, accum_out=ssf[:, i:i + 1]), reads=[hnr, ssfr], writes=[junkr, ssfr])
                S.op("dve", lambda h, i=i: h.tensor_scalar(rsf[:, i:i + 1], ssf[:, i:i + 1], 1.0 / D, EPS, ALU.mult, ALU.add), reads=[ssfr], writes=[rsfr])
                S.op("act", lambda h, i=i: h.activation(rsf[:, i:i + 1], rsf[:, i:i + 1], AF.Ln), reads=[rsfr], writes=[rsfr])
                S.op("act", lambda h, i=i: h.activation(rsf[:, i:i + 1], rsf[:, i:i + 1], AF.Exp, scale=-0.5), reads=[rsfr], writes=[rsfr])
                S.op("dve", lambda h, hn=hn, i=i: h.scalar_tensor_tensor(hn, hn, rsf[:, i:i + 1], fnw, ALU.mult, ALU.mult), reads=[hnr, rsfr, fnwr], writes=[hnr])
                S.dma("pool", k.out[i * 128:(i + 1) * 128, :], hn, reads=[hnr], sbreg=hnr)


_NC_CACHE = {}


def kernel(**inputs):
    x = np.ascontiguousarray(np.asarray(inputs["x"], dtype=np.float32))
    if "nc" not in _NC_CACHE:
        _NC_CACHE["nc"] = build_program(debug=False, nlayers=2)
    nc = _NC_CACHE["nc"]
    consts = make_consts()
    base = {}
    for n in PARAM_SHAPES:
        base[n] = np.ascontiguousarray(np.asarray(inputs[n], dtype=np.float32))
    for n, v in consts.items():
        base["c_" + n] = v
    in_maps = []
    for c in range(8):
        m = dict(base)
        m["x"] = np.ascontiguousarray(x[c % 4])
        in_maps.append(m)
    res = run_bass_kernel_spmd(nc, in_maps, core_ids=list(range(8)))
    out = np.stack([np.asarray(res.results[b]["out"], dtype=np.float32) for b in range(4)], axis=0)
    return out
```

```python
import numpy as np
import math
from contextlib import ExitStack
import concourse.bass as bass
import concourse.mybir as mybir
from concourse.bass_utils import run_bass_kernel_spmd

F32 = mybir.dt.float32
BF16 = mybir.dt.bfloat16
AF = mybir.ActivationFunctionType
ALU = mybir.AluOpType
AX = mybir.AxisListType

T = 4096
D = 1024
NT = 32
DIN = 3592
DFF = 2816
EPS = 1e-6


class Reg:
    __slots__ = ("name", "lw", "rd", "dsem", "dcnt")

    def __init__(self, name):
        self.name = name
        self.lw = None
        self.rd = {}
        self.dsem = None
        self.dcnt = 0


class Sched:
    ENG = ("pe", "act", "dve", "pool", "sp")

    def __init__(self, nc, stack):
        self.nc = nc
        self.stack = stack
        self.sem = {e: stack.enter_context(nc.semaphore("prog_" + e)) for e in self.ENG}
        self.cnt = {e: 0 for e in self.ENG}
        self.ops = {e: [] for e in self.ENG}
        self.waited = {e: {} for e in self.ENG}
        self.dsems = []
        self.regs = []
        self.nins = 0
        self.free_dsems = []

    def reg(self, name):
        r = Reg(name)
        self.regs.append(r)
        return r

    def _need(self, eng, dep, waits):
        if dep is None:
            return
        kind, who, cnt = dep
        if kind == "e" and who == "pe" and eng == "pe":
            return
        if kind == "e":
            key = ("e", who)
            semh = self.sem[who]
        else:
            key = ("d", id(who))
            semh = who.dsem
        if self.waited[eng].get(key, 0) >= cnt:
            return
        cur = waits.get(key)
        if cur is None or cur[1] < cnt:
            waits[key] = (semh, cnt)

    def _deps(self, eng, reads, writes, pe_accum=False):
        waits = {}
        for r in reads:
            self._need(eng, r.lw, waits)
        for r in writes:
            if not (pe_accum and r.lw is not None and r.lw[0] == "e" and r.lw[1] == "pe" and eng == "pe"):
                self._need(eng, r.lw, waits)
            for d in r.rd.values():
                self._need(eng, d, waits)
        for key, (semh, cnt) in waits.items():
            self.waited[eng][key] = cnt
        return list(waits.values())

    def op(self, eng, fn, reads=(), writes=(), pe_accum=False):
        waits = self._deps(eng, reads, writes, pe_accum)
        self.cnt[eng] += 1
        c = self.cnt[eng]
        sem = self.sem[eng]

        def emit(h, waits=waits, fn=fn, sem=sem):
            for (s, v) in waits:
                h.wait_ge(s, v)
            fn(h).then_inc(sem, 1)
        self.ops[eng].append(emit)
        self.nins += 1
        dep = ("e", eng, c)
        for r in writes:
            r.lw = dep
            r.rd = {}
        for r in reads:
            r.rd[("e", eng)] = dep
        return dep

    def dma(self, q, out_ap, in_ap, reads=(), writes=(), sbreg=None, **kw):
        assert sbreg is not None
        if sbreg.dsem is None:
            if self.free_dsems:
                sbreg.dsem, sbreg.dcnt = self.free_dsems.pop()
            else:
                sbreg.dsem = self.stack.enter_context(self.nc.semaphore("d%d" % len(self.dsems)))
                sbreg.dcnt = 0
            self.dsems.append(sbreg)
        waits = self._deps(q, reads, writes)
        sbreg.dcnt += 16
        c = sbreg.dcnt
        semh = sbreg.dsem

        def emit(h, waits=waits, semh=semh, out_ap=out_ap, in_ap=in_ap, kw=kw):
            for (s, v) in waits:
                h.wait_ge(s, v)
            h.dma_start(out=out_ap, in_=in_ap, **kw).then_inc(semh, 16)
        self.ops[q].append(emit)
        self.nins += 1
        dep = ("d", sbreg, c)
        for r in writes:
            r.lw = dep
            r.rd = {}
        for r in reads:
            r.rd[("d", id(sbreg))] = dep
        return dep

    def barrier(self, release=True):
        waits = []
        for e in self.ENG:
            if e != "sp" and self.cnt[e] > self.waited["sp"].get(("e", e), 0):
                waits.append((self.sem[e], self.cnt[e]))
                self.waited["sp"][("e", e)] = self.cnt[e]
        for r in self.dsems:
            key = ("d", id(r))
            if r.dcnt > self.waited["sp"].get(key, 0):
                waits.append((r.dsem, r.dcnt))
        self.cnt["sp"] += 1
        c = self.cnt["sp"]
        sem = self.sem["sp"]

        def emit(h, waits=waits, sem=sem):
            for (s, v) in waits:
                h.wait_ge(s, v)
            h.nop().then_inc(sem, 1)
        self.ops["sp"].append(emit)
        for e in self.ENG:
            if e == "sp":
                continue

            def emit2(h, sem=sem, c=c):
                h.wait_ge(sem, c)
            self.ops[e].append(emit2)
            self.waited[e][("e", "sp")] = c
            for e2 in self.ENG:
                if e2 != "sp":
                    self.waited[e][("e", e2)] = max(self.waited[e].get(("e", e2), 0), self.cnt[e2])
        for e2 in self.ENG:
            if e2 != "sp":
                self.waited["sp"][("e", e2)] = self.cnt[e2]
        for r in self.regs:
            r.lw = None
            r.rd = {}
        if release:
            for r in self.dsems:
                self.free_dsems.append((r.dsem, r.dcnt))
                r.dsem = None
                r.dcnt = 0
            self.dsems = []
            for e in self.ENG:
                self.waited[e] = {k: v for k, v in self.waited[e].items() if k[0] == "e"}
            self.regs = []

    def finish(self):
        nc = self.nc
        ops = self.ops
        with nc.Block() as block:
            @block.tensor
            def _(h):
                for f in ops["pe"]:
                    f(h)

            @block.scalar
            def _(h):
                for f in ops["act"]:
                    f(h)

            @block.vector
            def _(h):
                for f in ops["dve"]:
                    f(h)

            @block.gpsimd
            def _(h):
                for f in ops["pool"]:
                    f(h)

            @block.sync
            def _(h):
                for f in ops["sp"]:
                    f(h)


class Arena:
    def __init__(self, S, tens, nf32):
        self.S = S
        self.t = tens
        self.n = nf32
        self.off = 0

    def reset(self):
        self.off = 0

    def _view(self, ap, shape):
        if len(shape) == 2:
            return ap
        if len(shape) == 3:
            return ap.rearrange("p (a b) -> p a b", a=shape[1])
        if len(shape) == 4:
            return ap.rearrange("p (a b c) -> p a b c", a=shape[1], b=shape[2])
        raise ValueError

    def f32(self, name, shape, nreg=None):
        n = int(np.prod(shape[1:]))
        assert self.off + n <= self.n, (name, self.off, n, self.n)
        ap = self.t[0:shape[0], self.off:self.off + n]
        self.off += n
        v = self._view(ap, shape)
        if nreg is None:
            return v, self.S.reg(name)
        return v, [self.S.reg("%s%d" % (name, i)) for i in range(nreg)]

    def bf16(self, name, shape, nreg=None):
        n = int(np.prod(shape[1:]))
        n32 = (n + 1) // 2
        assert self.off + n32 <= self.n, (name, self.off, n32, self.n)
        ap = self.t[0:shape[0], self.off:self.off + n32].bitcast(BF16)[:, 0:n]
        self.off += n32
        v = self._view(ap, shape)
        if nreg is None:
            return v, self.S.reg(name)
        return v, [self.S.reg("%s%d" % (name, i)) for i in range(nreg)]


def run_interleaved(gens, width, stagger):
    gens = list(gens)
    active = []
    nxt = 0
    while nxt < len(gens) or active:
        if nxt < len(gens) and len(active) < width and (not active or active[-1][1] >= stagger):
            active.append([gens[nxt], 0])
            nxt += 1
        for item in list(active):
            try:
                next(item[0])
                item[1] += 1
            except StopIteration:
                active.remove(item)

PARAM_SHAPES = {
    "attn_norm_w": [2, 1024], "w_in": [2, 1024, 3592], "diff_lambda": [2, 4, 64], "diff_subln_w": [2, 128],
    "hgrn_lb_logits": [2, 256], "hgrn_norm_w": [2, 64], "dn_conv_w": [2, 4, 768], "dn_A_log": [2, 4],
    "dn_dt_bias": [2, 4], "dn_norm_w": [2, 64], "w_out": [2, 1024, 1024], "ffn_norm_w": [2, 1024],
    "ffn_w_up": [2, 1024, 5632], "ffn_conv_w": [2, 3, 5632], "ffn_conv_b": [2, 5632], "ffn_w_down": [2, 2816, 1024],
    "final_norm_w": [1024],
}
FM_COLS = [i * 128 for i in range(8)] + [1536, 1664, 1792, 1920] + [2560 + i * 128 for i in range(6)]
TM_BLOCKS = [(1024, 512, 0), (1792, 512, 512), (2304, 256, 1024), (3328, 264, 1280)]
PTM_W = 1544


def make_consts():
    c = {}
    c["ident"] = np.eye(128, dtype=np.float32)
    s = np.arange(128)[:, None]
    t = np.arange(128)[None, :]
    c["tri_incl"] = (s <= t).astype(np.float32)
    same16 = (s // 16 == t // 16)
    c["L16"] = (same16 & (s <= t)).astype(np.float32)
    c["SU16"] = (same16 & (s > t)).astype(np.float32)
    c["blk16"] = (np.arange(128)[:, None] // 16 == np.arange(8)[None, :]).astype(np.float32)
    s6 = np.arange(64)[:, None]
    t6 = np.arange(64)[None, :]
    c["L64"] = (s6 <= t6).astype(np.float32)
    c["SL64"] = (s6 < t6).astype(np.float32)
    c["LT64"] = (s6 >= t6).astype(np.float32)
    c["SLT64"] = (s6 > t6).astype(np.float32)
    c["ones64"] = np.ones((64, 64), np.float32)
    oh = np.zeros((4, 4, 64), np.float32)
    for h in range(4):
        oh[h, h, :] = 1.0
    c["onehot"] = oh.reshape(4, 256)
    return c


class K:
    pass


def build_program(debug=False, stop_after=None, nlayers=2, only=None):
    nc = bass.Bass("TRN2", target_bir_lowering=False)
    k = K()
    k.nc = nc
    k.debug = debug
    k.x = nc.dram_tensor("x", [T, D], F32, kind="ExternalInput").ap()
    k.p = {n: nc.dram_tensor(n, s, F32, kind="ExternalInput").ap() for n, s in PARAM_SHAPES.items()}
    consts = make_consts()
    k.c = {n: nc.dram_tensor("c_" + n, list(v.shape), F32, kind="ExternalInput").ap() for n, v in consts.items()}
    k.out = nc.dram_tensor("out", [T, D], F32, kind="ExternalOutput").ap()
    skind = "ExternalOutput" if debug else "Internal"
    k.mix_in = nc.dram_tensor("mix_in", [T, D], F32, kind="ExternalInput").ap() if debug == "mixin" else None

    def scratch(name, shape, dt):
        return nc.dram_tensor(name, shape, dt, kind=skind).ap()
    k.H = scratch("s_H", [T, D], F32)
    k.PFMb = scratch("s_PFMb", [1024, T], BF16)
    k.PFMf = scratch("s_PFMf", [1280, T], F32)
    k.PTM = scratch("s_PTM", [T, PTM_W], F32)
    k.MIX = scratch("s_MIX", [T, D], BF16)
    k.ACTS = scratch("s_ACTS", [DFF, T], BF16)
    k.GQT = scratch("s_GQT", [256, T], F32)
    k.GKT = scratch("s_GKT", [256, T], F32)
    k.GKtm = scratch("s_GKtm", [T, 256], F32)
    k.GVtm = scratch("s_GVtm", [T, 256], F32)
    k.HK = scratch("s_HK", [T, 256], BF16)
    k.GO = scratch("s_GO", [T, 256], F32)

    with ExitStack() as st:
        S = Sched(nc, st)
        k.S = S
        ARENA_N = 51000
        arena_t = st.enter_context(nc.sbuf_tensor("arena", [128, ARENA_N], F32))
        k.A = Arena(S, arena_t, ARENA_N)
        cst_t = st.enter_context(nc.sbuf_tensor("cst", [128, 2048], F32))
        k.CA = Arena(S, cst_t, 2048)
        k.banks = [st.enter_context(nc.psum_tensor("bank%d" % i, [128, 512], F32)) for i in range(8)]
        setup_consts(k)
        src = k.x
        for l in range(nlayers):
            if only is None or "inproj" in only:
                phase_inproj(k, l, src)
            if stop_after == ("inproj", l):
                break
            if only is None or "attn" in only:
                phase_attn(k, l)
            if stop_after == ("attn", l):
                break
            if only is None or "hgrn" in only:
                phase_hgrn(k, l)
            if stop_after == ("hgrn", l):
                break
            if only is None or "gdn" in only:
                phase_gdn(k, l)
            if stop_after == ("gdn", l):
                break
            if only is None or "ffn" in only:
                phase_outproj_ffn(k, l, src, last=(l == nlayers - 1))
            if stop_after == ("ffn", l):
                break
            src = k.H
        S.barrier()
        S.finish()
    return nc


def setup_consts(k):
    S, CA = k.S, k.CA
    k.ident, k.identr = CA.f32("ident", [128, 128])
    S.dma("sp", k.ident, k.c["ident"], writes=[k.identr], sbreg=k.identr)
    k.identb, k.identbr = CA.bf16("identb", [128, 128])
    S.op("dve", lambda h: h.tensor_copy(k.identb, k.ident), reads=[k.identr], writes=[k.identbr])
    k.trib, k.tribr = CA.bf16("trib", [128, 128])
    tmp, tmpr = CA.f32("tmpc", [128, 128])
    S.dma("sp", tmp, k.c["tri_incl"], writes=[tmpr], sbreg=tmpr)
    S.op("dve", lambda h: h.tensor_copy(k.trib, tmp), reads=[tmpr], writes=[k.tribr])
    for n in ("L16", "SU16"):
        ap, r = CA.f32(n, [128, 128])
        S.dma("sp", ap, k.c[n], writes=[r], sbreg=r)
        setattr(k, n, ap)
        setattr(k, n + "r", r)
    k.blk16, k.blk16r = CA.f32("blk16", [128, 8])
    S.dma("sp", k.blk16, k.c["blk16"], writes=[k.blk16r], sbreg=k.blk16r)
    for n in ("L64", "SL64", "LT64", "SLT64", "ones64"):
        ap, r = CA.f32(n, [64, 64])
        S.dma("sp", ap, k.c[n], writes=[r], sbreg=r)
        setattr(k, n, ap)
        setattr(k, n + "r", r)
    k.onehot, k.onehotr = CA.f32("onehot", [4, 4, 64])
    S.dma("sp", k.onehot, k.c["onehot"].rearrange("a (b c) -> a b c", b=4), writes=[k.onehotr], sbreg=k.onehotr)
    k.ssF, k.ssFr = CA.f32("ssF", [128, NT])
    k.ssA, k.ssAr = CA.f32("ssA", [128, NT])
    k.epst, k.epsr = CA.f32("epst", [128, 1])
    S.op("pool", lambda h: h.memset(k.epst, EPS), writes=[k.epsr])
    S.barrier(release=False)
    k.creg = [k.identr, k.identbr, k.tribr, k.L16r, k.SU16r, k.blk16r, k.L64r, k.SL64r, k.LT64r, k.SLT64r, k.ones64r, k.onehotr, k.epsr]


def rstd_from_ss(k, rs, rsr, ss, ssr, n, eng_small="dve"):
    S = k.S
    S.op("dve", lambda h: h.tensor_scalar(rs, ss, 1.0 / n, EPS, ALU.mult, ALU.add), reads=[ssr], writes=[rsr])
    S.op("act", lambda h: h.activation(rs, rs, AF.Ln), reads=[rsr], writes=[rsr])
    S.op("act", lambda h: h.activation(rs, rs, AF.Exp, scale=-0.5), reads=[rsr], writes=[rsr])


def norm_transpose_pass(k, src, uT, uTr, hb, ub, ss_name, do_norm=True, ss_pre=None, src_bf16=False):
    S, A = k.S, k.A
    rs, rsr = A.f32(ss_name + "_rs", [128, NT])
    if do_norm and ss_pre is not None:
        rstd_from_ss(k, rs, rsr, ss_pre[0], ss_pre[1], D)
    elif do_norm:
        ss, ssr = A.f32(ss_name, [128, NT])
        junk, junkr = A.f32(ss_name + "_junk", [128, D])
        S.op("dve", lambda h: h.memset(ss, 0.0), writes=[ssr])
        for i in range(NT):
            ht, htr = hb[i % len(hb)]
            S.dma("sp", ht, src[i * 128:(i + 1) * 128, :], writes=[htr], sbreg=htr)
            S.op("act", lambda h, ht=ht, i=i: h.activation(junk, ht, AF.Square, accum_out=ss[:, i:i + 1]),
                 reads=[htr, ssr], writes=[junkr, ssr])
        rstd_from_ss(k, rs, rsr, ss, ssr, D)

    def block(tb):
        for i in range(tb * 4, tb * 4 + 4):
            ht, htr = hb[i % len(hb)]
            u, ur = ub[i % len(ub)]
            if src_bf16:
                S.dma("sp", u, src[i * 128:(i + 1) * 128, :], writes=[ur], sbreg=ur)
            else:
                S.dma("sp", ht, src[i * 128:(i + 1) * 128, :], writes=[htr], sbreg=htr)
            if src_bf16:
                pass
            elif do_norm:
                S.op("dve", lambda h, ht=ht, u=u, i=i: h.tensor_scalar(u, ht, rs[:, i:i + 1], None, ALU.mult),
                     reads=[htr, rsr], writes=[ur])
            else:
                S.op("dve", lambda h, ht=ht, u=u: h.tensor_copy(u, ht), reads=[htr], writes=[ur])
            bank = k.banks[i % 2]
            bankr = k.bankr[i % 2]
            pt = bank[:, :].bitcast(BF16)
            for c in range(8):
                S.op("pe", lambda h, c=c, u=u, pt=pt: h.transpose(pt[:, c * 128:(c + 1) * 128], u[:, c * 128:(c + 1) * 128], k.identb),
                     reads=[ur, k.identbr], writes=[bankr], pe_accum=True)
            eng = "act" if i % 2 == 0 else "dve"
            if eng == "act":
                S.op("act", lambda h, pt=pt, i=i: h.copy(uT[:, :, i * 128:(i + 1) * 128], pt.rearrange("p (c t) -> p c t", c=8)),
                     reads=[bankr], writes=[uTr[i]])
            else:
                S.op("dve", lambda h, pt=pt, i=i: h.tensor_copy(uT[:, :, i * 128:(i + 1) * 128], pt.rearrange("p (c t) -> p c t", c=8)),
                     reads=[bankr], writes=[uTr[i]])
    return block


def phase_inproj(k, l, src):
    S, A = k.S, k.A
    S.barrier()
    A.reset()
    k.bankr = [S.reg("bank%d" % i) for i in range(8)]
    uT, uTr = A.bf16("uT", [128, 8, T], nreg=NT)
    Wb, Wbr = A.bf16("Wb", [128, 8, DIN], nreg=15)
    nw, nwr = A.f32("nw", [128, 8])
    S.dma("sp", nw, k.p["attn_norm_w"][l].rearrange("(c p) -> p c", p=128), writes=[nwr], sbreg=nwr, allow_slow_non_contiguous=True)
    wst = [A.f32("wst%d" % i, [128, 8, 256]) for i in range(2)]
    for bi, c0 in enumerate(range(0, DIN, 256)):
        cw = min(256, DIN - c0)
        wt, wtr = wst[bi % 2]
        S.dma("sp", wt[:, :, :cw], k.p["w_in"][l, :, c0:c0 + cw].rearrange("(c p) e -> p c e", p=128), writes=[wtr], sbreg=wtr)
        for c in range(8):
            if c % 2 == 0:
                S.op("dve", lambda h, c=c, wt=wt, c0=c0, cw=cw: h.tensor_scalar(Wb[:, c, c0:c0 + cw], wt[:, c, :cw], nw[:, c:c + 1], None, ALU.mult),
                     reads=[wtr, nwr], writes=[Wbr[bi]])
            else:
                S.op("act", lambda h, c=c, wt=wt, c0=c0, cw=cw: h.activation(Wb[:, c, c0:c0 + cw], wt[:, c, :cw], AF.Copy, scale=nw[:, c:c + 1]),
                     reads=[wtr, nwr], writes=[Wbr[bi]])
    hb = [A.f32("hb%d" % i, [128, D]) for i in range(3)]
    ub = [A.bf16("ub%d" % i, [128, D]) for i in range(2)]
    block = norm_transpose_pass(k, src, uT, uTr, hb, ub, "ssA", ss_pre=((k.ssA, k.ssAr) if l > 0 else None))
    fms = [A.f32("fms%d" % i, [128, 512]) for i in range(4)]
    tms = [A.f32("tms%d" % i, [128, PTM_W]) for i in range(2)]
    cnt = {"acc": 0, "fm": 0, "tm": 0, "ev": 0}

    def wreg(c0, cw):
        return [Wbr[b] for b in range(c0 // 256, (c0 + cw - 1) // 256 + 1)]

    def mm_block(tb):
        for ci, co in enumerate(FM_COLS):
            bank = k.banks[2 + cnt["acc"] % 4]
            bankr = k.bankr[2 + cnt["acc"] % 4]
            cnt["acc"] += 1
            for c in range(8):
                S.op("pe", lambda h, c=c, co=co, bank=bank: h.matmul(bank[:, :], Wb[:, c, co:co + 128], uT[:, c, tb * 512:(tb + 1) * 512], start=(c == 0), stop=(c == 7)),
                     reads=wreg(co, 128) + uTr[tb * 4:tb * 4 + 4], writes=[bankr], pe_accum=(c > 0))
            st_, str_ = fms[cnt["fm"] % 4]
            cnt["fm"] += 1
            if ci < 8:
                stv = st_[:, 0:256].bitcast(BF16)
                dst = k.PFMb[ci * 128:(ci + 1) * 128, tb * 512:(tb + 1) * 512]
            else:
                stv = st_
                dst = k.PFMf[(ci - 8) * 128:(ci - 7) * 128, tb * 512:(tb + 1) * 512]
            eng = "act" if cnt["ev"] % 2 == 0 else "dve"
            cnt["ev"] += 1
            if eng == "act":
                S.op("act", lambda h, stv=stv, bank=bank: h.copy(stv, bank[:, :]), reads=[bankr], writes=[str_])
            else:
                S.op("dve", lambda h, stv=stv, bank=bank: h.tensor_copy(stv, bank[:, :]), reads=[bankr], writes=[str_])
            S.dma("pool", dst, stv, reads=[str_], sbreg=str_)
        for i in range(tb * 4, tb * 4 + 4):
            st_, str_ = tms[cnt["tm"] % 2]
            cnt["tm"] += 1
            for (sc, w, dc) in TM_BLOCKS:
                bank = k.banks[2 + cnt["acc"] % 4]
                bankr = k.bankr[2 + cnt["acc"] % 4]
                cnt["acc"] += 1
                for c in range(8):
                    S.op("pe", lambda h, c=c, sc=sc, w=w, bank=bank, i=i: h.matmul(bank[:, :w], uT[:, c, i * 128:(i + 1) * 128], Wb[:, c, sc:sc + w], start=(c == 0), stop=(c == 7)),
                         reads=wreg(sc, w) + [uTr[i]], writes=[bankr], pe_accum=(c > 0))
                eng = "act" if cnt["ev"] % 2 == 0 else "dve"
                cnt["ev"] += 1
                if eng == "act":
                    S.op("act", lambda h, st_=st_, bank=bank, w=w, dc=dc: h.copy(st_[:, dc:dc + w], bank[:, :w]), reads=[bankr], writes=[str_])
                else:
                    S.op("dve", lambda h, st_=st_, bank=bank, w=w, dc=dc: h.tensor_copy(st_[:, dc:dc + w], bank[:, :w]), reads=[bankr], writes=[str_])
            S.dma("pool", k.PTM[i * 128:(i + 1) * 128, :], st_, reads=[str_], sbreg=str_)

    block(0)
    for tb in range(8):
        if tb + 1 < 8:
            block(tb + 1)
        mm_block(tb)

def phase_attn(k, l):
    S, A = k.S, k.A
    S.barrier()
    A.reset()
    k.bankr = [S.reg("bank%d" % i) for i in range(8)]
    lambda_init = 0.8 - 0.6 * math.exp(-0.3 * l)
    lp, lpr = A.f32("lp", [128, 4, 64])
    S.dma("sp", lp, k.p["diff_lambda"][l].rearrange("a b -> (a b)").partition_broadcast(128).rearrange("p (a b) -> p a b", a=4),
          writes=[lpr], sbreg=lpr)
    pr, prr = A.f32("lpp", [128, 2, 64])
    S.op("dve", lambda h: h.tensor_tensor(pr[:, 0, :], lp[:, 0, :], lp[:, 1, :], ALU.mult), reads=[lpr], writes=[prr])
    S.op("dve", lambda h: h.tensor_tensor(pr[:, 1, :], lp[:, 2, :], lp[:, 3, :], ALU.mult), reads=[lpr], writes=[prr])
    ls, lsr = A.f32("ls", [128, 2])
    S.op("dve", lambda h: h.tensor_reduce(ls, pr, AX.X, ALU.add), reads=[prr], writes=[lsr])
    S.op("act", lambda h: h.activation(ls, ls, AF.Exp), reads=[lsr], writes=[lsr])
    lam, lamr = A.f32("lam", [128, 1])
    S.op("dve", lambda h: h.tensor_tensor(lam, ls[:, 0:1], ls[:, 1:2], ALU.subtract), reads=[lsr], writes=[lamr])
    S.op("dve", lambda h: h.tensor_scalar(lam, lam, lambda_init, None, ALU.add), reads=[lamr], writes=[lamr])
    wsub, wsubr = A.f32("wsub", [128, 128])
    S.dma("sp", wsub, k.p["diff_subln_w"][l].partition_broadcast(128), writes=[wsubr], sbreg=wsubr)
    S.op("dve", lambda h: h.tensor_scalar(wsub, wsub, 1.0 - lambda_init, None, ALU.mult), reads=[wsubr], writes=[wsubr])

    kTb = [A.bf16("kT%d" % i, [128, T]) for i in range(2)]
    qzb = [[A.bf16("qz%d_%d" % (i, m), [128, T]) for m in range(2)] for i in range(2)]
    for i in range(2):
        S.op("pool", lambda h, i=i: h.memset(qzb[i][0][0][64:128, :], 0.0), writes=[qzb[i][0][1]])
        S.op("pool", lambda h, i=i: h.memset(qzb[i][1][0][0:64, :], 0.0), writes=[qzb[i][1][1]])
    Vf, Vfr = A.f32("Vf", [128, NT, 128])
    Vab = [A.bf16("Va%d" % i, [128, NT, 129]) for i in range(2)]
    PTs = [A.bf16("PT%d" % i, [128, 512]) for i in range(4)]
    Oacc = [A.f32("Oacc%d" % i, [128, 4, 129]) for i in range(2)]
    ot, otr = A.f32("ot", [128, 4, 128])
    t1, t1r = A.f32("t1", [128, 4, 128])
    outb = [A.bf16("outb%d" % i, [128, 4, 128]) for i in range(2)]
    r01, r01r = A.f32("r01", [128, 2, 4])
    ssq, ssqr = A.f32("ssq", [128, 4])
    rsq, rsqr = A.f32("rsq", [128, 4])
    cnt = {"s": 0, "pt": 0, "po": 0, "ob": 0}

    def load_head(hh):
        kT, kTr = kTb[hh % 2]
        Va, Var = Vab[hh % 2]
        for m in range(2):
            qz, qzr = qzb[hh % 2][m]
            S.dma("sp", qz[m * 64:(m + 1) * 64, :], k.PFMb[hh * 128 + m * 64:hh * 128 + (m + 1) * 64, :], writes=[qzr], sbreg=qzr)
        S.dma("sp", kT, k.PFMb[512 + hh * 128:512 + (hh + 1) * 128, :], writes=[kTr], sbreg=kTr)
        S.dma("sp", Vf, k.PTM[:, hh * 128:(hh + 1) * 128].rearrange("(n p) e -> p n e", p=128), writes=[Vfr], sbreg=Vfr)
        S.op("pool", lambda h: h.tensor_copy(Va[:, :, 0:128], Vf), reads=[Vfr], writes=[Var])
        S.op("pool", lambda h: h.memset(Va[:, :, 128:129], 1.0), writes=[Var])

    steps = []

    def make_head(hh):
        kT, kTr = kTb[hh % 2]
        Va, Var = Vab[hh % 2]
        for Q in range(8):
            for m in range(2):
                qT, qTr = qzb[hh % 2][m]
                pbanks = [k.banks[2 + c] for c in range(4)]
                pbankr = [k.bankr[2 + c] for c in range(4)]
                first = [True, True, True, True]
                nj = 4 * Q + 4
                for j in range(nj):
                    i = j - 4 * Q
                    i0 = max(i, 0)
                    ncols = 512 - i0 * 128
                    qs = Q * 512 + i0 * 128
                    st = {}

                    def s_stage(st=st, j=j, i=i, ncols=ncols, qs=qs, m=m, kT=kT, qT=qT, kTr=kTr, qTr=qTr, hh=hh, Q=Q):
                        sbi = (0, 1, 6, 7)[cnt["s"] % 4]
                        sb = k.banks[sbi]
                        sbr = k.bankr[sbi]
                        cnt["s"] += 1
                        S.op("pe", lambda h: h.matmul(sb[:, :ncols], kT[:, j * 128:(j + 1) * 128], qT[:, qs:qs + ncols], start=True, stop=True),
                             reads=[kTr, qTr], writes=[sbr])
                        PT, PTr = PTs[cnt["pt"] % 4]
                        cnt["pt"] += 1
                        S.op("act", lambda h: h.activation(PT[:, :ncols], sb[:, :ncols], AF.Exp, scale=0.125), reads=[sbr], writes=[PTr])
                        if i >= 0:
                            S.op("dve", lambda h: h.tensor_tensor(PT[:, 0:128], PT[:, 0:128], k.trib, ALU.mult), reads=[PTr, k.tribr], writes=[PTr])
                        st["PT"] = (PT, PTr)

                    def pv_stage(st=st, j=j, i0=i0, Q=Q, m=m, Va=Va, Var=Var, pbanks=pbanks, pbankr=pbankr, first=first, nj=nj, hh=hh):
                        PT, PTr = st["PT"]
                        if Q == 0 and m == 0 and j == 3 and hh + 1 < 4:
                            load_head(hh + 1)
                        for c in range(i0, 4):
                            pb = pbanks[c]
                            S.op("pe", lambda h, pb=pb, c=c: h.matmul(pb[:, 0:129], PT[:, (c - i0) * 128:(c - i0 + 1) * 128], Va[:, j, :], start=(j == 0), stop=(j == 4 * Q + c)),
                                 reads=[PTr, Var], writes=[pbankr[c]], pe_accum=not first[c])
                            first[c] = False
                        if j == nj - 1:
                            Oa, Oar = Oacc[m]
                            for c in range(4):
                                S.op("dve", lambda h, c=c: h.tensor_copy(Oa[:, c, :], pbanks[c][:, 0:129]), reads=[pbankr[c]], writes=[Oar])
                            if m == 1:
                                finalize(hh, Q)
                    steps.append((s_stage, pv_stage))

    def finalize(hh, Q):
        O0, O0r = Oacc[0]
        O1, O1r = Oacc[1]
        S.op("dve", lambda h: h.reciprocal(r01[:, 0, :], O0[:, :, 128]), reads=[O0r], writes=[r01r])
        S.op("dve", lambda h: h.reciprocal(r01[:, 1, :], O1[:, :, 128]), reads=[O1r], writes=[r01r])
        S.op("dve", lambda h: h.tensor_scalar(r01[:, 1, :], r01[:, 1, :], lam[:, 0:1], None, ALU.mult), reads=[r01r, lamr], writes=[r01r])
        S.op("pool", lambda h: h.tensor_tensor(t1, O1[:, :, 0:128], r01[:, 1, :].unsqueeze(2).to_broadcast([128, 4, 128]), ALU.mult),
             reads=[O1r, r01r], writes=[t1r])
        S.op("dve", lambda h: h.tensor_tensor(ot, O0[:, :, 0:128], r01[:, 0, :].unsqueeze(2).to_broadcast([128, 4, 128]), ALU.mult),
             reads=[O0r, r01r], writes=[otr])
        S.op("dve", lambda h: h.tensor_tensor(ot, ot, t1, ALU.subtract), reads=[otr, t1r], writes=[otr])
        S.op("pool", lambda h: h.tensor_tensor(t1, ot, ot, ALU.mult), reads=[otr], writes=[t1r])
        S.op("dve", lambda h: h.tensor_reduce(ssq, t1, AX.X, ALU.add), reads=[t1r], writes=[ssqr])
        rstd_from_ss(k, rsq, rsqr, ssq, ssqr, 128)
        ob, obr = outb[cnt["ob"] % 2]
        cnt["ob"] += 1
        S.op("dve", lambda h: h.tensor_tensor(ot, ot, rsq.unsqueeze(2).to_broadcast([128, 4, 128]), ALU.mult), reads=[otr, rsqr], writes=[otr])
        S.op("pool", lambda h: h.tensor_tensor(ob, ot, wsub.unsqueeze(1).to_broadcast([128, 4, 128]), ALU.mult), reads=[otr, wsubr], writes=[obr])
        S.dma("pool", k.MIX[Q * 512:(Q + 1) * 512, hh * 128:(hh + 1) * 128].rearrange("(c p) e -> p c e", p=128), ob, reads=[obr], sbreg=obr)

    load_head(0)
    for hh in range(4):
        make_head(hh)
    LOOK = 3
    for idx in range(min(LOOK, len(steps))):
        steps[idx][0]()
    for idx in range(len(steps)):
        if idx + LOOK < len(steps):
            steps[idx + LOOK][0]()
        steps[idx][1]()

def phase_hgrn(k, l):
    S, A = k.S, k.A
    S.barrier()
    A.reset()
    k.bankr = [S.reg("bank%d" % i) for i in range(8)]
    NCH = T // 16
    r16, r16r = A.f32("r16", [128, T])
    S.op("pool", lambda h: h.memset(r16, 1.0), writes=[r16r])
    S.op("pool", lambda h: h.memset(r16.rearrange("p (n c) -> p n c", c=16)[:, :, 0:1], 0.0), writes=[r16r])
    lbf, lbfr = A.f32("lbf", [128, 2])
    omf, omfr = A.f32("omf", [128, 2])
    lbt, lbtr = A.f32("lbt", [128, 256])
    omt, omtr = A.f32("omt", [128, 256])
    if l == 0:
        S.op("pool", lambda h: h.memset(lbf, 0.0), writes=[lbfr])
        S.op("pool", lambda h: h.memset(lbt, 0.0), writes=[lbtr])
    else:
        lg, lgr = A.f32("lg", [128, 2, 2])
        for li in range(2):
            S.dma("sp", lg[:, li, :], k.p["hgrn_lb_logits"][li].rearrange("(c p) -> p c", p=128), writes=[lgr], sbreg=lgr, allow_slow_non_contiguous=True)
        S.op("dve", lambda h: h.tensor_tensor(lbf, lg[:, 1, :], lg[:, 0, :], ALU.subtract), reads=[lgr], writes=[lbfr])
        S.op("act", lambda h: h.activation(lbf, lbf, AF.Sigmoid), reads=[lbfr], writes=[lbfr])
        lg2, lg2r = A.f32("lg2", [128, 2, 256])
        S.dma("sp", lg2, k.p["hgrn_lb_logits"].rearrange("a b -> (a b)").partition_broadcast(128).rearrange("p (a b) -> p a b", a=2), writes=[lg2r], sbreg=lg2r)
        S.op("dve", lambda h: h.tensor_tensor(lbt, lg2[:, 1, :], lg2[:, 0, :], ALU.subtract), reads=[lg2r], writes=[lbtr])
        S.op("act", lambda h: h.activation(lbt, lbt, AF.Sigmoid), reads=[lbtr], writes=[lbtr])
    S.op("dve", lambda h: h.tensor_scalar(omf, lbf, -1.0, 1.0, ALU.mult, ALU.add), reads=[lbfr], writes=[omfr])
    S.op("dve", lambda h: h.tensor_scalar(omt, lbt, -1.0, 1.0, ALU.mult, ALU.add), reads=[lbtr], writes=[omtr])
    hnw, hnwr = A.f32("hnw", [128, 64])
    S.dma("sp", hnw, k.p["hgrn_norm_w"][l].partition_broadcast(128), writes=[hnwr], sbreg=hnwr)

    mark0 = A.off
    Kall, Kallr = A.f32("Kall", [128, NT, 256], nreg=4)
    Lall, Lallr = A.f32("Lall", [128, NT, 256], nreg=4)
    for q in range(4):
        sl = slice(q * 8, (q + 1) * 8)
        Kq = Kall[:, sl, :]
        Lq = Lall[:, sl, :]
        S.dma("sp", Kq, k.PTM[q * 1024:(q + 1) * 1024, 512:768].rearrange("(n p) e -> p n e", p=128), writes=[Kallr[q]], sbreg=Kallr[q])
        S.op("act", lambda h, Kq=Kq: h.activation(Kq, Kq, AF.Sigmoid), reads=[Kallr[q]], writes=[Kallr[q]])
    for q in range(4):
        sl = slice(q * 8, (q + 1) * 8)
        Kq = Kall[:, sl, :]
        Lq = Lall[:, sl, :]
        S.op("dve", lambda h, Kq=Kq: h.tensor_tensor(Kq, Kq, omt.unsqueeze(1).to_broadcast([128, 8, 256]), ALU.mult), reads=[Kallr[q], omtr], writes=[Kallr[q]])
        S.op("dve", lambda h, Kq=Kq: h.tensor_tensor(Kq, Kq, lbt.unsqueeze(1).to_broadcast([128, 8, 256]), ALU.add), reads=[Kallr[q], lbtr], writes=[Kallr[q]])
        S.op("dve", lambda h, Kq=Kq, Lq=Lq: h.tensor_scalar(Lq, Kq, 1e-30, None, ALU.max), reads=[Kallr[q]], writes=[Lallr[q]])
        S.op("act", lambda h, Lq=Lq: h.activation(Lq, Lq, AF.Ln), reads=[Lallr[q]], writes=[Lallr[q]])
    for q in range(4):
        for ii in range(8):
            i = q * 8 + ii
            bank = k.banks[i % 4]
            bankr = k.bankr[i % 4]
            S.op("pe", lambda h, bank=bank, i=i: h.matmul(bank[:, 0:256], k.SU16, Lall[:, i, :], start=True, stop=True), reads=[Lallr[q], k.SU16r], writes=[bankr])
            S.op("act", lambda h, bank=bank, i=i: h.activation(Lall[:, i, :], bank[:, 0:256], AF.Exp), reads=[bankr], writes=[Lallr[q]])
        sl = slice(q * 8, (q + 1) * 8)
        Kq = Kall[:, sl, :]
        Lq = Lall[:, sl, :]
        S.op("dve", lambda h, Kq=Kq: h.tensor_scalar(Kq, Kq, -1.0, 1.0, ALU.mult, ALU.add), reads=[Kallr[q]], writes=[Kallr[q]])
        Kbq = Lq.rearrange("p n e -> p (n e)")[:, 0:1024].bitcast(BF16).rearrange("p (n e) -> p n e", n=8)
        S.op("dve", lambda h, Kq=Kq, Lq=Lq: h.tensor_tensor(Kq, Kq, Lq, ALU.mult), reads=[Kallr[q], Lallr[q]], writes=[Kallr[q]])
        S.op("pool", lambda h, Kq=Kq, Kbq=Kbq: h.tensor_copy(Kbq, Kq), reads=[Kallr[q]], writes=[Lallr[q]])
        S.dma("pool", k.HK[q * 1024:(q + 1) * 1024, :].rearrange("(n p) e -> p n e", p=128), Kbq, reads=[Lallr[q]], sbreg=Lallr[q])
    S.barrier(release=False)
    A.off = mark0

    QtT, QtTr = A.f32("QtT", [128, T])
    KtT, KtTr = A.f32("KtT", [128, T])
    Dn, Dnr = A.f32("Dn", [128, NCH])
    U, Ur = A.f32("U", [128, 64, NCH])
    Ur2 = S.reg("Uscan")
    Oall, Oallr = A.f32("Oall", [128, NT, 128])
    ssh, sshr = A.f32("ssh", [128, NT, 2])
    rsh, rshr = A.f32("rsh", [128, NT, 2])
    vex = [A.bf16("vex%d" % i, [128, 8, 64]) for i in range(4)]
    Qb, Qbr = A.bf16("Qb", [128, T])
    Kb, Kbr = A.bf16("Kb", [128, T])
    ats = [A.bf16("ats%d" % i, [128, 128]) for i in range(4)]
    ots = [A.f32("ots%d" % i, [64, 128]) for i in range(4)]
    ktl = [A.bf16("ktl%d" % i, [128, 128]) for i in range(2)]
    vtl = [A.f32("vtl%d" % i, [128, 128]) for i in range(3)]
    vbl = [A.bf16("vbl%d" % i, [128, 128]) for i in range(3)]
    mark = A.off
    fpT, fpTr = A.f32("fpT", [128, T])
    BT, BTr = A.f32("BT", [128, T])
    for c in range(2):
        S.dma("sp", fpT, k.PFMf[256 + c * 128:256 + (c + 1) * 128, :], writes=[fpTr], sbreg=fpTr)
        S.dma("sp", QtT, k.PFMf[c * 128:(c + 1) * 128, :], writes=[QtTr], sbreg=QtTr)
        S.op("act", lambda h: h.activation(fpT, fpT, AF.Sigmoid), reads=[fpTr], writes=[fpTr])
        S.op("dve", lambda h, c=c: h.tensor_scalar(fpT, fpT, omf[:, c:c + 1], lbf[:, c:c + 1], ALU.mult, ALU.add), reads=[fpTr, omfr, lbfr], writes=[fpTr])
        S.op("pool", lambda h: h.tensor_scalar(KtT, fpT, -1.0, 1.0, ALU.mult, ALU.add), reads=[fpTr], writes=[KtTr])
        S.op("dve", lambda h: h.tensor_scalar(fpT, fpT, 1e-30, None, ALU.max), reads=[fpTr], writes=[fpTr])
        S.op("act", lambda h: h.activation(fpT, fpT, AF.Ln), reads=[fpTr], writes=[fpTr])
        S.op("dve", lambda h: h.tensor_tensor_scan(BT, r16, fpT, 0.0, ALU.mult, ALU.add), reads=[fpTr, r16r], writes=[BTr])
        S.op("act", lambda h: h.activation(Dn, BT.rearrange("p (n c) -> p n c", c=16)[:, :, 15], AF.Exp), reads=[BTr], writes=[Dnr])
        S.op("act", lambda h: h.activation(fpT, BT, AF.Exp), reads=[BTr], writes=[fpTr])
        S.op("act", lambda h: h.activation(BT, BT, AF.Exp, scale=-1.0), reads=[BTr], writes=[BTr])
        S.op("dve", lambda h: h.tensor_tensor(KtT, KtT, BT, ALU.mult), reads=[KtTr, BTr], writes=[KtTr])
        S.op("act", lambda h: h.activation(QtT, QtT, AF.Silu), reads=[QtTr], writes=[QtTr])
        S.op("dve", lambda h: h.tensor_tensor(QtT, QtT, fpT, ALU.mult), reads=[QtTr, fpTr], writes=[QtTr])
        S.op("pool", lambda h: h.tensor_copy(Qb, QtT), reads=[QtTr], writes=[Qbr])
        S.op("pool", lambda h: h.tensor_copy(Kb, KtT), reads=[KtTr], writes=[Kbr])
        def u_unit(i, c=c):
            u = i % 2
            kt, ktr = ktl[i % 2]
            vt, vtr = vtl[i % 3]
            S.dma("sp", kt, k.HK[i * 128:(i + 1) * 128, c * 128:(c + 1) * 128], writes=[ktr], sbreg=ktr)
            S.dma("sp", vt, k.PTM[i * 128:(i + 1) * 128, 768 + c * 128:768 + (c + 1) * 128], writes=[vtr], sbreg=vtr)
            for hl in range(2):
                vx, vxr = vex[u * 2 + hl]
                eng = "dve" if hl == 0 else "pool"
                S.op(eng, lambda h, vx=vx, hl=hl: h.tensor_tensor(
                    vx, vt[:, hl * 64:(hl + 1) * 64].unsqueeze(1).to_broadcast([128, 8, 64]),
                    k.blk16.unsqueeze(2).to_broadcast([128, 8, 64]), ALU.mult), reads=[vtr, k.blk16r], writes=[vxr])
            yield
            for hl in range(2):
                vx, vxr = vex[u * 2 + hl]
                bank = k.banks[u * 2 + hl]
                bankr = k.bankr[u * 2 + hl]
                S.op("pe", lambda h, bank=bank, vx=vx: h.matmul(bank[:, :], kt, vx.rearrange("p n v -> p (n v)"), start=True, stop=True),
                     reads=[ktr, vxr], writes=[bankr])
            yield
            for hl in range(2):
                bank = k.banks[u * 2 + hl]
                bankr = k.bankr[u * 2 + hl]
                p0 = hl * 64
                if hl == 0:
                    S.op("act", lambda h, bank=bank, p0=p0: h.copy(U[p0:p0 + 64, :, i * 8:(i + 1) * 8], bank[p0:p0 + 64, :].rearrange("d (n v) -> d v n", n=8)), reads=[bankr], writes=[Ur])
                else:
                    S.op("dve", lambda h, bank=bank, p0=p0: h.tensor_copy(U[p0:p0 + 64, :, i * 8:(i + 1) * 8], bank[p0:p0 + 64, :].rearrange("d (n v) -> d v n", n=8)), reads=[bankr], writes=[Ur])
        run_interleaved([u_unit(i) for i in range(NT)], 2, 1)
        for v in range(64):
            S.op("dve", lambda h, v=v: h.tensor_tensor_scan(U[:, v, :], Dn, U[:, v, :], 0.0, ALU.mult, ALU.add), reads=[Ur, Dnr], writes=[Ur2] if v else [Ur, Ur2])
        Ub = k.A.t[0:128, mark:mark + 8192].bitcast(BF16).rearrange("p (v n) -> p v n", n=NCH)
        Ubr = S.reg("Ub")
        S.op("dve", lambda h: h.tensor_copy(Ub, U), reads=[Ur, Ur2, fpTr, BTr, QtTr, KtTr, Qbr, Kbr], writes=[Ubr, fpTr, BTr])
        def o_unit(i, c=c):
            u = i % 2
            vt, vtr = vtl[i % 3]
            vb, vbr = vbl[i % 3]
            S.dma("sp", vt, k.PTM[i * 128:(i + 1) * 128, 768 + c * 128:768 + (c + 1) * 128], writes=[vtr], sbreg=vtr)
            S.op("pool", lambda h: h.tensor_copy(vb, vt), reads=[vtr], writes=[vbr])
            for hl in range(2):
                p0 = hl * 64
                ba = k.banks[u * 2 + hl]
                bar = k.bankr[u * 2 + hl]
                S.op("pe", lambda h, ba=ba, p0=p0: h.matmul(ba[:, 0:128], Kb[p0:p0 + 64, i * 128:(i + 1) * 128], Qb[p0:p0 + 64, i * 128:(i + 1) * 128], start=True, stop=True),
                     reads=[Kbr, Qbr], writes=[bar])
            yield
            for hl in range(2):
                ba = k.banks[u * 2 + hl]
                bar = k.bankr[u * 2 + hl]
                at, atr = ats[u * 2 + hl]
                S.op("dve", lambda h, at=at, ba=ba: h.tensor_tensor(at, ba[:, 0:128], k.L16, ALU.mult), reads=[bar, k.L16r], writes=[atr])
            yield
            for hl in range(2):
                p0 = hl * 64
                at, atr = ats[u * 2 + hl]
                bo = k.banks[4 + u * 2 + hl]
                bor = k.bankr[4 + u * 2 + hl]
                S.op("pe", lambda h, bo=bo, at=at, p0=p0: h.matmul(bo[0:64, 0:128], vb[:, p0:p0 + 64], at, start=True, stop=False),
                     reads=[vbr, atr], writes=[bor])
                for n in range(8):
                    g = i * 8 + n
                    if g == 0:
                        continue
                    S.op("pe", lambda h, bo=bo, g=g, n=n, p0=p0: h.matmul(bo[0:64, n * 16:(n + 1) * 16], Ub[p0:p0 + 64, :, g - 1], Qb[p0:p0 + 64, g * 16:(g + 1) * 16], start=False, stop=(n == 7)),
                         reads=[Ubr, fpTr, BTr, Qbr], writes=[bor], pe_accum=True)
            yield
            for hl in range(2):
                bo = k.banks[4 + u * 2 + hl]
                bor = k.bankr[4 + u * 2 + hl]
                ot, otr = ots[u * 2 + hl]
                S.op("act", lambda h, ot=ot, bo=bo: h.copy(ot, bo[0:64, 0:128]), reads=[bor], writes=[otr])
            yield
            for hl in range(2):
                ot, otr = ots[u * 2 + hl]
                bt = k.banks[u * 2 + hl]
                btr = k.bankr[u * 2 + hl]
                S.op("pe", lambda h, bt=bt, ot=ot: h.transpose(bt[:, 0:64], ot, k.ident[0:64, 0:64]), reads=[otr, k.identr], writes=[btr])
            yield
            for hl in range(2):
                p0 = hl * 64
                bt = k.banks[u * 2 + hl]
                btr = k.bankr[u * 2 + hl]
                S.op("dve", lambda h, bt=bt, p0=p0: h.tensor_copy(Oall[:, i, p0:p0 + 64], bt[:, 0:64]), reads=[btr], writes=[Oallr])
        run_interleaved([o_unit(i) for i in range(NT)], 2, 3)
        Osq, Osqr = fpT.rearrange("p (n e) -> p n e", n=NT), fpTr
        Gall, Gallr = BT.rearrange("p (n e) -> p n e", n=NT), BTr
        S.dma("sp", Gall, k.PTM[:, 1024 + c * 128:1024 + (c + 1) * 128].rearrange("(n p) e -> p n e", p=128), writes=[Gallr], sbreg=Gallr)
        S.op("act", lambda h: h.activation(Gall, Gall, AF.Silu), reads=[Gallr], writes=[Gallr])
        S.op("pool", lambda h: h.tensor_tensor(Osq, Oall, Oall, ALU.mult), reads=[Oallr], writes=[Osqr])
        S.op("dve", lambda h: h.tensor_reduce(ssh.rearrange("p n g -> p (n g)"), Osq.rearrange("p n (g v) -> p (n g) v", g=2), AX.X, ALU.add), reads=[Osqr], writes=[sshr])
        rstd_from_ss(k, rsh.rearrange("p n g -> p (n g)"), rshr, ssh.rearrange("p n g -> p (n g)"), sshr, 64)
        S.op("dve", lambda h: h.tensor_tensor(Oall.rearrange("p n (g v) -> p (n g) v", g=2), Oall.rearrange("p n (g v) -> p (n g) v", g=2),
                                              rsh.rearrange("p n g -> p (n g)").unsqueeze(2).to_broadcast([128, NT * 2, 64]), ALU.mult), reads=[Oallr, rshr], writes=[Oallr])
        S.op("pool", lambda h: h.tensor_tensor(Oall.rearrange("p n (g v) -> p (n g) v", g=2), Oall.rearrange("p n (g v) -> p (n g) v", g=2),
                                               hnw.unsqueeze(1).to_broadcast([128, NT * 2, 64]), ALU.mult), reads=[Oallr, hnwr], writes=[Oallr])
        Ob = fpT[:, 0:2048].bitcast(BF16).rearrange("p (n e) -> p n e", n=NT)
        S.op("dve", lambda h: h.tensor_tensor(Ob, Oall, Gall, ALU.mult), reads=[Oallr, Gallr, fpTr], writes=[fpTr])
        S.dma("pool", k.MIX[:, 512 + c * 128:512 + (c + 1) * 128].rearrange("(n p) e -> p n e", p=128), Ob, reads=[fpTr], sbreg=fpTr)

def phase_gdn(k, l):
    S, A = k.S, k.A
    S.barrier()
    A.reset()
    k.bankr = [S.reg("bank%d" % i) for i in range(8)]
    cwd, cwdr = A.f32("cwd", [128, 4, 6])
    for j in range(4):
        S.dma("sp", cwd[:, j, :], k.p["dn_conv_w"][l, j].rearrange("(c p) -> p c", p=128), writes=[cwdr], sbreg=cwdr, allow_slow_non_contiguous=True)
    bo, bor = A.f32("bo128", [128, 128])
    S.op("pool", lambda h: h.memset(bo, 0.0), writes=[bor])
    S.op("pool", lambda h: h.memset(bo[0:64, 0:64], 1.0), writes=[bor])
    S.op("pool", lambda h: h.memset(bo[64:128, 64:128], 1.0), writes=[bor])
    Xb = [A.f32("Xc%d" % i, [128, T + 3]) for i in range(2)]
    Yb = [A.f32("Yc%d" % i, [128, T]) for i in range(2)]
    sqb = [A.f32("sqc%d" % i, [128, T]) for i in range(2)]
    tms = [A.f32("tmsg%d" % i, [128, NT, 128]) for i in range(2)]
    for i in range(2):
        S.op("pool", lambda h, i=i: h.memset(Xb[i][0][:, 0:3], 0.0), writes=[Xb[i][1]])
    for c in range(6):
        g1_chunk(k, c, Xb[c % 2], Yb[c % 2], sqb[c % 2], tms, cwd, cwdr, bo, bor)
    if False:
        X, Xr = Xb[0]; Y, Yr = Yb[0]; sq, sqr = sqb[0]; c = 0
        pass

    S.barrier()
    A.reset()
    k.bankr = [S.reg("bank%d" % i) for i in range(8)]
    NC = 64
    HP = 2
    ab, abr = A.f32("ab", [128, NC, 4])
    alog, alogr = A.f32("alog", [128, HP])
    dtb, dtbr = A.f32("dtb", [128, HP])
    for hb in range(2):
        ph = slice(hb * 64, (hb + 1) * 64)
        for which in range(2):
            c0 = 1536 + which * 4 + hb
            for hp in range(HP):
                S.dma("sp", ab[ph, :, which * 2 + hp:which * 2 + hp + 1], k.PTM[:, c0 + 2 * hp:c0 + 2 * hp + 1].rearrange("(n p) e -> p n e", p=64),
                      writes=[abr], sbreg=abr, allow_slow_non_contiguous=True)
        S.dma("sp", alog[ph, :], k.p["dn_A_log"][l].rearrange("(hp hb) -> hb hp", hb=2)[hb].partition_broadcast(64), writes=[alogr], sbreg=alogr, allow_slow_non_contiguous=True)
        S.dma("sp", dtb[ph, :], k.p["dn_dt_bias"][l].rearrange("(hp hb) -> hb hp", hb=2)[hb].partition_broadcast(64), writes=[dtbr], sbreg=dtbr, allow_slow_non_contiguous=True)
    S.op("act", lambda h: h.activation(alog, alog, AF.Exp), reads=[alogr], writes=[alogr])
    S.op("dve", lambda h: h.tensor_scalar(alog, alog, -1.0, None, ALU.mult), reads=[alogr], writes=[alogr])
    L64b, L64br = A.f32("L64b", [128, 128])
    on64b, on64br = A.f32("on64b", [128, 128])
    S.op("pool", lambda h: h.memset(L64b, 0.0), writes=[L64br])
    S.op("pool", lambda h: h.memset(on64b, 0.0), writes=[on64br])
    mk = {}
    for nm in ("L64", "SLT64"):
        mk[nm] = A.f32(nm + "x2", [128, 64])
    idx2, idx2r = A.f32("idx2", [128, 64])
    for hb in range(2):
        ph = slice(hb * 64, (hb + 1) * 64)
        S.dma("sp", L64b[ph, hb * 64:(hb + 1) * 64], k.c["L64"], writes=[L64br], sbreg=L64br)
        S.dma("sp", on64b[ph, hb * 64:(hb + 1) * 64], k.c["ones64"], writes=[on64br], sbreg=on64br)
        for nm in ("L64", "SLT64"):
            S.dma("sp", mk[nm][0][ph, :], k.c[nm], writes=[mk[nm][1]], sbreg=mk[nm][1])
        S.dma("sp", idx2[ph, :], k.c["ident"][0:64, 0:64], writes=[idx2r], sbreg=idx2r)
    L64x, L64xr = mk["L64"]
    SLT64x, SLT64xr = mk["SLT64"]

    def sc(name):
        return A.f32(name, [128, NC, HP])
    g, gr = sc("g")
    beta, betar = sc("beta")
    t0, t0r = sc("t0")
    t1, t1r = sc("t1")
    gc, gcr = sc("gc")
    gl, glr = sc("gl")
    egc, egcr = sc("egc")
    ekd, ekdr = sc("ekd")
    egl, eglr = sc("egl")
    nbeta, nbetar = sc("nbeta")
    begc, begcr = sc("begc")
    S.op("dve", lambda h: h.tensor_tensor(t0, ab[:, :, 0:2], dtb.unsqueeze(1).to_broadcast([128, NC, HP]), ALU.add), reads=[abr, dtbr], writes=[t0r])
    S.op("dve", lambda h: h.tensor_scalar(t1, t0, -1.0, None, ALU.mult), reads=[t0r], writes=[t1r])
    S.op("dve", lambda h: h.tensor_tensor(t1, t1, t0, ALU.max), reads=[t0r, t1r], writes=[t1r])
    S.op("act", lambda h: h.activation(t1, t1, AF.Exp, scale=-1.0), reads=[t1r], writes=[t1r])
    S.op("dve", lambda h: h.tensor_scalar(t1, t1, 1.0, None, ALU.add), reads=[t1r], writes=[t1r])
    S.op("act", lambda h: h.activation(t1, t1, AF.Ln), reads=[t1r], writes=[t1r])
    S.op("dve", lambda h: h.tensor_scalar(t0, t0, 0.0, None, ALU.max), reads=[t0r], writes=[t0r])
    S.op("dve", lambda h: h.tensor_tensor(t0, t0, t1, ALU.add), reads=[t0r, t1r], writes=[t0r])
    S.op("dve", lambda h: h.tensor_tensor(g, t0, alog.unsqueeze(1).to_broadcast([128, NC, HP]), ALU.mult), reads=[t0r, alogr], writes=[gr])
    S.op("act", lambda h: h.activation(beta, ab[:, :, 2:4], AF.Sigmoid), reads=[abr], writes=[betar])
    S.op("pe", lambda h: h.matmul(k.banks[0][:, 0:NC * HP], L64b, g.rearrange("p n h -> p (n h)"), start=True, stop=True), reads=[gr, L64br], writes=[k.bankr[0]])
    S.op("pe", lambda h: h.matmul(k.banks[1][:, 0:NC * HP], on64b, g.rearrange("p n h -> p (n h)"), start=True, stop=True), reads=[gr, on64br], writes=[k.bankr[1]])
    S.op("dve", lambda h: h.tensor_copy(gc.rearrange("p n h -> p (n h)"), k.banks[0][:, 0:NC * HP]), reads=[k.bankr[0]], writes=[gcr])
    S.op("dve", lambda h: h.tensor_copy(gl.rearrange("p n h -> p (n h)"), k.banks[1][:, 0:NC * HP]), reads=[k.bankr[1]], writes=[glr])
    S.op("act", lambda h: h.activation(egc, gc, AF.Exp), reads=[gcr], writes=[egcr])
    S.op("act", lambda h: h.activation(egl, gl, AF.Exp), reads=[glr], writes=[eglr])
    S.op("dve", lambda h: h.tensor_tensor(ekd, gl, gc, ALU.subtract), reads=[glr, gcr], writes=[ekdr])
    S.op("act", lambda h: h.activation(ekd, ekd, AF.Exp), reads=[ekdr], writes=[ekdr])
    S.op("dve", lambda h: h.tensor_scalar(nbeta, beta, -1.0, None, ALU.mult), reads=[betar], writes=[nbetar])
    S.op("dve", lambda h: h.tensor_tensor(begc, beta, egc, ALU.mult), reads=[betar, egcr], writes=[begcr])

    CB = 4
    NSL = CB * HP
    NSET = 2
    NB = 8 // CB
    qTg = [A.f32("qTg%d" % i, [128, HP, 512]) for i in range(2)]
    kTg = [A.f32("kTg%d" % i, [128, HP, 512]) for i in range(2)]
    ktg = [A.f32("ktg%d" % i, [128, 8, HP, 64]) for i in range(2)]
    vtg = [A.f32("vtg%d" % i, [128, 8, HP, 64]) for i in range(2)]
    og = [A.f32("og%d" % i, [128, 8, HP, 64]) for i in range(2)]
    qTb = [A.bf16("qTb%d" % i, [128, HP, 512]) for i in range(2)]
    kTb = [A.bf16("kTb%d" % i, [128, HP, 512]) for i in range(2)]

    def bt(name):
        return A.f32(name, [128, NSL, 64])

    def btb(name):
        return A.bf16(name, [128, NSL, 64])
    sets = []
    for si in range(NSET):
        d = {}
        for nm in ("raw", "mx", "mn", "tA", "tB", "ut", "gB"):
            d[nm] = bt(nm + str(si))
        for nm in ("Mt0", "Mt1", "Nt0", "Nt1", "Tt", "aqk", "Ru", "Rw", "kdec", "wT"):
            d[nm] = btb(nm + str(si))
        sets.append(d)
    Sst, Sstr = A.f32("Sst", [128, HP, 64])
    Ssb, Ssbr = A.bf16("Ssb", [128, HP, 64])
    vnew, vnewr = A.bf16("vnew", [128, HP, 64])
    otmp, otmpr = A.f32("otmp", [128, HP, 64])
    S.op("pool", lambda h: h.memset(Sst, 0.0), writes=[Sstr])
    S.op("pool", lambda h: h.memset(Ssb, 0.0), writes=[Ssbr])
    PH = [slice(0, 64), slice(64, 128)]

    def bc_s(ap2):
        return ap2.unsqueeze(2).to_broadcast([128, NSL, 64])

    def mask_b(m):
        return m.unsqueeze(1).to_broadcast([128, NSL, 64])

    def pv(bank):
        return bank[:, :].rearrange("p (a b) -> p a b", b=64)

    def pv2(bank):
        return bank[:, 0:HP * 64].rearrange("p (a b) -> p a b", b=64)

    def group_load(grp):
        qTt, qTtr = qTg[grp % 2]
        kTt, kTtr = kTg[grp % 2]
        ktt, kttr = ktg[grp % 2]
        vtt, vttr = vtg[grp % 2]
        qTtb, qTtbr = qTb[grp % 2]
        kTtb, kTtbr = kTb[grp % 2]
        tsl = slice(grp * 512, (grp + 1) * 512)
        S.dma("sp", qTt, k.GQT[:, tsl].rearrange("(hp p) t -> p hp t", p=128), writes=[qTtr], sbreg=qTtr)
        S.dma("sp", kTt, k.GKT[:, tsl].rearrange("(hp p) t -> p hp t", p=128), writes=[kTtr], sbreg=kTtr)
        S.op("pool", lambda h: h.tensor_copy(qTtb, qTt), reads=[qTtr], writes=[qTtbr])
        S.op("pool", lambda h: h.tensor_copy(kTtb, kTt), reads=[kTtr], writes=[kTtbr])
        for hb in range(2):
            for hp in range(HP):
                hcol = slice((2 * hp + hb) * 64, (2 * hp + hb + 1) * 64)
                S.dma("sp", ktt[PH[hb], :, hp, :], k.GKtm[tsl, hcol].rearrange("(n p) e -> p n e", p=64), writes=[kttr], sbreg=kttr)
                S.dma("sp", vtt[PH[hb], :, hp, :], k.GVtm[tsl, hcol].rearrange("(n p) e -> p n e", p=64), writes=[vttr], sbreg=vttr)

    def do_batch(gb):
        grp, b = divmod(gb, NB)
        if gb == 0:
            group_load(0)
            group_load(1)
        d = sets[gb % NSET]
        bs = (gb % NSET) * 4
        Bk = [k.banks[bs + i] for i in range(4)]
        Br = [k.bankr[bs + i] for i in range(4)]
        qTt, qTtr = qTb[grp % 2]
        kTt, kTtr = kTb[grp % 2]
        ktt, kttr = ktg[grp % 2]
        vtt, vttr = vtg[grp % 2]
        ogt, ogtr = og[grp % 2]
        raw, rawr = d["raw"]; mx, mxr = d["mx"]; mn, mnr = d["mn"]; tA, tAr = d["tA"]; tB, tBr = d["tB"]; gB, gBr = d["gB"]
        Mt = [d["Mt0"], d["Mt1"]]; Nt = [d["Nt0"], d["Nt1"]]
        Tt, Ttr = d["Tt"]; aqk, aqkr = d["aqk"]; Ru, Rur = d["Ru"]; Rw, Rwr = d["Rw"]; kdec, kdecr = d["kdec"]
        ut, utr = d["ut"]; wT, wTr = d["wT"]
        n0 = grp * 8 + b * CB
        ln0 = b * CB
        sc_sl = slice(n0, n0 + CB)

        def sview(ap):
            return ap[:, sc_sl, :].rearrange("p n h -> p (n h)")
        S.op("pool", lambda h: h.tensor_copy(gB, bc_s(sview(g))), reads=[gr], writes=[gBr])
        for cb in range(CB):
            tl = slice((ln0 + cb) * 64, (ln0 + cb + 1) * 64)
            for hp in range(HP):
                sl_ = cb * HP + hp
                col = slice(sl_ * 64, (sl_ + 1) * 64)
                for hb in range(2):
                    ph = PH[hb]
                    S.op("pe", lambda h, hp=hp, tl=tl, col=col, ph=ph: h.matmul(Bk[0][ph, col], kTt[ph, hp, tl], kTt[ph, hp, tl], start=True, stop=True), reads=[kTtr], writes=[Br[0]])
                    S.op("pe", lambda h, sl_=sl_, col=col, ph=ph: h.matmul(Bk[1][ph, col], gB[ph, sl_, :], L64x[ph, :], start=True, stop=True), reads=[gBr, L64xr], writes=[Br[1]])
                    S.op("pe", lambda h, hp=hp, tl=tl, col=col, ph=ph: h.matmul(Bk[2][ph, col], kTt[ph, hp, tl], qTt[ph, hp, tl], start=True, stop=True), reads=[kTtr, qTtr], writes=[Br[2]])
        yield
        S.op("dve", lambda h: h.tensor_tensor(raw, pv(Bk[1]), bc_s(sview(gc)), ALU.subtract), reads=[Br[1], gcr], writes=[rawr])
        yield
        S.op("dve", lambda h: h.tensor_scalar(mx, raw, 0.0, None, ALU.max), reads=[rawr], writes=[mxr])
        S.op("pool", lambda h: h.tensor_scalar(mn, raw, 0.0, None, ALU.min), reads=[rawr], writes=[mnr])
        yield
        S.op("act", lambda h: h.activation(mx, mx, AF.Exp, scale=-1.0), reads=[mxr], writes=[mxr])
        S.op("act", lambda h: h.activation(mn, mn, AF.Exp), reads=[mnr], writes=[mnr])
        vt3 = vtt[:, ln0:ln0 + CB, :, :].rearrange("p c h v -> p (c h) v")
        kt3 = ktt[:, ln0:ln0 + CB, :, :].rearrange("p c h v -> p (c h) v")
        S.op("pool", lambda h: h.tensor_tensor(Ru, vt3, bc_s(sview(beta)), ALU.mult), reads=[vttr, betar], writes=[Rur])
        S.op("pool", lambda h: h.tensor_tensor(Rw, kt3, bc_s(sview(begc)), ALU.mult), reads=[kttr, begcr], writes=[Rwr])
        S.op("pool", lambda h: h.tensor_tensor(kdec, kt3, bc_s(sview(ekd)), ALU.mult), reads=[kttr, ekdr], writes=[kdecr])
        yield
        S.op("dve", lambda h: h.tensor_tensor(tA, mx, mask_b(SLT64x), ALU.mult), reads=[mxr, SLT64xr], writes=[tAr])
        S.op("pool", lambda h: h.tensor_tensor(tB, mn, mask_b(L64x), ALU.mult), reads=[mnr, L64xr], writes=[tBr])
        yield
        S.op("dve", lambda h: h.tensor_tensor(tA, tA, bc_s(sview(nbeta)), ALU.mult), reads=[tAr, nbetar], writes=[tAr])
        yield
        M0, M0r = Mt[0]
        S.op("dve", lambda h: h.tensor_tensor(M0, pv(Bk[0]), tA, ALU.mult), reads=[Br[0], tAr], writes=[M0r])
        S.op("dve", lambda h: h.tensor_tensor(aqk, pv(Bk[2]), tB, ALU.mult), reads=[Br[2], tBr], writes=[aqkr])
        yield
        B3b = Bk[3][:, 0:256].bitcast(BF16)
        for e in range(NSL):
            for hb in range(2):
                ph = PH[hb]
                S.op("pe", lambda h, e=e, ph=ph: h.transpose(B3b[ph, e * 64:(e + 1) * 64], M0[ph, e, :], k.identb[ph, ph]), reads=[M0r, k.identbr], writes=[Br[3]])
        yield
        N0, N0r = Nt[0]
        S.op("act", lambda h: h.copy(N0, B3b.rearrange("p (a b) -> p a b", b=64)), reads=[Br[3]], writes=[N0r])
        yield
        S.op("dve", lambda h: h.tensor_tensor(Tt, N0, mask_b(idx2), ALU.add), reads=[N0r, idx2r], writes=[Ttr])
        for j in range(1, 6):
            Mp, Mpr = Mt[(j - 1) % 2]
            Np, Npr = Nt[(j - 1) % 2]
            Mc, Mcr = Mt[j % 2]
            Nc_, Ncr = Nt[j % 2]
            for e in range(NSL):
                col = slice(e * 64, (e + 1) * 64)
                for hb in range(2):
                    ph = PH[hb]
                    S.op("pe", lambda h, e=e, col=col, Np=Np, Mp=Mp, ph=ph: h.matmul(Bk[0][ph, col], Np[ph, e, :], Mp[ph, e, :], start=True, stop=True), reads=[Npr, Mpr], writes=[Br[0]])
            if j < 5:
                for e in range(NSL):
                    col = slice(e * 64, (e + 1) * 64)
                    for hb in range(2):
                        ph = PH[hb]
                        S.op("pe", lambda h, e=e, col=col, Np=Np, Mp=Mp, ph=ph: h.matmul(Bk[1][ph, col], Mp[ph, e, :], Np[ph, e, :], start=True, stop=True), reads=[Npr, Mpr], writes=[Br[1]])
            yield
            S.op("act", lambda h, Mc=Mc: h.copy(Mc, pv(Bk[0])), reads=[Br[0]], writes=[Mcr])
            if j < 5:
                S.op("act", lambda h, Nc_=Nc_: h.copy(Nc_, pv(Bk[1])), reads=[Br[1]], writes=[Ncr])
            yield
            for e in range(NSL):
                col = slice(e * 64, (e + 1) * 64)
                for hb in range(2):
                    ph = PH[hb]
                    S.op("pe", lambda h, e=e, col=col, Mc=Mc, ph=ph: h.matmul(Bk[2][ph, col], Mc[ph, e, :], Tt[ph, e, :], start=True, stop=True), reads=[Mcr, Ttr], writes=[Br[2]])
            yield
            S.op("dve", lambda h: h.tensor_tensor(Tt, Tt, pv(Bk[2]), ALU.add), reads=[Br[2], Ttr], writes=[Ttr])
            yield
        for e in range(NSL):
            col = slice(e * 64, (e + 1) * 64)
            for hb in range(2):
                ph = PH[hb]
                S.op("pe", lambda h, e=e, col=col, ph=ph: h.matmul(Bk[0][ph, col], Tt[ph, e, :], Ru[ph, e, :], start=True, stop=True), reads=[Ttr, Rur], writes=[Br[0]])
                S.op("pe", lambda h, e=e, col=col, ph=ph: h.matmul(Bk[1][ph, col], Rw[ph, e, :], Tt[ph, e, :], start=True, stop=True), reads=[Ttr, Rwr], writes=[Br[1]])
        yield
        S.op("act", lambda h: h.copy(ut, pv(Bk[0])), reads=[Br[0]], writes=[utr])
        S.op("act", lambda h: h.copy(wT, pv(Bk[1])), reads=[Br[1]], writes=[wTr])
        yield
        for cb in range(CB):
            n = n0 + cb
            tl = slice((ln0 + cb) * 64, (ln0 + cb + 1) * 64)
            es = slice(cb * HP, cb * HP + HP)
            for hp in range(HP):
                col = slice(hp * 64, (hp + 1) * 64)
                for hb in range(2):
                    ph = PH[hb]
                    S.op("pe", lambda h, hp=hp, col=col, cb=cb, ph=ph: h.matmul(Bk[3][ph, col], wT[ph, cb * HP + hp, :], Ssb[ph, hp, :], start=True, stop=True), reads=[wTr, Ssbr], writes=[Br[3]])
                    S.op("pe", lambda h, hp=hp, col=col, tl=tl, ph=ph: h.matmul(Bk[0][ph, col], qTt[ph, hp, tl], Ssb[ph, hp, :], start=True, stop=True), reads=[qTtr, Ssbr], writes=[Br[0]])
            yield
            S.op("dve", lambda h, es=es: h.tensor_tensor(vnew, ut[:, es, :], pv2(Bk[3]), ALU.subtract), reads=[utr, Br[3]], writes=[vnewr])
            yield
            for hp in range(HP):
                col = slice(hp * 64, (hp + 1) * 64)
                for hb in range(2):
                    ph = PH[hb]
                    S.op("pe", lambda h, hp=hp, col=col, cb=cb, ph=ph: h.matmul(Bk[2][ph, col], kdec[ph, cb * HP + hp, :], vnew[ph, hp, :], start=True, stop=True), reads=[kdecr, vnewr], writes=[Br[2]])
                    S.op("pe", lambda h, hp=hp, col=col, cb=cb, ph=ph: h.matmul(Bk[1][ph, col], aqk[ph, cb * HP + hp, :], vnew[ph, hp, :], start=True, stop=True), reads=[aqkr, vnewr], writes=[Br[1]])
            S.op("pool", lambda h, n=n: h.tensor_tensor(Sst, Sst, egl[:, n, :].unsqueeze(2).to_broadcast([128, HP, 64]), ALU.mult), reads=[Sstr, eglr], writes=[Sstr])
            yield
            S.op("dve", lambda h: h.tensor_tensor(Sst, Sst, pv2(Bk[2]), ALU.add), reads=[Br[2], Sstr], writes=[Sstr])
            S.op("act", lambda h: h.copy(Ssb, Sst), reads=[Sstr], writes=[Ssbr])
            S.op("dve", lambda h, n=n: h.tensor_tensor(otmp, pv2(Bk[0]), egc[:, n, :].unsqueeze(2).to_broadcast([128, HP, 64]), ALU.mult), reads=[Br[0], egcr], writes=[otmpr])
            S.op("dve", lambda h, cb=cb: h.tensor_tensor(ogt[:, ln0 + cb, :, :], otmp, pv2(Bk[1]), ALU.add), reads=[Br[1], otmpr], writes=[ogtr])
            yield
        if b == NB - 1:
            tsl = slice(grp * 512, (grp + 1) * 512)
            for hb in range(2):
                for hp in range(HP):
                    hcol = slice((2 * hp + hb) * 64, (2 * hp + hb + 1) * 64)
                    S.dma("pool", k.GO[tsl, hcol].rearrange("(n p) e -> p n e", p=64), ogt[PH[hb], :, hp, :], reads=[ogtr], sbreg=ogtr)
            if grp + 2 < NC // 8:
                group_load(grp + 2)

    run_interleaved([do_batch(gb) for gb in range(NC // CB)], NSET, 26)

    S.barrier()
    A.reset()
    Oall, Oallr = A.f32("OallG", [128, NT, 256])
    Zall, Zallr = A.f32("ZallG", [128, NT, 256])
    Osq, Osqr = A.f32("OsqG", [128, NT, 256])
    ssg, ssgr = A.f32("ssg", [128, NT * 4])
    rsg, rsgr = A.f32("rsg", [128, NT * 4])
    dnw, dnwr = A.f32("dnw", [128, 64])
    S.dma("sp", dnw, k.p["dn_norm_w"][l].partition_broadcast(128), writes=[dnwr], sbreg=dnwr)
    S.dma("sp", Oall, k.GO.rearrange("(n p) e -> p n e", p=128), writes=[Oallr], sbreg=Oallr)
    S.dma("sp", Zall, k.PTM[:, 1280:1536].rearrange("(n p) e -> p n e", p=128), writes=[Zallr], sbreg=Zallr)
    S.op("act", lambda h: h.activation(Zall, Zall, AF.Silu), reads=[Zallr], writes=[Zallr])
    S.op("pool", lambda h: h.tensor_tensor(Osq, Oall, Oall, ALU.mult), reads=[Oallr], writes=[Osqr])
    S.op("dve", lambda h: h.tensor_reduce(ssg, Osq.rearrange("p n (g v) -> p (n g) v", g=4), AX.X, ALU.add), reads=[Osqr], writes=[ssgr])
    rstd_from_ss(k, rsg, rsgr, ssg, ssgr, 64)
    O3 = Oall.rearrange("p n (g v) -> p (n g) v", g=4)
    S.op("dve", lambda h: h.tensor_tensor(O3, O3, rsg.unsqueeze(2).to_broadcast([128, NT * 4, 64]), ALU.mult), reads=[Oallr, rsgr], writes=[Oallr])
    S.op("pool", lambda h: h.tensor_tensor(O3, O3, dnw.unsqueeze(1).to_broadcast([128, NT * 4, 64]), ALU.mult), reads=[Oallr, dnwr], writes=[Oallr])
    ObG = Osq.rearrange("p n e -> p (n e)")[:, 0:4096].bitcast(BF16).rearrange("p (n e) -> p n e", n=NT)
    S.op("dve", lambda h: h.tensor_tensor(ObG, Oall, Zall, ALU.mult), reads=[Oallr, Zallr, Osqr, ssgr], writes=[Osqr])
    S.dma("pool", k.MIX[:, 768:1024].rearrange("(n p) e -> p n e", p=128), ObG, reads=[Osqr], sbreg=Osqr)


def g1_chunk(k, c, Xt, Yt, sqt, tms, cwd, cwdr, bo, bor):
    S = k.S
    X, Xr = Xt
    Y, Yr = Yt
    sq, sqr = sqt
    if True:
        S.dma("sp", X[:, 3:T + 3], k.PFMf[512 + c * 128:512 + (c + 1) * 128, :], writes=[Xr], sbreg=Xr)
        S.op("dve", lambda h, c=c: h.tensor_scalar(Y, X[:, 3:T + 3], cwd[:, 3, c:c + 1], None, ALU.mult), reads=[Xr, cwdr], writes=[Yr])
        S.op("dve", lambda h, c=c: h.scalar_tensor_tensor(Y, X[:, 2:T + 2], cwd[:, 2, c:c + 1], Y, ALU.mult, ALU.add), reads=[Xr, cwdr, Yr], writes=[Yr])
        S.op("dve", lambda h, c=c: h.scalar_tensor_tensor(Y, X[:, 1:T + 1], cwd[:, 1, c:c + 1], Y, ALU.mult, ALU.add), reads=[Xr, cwdr, Yr], writes=[Yr])
        S.op("dve", lambda h, c=c: h.scalar_tensor_tensor(Y, X[:, 0:T], cwd[:, 0, c:c + 1], Y, ALU.mult, ALU.add), reads=[Xr, cwdr, Yr], writes=[Yr])
        S.op("act", lambda h: h.activation(Y, Y, AF.Silu), reads=[Yr], writes=[Yr])
        if c < 4:
            S.op("act", lambda h: h.activation(sq, Y, AF.Square), reads=[Yr], writes=[sqr])
            for tb in range(8):
                bank = k.banks[tb % 2]
                bankr = k.bankr[tb % 2]
                S.op("pe", lambda h, bank=bank, tb=tb: h.matmul(bank[:, :], bo, sq[:, tb * 512:(tb + 1) * 512], start=True, stop=True), reads=[bor, sqr], writes=[bankr])
                S.op("act", lambda h, bank=bank, tb=tb: h.activation(X[:, 3 + tb * 512:3 + (tb + 1) * 512], bank[:, :], AF.Ln, bias=k.epst[:, 0:1]), reads=[bankr, k.epsr], writes=[Xr])
            S.op("act", lambda h: h.activation(X[:, 3:T + 3], X[:, 3:T + 3], AF.Exp, scale=-0.5), reads=[Xr], writes=[Xr])
            if c < 2:
                S.op("dve", lambda h: h.scalar_tensor_tensor(Y, Y, 0.125, X[:, 3:T + 3], ALU.mult, ALU.mult), reads=[Yr, Xr], writes=[Yr])
                S.dma("pool", k.GQT[c * 128:(c + 1) * 128, :], Y, reads=[Yr], sbreg=Yr)
            else:
                S.op("dve", lambda h: h.tensor_tensor(Y, Y, X[:, 3:T + 3], ALU.mult), reads=[Yr, Xr], writes=[Yr])
                S.dma("pool", k.GKT[(c - 2) * 128:(c - 1) * 128, :], Y, reads=[Yr], sbreg=Yr)
        if c >= 2:
            st_, str_ = tms[c % 2]
            for i in range(NT):
                bank = k.banks[2 + i % 4]
                bankr = k.bankr[2 + i % 4]
                S.op("pe", lambda h, bank=bank, i=i: h.transpose(bank[:, 0:128], Y[:, i * 128:(i + 1) * 128], k.ident), reads=[Yr, k.identr], writes=[bankr])
                if i % 2 == 0:
                    S.op("act", lambda h, bank=bank, i=i, st_=st_: h.copy(st_[:, i, :], bank[:, 0:128]), reads=[bankr], writes=[str_])
                else:
                    S.op("dve", lambda h, bank=bank, i=i, st_=st_: h.tensor_copy(st_[:, i, :], bank[:, 0:128]), reads=[bankr], writes=[str_])
            dst = k.GKtm if c < 4 else k.GVtm
            cc = (c - 2) % 2
            S.dma("pool", dst[:, cc * 128:(cc + 1) * 128].rearrange("(n p) e -> p n e", p=128), st_, reads=[str_], sbreg=str_)


def phase_outproj_ffn(k, l, src, last):
    S, A = k.S, k.A
    MIXsrc = k.mix_in if getattr(k, "mix_in", None) is not None else k.MIX
    S.barrier()
    A.reset()
    k.bankr = [S.reg("bank%d" % i) for i in range(8)]
    u2T, u2Tr = A.bf16("u2T", [128, 8, T], nreg=NT)
    mixT, mixTr = A.bf16("mixT", [128, 8, T], nreg=NT)
    Wob, Wobr = A.bf16("Wob", [128, 8, D], nreg=8)
    wst = [A.f32("wst%d" % i, [128, D]) for i in range(2)]
    for c in range(8):
        wt, wtr = wst[c % 2]
        S.dma("sp", wt, k.p["w_out"][l, c * 128:(c + 1) * 128, :], writes=[wtr], sbreg=wtr)
        S.op("pool", lambda h, wt=wt, c=c: h.tensor_copy(Wob[:, c, :], wt), reads=[wtr], writes=[Wobr[c]])
    hb = [A.f32("hb%d" % i, [128, D]) for i in range(3)]
    ub = [A.bf16("ub%d" % i, [128, D]) for i in range(2)]
    block = norm_transpose_pass(k, MIXsrc, mixT, mixTr, hb, ub, "ssO", do_norm=False, src_bf16=(getattr(k, "mix_in", None) is None))
    hres = [A.f32("hres%d" % i, [128, D]) for i in range(2)]
    hnew = [A.f32("hnew%d" % i, [128, D]) for i in range(2)]
    cnt = {"acc": 0}
    junkO, junkOr = A.f32("junkO", [128, D])
    S.op("pool", lambda h: h.memset(k.ssF, 0.0), writes=[k.ssFr])
    ub2 = [A.bf16("ub2_%d" % i, [128, D]) for i in range(2)]
    rsO, rsOr = A.f32("rsO", [128, NT])
    pend = []

    def emit_u2T(item):
        i, u2, u2r = item
        bank = k.banks[6 + i % 2]
        bankr = k.bankr[6 + i % 2]
        pt = bank[:, :].bitcast(BF16)
        for c in range(8):
            S.op("pe", lambda h, c=c, u2=u2, pt=pt: h.transpose(pt[:, c * 128:(c + 1) * 128], u2[:, c * 128:(c + 1) * 128], k.identb),
                 reads=[u2r, k.identbr], writes=[bankr], pe_accum=True)
        if i % 2 == 0:
            S.op("act", lambda h, pt=pt, i=i: h.copy(u2T[:, :, i * 128:(i + 1) * 128], pt.rearrange("p (c t) -> p c t", c=8)), reads=[bankr], writes=[u2Tr[i]])
        else:
            S.op("dve", lambda h, pt=pt, i=i: h.tensor_copy(u2T[:, :, i * 128:(i + 1) * 128], pt.rearrange("p (c t) -> p c t", c=8)), reads=[bankr], writes=[u2Tr[i]])

    def oproj(tb):
        for i in range(tb * 4, tb * 4 + 4):
            hr, hrr = hres[i % 2]
            hn, hnr = hnew[i % 2]
            S.dma("sp", hr, src[i * 128:(i + 1) * 128, :], writes=[hrr], sbreg=hrr)
            for half in range(2):
                bank = k.banks[2 + cnt["acc"] % 4]
                bankr = k.bankr[2 + cnt["acc"] % 4]
                cnt["acc"] += 1
                for c in range(8):
                    S.op("pe", lambda h, c=c, bank=bank, i=i, half=half: h.matmul(bank[:, :], mixT[:, c, i * 128:(i + 1) * 128], Wob[:, c, half * 512:(half + 1) * 512], start=(c == 0), stop=(c == 7)),
                         reads=[mixTr[i], Wobr[c]], writes=[bankr], pe_accum=(c > 0))
                S.op("dve", lambda h, hn=hn, hr=hr, bank=bank, half=half: h.tensor_tensor(hn[:, half * 512:(half + 1) * 512], bank[:, :], hr[:, half * 512:(half + 1) * 512], ALU.add),
                     reads=[bankr, hrr], writes=[hnr])
            S.op("act", lambda h, hn=hn, i=i: h.activation(junkO, hn, AF.Square, accum_out=k.ssF[:, i:i + 1]), reads=[hnr, k.ssFr], writes=[junkOr, k.ssFr])
            S.dma("pool", k.H[i * 128:(i + 1) * 128, :], hn, reads=[hnr], sbreg=hnr)
            S.op("dve", lambda h, i=i: h.tensor_scalar(rsO[:, i:i + 1], k.ssF[:, i:i + 1], 1.0 / D, EPS, ALU.mult, ALU.add), reads=[k.ssFr], writes=[rsOr])
            S.op("act", lambda h, i=i: h.activation(rsO[:, i:i + 1], rsO[:, i:i + 1], AF.Ln), reads=[rsOr], writes=[rsOr])
            S.op("act", lambda h, i=i: h.activation(rsO[:, i:i + 1], rsO[:, i:i + 1], AF.Exp, scale=-0.5), reads=[rsOr], writes=[rsOr])
            u2, u2r = ub2[i % 2]
            S.op("dve", lambda h, hn=hn, u2=u2, i=i: h.tensor_scalar(u2, hn, rsO[:, i:i + 1], None, ALU.mult), reads=[hnr, rsOr], writes=[u2r])
            pend.append((i, u2, u2r))
            if len(pend) > 1:
                emit_u2T(pend.pop(0))

    block(0)
    for tb in range(8):
        if tb + 1 < 8:
            block(tb + 1)
        oproj(tb)
    while pend:
        emit_u2T(pend.pop(0))

    S.barrier()
    A.reset()
    k.bankr = [S.reg("bank%d" % i) for i in range(8)]
    uT, uTr = A.bf16("u2T", [128, 8, T], nreg=NT)
    nw, nwr = A.f32("nw2", [128, 8])
    S.dma("sp", nw, k.p["ffn_norm_w"][l].rearrange("(c p) -> p c", p=128), writes=[nwr], sbreg=nwr, allow_slow_non_contiguous=True)
    cw, cwr = A.f32("cw", [128, 3, 44])
    cb, cbr = A.f32("cb", [128, 44])
    for j in range(3):
        S.dma("sp", cw[:, j, :], k.p["ffn_conv_w"][l, j].rearrange("(c p) -> p c", p=128), writes=[cwr], sbreg=cwr, allow_slow_non_contiguous=True)
    S.dma("sp", cb, k.p["ffn_conv_b"][l].rearrange("(c p) -> p c", p=128), writes=[cbr], sbreg=cbr, allow_slow_non_contiguous=True)
    wf = [[A.f32("wf%d_%d" % (i, j), [128, 8, 128]) for j in range(2)] for i in range(2)]
    wb = [[A.bf16("wb%d_%d" % (i, j), [128, 8, 128]) for j in range(2)] for i in range(2)]
    Xs = [[A.f32("X%d_%d" % (i, j), [128, T + 2]) for j in range(2)] for i in range(2)]
    Y = [A.f32("Y%d" % j, [128, T]) for j in range(2)]
    aout = [A.bf16("aout%d" % i, [128, T]) for i in range(1)]
    for i in range(2):
        for j in range(2):
            S.op("pool", lambda h, i=i, j=j: h.memset(Xs[i][j][0][:, 0:2], 0.0), writes=[Xs[i][j][1]])
    cnt = {"acc": 0, "ev": 0}
    def load_w(f):
        for j in range(2):
            wt, wtr = wf[f % 2][j]
            wbt, wbtr = wb[f % 2][j]
            col = j * DFF + f * 128
            S.dma("sp", wt, k.p["ffn_w_up"][l, :, col:col + 128].rearrange("(c p) e -> p c e", p=128), writes=[wtr], sbreg=wtr)
            S.op("pool", lambda h, wt=wt, wbt=wbt: h.tensor_tensor(wbt, wt, nw.unsqueeze(2).to_broadcast([128, 8, 128]), ALU.mult), reads=[wtr, nwr], writes=[wbtr])

    load_w(0)
    for f in range(22):
        X = Xs[f % 2]
        if f + 1 < 22:
            load_w(f + 1)
        for tb in range(8):
            for j in range(2):
                wbt, wbtr = wb[f % 2][j]
                bank = k.banks[cnt["acc"] % 4]
                bankr = k.bankr[cnt["acc"] % 4]
                cnt["acc"] += 1
                for c in range(8):
                    S.op("pe", lambda h, c=c, bank=bank, wbt=wbt, tb=tb: h.matmul(bank[:, :], wbt[:, c, :], uT[:, c, tb * 512:(tb + 1) * 512], start=(c == 0), stop=(c == 7)),
                         reads=[wbtr] + uTr[tb * 4:tb * 4 + 4], writes=[bankr], pe_accum=(c > 0))
                Xj, Xjr = X[j]
                S.op("act", lambda h, Xj=Xj, bank=bank, tb=tb: h.copy(Xj[:, 2 + tb * 512:2 + (tb + 1) * 512], bank[:, :]), reads=[bankr], writes=[Xjr])
        for j in range(2):
            Xj, Xjr = X[j]
            Yj, Yjr = Y[j]
            ch = j * 22 + f
            S.op("dve", lambda h, Xj=Xj, Yj=Yj, ch=ch: h.tensor_scalar(Yj, Xj[:, 2:T + 2], cw[:, 2, ch:ch + 1], cb[:, ch:ch + 1], ALU.mult, ALU.add),
                 reads=[Xjr, cwr, cbr], writes=[Yjr])
            S.op("dve", lambda h, Xj=Xj, Yj=Yj, ch=ch: h.scalar_tensor_tensor(Yj, Xj[:, 1:T + 1], cw[:, 1, ch:ch + 1], Yj, ALU.mult, ALU.add),
                 reads=[Xjr, cwr, Yjr], writes=[Yjr])
            S.op("dve", lambda h, Xj=Xj, Yj=Yj, ch=ch: h.scalar_tensor_tensor(Yj, Xj[:, 0:T], cw[:, 0, ch:ch + 1], Yj, ALU.mult, ALU.add),
                 reads=[Xjr, cwr, Yjr], writes=[Yjr])
        S.op("act", lambda h: h.activation(Y[0][0], Y[0][0], AF.Silu), reads=[Y[0][1]], writes=[Y[0][1]])
        ao, aor = aout[0]
        S.op("dve", lambda h, ao=ao: h.tensor_tensor(ao, Y[0][0], Y[1][0], ALU.mult), reads=[Y[0][1], Y[1][1]], writes=[aor])
        S.dma("pool", k.ACTS[f * 128:(f + 1) * 128, :], ao, reads=[aor], sbreg=aor)

    S.barrier()
    A.reset()
    k.bankr = [S.reg("bank%d" % i) for i in range(8)]
    Wdb, Wdbr = A.bf16("Wdb", [128, 22, D], nreg=22)
    wst = [A.f32("wst%d" % i, [128, D]) for i in range(2)]
    for f in range(22):
        wt, wtr = wst[f % 2]
        S.dma("sp", wt, k.p["ffn_w_down"][l, f * 128:(f + 1) * 128, :], writes=[wtr], sbreg=wtr)
        S.op("pool" if f % 2 else "dve", lambda h, wt=wt, f=f: h.tensor_copy(Wdb[:, f, :], wt), reads=[wtr], writes=[Wdbr[f]])
    acts = [A.bf16("acts%d" % i, [128, 22, 512]) for i in range(2)]
    hres = [A.f32("hres%d" % i, [128, D]) for i in range(2)]
    hnew = [A.f32("hnew%d" % i, [128, D]) for i in range(2)]
    if not last:
        junk2, junk2r = A.f32("junk2", [128, D])
        S.op("pool", lambda h: h.memset(k.ssA, 0.0), writes=[k.ssAr])
    if last:
        fnw, fnwr = A.f32("fnw", [128, D])
        S.dma("sp", fnw, k.p["final_norm_w"].partition_broadcast(128), writes=[fnwr], sbreg=fnwr)
        junk, junkr = A.f32("junkF", [128, D])
        ssf, ssfr = A.f32("ssf", [128, NT])
        rsf, rsfr = A.f32("rsf", [128, NT])
        S.op("dve", lambda h: h.memset(ssf, 0.0), writes=[ssfr])
    cnt = {"acc": 0}
    for tb in range(8):
        ac, acr = acts[tb % 2]
        S.dma("sp", ac, k.ACTS[:, tb * 512:(tb + 1) * 512].rearrange("(f p) t -> p f t", p=128), writes=[acr], sbreg=acr)
        for ti in range(4):
            i = tb * 4 + ti
            hr, hrr = hres[i % 2]
            hn, hnr = hnew[i % 2]
            S.dma("sp", hr, k.H[i * 128:(i + 1) * 128, :], writes=[hrr], sbreg=hrr)
            for half in range(2):
                bank = k.banks[cnt["acc"] % 4]
                bankr = k.bankr[cnt["acc"] % 4]
                cnt["acc"] += 1
                for f in range(22):
                    S.op("pe", lambda h, f=f, bank=bank, ac=ac, ti=ti, half=half: h.matmul(bank[:, :], ac[:, f, ti * 128:(ti + 1) * 128], Wdb[:, f, half * 512:(half + 1) * 512], start=(f == 0), stop=(f == 21)),
                         reads=[acr, Wdbr[f]], writes=[bankr], pe_accum=(f > 0))
                S.op("dve", lambda h, hn=hn, hr=hr, bank=bank, half=half: h.tensor_tensor(hn[:, half * 512:(half + 1) * 512], bank[:, :], hr[:, half * 512:(half + 1) * 512], ALU.add),
                     reads=[bankr, hrr], writes=[hnr])
            if not last:
                S.op("act", lambda h, hn=hn, i=i: h.activation(junk2, hn, AF.Square, accum_out=k.ssA[:, i:i + 1]), reads=[hnr, k.ssAr], writes=[junk2r, k.ssAr])
                S.dma("pool", k.H[i * 128:(i + 1) * 128, :], hn, reads=[hnr], sbreg=hnr)
            else:
                S.op("act", lambda h, hn=hn, i=i: h.activation(junk, hn, AF.Square, accum_out=ssf[:, i:i + 1]), reads=[hnr, ssfr], writes=[junkr, ssfr])
                S.op("dve", lambda h, i=i: h.tensor_scalar(rsf[:, i:i + 1], ssf[:, i:i + 1], 1.0 / D, EPS, ALU.mult, ALU.add), reads=[ssfr], writes=[rsfr])
                S.op("act", lambda h, i=i: h.activation(rsf[:, i:i + 1], rsf[:, i:i + 1], AF.Ln), reads=[rsfr], writes=[rsfr])
                S.op("act", lambda h, i=i: h.activation(rsf[:, i:i + 1], rsf[:, i:i + 1], AF.Exp, scale=-0.5), reads=[rsfr], writes=[rsfr])
                S.op("dve", lambda h, hn=hn, i=i: h.scalar_tensor_tensor(hn, hn, rsf[:, i:i + 1], fnw, ALU.mult, ALU.mult), reads=[hnr, rsfr, fnwr], writes=[hnr])
                S.dma("pool", k.out[i * 128:(i + 1) * 128, :], hn, reads=[hnr], sbreg=hnr)


_NC_CACHE = {}


def kernel(**inputs):
    x = np.ascontiguousarray(np.asarray(inputs["x"], dtype=np.float32))
    if "nc" not in _NC_CACHE:
        _NC_CACHE["nc"] = build_program(debug=False, nlayers=2)
    nc = _NC_CACHE["nc"]
    consts = make_consts()
    base = {}
    for n in PARAM_SHAPES:
        base[n] = np.ascontiguousarray(np.asarray(inputs[n], dtype=np.float32))
    for n, v in consts.items():
        base["c_" + n] = v
    in_maps = []
    for c in range(8):
        m = dict(base)
        m["x"] = np.ascontiguousarray(x[c % 4])
        in_maps.append(m)
    res = run_bass_kernel_spmd(nc, in_maps, core_ids=list(range(8)))
    out = np.stack([np.asarray(res.results[b]["out"], dtype=np.float32) for b in range(4)], axis=0)
    return out
```

```python
import numpy as np
import math
from contextlib import ExitStack
import concourse.bass as bass
import concourse.mybir as mybir
from concourse.bass_utils import run_bass_kernel_spmd

F32 = mybir.dt.float32
BF16 = mybir.dt.bfloat16
AF = mybir.ActivationFunctionType
ALU = mybir.AluOpType
AX = mybir.AxisListType

T = 4096
D = 1024
NT = 32
DIN = 3592
DFF = 2816
EPS = 1e-6


class Reg:
    __slots__ = ("name", "lw", "rd", "dsem", "dcnt")

    def __init__(self, name):
        self.name = name
        self.lw = None
        self.rd = {}
        self.dsem = None
        self.dcnt = 0


class Sched:
    ENG = ("pe", "act", "dve", "pool", "sp")

    def __init__(self, nc, stack):
        self.nc = nc
        self.stack = stack
        self.sem = {e: stack.enter_context(nc.semaphore("prog_" + e)) for e in self.ENG}
        self.cnt = {e: 0 for e in self.ENG}
        self.ops = {e: [] for e in self.ENG}
        self.waited = {e: {} for e in self.ENG}
        self.dsems = []
        self.regs = []
        self.nins = 0
        self.free_dsems = []

    def reg(self, name):
        r = Reg(name)
        self.regs.append(r)
        return r

    def _need(self, eng, dep, waits):
        if dep is None:
            return
        kind, who, cnt = dep
        if kind == "e" and who == "pe" and eng == "pe":
            return
        if kind == "e":
            key = ("e", who)
            semh = self.sem[who]
        else:
            key = ("d", id(who))
            semh = who.dsem
        if self.waited[eng].get(key, 0) >= cnt:
            return
        cur = waits.get(key)
        if cur is None or cur[1] < cnt:
            waits[key] = (semh, cnt)

    def _deps(self, eng, reads, writes, pe_accum=False):
        waits = {}
        for r in reads:
            self._need(eng, r.lw, waits)
        for r in writes:
            if not (pe_accum and r.lw is not None and r.lw[0] == "e" and r.lw[1] == "pe" and eng == "pe"):
                self._need(eng, r.lw, waits)
            for d in r.rd.values():
                self._need(eng, d, waits)
        for key, (semh, cnt) in waits.items():
            self.waited[eng][key] = cnt
        return list(waits.values())

    def op(self, eng, fn, reads=(), writes=(), pe_accum=False):
        waits = self._deps(eng, reads, writes, pe_accum)
        self.cnt[eng] += 1
        c = self.cnt[eng]
        sem = self.sem[eng]

        def emit(h, waits=waits, fn=fn, sem=sem):
            for (s, v) in waits:
                h.wait_ge(s, v)
            fn(h).then_inc(sem, 1)
        self.ops[eng].append(emit)
        self.nins += 1
        dep = ("e", eng, c)
        for r in writes:
            r.lw = dep
            r.rd = {}
        for r in reads:
            r.rd[("e", eng)] = dep
        return dep

    def dma(self, q, out_ap, in_ap, reads=(), writes=(), sbreg=None, **kw):
        assert sbreg is not None
        if sbreg.dsem is None:
            if self.free_dsems:
                sbreg.dsem, sbreg.dcnt = self.free_dsems.pop()
            else:
                sbreg.dsem = self.stack.enter_context(self.nc.semaphore("d%d" % len(self.dsems)))
                sbreg.dcnt = 0
            self.dsems.append(sbreg)
        waits = self._deps(q, reads, writes)
        sbreg.dcnt += 16
        c = sbreg.dcnt
        semh = sbreg.dsem

        def emit(h, waits=waits, semh=semh, out_ap=out_ap, in_ap=in_ap, kw=kw):
            for (s, v) in waits:
                h.wait_ge(s, v)
            h.dma_start(out=out_ap, in_=in_ap, **kw).then_inc(semh, 16)
        self.ops[q].append(emit)
        self.nins += 1
        dep = ("d", sbreg, c)
        for r in writes:
            r.lw = dep
            r.rd = {}
        for r in reads:
            r.rd[("d", id(sbreg))] = dep
        return dep

    def barrier(self, release=True):
        waits = []
        for e in self.ENG:
            if e != "sp" and self.cnt[e] > self.waited["sp"].get(("e", e), 0):
                waits.append((self.sem[e], self.cnt[e]))
                self.waited["sp"][("e", e)] = self.cnt[e]
        for r in self.dsems:
            key = ("d", id(r))
            if r.dcnt > self.waited["sp"].get(key, 0):
                waits.append((r.dsem, r.dcnt))
        self.cnt["sp"] += 1
        c = self.cnt["sp"]
        sem = self.sem["sp"]

        def emit(h, waits=waits, sem=sem):
            for (s, v) in waits:
                h.wait_ge(s, v)
            h.nop().then_inc(sem, 1)
        self.ops["sp"].append(emit)
        for e in self.ENG:
            if e == "sp":
                continue

            def emit2(h, sem=sem, c=c):
                h.wait_ge(sem, c)
            self.ops[e].append(emit2)
            self.waited[e][("e", "sp")] = c
            for e2 in self.ENG:
                if e2 != "sp":
                    self.waited[e][("e", e2)] = max(self.waited[e].get(("e", e2), 0), self.cnt[e2])
        for e2 in self.ENG:
            if e2 != "sp":
                self.waited["sp"][("e", e2)] = self.cnt[e2]
        for r in self.regs:
            r.lw = None
            r.rd = {}
        if release:
            for r in self.dsems:
                self.free_dsems.append((r.dsem, r.dcnt))
                r.dsem = None
                r.dcnt = 0
            self.dsems = []
            for e in self.ENG:
                self.waited[e] = {k: v for k, v in self.waited[e].items() if k[0] == "e"}
            self.regs = []

    def finish(self):
        nc = self.nc
        ops = self.ops
        with nc.Block() as block:
            @block.tensor
            def _(h):
                for f in ops["pe"]:
                    f(h)

            @block.scalar
            def _(h):
                for f in ops["act"]:
                    f(h)

            @block.vector
            def _(h):
                for f in ops["dve"]:
                    f(h)

            @block.gpsimd
            def _(h):
                for f in ops["pool"]:
                    f(h)

            @block.sync
            def _(h):
                for f in ops["sp"]:
                    f(h)


class Arena:
    def __init__(self, S, tens, nf32):
        self.S = S
        self.t = tens
        self.n = nf32
        self.off = 0

    def reset(self):
        self.off = 0

    def _view(self, ap, shape):
        if len(shape) == 2:
            return ap
        if len(shape) == 3:
            return ap.rearrange("p (a b) -> p a b", a=shape[1])
        if len(shape) == 4:
            return ap.rearrange("p (a b c) -> p a b c", a=shape[1], b=shape[2])
        raise ValueError

    def f32(self, name, shape, nreg=None):
        n = int(np.prod(shape[1:]))
        assert self.off + n <= self.n, (name, self.off, n, self.n)
        ap = self.t[0:shape[0], self.off:self.off + n]
        self.off += n
        v = self._view(ap, shape)
        if nreg is None:
            return v, self.S.reg(name)
        return v, [self.S.reg("%s%d" % (name, i)) for i in range(nreg)]

    def bf16(self, name, shape, nreg=None):
        n = int(np.prod(shape[1:]))
        n32 = (n + 1) // 2
        assert self.off + n32 <= self.n, (name, self.off, n32, self.n)
        ap = self.t[0:shape[0], self.off:self.off + n32].bitcast(BF16)[:, 0:n]
        self.off += n32
        v = self._view(ap, shape)
        if nreg is None:
            return v, self.S.reg(name)
        return v, [self.S.reg("%s%d" % (name, i)) for i in range(nreg)]


def run_interleaved(gens, width, stagger):
    gens = list(gens)
    active = []
    nxt = 0
    while nxt < len(gens) or active:
        if nxt < len(gens) and len(active) < width and (not active or active[-1][1] >= stagger):
            active.append([gens[nxt], 0])
            nxt += 1
        for item in list(active):
            try:
                next(item[0])
                item[1] += 1
            except StopIteration:
                active.remove(item)

PARAM_SHAPES = {
    "attn_norm_w": [2, 1024], "w_in": [2, 1024, 3592], "diff_lambda": [2, 4, 64], "diff_subln_w": [2, 128],
    "hgrn_lb_logits": [2, 256], "hgrn_norm_w": [2, 64], "dn_conv_w": [2, 4, 768], "dn_A_log": [2, 4],
    "dn_dt_bias": [2, 4], "dn_norm_w": [2, 64], "w_out": [2, 1024, 1024], "ffn_norm_w": [2, 1024],
    "ffn_w_up": [2, 1024, 5632], "ffn_conv_w": [2, 3, 5632], "ffn_conv_b": [2, 5632], "ffn_w_down": [2, 2816, 1024],
    "final_norm_w": [1024],
}
FM_COLS = [i * 128 for i in range(8)] + [1536, 1664, 1792, 1920] + [2560 + i * 128 for i in range(6)]
TM_BLOCKS = [(1024, 512, 0), (1792, 512, 512), (2304, 256, 1024), (3328, 264, 1280)]
PTM_W = 1544


def make_consts():
    c = {}
    c["ident"] = np.eye(128, dtype=np.float32)
    s = np.arange(128)[:, None]
    t = np.arange(128)[None, :]
    c["tri_incl"] = (s <= t).astype(np.float32)
    same16 = (s // 16 == t // 16)
    c["L16"] = (same16 & (s <= t)).astype(np.float32)
    c["SU16"] = (same16 & (s > t)).astype(np.float32)
    c["blk16"] = (np.arange(128)[:, None] // 16 == np.arange(8)[None, :]).astype(np.float32)
    s6 = np.arange(64)[:, None]
    t6 = np.arange(64)[None, :]
    c["L64"] = (s6 <= t6).astype(np.float32)
    c["SL64"] = (s6 < t6).astype(np.float32)
    c["LT64"] = (s6 >= t6).astype(np.float32)
    c["SLT64"] = (s6 > t6).astype(np.float32)
    c["ones64"] = np.ones((64, 64), np.float32)
    oh = np.zeros((4, 4, 64), np.float32)
    for h in range(4):
        oh[h, h, :] = 1.0
    c["onehot"] = oh.reshape(4, 256)
    return c


class K:
    pass


def build_program(debug=False, stop_after=None, nlayers=2, only=None):
    nc = bass.Bass("TRN2", target_bir_lowering=False)
    k = K()
    k.nc = nc
    k.debug = debug
    k.x = nc.dram_tensor("x", [T, D], F32, kind="ExternalInput").ap()
    k.p = {n: nc.dram_tensor(n, s, F32, kind="ExternalInput").ap() for n, s in PARAM_SHAPES.items()}
    consts = make_consts()
    k.c = {n: nc.dram_tensor("c_" + n, list(v.shape), F32, kind="ExternalInput").ap() for n, v in consts.items()}
    k.out = nc.dram_tensor("out", [T, D], F32, kind="ExternalOutput").ap()
    skind = "ExternalOutput" if debug else "Internal"
    k.mix_in = nc.dram_tensor("mix_in", [T, D], F32, kind="ExternalInput").ap() if debug == "mixin" else None

    def scratch(name, shape, dt):
        return nc.dram_tensor(name, shape, dt, kind=skind).ap()
    k.H = scratch("s_H", [T, D], F32)
    k.PFMb = scratch("s_PFMb", [1024, T], BF16)
    k.PFMf = scratch("s_PFMf", [1280, T], F32)
    k.PTM = scratch("s_PTM", [T, PTM_W], F32)
    k.MIX = scratch("s_MIX", [T, D], BF16)
    k.ACTS = scratch("s_ACTS", [DFF, T], BF16)
    k.GQT = scratch("s_GQT", [256, T], F32)
    k.GKT = scratch("s_GKT", [256, T], F32)
    k.GKtm = scratch("s_GKtm", [T, 256], F32)
    k.GVtm = scratch("s_GVtm", [T, 256], F32)
    k.HK = scratch("s_HK", [T, 256], BF16)
    k.GO = scratch("s_GO", [T, 256], F32)

    with ExitStack() as st:
        S = Sched(nc, st)
        k.S = S
        ARENA_N = 51000
        arena_t = st.enter_context(nc.sbuf_tensor("arena", [128, ARENA_N], F32))
        k.A = Arena(S, arena_t, ARENA_N)
        cst_t = st.enter_context(nc.sbuf_tensor("cst", [128, 2048], F32))
        k.CA = Arena(S, cst_t, 2048)
        k.banks = [st.enter_context(nc.psum_tensor("bank%d" % i, [128, 512], F32)) for i in range(8)]
        setup_consts(k)
        src = k.x
        for l in range(nlayers):
            if only is None or "inproj" in only:
                phase_inproj(k, l, src)
            if stop_after == ("inproj", l):
                break
            if only is None or "attn" in only:
                phase_attn(k, l)
            if stop_after == ("attn", l):
                break
            if only is None or "hgrn" in only:
                phase_hgrn(k, l)
            if stop_after == ("hgrn", l):
                break
            if only is None or "gdn" in only:
                phase_gdn(k, l)
            if stop_after == ("gdn", l):
                break
            if only is None or "ffn" in only:
                phase_outproj_ffn(k, l, src, last=(l == nlayers - 1))
            if stop_after == ("ffn", l):
                break
            src = k.H
        S.barrier()
        S.finish()
    return nc


def setup_consts(k):
    S, CA = k.S, k.CA
    k.ident, k.identr = CA.f32("ident", [128, 128])
    S.dma("sp", k.ident, k.c["ident"], writes=[k.identr], sbreg=k.identr)
    k.identb, k.identbr = CA.bf16("identb", [128, 128])
    S.op("dve", lambda h: h.tensor_copy(k.identb, k.ident), reads=[k.identr], writes=[k.identbr])
    k.trib, k.tribr = CA.bf16("trib", [128, 128])
    tmp, tmpr = CA.f32("tmpc", [128, 128])
    S.dma("sp", tmp, k.c["tri_incl"], writes=[tmpr], sbreg=tmpr)
    S.op("dve", lambda h: h.tensor_copy(k.trib, tmp), reads=[tmpr], writes=[k.tribr])
    for n in ("L16", "SU16"):
        ap, r = CA.f32(n, [128, 128])
        S.dma("sp", ap, k.c[n], writes=[r], sbreg=r)
        setattr(k, n, ap)
        setattr(k, n + "r", r)
    k.blk16, k.blk16r = CA.f32("blk16", [128, 8])
    S.dma("sp", k.blk16, k.c["blk16"], writes=[k.blk16r], sbreg=k.blk16r)
    for n in ("L64", "SL64", "LT64", "SLT64", "ones64"):
        ap, r = CA.f32(n, [64, 64])
        S.dma("sp", ap, k.c[n], writes=[r], sbreg=r)
        setattr(k, n, ap)
        setattr(k, n + "r", r)
    k.onehot, k.onehotr = CA.f32("onehot", [4, 4, 64])
    S.dma("sp", k.onehot, k.c["onehot"].rearrange("a (b c) -> a b c", b=4), writes=[k.onehotr], sbreg=k.onehotr)
    k.ssF, k.ssFr = CA.f32("ssF", [128, NT])
    k.ssA, k.ssAr = CA.f32("ssA", [128, NT])
    k.epst, k.epsr = CA.f32("epst", [128, 1])
    S.op("pool", lambda h: h.memset(k.epst, EPS), writes=[k.epsr])
    S.barrier(release=False)
    k.creg = [k.identr, k.identbr, k.tribr, k.L16r, k.SU16r, k.blk16r, k.L64r, k.SL64r, k.LT64r, k.SLT64r, k.ones64r, k.onehotr, k.epsr]


def rstd_from_ss(k, rs, rsr, ss, ssr, n, eng_small="dve"):
    S = k.S
    S.op("dve", lambda h: h.tensor_scalar(rs, ss, 1.0 / n, EPS, ALU.mult, ALU.add), reads=[ssr], writes=[rsr])
    S.op("act", lambda h: h.activation(rs, rs, AF.Ln), reads=[rsr], writes=[rsr])
    S.op("act", lambda h: h.activation(rs, rs, AF.Exp, scale=-0.5), reads=[rsr], writes=[rsr])


def norm_transpose_pass(k, src, uT, uTr, hb, ub, ss_name, do_norm=True, ss_pre=None, src_bf16=False):
    S, A = k.S, k.A
    rs, rsr = A.f32(ss_name + "_rs", [128, NT])
    if do_norm and ss_pre is not None:
        rstd_from_ss(k, rs, rsr, ss_pre[0], ss_pre[1], D)
    elif do_norm:
        ss, ssr = A.f32(ss_name, [128, NT])
        junk, junkr = A.f32(ss_name + "_junk", [128, D])
        S.op("dve", lambda h: h.memset(ss, 0.0), writes=[ssr])
        for i in range(NT):
            ht, htr = hb[i % len(hb)]
            S.dma("sp", ht, src[i * 128:(i + 1) * 128, :], writes=[htr], sbreg=htr)
            S.op("act", lambda h, ht=ht, i=i: h.activation(junk, ht, AF.Square, accum_out=ss[:, i:i + 1]),
                 reads=[htr, ssr], writes=[junkr, ssr])
        rstd_from_ss(k, rs, rsr, ss, ssr, D)

    def block(tb):
        for i in range(tb * 4, tb * 4 + 4):
            ht, htr = hb[i % len(hb)]
            u, ur = ub[i % len(ub)]
            if src_bf16:
                S.dma("sp", u, src[i * 128:(i + 1) * 128, :], writes=[ur], sbreg=ur)
            else:
                S.dma("sp", ht, src[i * 128:(i + 1) * 128, :], writes=[htr], sbreg=htr)
            if src_bf16:
                pass
            elif do_norm:
                S.op("dve", lambda h, ht=ht, u=u, i=i: h.tensor_scalar(u, ht, rs[:, i:i + 1], None, ALU.mult),
                     reads=[htr, rsr], writes=[ur])
            else:
                S.op("dve", lambda h, ht=ht, u=u: h.tensor_copy(u, ht), reads=[htr], writes=[ur])
            bank = k.banks[i % 2]
            bankr = k.bankr[i % 2]
            pt = bank[:, :].bitcast(BF16)
            for c in range(8):
                S.op("pe", lambda h, c=c, u=u, pt=pt: h.transpose(pt[:, c * 128:(c + 1) * 128], u[:, c * 128:(c + 1) * 128], k.identb),
                     reads=[ur, k.identbr], writes=[bankr], pe_accum=True)
            eng = "act" if i % 2 == 0 else "dve"
            if eng == "act":
                S.op("act", lambda h, pt=pt, i=i: h.copy(uT[:, :, i * 128:(i + 1) * 128], pt.rearrange("p (c t) -> p c t", c=8)),
                     reads=[bankr], writes=[uTr[i]])
            else:
                S.op("dve", lambda h, pt=pt, i=i: h.tensor_copy(uT[:, :, i * 128:(i + 1) * 128], pt.rearrange("p (c t) -> p c t", c=8)),
                     reads=[bankr], writes=[uTr[i]])
    return block


def phase_inproj(k, l, src):
    S, A = k.S, k.A
    S.barrier()
    A.reset()
    k.bankr = [S.reg("bank%d" % i) for i in range(8)]
    uT, uTr = A.bf16("uT", [128, 8, T], nreg=NT)
    Wb, Wbr = A.bf16("Wb", [128, 8, DIN], nreg=15)
    nw, nwr = A.f32("nw", [128, 8])
    S.dma("sp", nw, k.p["attn_norm_w"][l].rearrange("(c p) -> p c", p=128), writes=[nwr], sbreg=nwr, allow_slow_non_contiguous=True)
    wst = [A.f32("wst%d" % i, [128, 8, 256]) for i in range(2)]
    for bi, c0 in enumerate(range(0, DIN, 256)):
        cw = min(256, DIN - c0)
        wt, wtr = wst[bi % 2]
        S.dma("sp", wt[:, :, :cw], k.p["w_in"][l, :, c0:c0 + cw].rearrange("(c p) e -> p c e", p=128), writes=[wtr], sbreg=wtr)
        for c in range(8):
            if c % 2 == 0:
                S.op("dve", lambda h, c=c, wt=wt, c0=c0, cw=cw: h.tensor_scalar(Wb[:, c, c0:c0 + cw], wt[:, c, :cw], nw[:, c:c + 1], None, ALU.mult),
                     reads=[wtr, nwr], writes=[Wbr[bi]])
            else:
                S.op("act", lambda h, c=c, wt=wt, c0=c0, cw=cw: h.activation(Wb[:, c, c0:c0 + cw], wt[:, c, :cw], AF.Copy, scale=nw[:, c:c + 1]),
                     reads=[wtr, nwr], writes=[Wbr[bi]])
    hb = [A.f32("hb%d" % i, [128, D]) for i in range(3)]
    ub = [A.bf16("ub%d" % i, [128, D]) for i in range(2)]
    block = norm_transpose_pass(k, src, uT, uTr, hb, ub, "ssA", ss_pre=((k.ssA, k.ssAr) if l > 0 else None))
    fms = [A.f32("fms%d" % i, [128, 512]) for i in range(4)]
    tms = [A.f32("tms%d" % i, [128, PTM_W]) for i in range(2)]
    cnt = {"acc": 0, "fm": 0, "tm": 0, "ev": 0}

    def wreg(c0, cw):
        return [Wbr[b] for b in range(c0 // 256, (c0 + cw - 1) // 256 + 1)]

    def mm_block(tb):
        for ci, co in enumerate(FM_COLS):
            bank = k.banks[2 + cnt["acc"] % 4]
            bankr = k.bankr[2 + cnt["acc"] % 4]
            cnt["acc"] += 1
            for c in range(8):
                S.op("pe", lambda h, c=c, co=co, bank=bank: h.matmul(bank[:, :], Wb[:, c, co:co + 128], uT[:, c, tb * 512:(tb + 1) * 512], start=(c == 0), stop=(c == 7)),
                     reads=wreg(co, 128) + uTr[tb * 4:tb * 4 + 4], writes=[bankr], pe_accum=(c > 0))
            st_, str_ = fms[cnt["fm"] % 4]
            cnt["fm"] += 1
            if ci < 8:
                stv = st_[:, 0:256].bitcast(BF16)
                dst = k.PFMb[ci * 128:(ci + 1) * 128, tb * 512:(tb + 1) * 512]
            else:
                stv = st_
                dst = k.PFMf[(ci - 8) * 128:(ci - 7) * 128, tb * 512:(tb + 1) * 512]
            eng = "act" if cnt["ev"] % 2 == 0 else "dve"
            cnt["ev"] += 1
            if eng == "act":
                S.op("act", lambda h, stv=stv, bank=bank: h.copy(stv, bank[:, :]), reads=[bankr], writes=[str_])
            else:
                S.op("dve", lambda h, stv=stv, bank=bank: h.tensor_copy(stv, bank[:, :]), reads=[bankr], writes=[str_])
            S.dma("pool", dst, stv, reads=[str_], sbreg=str_)
        for i in range(tb * 4, tb * 4 + 4):
            st_, str_ = tms[cnt["tm"] % 2]
            cnt["tm"] += 1
            for (sc, w, dc) in TM_BLOCKS:
                bank = k.banks[2 + cnt["acc"] % 4]
                bankr = k.bankr[2 + cnt["acc"] % 4]
                cnt["acc"] += 1
                for c in range(8):
                    S.op("pe", lambda h, c=c, sc=sc, w=w, bank=bank, i=i: h.matmul(bank[:, :w], uT[:, c, i * 128:(i + 1) * 128], Wb[:, c, sc:sc + w], start=(c == 0), stop=(c == 7)),
                         reads=wreg(sc, w) + [uTr[i]], writes=[bankr], pe_accum=(c > 0))
                eng = "act" if cnt["ev"] % 2 == 0 else "dve"
                cnt["ev"] += 1
                if eng == "act":
                    S.op("act", lambda h, st_=st_, bank=bank, w=w, dc=dc: h.copy(st_[:, dc:dc + w], bank[:, :w]), reads=[bankr], writes=[str_])
                else:
                    S.op("dve", lambda h, st_=st_, bank=bank, w=w, dc=dc: h.tensor_copy(st_[:, dc:dc + w], bank[:, :w]), reads=[bankr], writes=[str_])
            S.dma("pool", k.PTM[i * 128:(i + 1) * 128, :], st_, reads=[str_], sbreg=str_)

    block(0)
    for tb in range(8):
        if tb + 1 < 8:
            block(tb + 1)
        mm_block(tb)

def phase_attn(k, l):
    S, A = k.S, k.A
    S.barrier()
    A.reset()
    k.bankr = [S.reg("bank%d" % i) for i in range(8)]
    lambda_init = 0.8 - 0.6 * math.exp(-0.3 * l)
    lp, lpr = A.f32("lp", [128, 4, 64])
    S.dma("sp", lp, k.p["diff_lambda"][l].rearrange("a b -> (a b)").partition_broadcast(128).rearrange("p (a b) -> p a b", a=4),
          writes=[lpr], sbreg=lpr)
    pr, prr = A.f32("lpp", [128, 2, 64])
    S.op("dve", lambda h: h.tensor_tensor(pr[:, 0, :], lp[:, 0, :], lp[:, 1, :], ALU.mult), reads=[lpr], writes=[prr])
    S.op("dve", lambda h: h.tensor_tensor(pr[:, 1, :], lp[:, 2, :], lp[:, 3, :], ALU.mult), reads=[lpr], writes=[prr])
    ls, lsr = A.f32("ls", [128, 2])
    S.op("dve", lambda h: h.tensor_reduce(ls, pr, AX.X, ALU.add), reads=[prr], writes=[lsr])
    S.op("act", lambda h: h.activation(ls, ls, AF.Exp), reads=[lsr], writes=[lsr])
    lam, lamr = A.f32("lam", [128, 1])
    S.op("dve", lambda h: h.tensor_tensor(lam, ls[:, 0:1], ls[:, 1:2], ALU.subtract), reads=[lsr], writes=[lamr])
    S.op("dve", lambda h: h.tensor_scalar(lam, lam, lambda_init, None, ALU.add), reads=[lamr], writes=[lamr])
    wsub, wsubr = A.f32("wsub", [128, 128])
    S.dma("sp", wsub, k.p["diff_subln_w"][l].partition_broadcast(128), writes=[wsubr], sbreg=wsubr)
    S.op("dve", lambda h: h.tensor_scalar(wsub, wsub, 1.0 - lambda_init, None, ALU.mult), reads=[wsubr], writes=[wsubr])

    kTb = [A.bf16("kT%d" % i, [128, T]) for i in range(2)]
    qzb = [[A.bf16("qz%d_%d" % (i, m), [128, T]) for m in range(2)] for i in range(2)]
    for i in range(2):
        S.op("pool", lambda h, i=i: h.memset(qzb[i][0][0][64:128, :], 0.0), writes=[qzb[i][0][1]])
        S.op("pool", lambda h, i=i: h.memset(qzb[i][1][0][0:64, :], 0.0), writes=[qzb[i][1][1]])
    Vf, Vfr = A.f32("Vf", [128, NT, 128])
    Vab = [A.bf16("Va%d" % i, [128, NT, 129]) for i in range(2)]
    PTs = [A.bf16("PT%d" % i, [128, 512]) for i in range(4)]
    Oacc = [A.f32("Oacc%d" % i, [128, 4, 129]) for i in range(2)]
    ot, otr = A.f32("ot", [128, 4, 128])
    t1, t1r = A.f32("t1", [128, 4, 128])
    outb = [A.bf16("outb%d" % i, [128, 4, 128]) for i in range(2)]
    r01, r01r = A.f32("r01", [128, 2, 4])
    ssq, ssqr = A.f32("ssq", [128, 4])
    rsq, rsqr = A.f32("rsq", [128, 4])
    cnt = {"s": 0, "pt": 0, "po": 0, "ob": 0}

    def load_head(hh):
        kT, kTr = kTb[hh % 2]
        Va, Var = Vab[hh % 2]
        for m in range(2):
            qz, qzr = qzb[hh % 2][m]
            S.dma("sp", qz[m * 64:(m + 1) * 64, :], k.PFMb[hh * 128 + m * 64:hh * 128 + (m + 1) * 64, :], writes=[qzr], sbreg=qzr)
        S.dma("sp", kT, k.PFMb[512 + hh * 128:512 + (hh + 1) * 128, :], writes=[kTr], sbreg=kTr)
        S.dma("sp", Vf, k.PTM[:, hh * 128:(hh + 1) * 128].rearrange("(n p) e -> p n e", p=128), writes=[Vfr], sbreg=Vfr)
        S.op("pool", lambda h: h.tensor_copy(Va[:, :, 0:128], Vf), reads=[Vfr], writes=[Var])
        S.op("pool", lambda h: h.memset(Va[:, :, 128:129], 1.0), writes=[Var])

    steps = []

    def make_head(hh):
        kT, kTr = kTb[hh % 2]
        Va, Var = Vab[hh % 2]
        for Q in range(8):
            for m in range(2):
                qT, qTr = qzb[hh % 2][m]
                pbanks = [k.banks[2 + c] for c in range(4)]
                pbankr = [k.bankr[2 + c] for c in range(4)]
                first = [True, True, True, True]
                nj = 4 * Q + 4
                for j in range(nj):
                    i = j - 4 * Q
                    i0 = max(i, 0)
                    ncols = 512 - i0 * 128
                    qs = Q * 512 + i0 * 128
                    st = {}

                    def s_stage(st=st, j=j, i=i, ncols=ncols, qs=qs, m=m, kT=kT, qT=qT, kTr=kTr, qTr=qTr, hh=hh, Q=Q):
                        sbi = (0, 1, 6, 7)[cnt["s"] % 4]
                        sb = k.banks[sbi]
                        sbr = k.bankr[sbi]
                        cnt["s"] += 1
                        S.op("pe", lambda h: h.matmul(sb[:, :ncols], kT[:, j * 128:(j + 1) * 128], qT[:, qs:qs + ncols], start=True, stop=True),
                             reads=[kTr, qTr], writes=[sbr])
                        PT, PTr = PTs[cnt["pt"] % 4]
                        cnt["pt"] += 1
                        S.op("act", lambda h: h.activation(PT[:, :ncols], sb[:, :ncols], AF.Exp, scale=0.125), reads=[sbr], writes=[PTr])
                        if i >= 0:
                            S.op("dve", lambda h: h.tensor_tensor(PT[:, 0:128], PT[:, 0:128], k.trib, ALU.mult), reads=[PTr, k.tribr], writes=[PTr])
                        st["PT"] = (PT, PTr)

                    def pv_stage(st=st, j=j, i0=i0, Q=Q, m=m, Va=Va, Var=Var, pbanks=pbanks, pbankr=pbankr, first=first, nj=nj, hh=hh):
                        PT, PTr = st["PT"]
                        if Q == 0 and m == 0 and j == 3 and hh + 1 < 4:
                            load_head(hh + 1)
                        for c in range(i0, 4):
                            pb = pbanks[c]
                            S.op("pe", lambda h, pb=pb, c=c: h.matmul(pb[:, 0:129], PT[:, (c - i0) * 128:(c - i0 + 1) * 128], Va[:, j, :], start=(j == 0), stop=(j == 4 * Q + c)),
                                 reads=[PTr, Var], writes=[pbankr[c]], pe_accum=not first[c])
                            first[c] = False
                        if j == nj - 1:
                            Oa, Oar = Oacc[m]
                            for c in range(4):
                                S.op("dve", lambda h, c=c: h.tensor_copy(Oa[:, c, :], pbanks[c][:, 0:129]), reads=[pbankr[c]], writes=[Oar])
                            if m == 1:
                                finalize(hh, Q)
                    steps.append((s_stage, pv_stage))

    def finalize(hh, Q):
        O0, O0r = Oacc[0]
        O1, O1r = Oacc[1]
        S.op("dve", lambda h: h.reciprocal(r01[:, 0, :], O0[:, :, 128]), reads=[O0r], writes=[r01r])
        S.op("dve", lambda h: h.reciprocal(r01[:, 1, :], O1[:, :, 128]), reads=[O1r], writes=[r01r])
        S.op("dve", lambda h: h.tensor_scalar(r01[:, 1, :], r01[:, 1, :], lam[:, 0:1], None, ALU.mult), reads=[r01r, lamr], writes=[r01r])
        S.op("pool", lambda h: h.tensor_tensor(t1, O1[:, :, 0:128], r01[:, 1, :].unsqueeze(2).to_broadcast([128, 4, 128]), ALU.mult),
             reads=[O1r, r01r], writes=[t1r])
        S.op("dve", lambda h: h.tensor_tensor(ot, O0[:, :, 0:128], r01[:, 0, :].unsqueeze(2).to_broadcast([128, 4, 128]), ALU.mult),
             reads=[O0r, r01r], writes=[otr])
        S.op("dve", lambda h: h.tensor_tensor(ot, ot, t1, ALU.subtract), reads=[otr, t1r], writes=[otr])
        S.op("pool", lambda h: h.tensor_tensor(t1, ot, ot, ALU.mult), reads=[otr], writes=[t1r])
        S.op("dve", lambda h: h.tensor_reduce(ssq, t1, AX.X, ALU.add), reads=[t1r], writes=[ssqr])
        rstd_from_ss(k, rsq, rsqr, ssq, ssqr, 128)
        ob, obr = outb[cnt["ob"] % 2]
        cnt["ob"] += 1
        S.op("dve", lambda h: h.tensor_tensor(ot, ot, rsq.unsqueeze(2).to_broadcast([128, 4, 128]), ALU.mult), reads=[otr, rsqr], writes=[otr])
        S.op("pool", lambda h: h.tensor_tensor(ob, ot, wsub.unsqueeze(1).to_broadcast([128, 4, 128]), ALU.mult), reads=[otr, wsubr], writes=[obr])
        S.dma("pool", k.MIX[Q * 512:(Q + 1) * 512, hh * 128:(hh + 1) * 128].rearrange("(c p) e -> p c e", p=128), ob, reads=[obr], sbreg=obr)

    load_head(0)
    for hh in range(4):
        make_head(hh)
    LOOK = 3
    for idx in range(min(LOOK, len(steps))):
        steps[idx][0]()
    for idx in range(len(steps)):
        if idx + LOOK < len(steps):
            steps[idx + LOOK][0]()
        steps[idx][1]()

def phase_hgrn(k, l):
    S, A = k.S, k.A
    S.barrier()
    A.reset()
    k.bankr = [S.reg("bank%d" % i) for i in range(8)]
    NCH = T // 16
    r16, r16r = A.f32("r16", [128, T])
    S.op("pool", lambda h: h.memset(r16, 1.0), writes=[r16r])
    S.op("pool", lambda h: h.memset(r16.rearrange("p (n c) -> p n c", c=16)[:, :, 0:1], 0.0), writes=[r16r])
    lbf, lbfr = A.f32("lbf", [128, 2])
    omf, omfr = A.f32("omf", [128, 2])
    lbt, lbtr = A.f32("lbt", [128, 256])
    omt, omtr = A.f32("omt", [128, 256])
    if l == 0:
        S.op("pool", lambda h: h.memset(lbf, 0.0), writes=[lbfr])
        S.op("pool", lambda h: h.memset(lbt, 0.0), writes=[lbtr])
    else:
        lg, lgr = A.f32("lg", [128, 2, 2])
        for li in range(2):
            S.dma("sp", lg[:, li, :], k.p["hgrn_lb_logits"][li].rearrange("(c p) -> p c", p=128), writes=[lgr], sbreg=lgr, allow_slow_non_contiguous=True)
        S.op("dve", lambda h: h.tensor_tensor(lbf, lg[:, 1, :], lg[:, 0, :], ALU.subtract), reads=[lgr], writes=[lbfr])
        S.op("act", lambda h: h.activation(lbf, lbf, AF.Sigmoid), reads=[lbfr], writes=[lbfr])
        lg2, lg2r = A.f32("lg2", [128, 2, 256])
        S.dma("sp", lg2, k.p["hgrn_lb_logits"].rearrange("a b -> (a b)").partition_broadcast(128).rearrange("p (a b) -> p a b", a=2), writes=[lg2r], sbreg=lg2r)
        S.op("dve", lambda h: h.tensor_tensor(lbt, lg2[:, 1, :], lg2[:, 0, :], ALU.subtract), reads=[lg2r], writes=[lbtr])
        S.op("act", lambda h: h.activation(lbt, lbt, AF.Sigmoid), reads=[lbtr], writes=[lbtr])
    S.op("dve", lambda h: h.tensor_scalar(omf, lbf, -1.0, 1.0, ALU.mult, ALU.add), reads=[lbfr], writes=[omfr])
    S.op("dve", lambda h: h.tensor_scalar(omt, lbt, -1.0, 1.0, ALU.mult, ALU.add), reads=[lbtr], writes=[omtr])
    hnw, hnwr = A.f32("hnw", [128, 64])
    S.dma("sp", hnw, k.p["hgrn_norm_w"][l].partition_broadcast(128), writes=[hnwr], sbreg=hnwr)

    mark0 = A.off
    Kall, Kallr = A.f32("Kall", [128, NT, 256], nreg=4)
    Lall, Lallr = A.f32("Lall", [128, NT, 256], nreg=4)
    for q in range(4):
        sl = slice(q * 8, (q + 1) * 8)
        Kq = Kall[:, sl, :]
        Lq = Lall[:, sl, :]
        S.dma("sp", Kq, k.PTM[q * 1024:(q + 1) * 1024, 512:768].rearrange("(n p) e -> p n e", p=128), writes=[Kallr[q]], sbreg=Kallr[q])
        S.op("act", lambda h, Kq=Kq: h.activation(Kq, Kq, AF.Sigmoid), reads=[Kallr[q]], writes=[Kallr[q]])
    for q in range(4):
        sl = slice(q * 8, (q + 1) * 8)
        Kq = Kall[:, sl, :]
        Lq = Lall[:, sl, :]
        S.op("dve", lambda h, Kq=Kq: h.tensor_tensor(Kq, Kq, omt.unsqueeze(1).to_broadcast([128, 8, 256]), ALU.mult), reads=[Kallr[q], omtr], writes=[Kallr[q]])
        S.op("dve", lambda h, Kq=Kq: h.tensor_tensor(Kq, Kq, lbt.unsqueeze(1).to_broadcast([128, 8, 256]), ALU.add), reads=[Kallr[q], lbtr], writes=[Kallr[q]])
        S.op("dve", lambda h, Kq=Kq, Lq=Lq: h.tensor_scalar(Lq, Kq, 1e-30, None, ALU.max), reads=[Kallr[q]], writes=[Lallr[q]])
        S.op("act", lambda h, Lq=Lq: h.activation(Lq, Lq, AF.Ln), reads=[Lallr[q]], writes=[Lallr[q]])
    for q in range(4):
        for ii in range(8):
            i = q * 8 + ii
            bank = k.banks[i % 4]
            bankr = k.bankr[i % 4]
            S.op("pe", lambda h, bank=bank, i=i: h.matmul(bank[:, 0:256], k.SU16, Lall[:, i, :], start=True, stop=True), reads=[Lallr[q], k.SU16r], writes=[bankr])
            S.op("act", lambda h, bank=bank, i=i: h.activation(Lall[:, i, :], bank[:, 0:256], AF.Exp), reads=[bankr], writes=[Lallr[q]])
        sl = slice(q * 8, (q + 1) * 8)
        Kq = Kall[:, sl, :]
        Lq = Lall[:, sl, :]
        S.op("dve", lambda h, Kq=Kq: h.tensor_scalar(Kq, Kq, -1.0, 1.0, ALU.mult, ALU.add), reads=[Kallr[q]], writes=[Kallr[q]])
        Kbq = Lq.rearrange("p n e -> p (n e)")[:, 0:1024].bitcast(BF16).rearrange("p (n e) -> p n e", n=8)
        S.op("dve", lambda h, Kq=Kq, Lq=Lq: h.tensor_tensor(Kq, Kq, Lq, ALU.mult), reads=[Kallr[q], Lallr[q]], writes=[Kallr[q]])
        S.op("act", lambda h, Kq=Kq, Kbq=Kbq: h.copy(Kbq, Kq), reads=[Kallr[q]], writes=[Lallr[q]])
        S.dma("pool", k.HK[q * 1024:(q + 1) * 1024, :].rearrange("(n p) e -> p n e", p=128), Kbq, reads=[Lallr[q]], sbreg=Lallr[q])
    S.barrier(release=False)
    A.off = mark0

    QtT, QtTr = A.f32("QtT", [128, T])
    KtT, KtTr = A.f32("KtT", [128, T])
    Dn, Dnr = A.f32("Dn", [128, NCH])
    U, Ur = A.f32("U", [128, 64, NCH])
    Ur2 = S.reg("Uscan")
    Oall, Oallr = A.f32("Oall", [128, NT, 128])
    ssh, sshr = A.f32("ssh", [128, NT, 2])
    rsh, rshr = A.f32("rsh", [128, NT, 2])
    vex = [A.bf16("vex%d" % i, [128, 8, 64]) for i in range(4)]
    Qb, Qbr = A.bf16("Qb", [128, T])
    Kb, Kbr = A.bf16("Kb", [128, T])
    ats = [A.bf16("ats%d" % i, [128, 128]) for i in range(4)]
    ots = [A.f32("ots%d" % i, [64, 128]) for i in range(4)]
    ktl = [A.bf16("ktl%d" % i, [128, 128]) for i in range(2)]
    vtl = [A.f32("vtl%d" % i, [128, 128]) for i in range(3)]
    vbl = [A.bf16("vbl%d" % i, [128, 128]) for i in range(3)]
    mark = A.off
    fpT, fpTr = A.f32("fpT", [128, T])
    BT, BTr = A.f32("BT", [128, T])
    for c in range(2):
        S.dma("sp", fpT, k.PFMf[256 + c * 128:256 + (c + 1) * 128, :], writes=[fpTr], sbreg=fpTr)
        S.dma("sp", QtT, k.PFMf[c * 128:(c + 1) * 128, :], writes=[QtTr], sbreg=QtTr)
        S.op("act", lambda h: h.activation(fpT, fpT, AF.Sigmoid), reads=[fpTr], writes=[fpTr])
        S.op("dve", lambda h, c=c: h.tensor_scalar(fpT, fpT, omf[:, c:c + 1], lbf[:, c:c + 1], ALU.mult, ALU.add), reads=[fpTr, omfr, lbfr], writes=[fpTr])
        S.op("pool", lambda h: h.tensor_scalar(KtT, fpT, -1.0, 1.0, ALU.mult, ALU.add), reads=[fpTr], writes=[KtTr])
        S.op("dve", lambda h: h.tensor_scalar(fpT, fpT, 1e-30, None, ALU.max), reads=[fpTr], writes=[fpTr])
        S.op("act", lambda h: h.activation(fpT, fpT, AF.Ln), reads=[fpTr], writes=[fpTr])
        S.op("dve", lambda h: h.tensor_tensor_scan(BT, r16, fpT, 0.0, ALU.mult, ALU.add), reads=[fpTr, r16r], writes=[BTr])
        S.op("act", lambda h: h.activation(Dn, BT.rearrange("p (n c) -> p n c", c=16)[:, :, 15], AF.Exp), reads=[BTr], writes=[Dnr])
        S.op("act", lambda h: h.activation(fpT, BT, AF.Exp), reads=[BTr], writes=[fpTr])
        S.op("act", lambda h: h.activation(BT, BT, AF.Exp, scale=-1.0), reads=[BTr], writes=[BTr])
        S.op("dve", lambda h: h.tensor_tensor(KtT, KtT, BT, ALU.mult), reads=[KtTr, BTr], writes=[KtTr])
        S.op("act", lambda h: h.activation(QtT, QtT, AF.Silu), reads=[QtTr], writes=[QtTr])
        S.op("dve", lambda h: h.tensor_tensor(QtT, QtT, fpT, ALU.mult), reads=[QtTr, fpTr], writes=[QtTr])
        S.op("pool", lambda h: h.tensor_copy(Qb, QtT), reads=[QtTr], writes=[Qbr])
        S.op("pool", lambda h: h.tensor_copy(Kb, KtT), reads=[KtTr], writes=[Kbr])
        def u_unit(i, c=c):
            u = i % 2
            kt, ktr = ktl[i % 2]
            vt, vtr = vtl[i % 3]
            S.dma("sp", kt, k.HK[i * 128:(i + 1) * 128, c * 128:(c + 1) * 128], writes=[ktr], sbreg=ktr)
            S.dma("sp", vt, k.PTM[i * 128:(i + 1) * 128, 768 + c * 128:768 + (c + 1) * 128], writes=[vtr], sbreg=vtr)
            for hl in range(2):
                vx, vxr = vex[u * 2 + hl]
                eng = "dve" if hl == 0 else "pool"
                S.op(eng, lambda h, vx=vx, hl=hl: h.tensor_tensor(
                    vx, vt[:, hl * 64:(hl + 1) * 64].unsqueeze(1).to_broadcast([128, 8, 64]),
                    k.blk16.unsqueeze(2).to_broadcast([128, 8, 64]), ALU.mult), reads=[vtr, k.blk16r], writes=[vxr])
            yield
            for hl in range(2):
                vx, vxr = vex[u * 2 + hl]
                bank = k.banks[u * 2 + hl]
                bankr = k.bankr[u * 2 + hl]
                S.op("pe", lambda h, bank=bank, vx=vx: h.matmul(bank[:, :], kt, vx.rearrange("p n v -> p (n v)"), start=True, stop=True),
                     reads=[ktr, vxr], writes=[bankr])
            yield
            for hl in range(2):
                bank = k.banks[u * 2 + hl]
                bankr = k.bankr[u * 2 + hl]
                p0 = hl * 64
                if hl == 0:
                    S.op("act", lambda h, bank=bank, p0=p0: h.copy(U[p0:p0 + 64, :, i * 8:(i + 1) * 8], bank[p0:p0 + 64, :].rearrange("d (n v) -> d v n", n=8)), reads=[bankr], writes=[Ur])
                else:
                    S.op("dve", lambda h, bank=bank, p0=p0: h.tensor_copy(U[p0:p0 + 64, :, i * 8:(i + 1) * 8], bank[p0:p0 + 64, :].rearrange("d (n v) -> d v n", n=8)), reads=[bankr], writes=[Ur])
        run_interleaved([u_unit(i) for i in range(NT)], 2, 1)
        for v in range(64):
            S.op("dve", lambda h, v=v: h.tensor_tensor_scan(U[:, v, :], Dn, U[:, v, :], 0.0, ALU.mult, ALU.add), reads=[Ur, Dnr], writes=[Ur2] if v else [Ur, Ur2])
        Ub = k.A.t[0:128, mark:mark + 8192].bitcast(BF16).rearrange("p (v n) -> p v n", n=NCH)
        Ubr = S.reg("Ub")
        S.op("dve", lambda h: h.tensor_copy(Ub, U), reads=[Ur, Ur2, fpTr, BTr, QtTr, KtTr, Qbr, Kbr], writes=[Ubr, fpTr, BTr])
        def o_unit(i, c=c):
            u = i % 2
            vt, vtr = vtl[i % 3]
            vb, vbr = vbl[i % 3]
            S.dma("sp", vt, k.PTM[i * 128:(i + 1) * 128, 768 + c * 128:768 + (c + 1) * 128], writes=[vtr], sbreg=vtr)
            S.op("pool", lambda h: h.tensor_copy(vb, vt), reads=[vtr], writes=[vbr])
            for hl in range(2):
                p0 = hl * 64
                ba = k.banks[u * 2 + hl]
                bar = k.bankr[u * 2 + hl]
                S.op("pe", lambda h, ba=ba, p0=p0: h.matmul(ba[:, 0:128], Kb[p0:p0 + 64, i * 128:(i + 1) * 128], Qb[p0:p0 + 64, i * 128:(i + 1) * 128], start=True, stop=True),
                     reads=[Kbr, Qbr], writes=[bar])
            yield
            for hl in range(2):
                ba = k.banks[u * 2 + hl]
                bar = k.bankr[u * 2 + hl]
                at, atr = ats[u * 2 + hl]
                S.op("dve", lambda h, at=at, ba=ba: h.tensor_tensor(at, ba[:, 0:128], k.L16, ALU.mult), reads=[bar, k.L16r], writes=[atr])
            yield
            for hl in range(2):
                p0 = hl * 64
                at, atr = ats[u * 2 + hl]
                bo = k.banks[4 + u * 2 + hl]
                bor = k.bankr[4 + u * 2 + hl]
                S.op("pe", lambda h, bo=bo, at=at, p0=p0: h.matmul(bo[0:64, 0:128], vb[:, p0:p0 + 64], at, start=True, stop=False),
                     reads=[vbr, atr], writes=[bor])
                for n in range(8):
                    g = i * 8 + n
                    if g == 0:
                        continue
                    S.op("pe", lambda h, bo=bo, g=g, n=n, p0=p0: h.matmul(bo[0:64, n * 16:(n + 1) * 16], Ub[p0:p0 + 64, :, g - 1], Qb[p0:p0 + 64, g * 16:(g + 1) * 16], start=False, stop=(n == 7)),
                         reads=[Ubr, fpTr, BTr, Qbr], writes=[bor], pe_accum=True)
            yield
            for hl in range(2):
                bo = k.banks[4 + u * 2 + hl]
                bor = k.bankr[4 + u * 2 + hl]
                ot, otr = ots[u * 2 + hl]
                S.op("act", lambda h, ot=ot, bo=bo: h.copy(ot, bo[0:64, 0:128]), reads=[bor], writes=[otr])
            yield
            for hl in range(2):
                ot, otr = ots[u * 2 + hl]
                bt = k.banks[u * 2 + hl]
                btr = k.bankr[u * 2 + hl]
                S.op("pe", lambda h, bt=bt, ot=ot: h.transpose(bt[:, 0:64], ot, k.ident[0:64, 0:64]), reads=[otr, k.identr], writes=[btr])
            yield
            for hl in range(2):
                p0 = hl * 64
                bt = k.banks[u * 2 + hl]
                btr = k.bankr[u * 2 + hl]
                S.op("dve", lambda h, bt=bt, p0=p0: h.tensor_copy(Oall[:, i, p0:p0 + 64], bt[:, 0:64]), reads=[btr], writes=[Oallr])
        run_interleaved([o_unit(i) for i in range(NT)], 2, 3)
        Osq, Osqr = fpT.rearrange("p (n e) -> p n e", n=NT), fpTr
        Gall, Gallr = BT.rearrange("p (n e) -> p n e", n=NT), BTr
        S.dma("sp", Gall, k.PTM[:, 1024 + c * 128:1024 + (c + 1) * 128].rearrange("(n p) e -> p n e", p=128), writes=[Gallr], sbreg=Gallr)
        S.op("act", lambda h: h.activation(Gall, Gall, AF.Silu), reads=[Gallr], writes=[Gallr])
        S.op("pool", lambda h: h.tensor_tensor(Osq, Oall, Oall, ALU.mult), reads=[Oallr], writes=[Osqr])
        S.op("dve", lambda h: h.tensor_reduce(ssh.rearrange("p n g -> p (n g)"), Osq.rearrange("p n (g v) -> p (n g) v", g=2), AX.X, ALU.add), reads=[Osqr], writes=[sshr])
        rstd_from_ss(k, rsh.rearrange("p n g -> p (n g)"), rshr, ssh.rearrange("p n g -> p (n g)"), sshr, 64)
        S.op("dve", lambda h: h.tensor_tensor(Oall.rearrange("p n (g v) -> p (n g) v", g=2), Oall.rearrange("p n (g v) -> p (n g) v", g=2),
                                              rsh.rearrange("p n g -> p (n g)").unsqueeze(2).to_broadcast([128, NT * 2, 64]), ALU.mult), reads=[Oallr, rshr], writes=[Oallr])
        S.op("pool", lambda h: h.tensor_tensor(Oall.rearrange("p n (g v) -> p (n g) v", g=2), Oall.rearrange("p n (g v) -> p (n g) v", g=2),
                                               hnw.unsqueeze(1).to_broadcast([128, NT * 2, 64]), ALU.mult), reads=[Oallr, hnwr], writes=[Oallr])
        Ob = fpT[:, 0:2048].bitcast(BF16).rearrange("p (n e) -> p n e", n=NT)
        S.op("dve", lambda h: h.tensor_tensor(Ob, Oall, Gall, ALU.mult), reads=[Oallr, Gallr, fpTr], writes=[fpTr])
        S.dma("pool", k.MIX[:, 512 + c * 128:512 + (c + 1) * 128].rearrange("(n p) e -> p n e", p=128), Ob, reads=[fpTr], sbreg=fpTr)

def phase_gdn(k, l):
    S, A = k.S, k.A
    S.barrier()
    A.reset()
    k.bankr = [S.reg("bank%d" % i) for i in range(8)]
    cwd, cwdr = A.f32("cwd", [128, 4, 6])
    for j in range(4):
        S.dma("sp", cwd[:, j, :], k.p["dn_conv_w"][l, j].rearrange("(c p) -> p c", p=128), writes=[cwdr], sbreg=cwdr, allow_slow_non_contiguous=True)
    bo, bor = A.f32("bo128", [128, 128])
    S.op("pool", lambda h: h.memset(bo, 0.0), writes=[bor])
    S.op("pool", lambda h: h.memset(bo[0:64, 0:64], 1.0), writes=[bor])
    S.op("pool", lambda h: h.memset(bo[64:128, 64:128], 1.0), writes=[bor])
    Xb = [A.f32("Xc%d" % i, [128, T + 3]) for i in range(2)]
    Yb = [A.f32("Yc%d" % i, [128, T]) for i in range(2)]
    sqb = [A.f32("sqc%d" % i, [128, T]) for i in range(2)]
    tms = [A.f32("tmsg%d" % i, [128, NT, 128]) for i in range(2)]
    for i in range(2):
        S.op("pool", lambda h, i=i: h.memset(Xb[i][0][:, 0:3], 0.0), writes=[Xb[i][1]])
    for c in range(6):
        g1_chunk(k, c, Xb[c % 2], Yb[c % 2], sqb[c % 2], tms, cwd, cwdr, bo, bor)
    if False:
        X, Xr = Xb[0]; Y, Yr = Yb[0]; sq, sqr = sqb[0]; c = 0
        pass

    S.barrier()
    A.reset()
    k.bankr = [S.reg("bank%d" % i) for i in range(8)]
    NC = 64
    HP = 2
    ab, abr = A.f32("ab", [128, NC, 4])
    alog, alogr = A.f32("alog", [128, HP])
    dtb, dtbr = A.f32("dtb", [128, HP])
    for hb in range(2):
        ph = slice(hb * 64, (hb + 1) * 64)
        for which in range(2):
            c0 = 1536 + which * 4 + hb
            for hp in range(HP):
                S.dma("sp", ab[ph, :, which * 2 + hp:which * 2 + hp + 1], k.PTM[:, c0 + 2 * hp:c0 + 2 * hp + 1].rearrange("(n p) e -> p n e", p=64),
                      writes=[abr], sbreg=abr, allow_slow_non_contiguous=True)
        S.dma("sp", alog[ph, :], k.p["dn_A_log"][l].rearrange("(hp hb) -> hb hp", hb=2)[hb].partition_broadcast(64), writes=[alogr], sbreg=alogr, allow_slow_non_contiguous=True)
        S.dma("sp", dtb[ph, :], k.p["dn_dt_bias"][l].rearrange("(hp hb) -> hb hp", hb=2)[hb].partition_broadcast(64), writes=[dtbr], sbreg=dtbr, allow_slow_non_contiguous=True)
    S.op("act", lambda h: h.activation(alog, alog, AF.Exp), reads=[alogr], writes=[alogr])
    S.op("dve", lambda h: h.tensor_scalar(alog, alog, -1.0, None, ALU.mult), reads=[alogr], writes=[alogr])
    L64b, L64br = A.f32("L64b", [128, 128])
    on64b, on64br = A.f32("on64b", [128, 128])
    S.op("pool", lambda h: h.memset(L64b, 0.0), writes=[L64br])
    S.op("pool", lambda h: h.memset(on64b, 0.0), writes=[on64br])
    mk = {}
    for nm in ("L64", "SLT64"):
        mk[nm] = A.f32(nm + "x2", [128, 64])
    idx2, idx2r = A.f32("idx2", [128, 64])
    for hb in range(2):
        ph = slice(hb * 64, (hb + 1) * 64)
        S.dma("sp", L64b[ph, hb * 64:(hb + 1) * 64], k.c["L64"], writes=[L64br], sbreg=L64br)
        S.dma("sp", on64b[ph, hb * 64:(hb + 1) * 64], k.c["ones64"], writes=[on64br], sbreg=on64br)
        for nm in ("L64", "SLT64"):
            S.dma("sp", mk[nm][0][ph, :], k.c[nm], writes=[mk[nm][1]], sbreg=mk[nm][1])
        S.dma("sp", idx2[ph, :], k.c["ident"][0:64, 0:64], writes=[idx2r], sbreg=idx2r)
    L64x, L64xr = mk["L64"]
    SLT64x, SLT64xr = mk["SLT64"]

    def sc(name):
        return A.f32(name, [128, NC, HP])
    g, gr = sc("g")
    beta, betar = sc("beta")
    t0, t0r = sc("t0")
    t1, t1r = sc("t1")
    gc, gcr = sc("gc")
    gl, glr = sc("gl")
    egc, egcr = sc("egc")
    ekd, ekdr = sc("ekd")
    egl, eglr = sc("egl")
    nbeta, nbetar = sc("nbeta")
    begc, begcr = sc("begc")
    S.op("dve", lambda h: h.tensor_tensor(t0, ab[:, :, 0:2], dtb.unsqueeze(1).to_broadcast([128, NC, HP]), ALU.add), reads=[abr, dtbr], writes=[t0r])
    S.op("dve", lambda h: h.tensor_scalar(t1, t0, -1.0, None, ALU.mult), reads=[t0r], writes=[t1r])
    S.op("dve", lambda h: h.tensor_tensor(t1, t1, t0, ALU.max), reads=[t0r, t1r], writes=[t1r])
    S.op("act", lambda h: h.activation(t1, t1, AF.Exp, scale=-1.0), reads=[t1r], writes=[t1r])
    S.op("dve", lambda h: h.tensor_scalar(t1, t1, 1.0, None, ALU.add), reads=[t1r], writes=[t1r])
    S.op("act", lambda h: h.activation(t1, t1, AF.Ln), reads=[t1r], writes=[t1r])
    S.op("dve", lambda h: h.tensor_scalar(t0, t0, 0.0, None, ALU.max), reads=[t0r], writes=[t0r])
    S.op("dve", lambda h: h.tensor_tensor(t0, t0, t1, ALU.add), reads=[t0r, t1r], writes=[t0r])
    S.op("dve", lambda h: h.tensor_tensor(g, t0, alog.unsqueeze(1).to_broadcast([128, NC, HP]), ALU.mult), reads=[t0r, alogr], writes=[gr])
    S.op("act", lambda h: h.activation(beta, ab[:, :, 2:4], AF.Sigmoid), reads=[abr], writes=[betar])
    S.op("pe", lambda h: h.matmul(k.banks[0][:, 0:NC * HP], L64b, g.rearrange("p n h -> p (n h)"), start=True, stop=True), reads=[gr, L64br], writes=[k.bankr[0]])
    S.op("pe", lambda h: h.matmul(k.banks[1][:, 0:NC * HP], on64b, g.rearrange("p n h -> p (n h)"), start=True, stop=True), reads=[gr, on64br], writes=[k.bankr[1]])
    S.op("dve", lambda h: h.tensor_copy(gc.rearrange("p n h -> p (n h)"), k.banks[0][:, 0:NC * HP]), reads=[k.bankr[0]], writes=[gcr])
    S.op("dve", lambda h: h.tensor_copy(gl.rearrange("p n h -> p (n h)"), k.banks[1][:, 0:NC * HP]), reads=[k.bankr[1]], writes=[glr])
    S.op("act", lambda h: h.activation(egc, gc, AF.Exp), reads=[gcr], writes=[egcr])
    S.op("act", lambda h: h.activation(egl, gl, AF.Exp), reads=[glr], writes=[eglr])
    S.op("dve", lambda h: h.tensor_tensor(ekd, gl, gc, ALU.subtract), reads=[glr, gcr], writes=[ekdr])
    S.op("act", lambda h: h.activation(ekd, ekd, AF.Exp), reads=[ekdr], writes=[ekdr])
    S.op("dve", lambda h: h.tensor_scalar(nbeta, beta, -1.0, None, ALU.mult), reads=[betar], writes=[nbetar])
    S.op("dve", lambda h: h.tensor_tensor(begc, beta, egc, ALU.mult), reads=[betar, egcr], writes=[begcr])

    CB = 4
    NSL = CB * HP
    NSET = 2
    NB = 8 // CB
    qTg = [A.f32("qTg%d" % i, [128, HP, 512]) for i in range(2)]
    kTg = [A.f32("kTg%d" % i, [128, HP, 512]) for i in range(2)]
    ktg = [A.f32("ktg%d" % i, [128, 8, HP, 64]) for i in range(2)]
    vtg = [A.f32("vtg%d" % i, [128, 8, HP, 64]) for i in range(2)]
    og = [A.f32("og%d" % i, [128, 8, HP, 64]) for i in range(2)]
    qTb = [A.bf16("qTb%d" % i, [128, HP, 512]) for i in range(2)]
    kTb = [A.bf16("kTb%d" % i, [128, HP, 512]) for i in range(2)]

    def bt(name):
        return A.f32(name, [128, NSL, 64])

    def btb(name):
        return A.bf16(name, [128, NSL, 64])
    sets = []
    for si in range(NSET):
        d = {}
        for nm in ("raw", "mx", "mn", "tA", "tB", "ut", "gB"):
            d[nm] = bt(nm + str(si))
        for nm in ("Mt0", "Mt1", "Nt0", "Nt1", "Tt", "aqk", "Ru", "Rw", "kdec", "wT"):
            d[nm] = btb(nm + str(si))
        sets.append(d)
    Sst, Sstr = A.f32("Sst", [128, HP, 64])
    Ssb, Ssbr = A.bf16("Ssb", [128, HP, 64])
    vnew, vnewr = A.bf16("vnew", [128, HP, 64])
    otmp, otmpr = A.f32("otmp", [128, HP, 64])
    S.op("pool", lambda h: h.memset(Sst, 0.0), writes=[Sstr])
    S.op("pool", lambda h: h.memset(Ssb, 0.0), writes=[Ssbr])
    PH = [slice(0, 64), slice(64, 128)]

    def bc_s(ap2):
        return ap2.unsqueeze(2).to_broadcast([128, NSL, 64])

    def mask_b(m):
        return m.unsqueeze(1).to_broadcast([128, NSL, 64])

    def pv(bank):
        return bank[:, :].rearrange("p (a b) -> p a b", b=64)

    def pv2(bank):
        return bank[:, 0:HP * 64].rearrange("p (a b) -> p a b", b=64)

    def group_load(grp):
        qTt, qTtr = qTg[grp % 2]
        kTt, kTtr = kTg[grp % 2]
        ktt, kttr = ktg[grp % 2]
        vtt, vttr = vtg[grp % 2]
        qTtb, qTtbr = qTb[grp % 2]
        kTtb, kTtbr = kTb[grp % 2]
        tsl = slice(grp * 512, (grp + 1) * 512)
        S.dma("sp", qTt, k.GQT[:, tsl].rearrange("(hp p) t -> p hp t", p=128), writes=[qTtr], sbreg=qTtr)
        S.dma("sp", kTt, k.GKT[:, tsl].rearrange("(hp p) t -> p hp t", p=128), writes=[kTtr], sbreg=kTtr)
        S.op("pool", lambda h: h.tensor_copy(qTtb, qTt), reads=[qTtr], writes=[qTtbr])
        S.op("pool", lambda h: h.tensor_copy(kTtb, kTt), reads=[kTtr], writes=[kTtbr])
        for hb in range(2):
            for hp in range(HP):
                hcol = slice((2 * hp + hb) * 64, (2 * hp + hb + 1) * 64)
                S.dma("sp", ktt[PH[hb], :, hp, :], k.GKtm[tsl, hcol].rearrange("(n p) e -> p n e", p=64), writes=[kttr], sbreg=kttr)
                S.dma("sp", vtt[PH[hb], :, hp, :], k.GVtm[tsl, hcol].rearrange("(n p) e -> p n e", p=64), writes=[vttr], sbreg=vttr)

    def do_batch(gb):
        grp, b = divmod(gb, NB)
        if gb == 0:
            group_load(0)
            group_load(1)
        d = sets[gb % NSET]
        bs = (gb % NSET) * 4
        Bk = [k.banks[bs + i] for i in range(4)]
        Br = [k.bankr[bs + i] for i in range(4)]
        qTt, qTtr = qTb[grp % 2]
        kTt, kTtr = kTb[grp % 2]
        ktt, kttr = ktg[grp % 2]
        vtt, vttr = vtg[grp % 2]
        ogt, ogtr = og[grp % 2]
        raw, rawr = d["raw"]; mx, mxr = d["mx"]; mn, mnr = d["mn"]; tA, tAr = d["tA"]; tB, tBr = d["tB"]; gB, gBr = d["gB"]
        Mt = [d["Mt0"], d["Mt1"]]; Nt = [d["Nt0"], d["Nt1"]]
        Tt, Ttr = d["Tt"]; aqk, aqkr = d["aqk"]; Ru, Rur = d["Ru"]; Rw, Rwr = d["Rw"]; kdec, kdecr = d["kdec"]
        ut, utr = d["ut"]; wT, wTr = d["wT"]
        n0 = grp * 8 + b * CB
        ln0 = b * CB
        sc_sl = slice(n0, n0 + CB)

        def sview(ap):
            return ap[:, sc_sl, :].rearrange("p n h -> p (n h)")
        S.op("pool", lambda h: h.tensor_copy(gB, bc_s(sview(g))), reads=[gr], writes=[gBr])
        for cb in range(CB):
            tl = slice((ln0 + cb) * 64, (ln0 + cb + 1) * 64)
            for hp in range(HP):
                sl_ = cb * HP + hp
                col = slice(sl_ * 64, (sl_ + 1) * 64)
                for hb in range(2):
                    ph = PH[hb]
                    S.op("pe", lambda h, hp=hp, tl=tl, col=col, ph=ph: h.matmul(Bk[0][ph, col], kTt[ph, hp, tl], kTt[ph, hp, tl], start=True, stop=True), reads=[kTtr], writes=[Br[0]])
                    S.op("pe", lambda h, sl_=sl_, col=col, ph=ph: h.matmul(Bk[1][ph, col], gB[ph, sl_, :], L64x[ph, :], start=True, stop=True), reads=[gBr, L64xr], writes=[Br[1]])
                    S.op("pe", lambda h, hp=hp, tl=tl, col=col, ph=ph: h.matmul(Bk[2][ph, col], kTt[ph, hp, tl], qTt[ph, hp, tl], start=True, stop=True), reads=[kTtr, qTtr], writes=[Br[2]])
        yield
        S.op("dve", lambda h: h.tensor_tensor(raw, pv(Bk[1]), bc_s(sview(gc)), ALU.subtract), reads=[Br[1], gcr], writes=[rawr])
        yield
        S.op("dve", lambda h: h.tensor_scalar(mx, raw, 0.0, None, ALU.max), reads=[rawr], writes=[mxr])
        S.op("pool", lambda h: h.tensor_scalar(mn, raw, 0.0, None, ALU.min), reads=[rawr], writes=[mnr])
        yield
        S.op("act", lambda h: h.activation(mx, mx, AF.Exp, scale=-1.0), reads=[mxr], writes=[mxr])
        S.op("act", lambda h: h.activation(mn, mn, AF.Exp), reads=[mnr], writes=[mnr])
        vt3 = vtt[:, ln0:ln0 + CB, :, :].rearrange("p c h v -> p (c h) v")
        kt3 = ktt[:, ln0:ln0 + CB, :, :].rearrange("p c h v -> p (c h) v")
        S.op("pool", lambda h: h.tensor_tensor(Ru, vt3, bc_s(sview(beta)), ALU.mult), reads=[vttr, betar], writes=[Rur])
        S.op("pool", lambda h: h.tensor_tensor(Rw, kt3, bc_s(sview(begc)), ALU.mult), reads=[kttr, begcr], writes=[Rwr])
        S.op("pool", lambda h: h.tensor_tensor(kdec, kt3, bc_s(sview(ekd)), ALU.mult), reads=[kttr, ekdr], writes=[kdecr])
        yield
        S.op("dve", lambda h: h.tensor_tensor(tA, mx, mask_b(SLT64x), ALU.mult), reads=[mxr, SLT64xr], writes=[tAr])
        S.op("pool", lambda h: h.tensor_tensor(tB, mn, mask_b(L64x), ALU.mult), reads=[mnr, L64xr], writes=[tBr])
        yield
        S.op("dve", lambda h: h.tensor_tensor(tA, tA, bc_s(sview(nbeta)), ALU.mult), reads=[tAr, nbetar], writes=[tAr])
        yield
        M0, M0r = Mt[0]
        S.op("dve", lambda h: h.tensor_tensor(M0, pv(Bk[0]), tA, ALU.mult), reads=[Br[0], tAr], writes=[M0r])
        S.op("dve", lambda h: h.tensor_tensor(aqk, pv(Bk[2]), tB, ALU.mult), reads=[Br[2], tBr], writes=[aqkr])
        yield
        B3b = Bk[3][:, 0:256].bitcast(BF16)
        for e in range(NSL):
            for hb in range(2):
                ph = PH[hb]
                S.op("pe", lambda h, e=e, ph=ph: h.transpose(B3b[ph, e * 64:(e + 1) * 64], M0[ph, e, :], k.identb[ph, ph]), reads=[M0r, k.identbr], writes=[Br[3]])
        yield
        N0, N0r = Nt[0]
        S.op("act", lambda h: h.copy(N0, B3b.rearrange("p (a b) -> p a b", b=64)), reads=[Br[3]], writes=[N0r])
        yield
        S.op("dve", lambda h: h.tensor_tensor(Tt, N0, mask_b(idx2), ALU.add), reads=[N0r, idx2r], writes=[Ttr])
        for j in range(1, 6):
            Mp, Mpr = Mt[(j - 1) % 2]
            Np, Npr = Nt[(j - 1) % 2]
            Mc, Mcr = Mt[j % 2]
            Nc_, Ncr = Nt[j % 2]
            for e in range(NSL):
                col = slice(e * 64, (e + 1) * 64)
                for hb in range(2):
                    ph = PH[hb]
                    S.op("pe", lambda h, e=e, col=col, Np=Np, Mp=Mp, ph=ph: h.matmul(Bk[0][ph, col], Np[ph, e, :], Mp[ph, e, :], start=True, stop=True), reads=[Npr, Mpr], writes=[Br[0]])
            if j < 5:
                for e in range(NSL):
                    col = slice(e * 64, (e + 1) * 64)
                    for hb in range(2):
                        ph = PH[hb]
                        S.op("pe", lambda h, e=e, col=col, Np=Np, Mp=Mp, ph=ph: h.matmul(Bk[1][ph, col], Mp[ph, e, :], Np[ph, e, :], start=True, stop=True), reads=[Npr, Mpr], writes=[Br[1]])
            yield
            S.op("act", lambda h, Mc=Mc: h.copy(Mc, pv(Bk[0])), reads=[Br[0]], writes=[Mcr])
            if j < 5:
                S.op("act", lambda h, Nc_=Nc_: h.copy(Nc_, pv(Bk[1])), reads=[Br[1]], writes=[Ncr])
            yield
            for e in range(NSL):
                col = slice(e * 64, (e + 1) * 64)
                for hb in range(2):
                    ph = PH[hb]
                    S.op("pe", lambda h, e=e, col=col, Mc=Mc, ph=ph: h.matmul(Bk[2][ph, col], Mc[ph, e, :], Tt[ph, e, :], start=True, stop=True), reads=[Mcr, Ttr], writes=[Br[2]])
            yield
            S.op("dve", lambda h: h.tensor_tensor(Tt, Tt, pv(Bk[2]), ALU.add), reads=[Br[2], Ttr], writes=[Ttr])
            yield
        for e in range(NSL):
            col = slice(e * 64, (e + 1) * 64)
            for hb in range(2):
                ph = PH[hb]
                S.op("pe", lambda h, e=e, col=col, ph=ph: h.matmul(Bk[0][ph, col], Tt[ph, e, :], Ru[ph, e, :], start=True, stop=True), reads=[Ttr, Rur], writes=[Br[0]])
                S.op("pe", lambda h, e=e, col=col, ph=ph: h.matmul(Bk[1][ph, col], Rw[ph, e, :], Tt[ph, e, :], start=True, stop=True), reads=[Ttr, Rwr], writes=[Br[1]])
        yield
        S.op("act", lambda h: h.copy(ut, pv(Bk[0])), reads=[Br[0]], writes=[utr])
        S.op("act", lambda h: h.copy(wT, pv(Bk[1])), reads=[Br[1]], writes=[wTr])
        yield
        for cb in range(CB):
            n = n0 + cb
            tl = slice((ln0 + cb) * 64, (ln0 + cb + 1) * 64)
            es = slice(cb * HP, cb * HP + HP)
            for hp in range(HP):
                col = slice(hp * 64, (hp + 1) * 64)
                for hb in range(2):
                    ph = PH[hb]
                    S.op("pe", lambda h, hp=hp, col=col, cb=cb, ph=ph: h.matmul(Bk[3][ph, col], wT[ph, cb * HP + hp, :], Ssb[ph, hp, :], start=True, stop=True), reads=[wTr, Ssbr], writes=[Br[3]])
                    S.op("pe", lambda h, hp=hp, col=col, tl=tl, ph=ph: h.matmul(Bk[0][ph, col], qTt[ph, hp, tl], Ssb[ph, hp, :], start=True, stop=True), reads=[qTtr, Ssbr], writes=[Br[0]])
            yield
            S.op("dve", lambda h, es=es: h.tensor_tensor(vnew, ut[:, es, :], pv2(Bk[3]), ALU.subtract), reads=[utr, Br[3]], writes=[vnewr])
            yield
            for hp in range(HP):
                col = slice(hp * 64, (hp + 1) * 64)
                for hb in range(2):
                    ph = PH[hb]
                    S.op("pe", lambda h, hp=hp, col=col, cb=cb, ph=ph: h.matmul(Bk[2][ph, col], kdec[ph, cb * HP + hp, :], vnew[ph, hp, :], start=True, stop=True), reads=[kdecr, vnewr], writes=[Br[2]])
                    S.op("pe", lambda h, hp=hp, col=col, cb=cb, ph=ph: h.matmul(Bk[1][ph, col], aqk[ph, cb * HP + hp, :], vnew[ph, hp, :], start=True, stop=True), reads=[aqkr, vnewr], writes=[Br[1]])
            S.op("pool", lambda h, n=n: h.tensor_tensor(Sst, Sst, egl[:, n, :].unsqueeze(2).to_broadcast([128, HP, 64]), ALU.mult), reads=[Sstr, eglr], writes=[Sstr])
            yield
            S.op("dve", lambda h: h.tensor_tensor(Sst, Sst, pv2(Bk[2]), ALU.add), reads=[Br[2], Sstr], writes=[Sstr])
            S.op("act", lambda h: h.copy(Ssb, Sst), reads=[Sstr], writes=[Ssbr])
            S.op("dve", lambda h, n=n: h.tensor_tensor(otmp, pv2(Bk[0]), egc[:, n, :].unsqueeze(2).to_broadcast([128, HP, 64]), ALU.mult), reads=[Br[0], egcr], writes=[otmpr])
            S.op("dve", lambda h, cb=cb: h.tensor_tensor(ogt[:, ln0 + cb, :, :], otmp, pv2(Bk[1]), ALU.add), reads=[Br[1], otmpr], writes=[ogtr])
            yield
        if b == NB - 1:
            tsl = slice(grp * 512, (grp + 1) * 512)
            for hb in range(2):
                for hp in range(HP):
                    hcol = slice((2 * hp + hb) * 64, (2 * hp + hb + 1) * 64)
                    S.dma("pool", k.GO[tsl, hcol].rearrange("(n p) e -> p n e", p=64), ogt[PH[hb], :, hp, :], reads=[ogtr], sbreg=ogtr)
            if grp + 2 < NC // 8:
                group_load(grp + 2)

    run_interleaved([do_batch(gb) for gb in range(NC // CB)], NSET, 26)

    S.barrier()
    A.reset()
    Oall, Oallr = A.f32("OallG", [128, NT, 256])
    Zall, Zallr = A.f32("ZallG", [128, NT, 256])
    Osq, Osqr = A.f32("OsqG", [128, NT, 256])
    ssg, ssgr = A.f32("ssg", [128, NT * 4])
    rsg, rsgr = A.f32("rsg", [128, NT * 4])
    dnw, dnwr = A.f32("dnw", [128, 64])
    S.dma("sp", dnw, k.p["dn_norm_w"][l].partition_broadcast(128), writes=[dnwr], sbreg=dnwr)
    S.dma("sp", Oall, k.GO.rearrange("(n p) e -> p n e", p=128), writes=[Oallr], sbreg=Oallr)
    S.dma("sp", Zall, k.PTM[:, 1280:1536].rearrange("(n p) e -> p n e", p=128), writes=[Zallr], sbreg=Zallr)
    S.op("act", lambda h: h.activation(Zall, Zall, AF.Silu), reads=[Zallr], writes=[Zallr])
    S.op("pool", lambda h: h.tensor_tensor(Osq, Oall, Oall, ALU.mult), reads=[Oallr], writes=[Osqr])
    S.op("dve", lambda h: h.tensor_reduce(ssg, Osq.rearrange("p n (g v) -> p (n g) v", g=4), AX.X, ALU.add), reads=[Osqr], writes=[ssgr])
    rstd_from_ss(k, rsg, rsgr, ssg, ssgr, 64)
    O3 = Oall.rearrange("p n (g v) -> p (n g) v", g=4)
    S.op("dve", lambda h: h.tensor_tensor(O3, O3, rsg.unsqueeze(2).to_broadcast([128, NT * 4, 64]), ALU.mult), reads=[Oallr, rsgr], writes=[Oallr])
    S.op("pool", lambda h: h.tensor_tensor(O3, O3, dnw.unsqueeze(1).to_broadcast([128, NT * 4, 64]), ALU.mult), reads=[Oallr, dnwr], writes=[Oallr])
    ObG = Osq.rearrange("p n e -> p (n e)")[:, 0:4096].bitcast(BF16).rearrange("p (n e) -> p n e", n=NT)
    S.op("dve", lambda h: h.tensor_tensor(ObG, Oall, Zall, ALU.mult), reads=[Oallr, Zallr, Osqr, ssgr], writes=[Osqr])
    S.dma("pool", k.MIX[:, 768:1024].rearrange("(n p) e -> p n e", p=128), ObG, reads=[Osqr], sbreg=Osqr)


def g1_chunk(k, c, Xt, Yt, sqt, tms, cwd, cwdr, bo, bor):
    S = k.S
    X, Xr = Xt
    Y, Yr = Yt
    sq, sqr = sqt
    if True:
        S.dma("sp", X[:, 3:T + 3], k.PFMf[512 + c * 128:512 + (c + 1) * 128, :], writes=[Xr], sbreg=Xr)
        S.op("dve", lambda h, c=c: h.tensor_scalar(Y, X[:, 3:T + 3], cwd[:, 3, c:c + 1], None, ALU.mult), reads=[Xr, cwdr], writes=[Yr])
        S.op("dve", lambda h, c=c: h.scalar_tensor_tensor(Y, X[:, 2:T + 2], cwd[:, 2, c:c + 1], Y, ALU.mult, ALU.add), reads=[Xr, cwdr, Yr], writes=[Yr])
        S.op("dve", lambda h, c=c: h.scalar_tensor_tensor(Y, X[:, 1:T + 1], cwd[:, 1, c:c + 1], Y, ALU.mult, ALU.add), reads=[Xr, cwdr, Yr], writes=[Yr])
        S.op("dve", lambda h, c=c: h.scalar_tensor_tensor(Y, X[:, 0:T], cwd[:, 0, c:c + 1], Y, ALU.mult, ALU.add), reads=[Xr, cwdr, Yr], writes=[Yr])
        S.op("act", lambda h: h.activation(Y, Y, AF.Silu), reads=[Yr], writes=[Yr])
        if c < 4:
            S.op("act", lambda h: h.activation(sq, Y, AF.Square), reads=[Yr], writes=[sqr])
            for tb in range(8):
                bank = k.banks[tb % 2]
                bankr = k.bankr[tb % 2]
                S.op("pe", lambda h, bank=bank, tb=tb: h.matmul(bank[:, :], bo, sq[:, tb * 512:(tb + 1) * 512], start=True, stop=True), reads=[bor, sqr], writes=[bankr])
                S.op("act", lambda h, bank=bank, tb=tb: h.activation(X[:, 3 + tb * 512:3 + (tb + 1) * 512], bank[:, :], AF.Ln, bias=k.epst[:, 0:1]), reads=[bankr, k.epsr], writes=[Xr])
            S.op("act", lambda h: h.activation(X[:, 3:T + 3], X[:, 3:T + 3], AF.Exp, scale=-0.5), reads=[Xr], writes=[Xr])
            if c < 2:
                S.op("dve", lambda h: h.scalar_tensor_tensor(Y, Y, 0.125, X[:, 3:T + 3], ALU.mult, ALU.mult), reads=[Yr, Xr], writes=[Yr])
                S.dma("pool", k.GQT[c * 128:(c + 1) * 128, :], Y, reads=[Yr], sbreg=Yr)
            else:
                S.op("dve", lambda h: h.tensor_tensor(Y, Y, X[:, 3:T + 3], ALU.mult), reads=[Yr, Xr], writes=[Yr])
                S.dma("pool", k.GKT[(c - 2) * 128:(c - 1) * 128, :], Y, reads=[Yr], sbreg=Yr)
        if c >= 2:
            st_, str_ = tms[c % 2]
            for i in range(NT):
                bank = k.banks[2 + i % 4]
                bankr = k.bankr[2 + i % 4]
                S.op("pe", lambda h, bank=bank, i=i: h.transpose(bank[:, 0:128], Y[:, i * 128:(i + 1) * 128], k.ident), reads=[Yr, k.identr], writes=[bankr])
                if i % 2 == 0:
                    S.op("act", lambda h, bank=bank, i=i, st_=st_: h.copy(st_[:, i, :], bank[:, 0:128]), reads=[bankr], writes=[str_])
                else:
                    S.op("dve", lambda h, bank=bank, i=i, st_=st_: h.tensor_copy(st_[:, i, :], bank[:, 0:128]), reads=[bankr], writes=[str_])
            dst = k.GKtm if c < 4 else k.GVtm
            cc = (c - 2) % 2
            S.dma("pool", dst[:, cc * 128:(cc + 1) * 128].rearrange("(n p) e -> p n e", p=128), st_, reads=[str_], sbreg=str_)


def phase_outproj_ffn(k, l, src, last):
    S, A = k.S, k.A
    MIXsrc = k.mix_in if getattr(k, "mix_in", None) is not None else k.MIX
    S.barrier()
    A.reset()
    k.bankr = [S.reg("bank%d" % i) for i in range(8)]
    u2T, u2Tr = A.bf16("u2T", [128, 8, T], nreg=NT)
    mixT, mixTr = A.bf16("mixT", [128, 8, T], nreg=NT)
    Wob, Wobr = A.bf16("Wob", [128, 8, D], nreg=8)
    wst = [A.f32("wst%d" % i, [128, D]) for i in range(2)]
    for c in range(8):
        wt, wtr = wst[c % 2]
        S.dma("sp", wt, k.p["w_out"][l, c * 128:(c + 1) * 128, :], writes=[wtr], sbreg=wtr)
        if c % 2 == 0:
            S.op("dve", lambda h, wt=wt, c=c: h.tensor_copy(Wob[:, c, :], wt), reads=[wtr], writes=[Wobr[c]])
        else:
            S.op("act", lambda h, wt=wt, c=c: h.copy(Wob[:, c, :], wt), reads=[wtr], writes=[Wobr[c]])
    hb = [A.f32("hb%d" % i, [128, D]) for i in range(3)]
    ub = [A.bf16("ub%d" % i, [128, D]) for i in range(2)]
    block = norm_transpose_pass(k, MIXsrc, mixT, mixTr, hb, ub, "ssO", do_norm=False, src_bf16=(getattr(k, "mix_in", None) is None))
    hres = [A.f32("hres%d" % i, [128, D]) for i in range(2)]
    hnew = [A.f32("hnew%d" % i, [128, D]) for i in range(2)]
    cnt = {"acc": 0}
    junkO, junkOr = A.f32("junkO", [128, D])
    S.op("pool", lambda h: h.memset(k.ssF, 0.0), writes=[k.ssFr])
    ub2 = [A.bf16("ub2_%d" % i, [128, D]) for i in range(2)]
    rsO, rsOr = A.f32("rsO", [128, NT])
    pend = []

    def emit_u2T(item):
        i, u2, u2r = item
        bank = k.banks[6 + i % 2]
        bankr = k.bankr[6 + i % 2]
        pt = bank[:, :].bitcast(BF16)
        for c in range(8):
            S.op("pe", lambda h, c=c, u2=u2, pt=pt: h.transpose(pt[:, c * 128:(c + 1) * 128], u2[:, c * 128:(c + 1) * 128], k.identb),
                 reads=[u2r, k.identbr], writes=[bankr], pe_accum=True)
        if i % 2 == 0:
            S.op("act", lambda h, pt=pt, i=i: h.copy(u2T[:, :, i * 128:(i + 1) * 128], pt.rearrange("p (c t) -> p c t", c=8)), reads=[bankr], writes=[u2Tr[i]])
        else:
            S.op("dve", lambda h, pt=pt, i=i: h.tensor_copy(u2T[:, :, i * 128:(i + 1) * 128], pt.rearrange("p (c t) -> p c t", c=8)), reads=[bankr], writes=[u2Tr[i]])

    def oproj(tb):
        for i in range(tb * 4, tb * 4 + 4):
            hr, hrr = hres[i % 2]
            hn, hnr = hnew[i % 2]
            S.dma("sp", hr, src[i * 128:(i + 1) * 128, :], writes=[hrr], sbreg=hrr)
            for half in range(2):
                bank = k.banks[2 + cnt["acc"] % 4]
                bankr = k.bankr[2 + cnt["acc"] % 4]
                cnt["acc"] += 1
                for c in range(8):
                    S.op("pe", lambda h, c=c, bank=bank, i=i, half=half: h.matmul(bank[:, :], mixT[:, c, i * 128:(i + 1) * 128], Wob[:, c, half * 512:(half + 1) * 512], start=(c == 0), stop=(c == 7)),
                         reads=[mixTr[i], Wobr[c]], writes=[bankr], pe_accum=(c > 0))
                S.op("dve", lambda h, hn=hn, hr=hr, bank=bank, half=half: h.tensor_tensor(hn[:, half * 512:(half + 1) * 512], bank[:, :], hr[:, half * 512:(half + 1) * 512], ALU.add),
                     reads=[bankr, hrr], writes=[hnr])
            S.op("act", lambda h, hn=hn, i=i: h.activation(junkO, hn, AF.Square, accum_out=k.ssF[:, i:i + 1]), reads=[hnr, k.ssFr], writes=[junkOr, k.ssFr])
            S.dma("pool", k.H[i * 128:(i + 1) * 128, :], hn, reads=[hnr], sbreg=hnr)
            S.op("dve", lambda h, i=i: h.tensor_scalar(rsO[:, i:i + 1], k.ssF[:, i:i + 1], 1.0 / D, EPS, ALU.mult, ALU.add), reads=[k.ssFr], writes=[rsOr])
            S.op("act", lambda h, i=i: h.activation(rsO[:, i:i + 1], rsO[:, i:i + 1], AF.Ln), reads=[rsOr], writes=[rsOr])
            S.op("act", lambda h, i=i: h.activation(rsO[:, i:i + 1], rsO[:, i:i + 1], AF.Exp, scale=-0.5), reads=[rsOr], writes=[rsOr])
            u2, u2r = ub2[i % 2]
            S.op("dve", lambda h, hn=hn, u2=u2, i=i: h.tensor_scalar(u2, hn, rsO[:, i:i + 1], None, ALU.mult), reads=[hnr, rsOr], writes=[u2r])
            pend.append((i, u2, u2r))
            if len(pend) > 1:
                emit_u2T(pend.pop(0))

    block(0)
    for tb in range(8):
        if tb + 1 < 8:
            block(tb + 1)
        oproj(tb)
    while pend:
        emit_u2T(pend.pop(0))

    S.barrier()
    A.reset()
    k.bankr = [S.reg("bank%d" % i) for i in range(8)]
    uT, uTr = A.bf16("u2T", [128, 8, T], nreg=NT)
    nw, nwr = A.f32("nw2", [128, 8])
    S.dma("sp", nw, k.p["ffn_norm_w"][l].rearrange("(c p) -> p c", p=128), writes=[nwr], sbreg=nwr, allow_slow_non_contiguous=True)
    cw, cwr = A.f32("cw", [128, 3, 44])
    cb, cbr = A.f32("cb", [128, 44])
    for j in range(3):
        S.dma("sp", cw[:, j, :], k.p["ffn_conv_w"][l, j].rearrange("(c p) -> p c", p=128), writes=[cwr], sbreg=cwr, allow_slow_non_contiguous=True)
    S.dma("sp", cb, k.p["ffn_conv_b"][l].rearrange("(c p) -> p c", p=128), writes=[cbr], sbreg=cbr, allow_slow_non_contiguous=True)
    wf = [[A.f32("wf%d_%d" % (i, j), [128, 8, 128]) for j in range(2)] for i in range(2)]
    wb = [[A.bf16("wb%d_%d" % (i, j), [128, 8, 128]) for j in range(2)] for i in range(2)]
    Xs = [[A.f32("X%d_%d" % (i, j), [128, T + 2]) for j in range(2)] for i in range(2)]
    Y = [A.f32("Y%d" % j, [128, T]) for j in range(2)]
    aout = [A.bf16("aout%d" % i, [128, T]) for i in range(1)]
    for i in range(2):
        for j in range(2):
            S.op("pool", lambda h, i=i, j=j: h.memset(Xs[i][j][0][:, 0:2], 0.0), writes=[Xs[i][j][1]])
    cnt = {"acc": 0, "ev": 0}
    def load_w(f):
        for j in range(2):
            wt, wtr = wf[f % 2][j]
            wbt, wbtr = wb[f % 2][j]
            col = j * DFF + f * 128
            S.dma("sp", wt, k.p["ffn_w_up"][l, :, col:col + 128].rearrange("(c p) e -> p c e", p=128), writes=[wtr], sbreg=wtr)
            S.op("pool", lambda h, wt=wt, wbt=wbt: h.tensor_tensor(wbt, wt, nw.unsqueeze(2).to_broadcast([128, 8, 128]), ALU.mult), reads=[wtr, nwr], writes=[wbtr])

    load_w(0)
    for f in range(22):
        X = Xs[f % 2]
        if f + 1 < 22:
            load_w(f + 1)
        for tb in range(8):
            for j in range(2):
                wbt, wbtr = wb[f % 2][j]
                bank = k.banks[cnt["acc"] % 4]
                bankr = k.bankr[cnt["acc"] % 4]
                cnt["acc"] += 1
                for c in range(8):
                    S.op("pe", lambda h, c=c, bank=bank, wbt=wbt, tb=tb: h.matmul(bank[:, :], wbt[:, c, :], uT[:, c, tb * 512:(tb + 1) * 512], start=(c == 0), stop=(c == 7)),
                         reads=[wbtr] + uTr[tb * 4:tb * 4 + 4], writes=[bankr], pe_accum=(c > 0))
                Xj, Xjr = X[j]
                S.op("act", lambda h, Xj=Xj, bank=bank, tb=tb: h.copy(Xj[:, 2 + tb * 512:2 + (tb + 1) * 512], bank[:, :]), reads=[bankr], writes=[Xjr])
        for j in range(2):
            Xj, Xjr = X[j]
            Yj, Yjr = Y[j]
            ch = j * 22 + f
            S.op("dve", lambda h, Xj=Xj, Yj=Yj, ch=ch: h.tensor_scalar(Yj, Xj[:, 2:T + 2], cw[:, 2, ch:ch + 1], cb[:, ch:ch + 1], ALU.mult, ALU.add),
                 reads=[Xjr, cwr, cbr], writes=[Yjr])
            S.op("dve", lambda h, Xj=Xj, Yj=Yj, ch=ch: h.scalar_tensor_tensor(Yj, Xj[:, 1:T + 1], cw[:, 1, ch:ch + 1], Yj, ALU.mult, ALU.add),
                 reads=[Xjr, cwr, Yjr], writes=[Yjr])
            S.op("dve", lambda h, Xj=Xj, Yj=Yj, ch=ch: h.scalar_tensor_tensor(Yj, Xj[:, 0:T], cw[:, 0, ch:ch + 1], Yj, ALU.mult, ALU.add),
                 reads=[Xjr, cwr, Yjr], writes=[Yjr])
        S.op("act", lambda h: h.activation(Y[0][0], Y[0][0], AF.Silu), reads=[Y[0][1]], writes=[Y[0][1]])
        ao, aor = aout[0]
        S.op("dve", lambda h, ao=ao: h.tensor_tensor(ao, Y[0][0], Y[1][0], ALU.mult), reads=[Y[0][1], Y[1][1]], writes=[aor])
        S.dma("pool", k.ACTS[f * 128:(f + 1) * 128, :], ao, reads=[aor], sbreg=aor)

    S.barrier()
    A.reset()
    k.bankr = [S.reg("bank%d" % i) for i in range(8)]
    Wdb, Wdbr = A.bf16("Wdb", [128, 22, D], nreg=22)
    wst = [A.f32("wst%d" % i, [128, D]) for i in range(2)]
    for f in range(22):
        wt, wtr = wst[f % 2]
        S.dma("sp", wt, k.p["ffn_w_down"][l, f * 128:(f + 1) * 128, :], writes=[wtr], sbreg=wtr)
        S.op("pool" if f % 2 else "dve", lambda h, wt=wt, f=f: h.tensor_copy(Wdb[:, f, :], wt), reads=[wtr], writes=[Wdbr[f]])
    acts = [A.bf16("acts%d" % i, [128, 22, 512]) for i in range(2)]
    hres = [A.f32("hres%d" % i, [128, D]) for i in range(2)]
    hnew = [A.f32("hnew%d" % i, [128, D]) for i in range(2)]
    if not last:
        junk2, junk2r = A.f32("junk2", [128, D])
        S.op("pool", lambda h: h.memset(k.ssA, 0.0), writes=[k.ssAr])
    if last:
        fnw, fnwr = A.f32("fnw", [128, D])
        S.dma("sp", fnw, k.p["final_norm_w"].partition_broadcast(128), writes=[fnwr], sbreg=fnwr)
        junk, junkr = A.f32("junkF", [128, D])
        ssf, ssfr = A.f32("ssf", [128, NT])
        rsf, rsfr = A.f32("rsf", [128, NT])
        S.op("dve", lambda h: h.memset(ssf, 0.0), writes=[ssfr])
    cnt = {"acc": 0}
    for tb in range(8):
        ac, acr = acts[tb % 2]
        S.dma("sp", ac, k.ACTS[:, tb * 512:(tb + 1) * 512].rearrange("(f p) t -> p f t", p=128), writes=[acr], sbreg=acr)
        for ti in range(4):
            i = tb * 4 + ti
            hr, hrr = hres[i % 2]
            hn, hnr = hnew[i % 2]
            S.dma("sp", hr, k.H[i * 128:(i + 1) * 128, :], writes=[hrr], sbreg=hrr)
            for half in range(2):
                bank = k.banks[cnt["acc"] % 4]
                bankr = k.bankr[cnt["acc"] % 4]
                cnt["acc"] += 1
                for f in range(22):
                    S.op("pe", lambda h, f=f, bank=bank, ac=ac, ti=ti, half=half: h.matmul(bank[:, :], ac[:, f, ti * 128:(ti + 1) * 128], Wdb[:, f, half * 512:(half + 1) * 512], start=(f == 0), stop=(f == 21)),
                         reads=[acr, Wdbr[f]], writes=[bankr], pe_accum=(f > 0))
                S.op("dve", lambda h, hn=hn, hr=hr, bank=bank, half=half: h.tensor_tensor(hn[:, half * 512:(half + 1) * 512], bank[:, :], hr[:, half * 512:(half + 1) * 512], ALU.add),
                     reads=[bankr, hrr], writes=[hnr])
            if not last:
                S.op("act", lambda h, hn=hn, i=i: h.activation(junk2, hn, AF.Square, accum_out=k.ssA[:, i:i + 1]), reads=[hnr, k.ssAr], writes=[junk2r, k.ssAr])
                S.dma("pool", k.H[i * 128:(i + 1) * 128, :], hn, reads=[hnr], sbreg=hnr)
            else:
                S.op("act", lambda h, hn=hn, i=i: h.activation(junk, hn, AF.Square, accum_out=ssf[:, i:i + 1]), reads=[hnr, ssfr], writes=[junkr, ssfr])
                S.op("dve", lambda h, i=i: h.tensor_scalar(rsf[:, i:i + 1], ssf[:, i:i + 1], 1.0 / D, EPS, ALU.mult, ALU.add), reads=[ssfr], writes=[rsfr])
                S.op("act", lambda h, i=i: h.activation(rsf[:, i:i + 1], rsf[:, i:i + 1], AF.Ln), reads=[rsfr], writes=[rsfr])
                S.op("act", lambda h, i=i: h.activation(rsf[:, i:i + 1], rsf[:, i:i + 1], AF.Exp, scale=-0.5), reads=[rsfr], writes=[rsfr])
                S.op("dve", lambda h, hn=hn, i=i: h.scalar_tensor_tensor(hn, hn, rsf[:, i:i + 1], fnw, ALU.mult, ALU.mult), reads=[hnr, rsfr, fnwr], writes=[hnr])
                S.dma("pool", k.out[i * 128:(i + 1) * 128, :], hn, reads=[hnr], sbreg=hnr)


_NC_CACHE = {}


def kernel(**inputs):
    x = np.ascontiguousarray(np.asarray(inputs["x"], dtype=np.float32))
    if "nc" not in _NC_CACHE:
        _NC_CACHE["nc"] = build_program(debug=False, nlayers=2)
    nc = _NC_CACHE["nc"]
    consts = make_consts()
    base = {}
    for n in PARAM_SHAPES:
        base[n] = np.ascontiguousarray(np.asarray(inputs[n], dtype=np.float32))
    for n, v in consts.items():
        base["c_" + n] = v
    in_maps = []
    for c in range(8):
        m = dict(base)
        m["x"] = np.ascontiguousarray(x[c % 4])
        in_maps.append(m)
    res = run_bass_kernel_spmd(nc, in_maps, core_ids=list(range(8)))
    out = np.stack([np.asarray(res.results[b]["out"], dtype=np.float32) for b in range(4)], axis=0)
    return out
```

```python
import numpy as np
import math
from contextlib import ExitStack
import concourse.bass as bass
import concourse.mybir as mybir
from concourse.bass_utils import run_bass_kernel_spmd

F32 = mybir.dt.float32
BF16 = mybir.dt.bfloat16
AF = mybir.ActivationFunctionType
ALU = mybir.AluOpType
AX = mybir.AxisListType

T = 4096
D = 1024
NT = 32
DIN = 3592
DFF = 2816
EPS = 1e-6


class Reg:
    __slots__ = ("name", "lw", "rd", "dsem", "dcnt")

    def __init__(self, name):
        self.name = name
        self.lw = None
        self.rd = {}
        self.dsem = None
        self.dcnt = 0


class Sched:
    ENG = ("pe", "act", "dve", "pool", "sp")

    def __init__(self, nc, stack):
        self.nc = nc
        self.stack = stack
        self.sem = {e: stack.enter_context(nc.semaphore("prog_" + e)) for e in self.ENG}
        self.cnt = {e: 0 for e in self.ENG}
        self.ops = {e: [] for e in self.ENG}
        self.waited = {e: {} for e in self.ENG}
        self.dsems = []
        self.regs = []
        self.nins = 0
        self.free_dsems = []

    def reg(self, name):
        r = Reg(name)
        self.regs.append(r)
        return r

    def _need(self, eng, dep, waits):
        if dep is None:
            return
        kind, who, cnt = dep
        if kind == "e" and who == "pe" and eng == "pe":
            return
        if kind == "e":
            key = ("e", who)
            semh = self.sem[who]
        else:
            key = ("d", id(who))
            semh = who.dsem
        if self.waited[eng].get(key, 0) >= cnt:
            return
        cur = waits.get(key)
        if cur is None or cur[1] < cnt:
            waits[key] = (semh, cnt)

    def _deps(self, eng, reads, writes, pe_accum=False):
        waits = {}
        for r in reads:
            self._need(eng, r.lw, waits)
        for r in writes:
            if not (pe_accum and r.lw is not None and r.lw[0] == "e" and r.lw[1] == "pe" and eng == "pe"):
                self._need(eng, r.lw, waits)
            for d in r.rd.values():
                self._need(eng, d, waits)
        for key, (semh, cnt) in waits.items():
            self.waited[eng][key] = cnt
        return list(waits.values())

    def op(self, eng, fn, reads=(), writes=(), pe_accum=False):
        waits = self._deps(eng, reads, writes, pe_accum)
        self.cnt[eng] += 1
        c = self.cnt[eng]
        sem = self.sem[eng]

        def emit(h, waits=waits, fn=fn, sem=sem):
            for (s, v) in waits:
                h.wait_ge(s, v)
            fn(h).then_inc(sem, 1)
        self.ops[eng].append(emit)
        self.nins += 1
        dep = ("e", eng, c)
        for r in writes:
            r.lw = dep
            r.rd = {}
        for r in reads:
            r.rd[("e", eng)] = dep
        return dep

    def dma(self, q, out_ap, in_ap, reads=(), writes=(), sbreg=None, **kw):
        assert sbreg is not None
        if sbreg.dsem is None:
            if self.free_dsems:
                sbreg.dsem, sbreg.dcnt = self.free_dsems.pop()
            else:
                sbreg.dsem = self.stack.enter_context(self.nc.semaphore("d%d" % len(self.dsems)))
                sbreg.dcnt = 0
            self.dsems.append(sbreg)
        waits = self._deps(q, reads, writes)
        sbreg.dcnt += 16
        c = sbreg.dcnt
        semh = sbreg.dsem

        def emit(h, waits=waits, semh=semh, out_ap=out_ap, in_ap=in_ap, kw=kw):
            for (s, v) in waits:
                h.wait_ge(s, v)
            h.dma_start(out=out_ap, in_=in_ap, **kw).then_inc(semh, 16)
        self.ops[q].append(emit)
        self.nins += 1
        dep = ("d", sbreg, c)
        for r in writes:
            r.lw = dep
            r.rd = {}
        for r in reads:
            r.rd[("d", id(sbreg))] = dep
        return dep

    def barrier(self, release=True):
        waits = []
        for e in self.ENG:
            if e != "sp" and self.cnt[e] > self.waited["sp"].get(("e", e), 0):
                waits.append((self.sem[e], self.cnt[e]))
                self.waited["sp"][("e", e)] = self.cnt[e]
        for r in self.dsems:
            key = ("d", id(r))
            if r.dcnt > self.waited["sp"].get(key, 0):
                waits.append((r.dsem, r.dcnt))
        self.cnt["sp"] += 1
        c = self.cnt["sp"]
        sem = self.sem["sp"]

        def emit(h, waits=waits, sem=sem):
            for (s, v) in waits:
                h.wait_ge(s, v)
            h.nop().then_inc(sem, 1)
        self.ops["sp"].append(emit)
        for e in self.ENG:
            if e == "sp":
                continue

            def emit2(h, sem=sem, c=c):
                h.wait_ge(sem, c)
            self.ops[e].append(emit2)
            self.waited[e][("e", "sp")] = c
            for e2 in self.ENG:
                if e2 != "sp":
                    self.waited[e][("e", e2)] = max(self.waited[e].get(("e", e2), 0), self.cnt[e2])
        for e2 in self.ENG:
            if e2 != "sp":
                self.waited["sp"][("e", e2)] = self.cnt[e2]
        for r in self.regs:
            r.lw = None
            r.rd = {}
        if release:
            for r in self.dsems:
                self.free_dsems.append((r.dsem, r.dcnt))
                r.dsem = None
                r.dcnt = 0
            self.dsems = []
            for e in self.ENG:
                self.waited[e] = {k: v for k, v in self.waited[e].items() if k[0] == "e"}
            self.regs = []

    def finish(self):
        nc = self.nc
        ops = self.ops
        with nc.Block() as block:
            @block.tensor
            def _(h):
                for f in ops["pe"]:
                    f(h)

            @block.scalar
            def _(h):
                for f in ops["act"]:
                    f(h)

            @block.vector
            def _(h):
                for f in ops["dve"]:
                    f(h)

            @block.gpsimd
            def _(h):
                for f in ops["pool"]:
                    f(h)

            @block.sync
            def _(h):
                for f in ops["sp"]:
                    f(h)


class Arena:
    def __init__(self, S, tens, nf32):
        self.S = S
        self.t = tens
        self.n = nf32
        self.off = 0

    def reset(self):
        self.off = 0

    def _view(self, ap, shape):
        if len(shape) == 2:
            return ap
        if len(shape) == 3:
            return ap.rearrange("p (a b) -> p a b", a=shape[1])
        if len(shape) == 4:
            return ap.rearrange("p (a b c) -> p a b c", a=shape[1], b=shape[2])
        raise ValueError

    def f32(self, name, shape, nreg=None):
        n = int(np.prod(shape[1:]))
        assert self.off + n <= self.n, (name, self.off, n, self.n)
        ap = self.t[0:shape[0], self.off:self.off + n]
        self.off += n
        v = self._view(ap, shape)
        if nreg is None:
            return v, self.S.reg(name)
        return v, [self.S.reg("%s%d" % (name, i)) for i in range(nreg)]

    def bf16(self, name, shape, nreg=None):
        n = int(np.prod(shape[1:]))
        n32 = (n + 1) // 2
        assert self.off + n32 <= self.n, (name, self.off, n32, self.n)
        ap = self.t[0:shape[0], self.off:self.off + n32].bitcast(BF16)[:, 0:n]
        self.off += n32
        v = self._view(ap, shape)
        if nreg is None:
            return v, self.S.reg(name)
        return v, [self.S.reg("%s%d" % (name, i)) for i in range(nreg)]


def run_interleaved(gens, width, stagger):
    gens = list(gens)
    active = []
    nxt = 0
    while nxt < len(gens) or active:
        if nxt < len(gens) and len(active) < width and (not active or active[-1][1] >= stagger):
            active.append([gens[nxt], 0])
            nxt += 1
        for item in list(active):
            try:
                next(item[0])
                item[1] += 1
            except StopIteration:
                active.remove(item)

PARAM_SHAPES = {
    "attn_norm_w": [2, 1024], "w_in": [2, 1024, 3592], "diff_lambda": [2, 4, 64], "diff_subln_w": [2, 128],
    "hgrn_lb_logits": [2, 256], "hgrn_norm_w": [2, 64], "dn_conv_w": [2, 4, 768], "dn_A_log": [2, 4],
    "dn_dt_bias": [2, 4], "dn_norm_w": [2, 64], "w_out": [2, 1024, 1024], "ffn_norm_w": [2, 1024],
    "ffn_w_up": [2, 1024, 5632], "ffn_conv_w": [2, 3, 5632], "ffn_conv_b": [2, 5632], "ffn_w_down": [2, 2816, 1024],
    "final_norm_w": [1024],
}
FM_COLS = [i * 128 for i in range(8)] + [1536, 1664, 1792, 1920] + [2560 + i * 128 for i in range(6)]
TM_BLOCKS = [(1024, 512, 0), (1792, 512, 512), (2304, 256, 1024), (3328, 264, 1280)]
PTM_W = 1544


def make_consts():
    c = {}
    c["ident"] = np.eye(128, dtype=np.float32)
    s = np.arange(128)[:, None]
    t = np.arange(128)[None, :]
    c["tri_incl"] = (s <= t).astype(np.float32)
    same16 = (s // 16 == t // 16)
    c["L16"] = (same16 & (s <= t)).astype(np.float32)
    c["SU16"] = (same16 & (s > t)).astype(np.float32)
    c["blk16"] = (np.arange(128)[:, None] // 16 == np.arange(8)[None, :]).astype(np.float32)
    s6 = np.arange(64)[:, None]
    t6 = np.arange(64)[None, :]
    c["L64"] = (s6 <= t6).astype(np.float32)
    c["SL64"] = (s6 < t6).astype(np.float32)
    c["LT64"] = (s6 >= t6).astype(np.float32)
    c["SLT64"] = (s6 > t6).astype(np.float32)
    c["ones64"] = np.ones((64, 64), np.float32)
    oh = np.zeros((4, 4, 64), np.float32)
    for h in range(4):
        oh[h, h, :] = 1.0
    c["onehot"] = oh.reshape(4, 256)
    return c


class K:
    pass


def build_program(debug=False, stop_after=None, nlayers=2, only=None):
    nc = bass.Bass("TRN2", target_bir_lowering=False)
    k = K()
    k.nc = nc
    k.debug = debug
    k.x = nc.dram_tensor("x", [T, D], F32, kind="ExternalInput").ap()
    k.p = {n: nc.dram_tensor(n, s, F32, kind="ExternalInput").ap() for n, s in PARAM_SHAPES.items()}
    consts = make_consts()
    k.c = {n: nc.dram_tensor("c_" + n, list(v.shape), F32, kind="ExternalInput").ap() for n, v in consts.items()}
    k.out = nc.dram_tensor("out", [T, D], F32, kind="ExternalOutput").ap()
    skind = "ExternalOutput" if debug else "Internal"
    k.mix_in = nc.dram_tensor("mix_in", [T, D], F32, kind="ExternalInput").ap() if debug == "mixin" else None

    def scratch(name, shape, dt):
        return nc.dram_tensor(name, shape, dt, kind=skind).ap()
    k.H = scratch("s_H", [T, D], F32)
    k.PFMb = scratch("s_PFMb", [1024, T], BF16)
    k.PFMf = scratch("s_PFMf", [1280, T], F32)
    k.PTM = scratch("s_PTM", [T, PTM_W], F32)
    k.MIX = scratch("s_MIX", [T, D], BF16)
    k.ACTS = scratch("s_ACTS", [DFF, T], BF16)
    k.GQT = scratch("s_GQT", [256, T], F32)
    k.GKT = scratch("s_GKT", [256, T], F32)
    k.GKtm = scratch("s_GKtm", [T, 256], F32)
    k.GVtm = scratch("s_GVtm", [T, 256], F32)
    k.HK = scratch("s_HK", [T, 256], BF16)
    k.GO = scratch("s_GO", [T, 256], F32)

    with ExitStack() as st:
        S = Sched(nc, st)
        k.S = S
        ARENA_N = 51000
        arena_t = st.enter_context(nc.sbuf_tensor("arena", [128, ARENA_N], F32))
        k.A = Arena(S, arena_t, ARENA_N)
        cst_t = st.enter_context(nc.sbuf_tensor("cst", [128, 2048], F32))
        k.CA = Arena(S, cst_t, 2048)
        k.banks = [st.enter_context(nc.psum_tensor("bank%d" % i, [128, 512], F32)) for i in range(8)]
        setup_consts(k)
        src = k.x
        for l in range(nlayers):
            if only is None or "inproj" in only:
                phase_inproj(k, l, src)
            if stop_after == ("inproj", l):
                break
            if only is None or "attn" in only:
                phase_attn(k, l)
            if stop_after == ("attn", l):
                break
            if only is None or "hgrn" in only:
                phase_hgrn(k, l)
            if stop_after == ("hgrn", l):
                break
            if only is None or "gdn" in only:
                phase_gdn(k, l)
            if stop_after == ("gdn", l):
                break
            if only is None or "ffn" in only:
                phase_outproj_ffn(k, l, src, last=(l == nlayers - 1))
            if stop_after == ("ffn", l):
                break
            src = k.H
        S.barrier()
        S.finish()
    return nc


def setup_consts(k):
    S, CA = k.S, k.CA
    k.ident, k.identr = CA.f32("ident", [128, 128])
    S.dma("sp", k.ident, k.c["ident"], writes=[k.identr], sbreg=k.identr)
    k.identb, k.identbr = CA.bf16("identb", [128, 128])
    S.op("dve", lambda h: h.tensor_copy(k.identb, k.ident), reads=[k.identr], writes=[k.identbr])
    k.trib, k.tribr = CA.bf16("trib", [128, 128])
    tmp, tmpr = CA.f32("tmpc", [128, 128])
    S.dma("sp", tmp, k.c["tri_incl"], writes=[tmpr], sbreg=tmpr)
    S.op("dve", lambda h: h.tensor_copy(k.trib, tmp), reads=[tmpr], writes=[k.tribr])
    for n in ("L16", "SU16"):
        ap, r = CA.f32(n, [128, 128])
        S.dma("sp", ap, k.c[n], writes=[r], sbreg=r)
        setattr(k, n, ap)
        setattr(k, n + "r", r)
    k.blk16, k.blk16r = CA.f32("blk16", [128, 8])
    S.dma("sp", k.blk16, k.c["blk16"], writes=[k.blk16r], sbreg=k.blk16r)
    for n in ("L64", "SL64", "LT64", "SLT64", "ones64"):
        ap, r = CA.f32(n, [64, 64])
        S.dma("sp", ap, k.c[n], writes=[r], sbreg=r)
        setattr(k, n, ap)
        setattr(k, n + "r", r)
    k.onehot, k.onehotr = CA.f32("onehot", [4, 4, 64])
    S.dma("sp", k.onehot, k.c["onehot"].rearrange("a (b c) -> a b c", b=4), writes=[k.onehotr], sbreg=k.onehotr)
    k.ssF, k.ssFr = CA.f32("ssF", [128, NT])
    k.ssA, k.ssAr = CA.f32("ssA", [128, NT])
    k.epst, k.epsr = CA.f32("epst", [128, 1])
    S.op("pool", lambda h: h.memset(k.epst, EPS), writes=[k.epsr])
    S.barrier(release=False)
    k.creg = [k.identr, k.identbr, k.tribr, k.L16r, k.SU16r, k.blk16r, k.L64r, k.SL64r, k.LT64r, k.SLT64r, k.ones64r, k.onehotr, k.epsr]


def rstd_from_ss(k, rs, rsr, ss, ssr, n, eng_small="dve"):
    S = k.S
    S.op("dve", lambda h: h.tensor_scalar(rs, ss, 1.0 / n, EPS, ALU.mult, ALU.add), reads=[ssr], writes=[rsr])
    S.op("act", lambda h: h.activation(rs, rs, AF.Ln), reads=[rsr], writes=[rsr])
    S.op("act", lambda h: h.activation(rs, rs, AF.Exp, scale=-0.5), reads=[rsr], writes=[rsr])


def norm_transpose_pass(k, src, uT, uTr, hb, ub, ss_name, do_norm=True, ss_pre=None, src_bf16=False):
    S, A = k.S, k.A
    rs, rsr = A.f32(ss_name + "_rs", [128, NT])
    if do_norm and ss_pre is not None:
        rstd_from_ss(k, rs, rsr, ss_pre[0], ss_pre[1], D)
    elif do_norm:
        ss, ssr = A.f32(ss_name, [128, NT])
        junk, junkr = A.f32(ss_name + "_junk", [128, D])
        S.op("dve", lambda h: h.memset(ss, 0.0), writes=[ssr])
        for i in range(NT):
            ht, htr = hb[i % len(hb)]
            S.dma("sp", ht, src[i * 128:(i + 1) * 128, :], writes=[htr], sbreg=htr)
            S.op("act", lambda h, ht=ht, i=i: h.activation(junk, ht, AF.Square, accum_out=ss[:, i:i + 1]),
                 reads=[htr, ssr], writes=[junkr, ssr])
        rstd_from_ss(k, rs, rsr, ss, ssr, D)

    def block(tb):
        for i in range(tb * 4, tb * 4 + 4):
            ht, htr = hb[i % len(hb)]
            u, ur = ub[i % len(ub)]
            if src_bf16:
                S.dma("sp", u, src[i * 128:(i + 1) * 128, :], writes=[ur], sbreg=ur)
            else:
                S.dma("sp", ht, src[i * 128:(i + 1) * 128, :], writes=[htr], sbreg=htr)
            if src_bf16:
                pass
            elif do_norm:
                S.op("dve", lambda h, ht=ht, u=u, i=i: h.tensor_scalar(u, ht, rs[:, i:i + 1], None, ALU.mult),
                     reads=[htr, rsr], writes=[ur])
            else:
                S.op("dve", lambda h, ht=ht, u=u: h.tensor_copy(u, ht), reads=[htr], writes=[ur])
            bank = k.banks[i % 2]
            bankr = k.bankr[i % 2]
            pt = bank[:, :].bitcast(BF16)
            for c in range(8):
                S.op("pe", lambda h, c=c, u=u, pt=pt: h.transpose(pt[:, c * 128:(c + 1) * 128], u[:, c * 128:(c + 1) * 128], k.identb),
                     reads=[ur, k.identbr], writes=[bankr], pe_accum=True)
            eng = "act" if i % 2 == 0 else "dve"
            if eng == "act":
                S.op("act", lambda h, pt=pt, i=i: h.copy(uT[:, :, i * 128:(i + 1) * 128], pt.rearrange("p (c t) -> p c t", c=8)),
                     reads=[bankr], writes=[uTr[i]])
            else:
                S.op("dve", lambda h, pt=pt, i=i: h.tensor_copy(uT[:, :, i * 128:(i + 1) * 128], pt.rearrange("p (c t) -> p c t", c=8)),
                     reads=[bankr], writes=[uTr[i]])
    return block


def phase_inproj(k, l, src):
    S, A = k.S, k.A
    S.barrier()
    A.reset()
    k.bankr = [S.reg("bank%d" % i) for i in range(8)]
    uT, uTr = A.bf16("uT", [128, 8, T], nreg=NT)
    Wb, Wbr = A.bf16("Wb", [128, 8, DIN], nreg=15)
    nw, nwr = A.f32("nw", [128, 8])
    S.dma("sp", nw, k.p["attn_norm_w"][l].rearrange("(c p) -> p c", p=128), writes=[nwr], sbreg=nwr, allow_slow_non_contiguous=True)
    wst = [A.f32("wst%d" % i, [128, 8, 256]) for i in range(2)]
    for bi, c0 in enumerate(range(0, DIN, 256)):
        cw = min(256, DIN - c0)
        wt, wtr = wst[bi % 2]
        S.dma("sp", wt[:, :, :cw], k.p["w_in"][l, :, c0:c0 + cw].rearrange("(c p) e -> p c e", p=128), writes=[wtr], sbreg=wtr)
        for c in range(8):
            if c % 2 == 0:
                S.op("dve", lambda h, c=c, wt=wt, c0=c0, cw=cw: h.tensor_scalar(Wb[:, c, c0:c0 + cw], wt[:, c, :cw], nw[:, c:c + 1], None, ALU.mult),
                     reads=[wtr, nwr], writes=[Wbr[bi]])
            else:
                S.op("act", lambda h, c=c, wt=wt, c0=c0, cw=cw: h.activation(Wb[:, c, c0:c0 + cw], wt[:, c, :cw], AF.Copy, scale=nw[:, c:c + 1]),
                     reads=[wtr, nwr], writes=[Wbr[bi]])
    hb = [A.f32("hb%d" % i, [128, D]) for i in range(3)]
    ub = [A.bf16("ub%d" % i, [128, D]) for i in range(2)]
    block = norm_transpose_pass(k, src, uT, uTr, hb, ub, "ssA", ss_pre=((k.ssA, k.ssAr) if l > 0 else None))
    fms = [A.f32("fms%d" % i, [128, 512]) for i in range(4)]
    tms = [A.f32("tms%d" % i, [128, PTM_W]) for i in range(2)]
    cnt = {"acc": 0, "fm": 0, "tm": 0, "ev": 0}

    def wreg(c0, cw):
        return [Wbr[b] for b in range(c0 // 256, (c0 + cw - 1) // 256 + 1)]

    def mm_block(tb):
        for ci, co in enumerate(FM_COLS):
            bank = k.banks[2 + cnt["acc"] % 4]
            bankr = k.bankr[2 + cnt["acc"] % 4]
            cnt["acc"] += 1
            for c in range(8):
                S.op("pe", lambda h, c=c, co=co, bank=bank: h.matmul(bank[:, :], Wb[:, c, co:co + 128], uT[:, c, tb * 512:(tb + 1) * 512], start=(c == 0), stop=(c == 7)),
                     reads=wreg(co, 128) + uTr[tb * 4:tb * 4 + 4], writes=[bankr], pe_accum=(c > 0))
            st_, str_ = fms[cnt["fm"] % 4]
            cnt["fm"] += 1
            if ci < 8:
                stv = st_[:, 0:256].bitcast(BF16)
                dst = k.PFMb[ci * 128:(ci + 1) * 128, tb * 512:(tb + 1) * 512]
            else:
                stv = st_
                dst = k.PFMf[(ci - 8) * 128:(ci - 7) * 128, tb * 512:(tb + 1) * 512]
            eng = "act" if cnt["ev"] % 2 == 0 else "dve"
            cnt["ev"] += 1
            if eng == "act":
                S.op("act", lambda h, stv=stv, bank=bank: h.copy(stv, bank[:, :]), reads=[bankr], writes=[str_])
            else:
                S.op("dve", lambda h, stv=stv, bank=bank: h.tensor_copy(stv, bank[:, :]), reads=[bankr], writes=[str_])
            S.dma("pool", dst, stv, reads=[str_], sbreg=str_)
        for i in range(tb * 4, tb * 4 + 4):
            st_, str_ = tms[cnt["tm"] % 2]
            cnt["tm"] += 1
            for (sc, w, dc) in TM_BLOCKS:
                bank = k.banks[2 + cnt["acc"] % 4]
                bankr = k.bankr[2 + cnt["acc"] % 4]
                cnt["acc"] += 1
                for c in range(8):
                    S.op("pe", lambda h, c=c, sc=sc, w=w, bank=bank, i=i: h.matmul(bank[:, :w], uT[:, c, i * 128:(i + 1) * 128], Wb[:, c, sc:sc + w], start=(c == 0), stop=(c == 7)),
                         reads=wreg(sc, w) + [uTr[i]], writes=[bankr], pe_accum=(c > 0))
                eng = "act" if cnt["ev"] % 2 == 0 else "dve"
                cnt["ev"] += 1
                if eng == "act":
                    S.op("act", lambda h, st_=st_, bank=bank, w=w, dc=dc: h.copy(st_[:, dc:dc + w], bank[:, :w]), reads=[bankr], writes=[str_])
                else:
                    S.op("dve", lambda h, st_=st_, bank=bank, w=w, dc=dc: h.tensor_copy(st_[:, dc:dc + w], bank[:, :w]), reads=[bankr], writes=[str_])
            S.dma("pool", k.PTM[i * 128:(i + 1) * 128, :], st_, reads=[str_], sbreg=str_)

    block(0)
    for tb in range(8):
        if tb + 1 < 8:
            block(tb + 1)
        mm_block(tb)

def phase_attn(k, l):
    S, A = k.S, k.A
    S.barrier()
    A.reset()
    k.bankr = [S.reg("bank%d" % i) for i in range(8)]
    lambda_init = 0.8 - 0.6 * math.exp(-0.3 * l)
    lp, lpr = A.f32("lp", [128, 4, 64])
    S.dma("sp", lp, k.p["diff_lambda"][l].rearrange("a b -> (a b)").partition_broadcast(128).rearrange("p (a b) -> p a b", a=4),
          writes=[lpr], sbreg=lpr)
    pr, prr = A.f32("lpp", [128, 2, 64])
    S.op("dve", lambda h: h.tensor_tensor(pr[:, 0, :], lp[:, 0, :], lp[:, 1, :], ALU.mult), reads=[lpr], writes=[prr])
    S.op("dve", lambda h: h.tensor_tensor(pr[:, 1, :], lp[:, 2, :], lp[:, 3, :], ALU.mult), reads=[lpr], writes=[prr])
    ls, lsr = A.f32("ls", [128, 2])
    S.op("dve", lambda h: h.tensor_reduce(ls, pr, AX.X, ALU.add), reads=[prr], writes=[lsr])
    S.op("act", lambda h: h.activation(ls, ls, AF.Exp), reads=[lsr], writes=[lsr])
    lam, lamr = A.f32("lam", [128, 1])
    S.op("dve", lambda h: h.tensor_tensor(lam, ls[:, 0:1], ls[:, 1:2], ALU.subtract), reads=[lsr], writes=[lamr])
    S.op("dve", lambda h: h.tensor_scalar(lam, lam, lambda_init, None, ALU.add), reads=[lamr], writes=[lamr])
    wsub, wsubr = A.f32("wsub", [128, 128])
    S.dma("sp", wsub, k.p["diff_subln_w"][l].partition_broadcast(128), writes=[wsubr], sbreg=wsubr)
    S.op("dve", lambda h: h.tensor_scalar(wsub, wsub, 1.0 - lambda_init, None, ALU.mult), reads=[wsubr], writes=[wsubr])

    kTb = [A.bf16("kT%d" % i, [128, T]) for i in range(2)]
    qzb = [[A.bf16("qz%d_%d" % (i, m), [128, T]) for m in range(2)] for i in range(2)]
    for i in range(2):
        S.op("pool", lambda h, i=i: h.memset(qzb[i][0][0][64:128, :], 0.0), writes=[qzb[i][0][1]])
        S.op("pool", lambda h, i=i: h.memset(qzb[i][1][0][0:64, :], 0.0), writes=[qzb[i][1][1]])
    Vf, Vfr = A.f32("Vf", [128, NT, 128])
    Vab = [A.bf16("Va%d" % i, [128, NT, 129]) for i in range(2)]
    PTs = [A.bf16("PT%d" % i, [128, 512]) for i in range(4)]
    Oacc = [A.f32("Oacc%d" % i, [128, 4, 129]) for i in range(2)]
    ot, otr = A.f32("ot", [128, 4, 128])
    t1, t1r = A.f32("t1", [128, 4, 128])
    outb = [A.bf16("outb%d" % i, [128, 4, 128]) for i in range(2)]
    r01, r01r = A.f32("r01", [128, 2, 4])
    ssq, ssqr = A.f32("ssq", [128, 4])
    rsq, rsqr = A.f32("rsq", [128, 4])
    cnt = {"s": 0, "pt": 0, "po": 0, "ob": 0}

    def load_head(hh):
        kT, kTr = kTb[hh % 2]
        Va, Var = Vab[hh % 2]
        for m in range(2):
            qz, qzr = qzb[hh % 2][m]
            S.dma("sp", qz[m * 64:(m + 1) * 64, :], k.PFMb[hh * 128 + m * 64:hh * 128 + (m + 1) * 64, :], writes=[qzr], sbreg=qzr)
        S.dma("sp", kT, k.PFMb[512 + hh * 128:512 + (hh + 1) * 128, :], writes=[kTr], sbreg=kTr)
        S.dma("sp", Vf, k.PTM[:, hh * 128:(hh + 1) * 128].rearrange("(n p) e -> p n e", p=128), writes=[Vfr], sbreg=Vfr)
        S.op("pool", lambda h: h.tensor_copy(Va[:, :, 0:128], Vf), reads=[Vfr], writes=[Var])
        S.op("pool", lambda h: h.memset(Va[:, :, 128:129], 1.0), writes=[Var])

    steps = []

    def make_head(hh):
        kT, kTr = kTb[hh % 2]
        Va, Var = Vab[hh % 2]
        for Q in range(8):
            for m in range(2):
                qT, qTr = qzb[hh % 2][m]
                pbanks = [k.banks[2 + c] for c in range(4)]
                pbankr = [k.bankr[2 + c] for c in range(4)]
                first = [True, True, True, True]
                nj = 4 * Q + 4
                for j in range(nj):
                    i = j - 4 * Q
                    i0 = max(i, 0)
                    ncols = 512 - i0 * 128
                    qs = Q * 512 + i0 * 128
                    st = {}

                    def s_stage(st=st, j=j, i=i, ncols=ncols, qs=qs, m=m, kT=kT, qT=qT, kTr=kTr, qTr=qTr, hh=hh, Q=Q):
                        sbi = (0, 1, 6, 7)[cnt["s"] % 4]
                        sb = k.banks[sbi]
                        sbr = k.bankr[sbi]
                        cnt["s"] += 1
                        S.op("pe", lambda h: h.matmul(sb[:, :ncols], kT[:, j * 128:(j + 1) * 128], qT[:, qs:qs + ncols], start=True, stop=True),
                             reads=[kTr, qTr], writes=[sbr])
                        PT, PTr = PTs[cnt["pt"] % 4]
                        cnt["pt"] += 1
                        S.op("act", lambda h: h.activation(PT[:, :ncols], sb[:, :ncols], AF.Exp, scale=0.125), reads=[sbr], writes=[PTr])
                        if i >= 0:
                            S.op("dve", lambda h: h.tensor_tensor(PT[:, 0:128], PT[:, 0:128], k.trib, ALU.mult), reads=[PTr, k.tribr], writes=[PTr])
                        st["PT"] = (PT, PTr)

                    def pv_stage(st=st, j=j, i0=i0, Q=Q, m=m, Va=Va, Var=Var, pbanks=pbanks, pbankr=pbankr, first=first, nj=nj, hh=hh):
                        PT, PTr = st["PT"]
                        if Q == 0 and m == 0 and j == 3 and hh + 1 < 4:
                            load_head(hh + 1)
                        for c in range(i0, 4):
                            pb = pbanks[c]
                            S.op("pe", lambda h, pb=pb, c=c: h.matmul(pb[:, 0:129], PT[:, (c - i0) * 128:(c - i0 + 1) * 128], Va[:, j, :], start=(j == 0), stop=(j == 4 * Q + c)),
                                 reads=[PTr, Var], writes=[pbankr[c]], pe_accum=not first[c])
                            first[c] = False
                        if j == nj - 1:
                            Oa, Oar = Oacc[m]
                            for c in range(4):
                                S.op("dve", lambda h, c=c: h.tensor_copy(Oa[:, c, :], pbanks[c][:, 0:129]), reads=[pbankr[c]], writes=[Oar])
                            if m == 1:
                                finalize(hh, Q)
                    steps.append((s_stage, pv_stage))

    def finalize(hh, Q):
        O0, O0r = Oacc[0]
        O1, O1r = Oacc[1]
        S.op("dve", lambda h: h.reciprocal(r01[:, 0, :], O0[:, :, 128]), reads=[O0r], writes=[r01r])
        S.op("dve", lambda h: h.reciprocal(r01[:, 1, :], O1[:, :, 128]), reads=[O1r], writes=[r01r])
        S.op("dve", lambda h: h.tensor_scalar(r01[:, 1, :], r01[:, 1, :], lam[:, 0:1], None, ALU.mult), reads=[r01r, lamr], writes=[r01r])
        S.op("pool", lambda h: h.tensor_tensor(t1, O1[:, :, 0:128], r01[:, 1, :].unsqueeze(2).to_broadcast([128, 4, 128]), ALU.mult),
             reads=[O1r, r01r], writes=[t1r])
        S.op("dve", lambda h: h.tensor_tensor(ot, O0[:, :, 0:128], r01[:, 0, :].unsqueeze(2).to_broadcast([128, 4, 128]), ALU.mult),
             reads=[O0r, r01r], writes=[otr])
        S.op("dve", lambda h: h.tensor_tensor(ot, ot, t1, ALU.subtract), reads=[otr, t1r], writes=[otr])
        S.op("pool", lambda h: h.tensor_tensor(t1, ot, ot, ALU.mult), reads=[otr], writes=[t1r])
        S.op("dve", lambda h: h.tensor_reduce(ssq, t1, AX.X, ALU.add), reads=[t1r], writes=[ssqr])
        rstd_from_ss(k, rsq, rsqr, ssq, ssqr, 128)
        ob, obr = outb[cnt["ob"] % 2]
        cnt["ob"] += 1
        S.op("dve", lambda h: h.tensor_tensor(ot, ot, rsq.unsqueeze(2).to_broadcast([128, 4, 128]), ALU.mult), reads=[otr, rsqr], writes=[otr])
        S.op("pool", lambda h: h.tensor_tensor(ob, ot, wsub.unsqueeze(1).to_broadcast([128, 4, 128]), ALU.mult), reads=[otr, wsubr], writes=[obr])
        S.dma("pool", k.MIX[Q * 512:(Q + 1) * 512, hh * 128:(hh + 1) * 128].rearrange("(c p) e -> p c e", p=128), ob, reads=[obr], sbreg=obr)

    load_head(0)
    for hh in range(4):
        make_head(hh)
    LOOK = 3
    for idx in range(min(LOOK, len(steps))):
        steps[idx][0]()
    for idx in range(len(steps)):
        if idx + LOOK < len(steps):
            steps[idx + LOOK][0]()
        steps[idx][1]()

def phase_hgrn(k, l):
    S, A = k.S, k.A
    S.barrier()
    A.reset()
    k.bankr = [S.reg("bank%d" % i) for i in range(8)]
    NCH = T // 16
    r16, r16r = A.f32("r16", [128, T])
    S.op("pool", lambda h: h.memset(r16, 1.0), writes=[r16r])
    S.op("pool", lambda h: h.memset(r16.rearrange("p (n c) -> p n c", c=16)[:, :, 0:1], 0.0), writes=[r16r])
    lbf, lbfr = A.f32("lbf", [128, 2])
    omf, omfr = A.f32("omf", [128, 2])
    lbt, lbtr = A.f32("lbt", [128, 256])
    omt, omtr = A.f32("omt", [128, 256])
    if l == 0:
        S.op("pool", lambda h: h.memset(lbf, 0.0), writes=[lbfr])
        S.op("pool", lambda h: h.memset(lbt, 0.0), writes=[lbtr])
    else:
        lg, lgr = A.f32("lg", [128, 2, 2])
        for li in range(2):
            S.dma("sp", lg[:, li, :], k.p["hgrn_lb_logits"][li].rearrange("(c p) -> p c", p=128), writes=[lgr], sbreg=lgr, allow_slow_non_contiguous=True)
        S.op("dve", lambda h: h.tensor_tensor(lbf, lg[:, 1, :], lg[:, 0, :], ALU.subtract), reads=[lgr], writes=[lbfr])
        S.op("act", lambda h: h.activation(lbf, lbf, AF.Sigmoid), reads=[lbfr], writes=[lbfr])
        lg2, lg2r = A.f32("lg2", [128, 2, 256])
        S.dma("sp", lg2, k.p["hgrn_lb_logits"].rearrange("a b -> (a b)").partition_broadcast(128).rearrange("p (a b) -> p a b", a=2), writes=[lg2r], sbreg=lg2r)
        S.op("dve", lambda h: h.tensor_tensor(lbt, lg2[:, 1, :], lg2[:, 0, :], ALU.subtract), reads=[lg2r], writes=[lbtr])
        S.op("act", lambda h: h.activation(lbt, lbt, AF.Sigmoid), reads=[lbtr], writes=[lbtr])
    S.op("dve", lambda h: h.tensor_scalar(omf, lbf, -1.0, 1.0, ALU.mult, ALU.add), reads=[lbfr], writes=[omfr])
    S.op("dve", lambda h: h.tensor_scalar(omt, lbt, -1.0, 1.0, ALU.mult, ALU.add), reads=[lbtr], writes=[omtr])
    hnw, hnwr = A.f32("hnw", [128, 64])
    S.dma("sp", hnw, k.p["hgrn_norm_w"][l].partition_broadcast(128), writes=[hnwr], sbreg=hnwr)

    mark0 = A.off
    Kall, Kallr = A.f32("Kall", [128, NT, 256], nreg=4)
    Lall, Lallr = A.f32("Lall", [128, NT, 256], nreg=4)
    for q in range(4):
        sl = slice(q * 8, (q + 1) * 8)
        Kq = Kall[:, sl, :]
        Lq = Lall[:, sl, :]
        S.dma("sp", Kq, k.PTM[q * 1024:(q + 1) * 1024, 512:768].rearrange("(n p) e -> p n e", p=128), writes=[Kallr[q]], sbreg=Kallr[q])
        S.op("act", lambda h, Kq=Kq: h.activation(Kq, Kq, AF.Sigmoid), reads=[Kallr[q]], writes=[Kallr[q]])
    for q in range(4):
        sl = slice(q * 8, (q + 1) * 8)
        Kq = Kall[:, sl, :]
        Lq = Lall[:, sl, :]
        S.op("dve", lambda h, Kq=Kq: h.tensor_tensor(Kq, Kq, omt.unsqueeze(1).to_broadcast([128, 8, 256]), ALU.mult), reads=[Kallr[q], omtr], writes=[Kallr[q]])
        S.op("dve", lambda h, Kq=Kq: h.tensor_tensor(Kq, Kq, lbt.unsqueeze(1).to_broadcast([128, 8, 256]), ALU.add), reads=[Kallr[q], lbtr], writes=[Kallr[q]])
        S.op("dve", lambda h, Kq=Kq, Lq=Lq: h.tensor_scalar(Lq, Kq, 1e-30, None, ALU.max), reads=[Kallr[q]], writes=[Lallr[q]])
        S.op("act", lambda h, Lq=Lq: h.activation(Lq, Lq, AF.Ln), reads=[Lallr[q]], writes=[Lallr[q]])
    for q in range(4):
        for ii in range(8):
            i = q * 8 + ii
            bank = k.banks[i % 4]
            bankr = k.bankr[i % 4]
            S.op("pe", lambda h, bank=bank, i=i: h.matmul(bank[:, 0:256], k.SU16, Lall[:, i, :], start=True, stop=True), reads=[Lallr[q], k.SU16r], writes=[bankr])
            S.op("act", lambda h, bank=bank, i=i: h.activation(Lall[:, i, :], bank[:, 0:256], AF.Exp), reads=[bankr], writes=[Lallr[q]])
        sl = slice(q * 8, (q + 1) * 8)
        Kq = Kall[:, sl, :]
        Lq = Lall[:, sl, :]
        S.op("dve", lambda h, Kq=Kq: h.tensor_scalar(Kq, Kq, -1.0, 1.0, ALU.mult, ALU.add), reads=[Kallr[q]], writes=[Kallr[q]])
        Kbq = Lq.rearrange("p n e -> p (n e)")[:, 0:1024].bitcast(BF16).rearrange("p (n e) -> p n e", n=8)
        S.op("dve", lambda h, Kq=Kq, Lq=Lq: h.tensor_tensor(Kq, Kq, Lq, ALU.mult), reads=[Kallr[q], Lallr[q]], writes=[Kallr[q]])
        S.op("pool", lambda h, Kq=Kq, Kbq=Kbq: h.tensor_copy(Kbq, Kq), reads=[Kallr[q]], writes=[Lallr[q]])
        S.dma("pool", k.HK[q * 1024:(q + 1) * 1024, :].rearrange("(n p) e -> p n e", p=128), Kbq, reads=[Lallr[q]], sbreg=Lallr[q])
    S.barrier(release=False)
    A.off = mark0

    QtT, QtTr = A.f32("QtT", [128, T])
    KtT, KtTr = A.f32("KtT", [128, T])
    Dn, Dnr = A.f32("Dn", [128, NCH])
    U, Ur = A.f32("U", [128, 64, NCH])
    Ur2 = S.reg("Uscan")
    Oall, Oallr = A.f32("Oall", [128, NT, 128])
    ssh, sshr = A.f32("ssh", [128, NT, 2])
    rsh, rshr = A.f32("rsh", [128, NT, 2])
    vex = [A.bf16("vex%d" % i, [128, 8, 64]) for i in range(4)]
    Qb, Qbr = A.bf16("Qb", [128, T])
    Kb, Kbr = A.bf16("Kb", [128, T])
    ats = [A.bf16("ats%d" % i, [128, 128]) for i in range(4)]
    ots = [A.f32("ots%d" % i, [64, 128]) for i in range(4)]
    ktl = [A.bf16("ktl%d" % i, [128, 128]) for i in range(2)]
    vtl = [A.f32("vtl%d" % i, [128, 128]) for i in range(3)]
    vbl = [A.bf16("vbl%d" % i, [128, 128]) for i in range(3)]
    mark = A.off
    fpT, fpTr = A.f32("fpT", [128, T])
    BT, BTr = A.f32("BT", [128, T])
    for c in range(2):
        S.dma("sp", fpT, k.PFMf[256 + c * 128:256 + (c + 1) * 128, :], writes=[fpTr], sbreg=fpTr)
        S.dma("sp", QtT, k.PFMf[c * 128:(c + 1) * 128, :], writes=[QtTr], sbreg=QtTr)
        S.op("act", lambda h: h.activation(fpT, fpT, AF.Sigmoid), reads=[fpTr], writes=[fpTr])
        S.op("dve", lambda h, c=c: h.tensor_scalar(fpT, fpT, omf[:, c:c + 1], lbf[:, c:c + 1], ALU.mult, ALU.add), reads=[fpTr, omfr, lbfr], writes=[fpTr])
        S.op("pool", lambda h: h.tensor_scalar(KtT, fpT, -1.0, 1.0, ALU.mult, ALU.add), reads=[fpTr], writes=[KtTr])
        S.op("dve", lambda h: h.tensor_scalar(fpT, fpT, 1e-30, None, ALU.max), reads=[fpTr], writes=[fpTr])
        S.op("act", lambda h: h.activation(fpT, fpT, AF.Ln), reads=[fpTr], writes=[fpTr])
        S.op("dve", lambda h: h.tensor_tensor_scan(BT, r16, fpT, 0.0, ALU.mult, ALU.add), reads=[fpTr, r16r], writes=[BTr])
        S.op("act", lambda h: h.activation(Dn, BT.rearrange("p (n c) -> p n c", c=16)[:, :, 15], AF.Exp), reads=[BTr], writes=[Dnr])
        S.op("act", lambda h: h.activation(fpT, BT, AF.Exp), reads=[BTr], writes=[fpTr])
        S.op("act", lambda h: h.activation(BT, BT, AF.Exp, scale=-1.0), reads=[BTr], writes=[BTr])
        S.op("dve", lambda h: h.tensor_tensor(KtT, KtT, BT, ALU.mult), reads=[KtTr, BTr], writes=[KtTr])
        S.op("act", lambda h: h.activation(QtT, QtT, AF.Silu), reads=[QtTr], writes=[QtTr])
        S.op("dve", lambda h: h.tensor_tensor(QtT, QtT, fpT, ALU.mult), reads=[QtTr, fpTr], writes=[QtTr])
        S.op("pool", lambda h: h.tensor_copy(Qb, QtT), reads=[QtTr], writes=[Qbr])
        S.op("pool", lambda h: h.tensor_copy(Kb, KtT), reads=[KtTr], writes=[Kbr])
        def u_unit(i, c=c):
            u = i % 2
            kt, ktr = ktl[i % 2]
            vt, vtr = vtl[i % 3]
            S.dma("sp", kt, k.HK[i * 128:(i + 1) * 128, c * 128:(c + 1) * 128], writes=[ktr], sbreg=ktr)
            S.dma("sp", vt, k.PTM[i * 128:(i + 1) * 128, 768 + c * 128:768 + (c + 1) * 128], writes=[vtr], sbreg=vtr)
            for hl in range(2):
                vx, vxr = vex[u * 2 + hl]
                eng = "dve" if hl == 0 else "pool"
                S.op(eng, lambda h, vx=vx, hl=hl: h.tensor_tensor(
                    vx, vt[:, hl * 64:(hl + 1) * 64].unsqueeze(1).to_broadcast([128, 8, 64]),
                    k.blk16.unsqueeze(2).to_broadcast([128, 8, 64]), ALU.mult), reads=[vtr, k.blk16r], writes=[vxr])
            yield
            for hl in range(2):
                vx, vxr = vex[u * 2 + hl]
                bank = k.banks[u * 2 + hl]
                bankr = k.bankr[u * 2 + hl]
                S.op("pe", lambda h, bank=bank, vx=vx: h.matmul(bank[:, :], kt, vx.rearrange("p n v -> p (n v)"), start=True, stop=True),
                     reads=[ktr, vxr], writes=[bankr])
            yield
            for hl in range(2):
                bank = k.banks[u * 2 + hl]
                bankr = k.bankr[u * 2 + hl]
                p0 = hl * 64
                if hl == 0:
                    S.op("act", lambda h, bank=bank, p0=p0: h.copy(U[p0:p0 + 64, :, i * 8:(i + 1) * 8], bank[p0:p0 + 64, :].rearrange("d (n v) -> d v n", n=8)), reads=[bankr], writes=[Ur])
                else:
                    S.op("dve", lambda h, bank=bank, p0=p0: h.tensor_copy(U[p0:p0 + 64, :, i * 8:(i + 1) * 8], bank[p0:p0 + 64, :].rearrange("d (n v) -> d v n", n=8)), reads=[bankr], writes=[Ur])
        run_interleaved([u_unit(i) for i in range(NT)], 2, 1)
        for v in range(64):
            S.op("dve", lambda h, v=v: h.tensor_tensor_scan(U[:, v, :], Dn, U[:, v, :], 0.0, ALU.mult, ALU.add), reads=[Ur, Dnr], writes=[Ur2] if v else [Ur, Ur2])
        Ub = k.A.t[0:128, mark:mark + 8192].bitcast(BF16).rearrange("p (v n) -> p v n", n=NCH)
        Ubr = S.reg("Ub")
        S.op("dve", lambda h: h.tensor_copy(Ub, U), reads=[Ur, Ur2, fpTr, BTr, QtTr, KtTr, Qbr, Kbr], writes=[Ubr, fpTr, BTr])
        def o_unit(i, c=c):
            u = i % 2
            vt, vtr = vtl[i % 3]
            vb, vbr = vbl[i % 3]
            S.dma("sp", vt, k.PTM[i * 128:(i + 1) * 128, 768 + c * 128:768 + (c + 1) * 128], writes=[vtr], sbreg=vtr)
            S.op("pool", lambda h: h.tensor_copy(vb, vt), reads=[vtr], writes=[vbr])
            for hl in range(2):
                p0 = hl * 64
                ba = k.banks[u * 2 + hl]
                bar = k.bankr[u * 2 + hl]
                S.op("pe", lambda h, ba=ba, p0=p0: h.matmul(ba[:, 0:128], Kb[p0:p0 + 64, i * 128:(i + 1) * 128], Qb[p0:p0 + 64, i * 128:(i + 1) * 128], start=True, stop=True),
                     reads=[Kbr, Qbr], writes=[bar])
            yield
            for hl in range(2):
                ba = k.banks[u * 2 + hl]
                bar = k.bankr[u * 2 + hl]
                at, atr = ats[u * 2 + hl]
                S.op("dve", lambda h, at=at, ba=ba: h.tensor_tensor(at, ba[:, 0:128], k.L16, ALU.mult), reads=[bar, k.L16r], writes=[atr])
            yield
            for hl in range(2):
                p0 = hl * 64
                at, atr = ats[u * 2 + hl]
                bo = k.banks[4 + u * 2 + hl]
                bor = k.bankr[4 + u * 2 + hl]
                S.op("pe", lambda h, bo=bo, at=at, p0=p0: h.matmul(bo[0:64, 0:128], vb[:, p0:p0 + 64], at, start=True, stop=False),
                     reads=[vbr, atr], writes=[bor])
                for n in range(8):
                    g = i * 8 + n
                    if g == 0:
                        continue
                    S.op("pe", lambda h, bo=bo, g=g, n=n, p0=p0: h.matmul(bo[0:64, n * 16:(n + 1) * 16], Ub[p0:p0 + 64, :, g - 1], Qb[p0:p0 + 64, g * 16:(g + 1) * 16], start=False, stop=(n == 7)),
                         reads=[Ubr, fpTr, BTr, Qbr], writes=[bor], pe_accum=True)
            yield
            for hl in range(2):
                bo = k.banks[4 + u * 2 + hl]
                bor = k.bankr[4 + u * 2 + hl]
                ot, otr = ots[u * 2 + hl]
                S.op("act", lambda h, ot=ot, bo=bo: h.copy(ot, bo[0:64, 0:128]), reads=[bor], writes=[otr])
            yield
            for hl in range(2):
                ot, otr = ots[u * 2 + hl]
                bt = k.banks[u * 2 + hl]
                btr = k.bankr[u * 2 + hl]
                S.op("pe", lambda h, bt=bt, ot=ot: h.transpose(bt[:, 0:64], ot, k.ident[0:64, 0:64]), reads=[otr, k.identr], writes=[btr])
            yield
            for hl in range(2):
                p0 = hl * 64
                bt = k.banks[u * 2 + hl]
                btr = k.bankr[u * 2 + hl]
                S.op("dve", lambda h, bt=bt, p0=p0: h.tensor_copy(Oall[:, i, p0:p0 + 64], bt[:, 0:64]), reads=[btr], writes=[Oallr])
        run_interleaved([o_unit(i) for i in range(NT)], 2, 3)
        Osq, Osqr = fpT.rearrange("p (n e) -> p n e", n=NT), fpTr
        Gall, Gallr = BT.rearrange("p (n e) -> p n e", n=NT), BTr
        S.dma("sp", Gall, k.PTM[:, 1024 + c * 128:1024 + (c + 1) * 128].rearrange("(n p) e -> p n e", p=128), writes=[Gallr], sbreg=Gallr)
        S.op("act", lambda h: h.activation(Gall, Gall, AF.Silu), reads=[Gallr], writes=[Gallr])
        S.op("pool", lambda h: h.tensor_tensor(Osq, Oall, Oall, ALU.mult), reads=[Oallr], writes=[Osqr])
        S.op("dve", lambda h: h.tensor_reduce(ssh.rearrange("p n g -> p (n g)"), Osq.rearrange("p n (g v) -> p (n g) v", g=2), AX.X, ALU.add), reads=[Osqr], writes=[sshr])
        rstd_from_ss(k, rsh.rearrange("p n g -> p (n g)"), rshr, ssh.rearrange("p n g -> p (n g)"), sshr, 64)
        S.op("dve", lambda h: h.tensor_tensor(Oall.rearrange("p n (g v) -> p (n g) v", g=2), Oall.rearrange("p n (g v) -> p (n g) v", g=2),
                                              rsh.rearrange("p n g -> p (n g)").unsqueeze(2).to_broadcast([128, NT * 2, 64]), ALU.mult), reads=[Oallr, rshr], writes=[Oallr])
        S.op("pool", lambda h: h.tensor_tensor(Oall.rearrange("p n (g v) -> p (n g) v", g=2), Oall.rearrange("p n (g v) -> p (n g) v", g=2),
                                               hnw.unsqueeze(1).to_broadcast([128, NT * 2, 64]), ALU.mult), reads=[Oallr, hnwr], writes=[Oallr])
        Ob = fpT[:, 0:2048].bitcast(BF16).rearrange("p (n e) -> p n e", n=NT)
        S.op("dve", lambda h: h.tensor_tensor(Ob, Oall, Gall, ALU.mult), reads=[Oallr, Gallr, fpTr], writes=[fpTr])
        S.dma("pool", k.MIX[:, 512 + c * 128:512 + (c + 1) * 128].rearrange("(n p) e -> p n e", p=128), Ob, reads=[fpTr], sbreg=fpTr)

def phase_gdn(k, l):
    S, A = k.S, k.A
    S.barrier()
    A.reset()
    k.bankr = [S.reg("bank%d" % i) for i in range(8)]
    cwd, cwdr = A.f32("cwd", [128, 4, 6])
    for j in range(4):
        S.dma("sp", cwd[:, j, :], k.p["dn_conv_w"][l, j].rearrange("(c p) -> p c", p=128), writes=[cwdr], sbreg=cwdr, allow_slow_non_contiguous=True)
    bo, bor = A.f32("bo128", [128, 128])
    S.op("pool", lambda h: h.memset(bo, 0.0), writes=[bor])
    S.op("pool", lambda h: h.memset(bo[0:64, 0:64], 1.0), writes=[bor])
    S.op("pool", lambda h: h.memset(bo[64:128, 64:128], 1.0), writes=[bor])
    Xb = [A.f32("Xc%d" % i, [128, T + 3]) for i in range(2)]
    Yb = [A.f32("Yc%d" % i, [128, T]) for i in range(2)]
    sqb = [A.f32("sqc%d" % i, [128, T]) for i in range(2)]
    tms = [A.f32("tmsg%d" % i, [128, NT, 128]) for i in range(2)]
    for i in range(2):
        S.op("pool", lambda h, i=i: h.memset(Xb[i][0][:, 0:3], 0.0), writes=[Xb[i][1]])
    for c in range(6):
        g1_chunk(k, c, Xb[c % 2], Yb[c % 2], sqb[c % 2], tms, cwd, cwdr, bo, bor)
    if False:
        X, Xr = Xb[0]; Y, Yr = Yb[0]; sq, sqr = sqb[0]; c = 0
        pass

    S.barrier()
    A.reset()
    k.bankr = [S.reg("bank%d" % i) for i in range(8)]
    NC = 64
    HP = 2
    ab, abr = A.f32("ab", [128, NC, 4])
    alog, alogr = A.f32("alog", [128, HP])
    dtb, dtbr = A.f32("dtb", [128, HP])
    for hb in range(2):
        ph = slice(hb * 64, (hb + 1) * 64)
        for which in range(2):
            c0 = 1536 + which * 4 + hb
            for hp in range(HP):
                S.dma("sp", ab[ph, :, which * 2 + hp:which * 2 + hp + 1], k.PTM[:, c0 + 2 * hp:c0 + 2 * hp + 1].rearrange("(n p) e -> p n e", p=64),
                      writes=[abr], sbreg=abr, allow_slow_non_contiguous=True)
        S.dma("sp", alog[ph, :], k.p["dn_A_log"][l].rearrange("(hp hb) -> hb hp", hb=2)[hb].partition_broadcast(64), writes=[alogr], sbreg=alogr, allow_slow_non_contiguous=True)
        S.dma("sp", dtb[ph, :], k.p["dn_dt_bias"][l].rearrange("(hp hb) -> hb hp", hb=2)[hb].partition_broadcast(64), writes=[dtbr], sbreg=dtbr, allow_slow_non_contiguous=True)
    S.op("act", lambda h: h.activation(alog, alog, AF.Exp), reads=[alogr], writes=[alogr])
    S.op("dve", lambda h: h.tensor_scalar(alog, alog, -1.0, None, ALU.mult), reads=[alogr], writes=[alogr])
    L64b, L64br = A.f32("L64b", [128, 128])
    on64b, on64br = A.f32("on64b", [128, 128])
    S.op("pool", lambda h: h.memset(L64b, 0.0), writes=[L64br])
    S.op("pool", lambda h: h.memset(on64b, 0.0), writes=[on64br])
    mk = {}
    for nm in ("L64", "SLT64"):
        mk[nm] = A.f32(nm + "x2", [128, 64])
    idx2, idx2r = A.f32("idx2", [128, 64])
    for hb in range(2):
        ph = slice(hb * 64, (hb + 1) * 64)
        S.dma("sp", L64b[ph, hb * 64:(hb + 1) * 64], k.c["L64"], writes=[L64br], sbreg=L64br)
        S.dma("sp", on64b[ph, hb * 64:(hb + 1) * 64], k.c["ones64"], writes=[on64br], sbreg=on64br)
        for nm in ("L64", "SLT64"):
            S.dma("sp", mk[nm][0][ph, :], k.c[nm], writes=[mk[nm][1]], sbreg=mk[nm][1])
        S.dma("sp", idx2[ph, :], k.c["ident"][0:64, 0:64], writes=[idx2r], sbreg=idx2r)
    L64x, L64xr = mk["L64"]
    SLT64x, SLT64xr = mk["SLT64"]

    def sc(name):
        return A.f32(name, [128, NC, HP])
    g, gr = sc("g")
    beta, betar = sc("beta")
    t0, t0r = sc("t0")
    t1, t1r = sc("t1")
    gc, gcr = sc("gc")
    gl, glr = sc("gl")
    egc, egcr = sc("egc")
    ekd, ekdr = sc("ekd")
    egl, eglr = sc("egl")
    nbeta, nbetar = sc("nbeta")
    begc, begcr = sc("begc")
    S.op("dve", lambda h: h.tensor_tensor(t0, ab[:, :, 0:2], dtb.unsqueeze(1).to_broadcast([128, NC, HP]), ALU.add), reads=[abr, dtbr], writes=[t0r])
    S.op("dve", lambda h: h.tensor_scalar(t1, t0, -1.0, None, ALU.mult), reads=[t0r], writes=[t1r])
    S.op("dve", lambda h: h.tensor_tensor(t1, t1, t0, ALU.max), reads=[t0r, t1r], writes=[t1r])
    S.op("act", lambda h: h.activation(t1, t1, AF.Exp, scale=-1.0), reads=[t1r], writes=[t1r])
    S.op("dve", lambda h: h.tensor_scalar(t1, t1, 1.0, None, ALU.add), reads=[t1r], writes=[t1r])
    S.op("act", lambda h: h.activation(t1, t1, AF.Ln), reads=[t1r], writes=[t1r])
    S.op("dve", lambda h: h.tensor_scalar(t0, t0, 0.0, None, ALU.max), reads=[t0r], writes=[t0r])
    S.op("dve", lambda h: h.tensor_tensor(t0, t0, t1, ALU.add), reads=[t0r, t1r], writes=[t0r])
    S.op("dve", lambda h: h.tensor_tensor(g, t0, alog.unsqueeze(1).to_broadcast([128, NC, HP]), ALU.mult), reads=[t0r, alogr], writes=[gr])
    S.op("act", lambda h: h.activation(beta, ab[:, :, 2:4], AF.Sigmoid), reads=[abr], writes=[betar])
    S.op("pe", lambda h: h.matmul(k.banks[0][:, 0:NC * HP], L64b, g.rearrange("p n h -> p (n h)"), start=True, stop=True), reads=[gr, L64br], writes=[k.bankr[0]])
    S.op("pe", lambda h: h.matmul(k.banks[1][:, 0:NC * HP], on64b, g.rearrange("p n h -> p (n h)"), start=True, stop=True), reads=[gr, on64br], writes=[k.bankr[1]])
    S.op("dve", lambda h: h.tensor_copy(gc.rearrange("p n h -> p (n h)"), k.banks[0][:, 0:NC * HP]), reads=[k.bankr[0]], writes=[gcr])
    S.op("dve", lambda h: h.tensor_copy(gl.rearrange("p n h -> p (n h)"), k.banks[1][:, 0:NC * HP]), reads=[k.bankr[1]], writes=[glr])
    S.op("act", lambda h: h.activation(egc, gc, AF.Exp), reads=[gcr], writes=[egcr])
    S.op("act", lambda h: h.activation(egl, gl, AF.Exp), reads=[glr], writes=[eglr])
    S.op("dve", lambda h: h.tensor_tensor(ekd, gl, gc, ALU.subtract), reads=[glr, gcr], writes=[ekdr])
    S.op("act", lambda h: h.activation(ekd, ekd, AF.Exp), reads=[ekdr], writes=[ekdr])
    S.op("dve", lambda h: h.tensor_scalar(nbeta, beta, -1.0, None, ALU.mult), reads=[betar], writes=[nbetar])
    S.op("dve", lambda h: h.tensor_tensor(begc, beta, egc, ALU.mult), reads=[betar, egcr], writes=[begcr])

    CB = 4
    NSL = CB * HP
    NSET = 2
    NB = 8 // CB
    qTg = [A.f32("qTg%d" % i, [128, HP, 512]) for i in range(2)]
    kTg = [A.f32("kTg%d" % i, [128, HP, 512]) for i in range(2)]
    ktg = [A.f32("ktg%d" % i, [128, 8, HP, 64]) for i in range(2)]
    vtg = [A.f32("vtg%d" % i, [128, 8, HP, 64]) for i in range(2)]
    og = [A.f32("og%d" % i, [128, 8, HP, 64]) for i in range(2)]
    qTb = [A.bf16("qTb%d" % i, [128, HP, 512]) for i in range(2)]
    kTb = [A.bf16("kTb%d" % i, [128, HP, 512]) for i in range(2)]

    def bt(name):
        return A.f32(name, [128, NSL, 64])

    def btb(name):
        return A.bf16(name, [128, NSL, 64])
    sets = []
    for si in range(NSET):
        d = {}
        for nm in ("raw", "mx", "mn", "tA", "tB", "ut", "gB"):
            d[nm] = bt(nm + str(si))
        for nm in ("Mt0", "Mt1", "Nt0", "Nt1", "Tt", "aqk", "Ru", "Rw", "kdec", "wT"):
            d[nm] = btb(nm + str(si))
        sets.append(d)
    Sst, Sstr = A.f32("Sst", [128, HP, 64])
    Ssb, Ssbr = A.bf16("Ssb", [128, HP, 64])
    vnew, vnewr = A.bf16("vnew", [128, HP, 64])
    otmp, otmpr = A.f32("otmp", [128, HP, 64])
    S.op("pool", lambda h: h.memset(Sst, 0.0), writes=[Sstr])
    S.op("pool", lambda h: h.memset(Ssb, 0.0), writes=[Ssbr])
    PH = [slice(0, 64), slice(64, 128)]

    def bc_s(ap2):
        return ap2.unsqueeze(2).to_broadcast([128, NSL, 64])

    def mask_b(m):
        return m.unsqueeze(1).to_broadcast([128, NSL, 64])

    def pv(bank):
        return bank[:, :].rearrange("p (a b) -> p a b", b=64)

    def pv2(bank):
        return bank[:, 0:HP * 64].rearrange("p (a b) -> p a b", b=64)

    def group_load(grp):
        qTt, qTtr = qTg[grp % 2]
        kTt, kTtr = kTg[grp % 2]
        ktt, kttr = ktg[grp % 2]
        vtt, vttr = vtg[grp % 2]
        qTtb, qTtbr = qTb[grp % 2]
        kTtb, kTtbr = kTb[grp % 2]
        tsl = slice(grp * 512, (grp + 1) * 512)
        S.dma("sp", qTt, k.GQT[:, tsl].rearrange("(hp p) t -> p hp t", p=128), writes=[qTtr], sbreg=qTtr)
        S.dma("sp", kTt, k.GKT[:, tsl].rearrange("(hp p) t -> p hp t", p=128), writes=[kTtr], sbreg=kTtr)
        S.op("pool", lambda h: h.tensor_copy(qTtb, qTt), reads=[qTtr], writes=[qTtbr])
        S.op("pool", lambda h: h.tensor_copy(kTtb, kTt), reads=[kTtr], writes=[kTtbr])
        for hb in range(2):
            for hp in range(HP):
                hcol = slice((2 * hp + hb) * 64, (2 * hp + hb + 1) * 64)
                S.dma("sp", ktt[PH[hb], :, hp, :], k.GKtm[tsl, hcol].rearrange("(n p) e -> p n e", p=64), writes=[kttr], sbreg=kttr)
                S.dma("sp", vtt[PH[hb], :, hp, :], k.GVtm[tsl, hcol].rearrange("(n p) e -> p n e", p=64), writes=[vttr], sbreg=vttr)

    def do_batch(gb):
        grp, b = divmod(gb, NB)
        if gb == 0:
            group_load(0)
            group_load(1)
        d = sets[gb % NSET]
        bs = (gb % NSET) * 4
        Bk = [k.banks[bs + i] for i in range(4)]
        Br = [k.bankr[bs + i] for i in range(4)]
        qTt, qTtr = qTb[grp % 2]
        kTt, kTtr = kTb[grp % 2]
        ktt, kttr = ktg[grp % 2]
        vtt, vttr = vtg[grp % 2]
        ogt, ogtr = og[grp % 2]
        raw, rawr = d["raw"]; mx, mxr = d["mx"]; mn, mnr = d["mn"]; tA, tAr = d["tA"]; tB, tBr = d["tB"]; gB, gBr = d["gB"]
        Mt = [d["Mt0"], d["Mt1"]]; Nt = [d["Nt0"], d["Nt1"]]
        Tt, Ttr = d["Tt"]; aqk, aqkr = d["aqk"]; Ru, Rur = d["Ru"]; Rw, Rwr = d["Rw"]; kdec, kdecr = d["kdec"]
        ut, utr = d["ut"]; wT, wTr = d["wT"]
        n0 = grp * 8 + b * CB
        ln0 = b * CB
        sc_sl = slice(n0, n0 + CB)

        def sview(ap):
            return ap[:, sc_sl, :].rearrange("p n h -> p (n h)")
        S.op("pool", lambda h: h.tensor_copy(gB, bc_s(sview(g))), reads=[gr], writes=[gBr])
        for cb in range(CB):
            tl = slice((ln0 + cb) * 64, (ln0 + cb + 1) * 64)
            for hp in range(HP):
                sl_ = cb * HP + hp
                col = slice(sl_ * 64, (sl_ + 1) * 64)
                for hb in range(2):
                    ph = PH[hb]
                    S.op("pe", lambda h, hp=hp, tl=tl, col=col, ph=ph: h.matmul(Bk[0][ph, col], kTt[ph, hp, tl], kTt[ph, hp, tl], start=True, stop=True), reads=[kTtr], writes=[Br[0]])
                    S.op("pe", lambda h, sl_=sl_, col=col, ph=ph: h.matmul(Bk[1][ph, col], gB[ph, sl_, :], L64x[ph, :], start=True, stop=True), reads=[gBr, L64xr], writes=[Br[1]])
                    S.op("pe", lambda h, hp=hp, tl=tl, col=col, ph=ph: h.matmul(Bk[2][ph, col], kTt[ph, hp, tl], qTt[ph, hp, tl], start=True, stop=True), reads=[kTtr, qTtr], writes=[Br[2]])
        yield
        S.op("dve", lambda h: h.tensor_tensor(raw, pv(Bk[1]), bc_s(sview(gc)), ALU.subtract), reads=[Br[1], gcr], writes=[rawr])
        yield
        S.op("dve", lambda h: h.tensor_scalar(mx, raw, 0.0, None, ALU.max), reads=[rawr], writes=[mxr])
        S.op("pool", lambda h: h.tensor_scalar(mn, raw, 0.0, None, ALU.min), reads=[rawr], writes=[mnr])
        yield
        S.op("act", lambda h: h.activation(mx, mx, AF.Exp, scale=-1.0), reads=[mxr], writes=[mxr])
        S.op("act", lambda h: h.activation(mn, mn, AF.Exp), reads=[mnr], writes=[mnr])
        vt3 = vtt[:, ln0:ln0 + CB, :, :].rearrange("p c h v -> p (c h) v")
        kt3 = ktt[:, ln0:ln0 + CB, :, :].rearrange("p c h v -> p (c h) v")
        S.op("pool", lambda h: h.tensor_tensor(Ru, vt3, bc_s(sview(beta)), ALU.mult), reads=[vttr, betar], writes=[Rur])
        S.op("pool", lambda h: h.tensor_tensor(Rw, kt3, bc_s(sview(begc)), ALU.mult), reads=[kttr, begcr], writes=[Rwr])
        S.op("pool", lambda h: h.tensor_tensor(kdec, kt3, bc_s(sview(ekd)), ALU.mult), reads=[kttr, ekdr], writes=[kdecr])
        yield
        S.op("dve", lambda h: h.tensor_tensor(tA, mx, mask_b(SLT64x), ALU.mult), reads=[mxr, SLT64xr], writes=[tAr])
        S.op("pool", lambda h: h.tensor_tensor(tB, mn, mask_b(L64x), ALU.mult), reads=[mnr, L64xr], writes=[tBr])
        yield
        S.op("dve", lambda h: h.tensor_tensor(tA, tA, bc_s(sview(nbeta)), ALU.mult), reads=[tAr, nbetar], writes=[tAr])
        yield
        M0, M0r = Mt[0]
        S.op("dve", lambda h: h.tensor_tensor(M0, pv(Bk[0]), tA, ALU.mult), reads=[Br[0], tAr], writes=[M0r])
        S.op("dve", lambda h: h.tensor_tensor(aqk, pv(Bk[2]), tB, ALU.mult), reads=[Br[2], tBr], writes=[aqkr])
        yield
        B3b = Bk[3][:, 0:256].bitcast(BF16)
        for e in range(NSL):
            for hb in range(2):
                ph = PH[hb]
                S.op("pe", lambda h, e=e, ph=ph: h.transpose(B3b[ph, e * 64:(e + 1) * 64], M0[ph, e, :], k.identb[ph, ph]), reads=[M0r, k.identbr], writes=[Br[3]])
        yield
        N0, N0r = Nt[0]
        S.op("act", lambda h: h.copy(N0, B3b.rearrange("p (a b) -> p a b", b=64)), reads=[Br[3]], writes=[N0r])
        yield
        S.op("dve", lambda h: h.tensor_tensor(Tt, N0, mask_b(idx2), ALU.add), reads=[N0r, idx2r], writes=[Ttr])
        for j in range(1, 6):
            Mp, Mpr = Mt[(j - 1) % 2]
            Np, Npr = Nt[(j - 1) % 2]
            Mc, Mcr = Mt[j % 2]
            Nc_, Ncr = Nt[j % 2]
            for e in range(NSL):
                col = slice(e * 64, (e + 1) * 64)
                for hb in range(2):
                    ph = PH[hb]
                    S.op("pe", lambda h, e=e, col=col, Np=Np, Mp=Mp, ph=ph: h.matmul(Bk[0][ph, col], Np[ph, e, :], Mp[ph, e, :], start=True, stop=True), reads=[Npr, Mpr], writes=[Br[0]])
            if j < 5:
                for e in range(NSL):
                    col = slice(e * 64, (e + 1) * 64)
                    for hb in range(2):
                        ph = PH[hb]
                        S.op("pe", lambda h, e=e, col=col, Np=Np, Mp=Mp, ph=ph: h.matmul(Bk[1][ph, col], Mp[ph, e, :], Np[ph, e, :], start=True, stop=True), reads=[Npr, Mpr], writes=[Br[1]])
            yield
            S.op("act", lambda h, Mc=Mc: h.copy(Mc, pv(Bk[0])), reads=[Br[0]], writes=[Mcr])
            if j < 5:
                S.op("act", lambda h, Nc_=Nc_: h.copy(Nc_, pv(Bk[1])), reads=[Br[1]], writes=[Ncr])
            yield
            for e in range(NSL):
                col = slice(e * 64, (e + 1) * 64)
                for hb in range(2):
                    ph = PH[hb]
                    S.op("pe", lambda h, e=e, col=col, Mc=Mc, ph=ph: h.matmul(Bk[2][ph, col], Mc[ph, e, :], Tt[ph, e, :], start=True, stop=True), reads=[Mcr, Ttr], writes=[Br[2]])
            yield
            S.op("dve", lambda h: h.tensor_tensor(Tt, Tt, pv(Bk[2]), ALU.add), reads=[Br[2], Ttr], writes=[Ttr])
            yield
        for e in range(NSL):
            col = slice(e * 64, (e + 1) * 64)
            for hb in range(2):
                ph = PH[hb]
                S.op("pe", lambda h, e=e, col=col, ph=ph: h.matmul(Bk[0][ph, col], Tt[ph, e, :], Ru[ph, e, :], start=True, stop=True), reads=[Ttr, Rur], writes=[Br[0]])
                S.op("pe", lambda h, e=e, col=col, ph=ph: h.matmul(Bk[1][ph, col], Rw[ph, e, :], Tt[ph, e, :], start=True, stop=True), reads=[Ttr, Rwr], writes=[Br[1]])
        yield
        S.op("act", lambda h: h.copy(ut, pv(Bk[0])), reads=[Br[0]], writes=[utr])
        S.op("act", lambda h: h.copy(wT, pv(Bk[1])), reads=[Br[1]], writes=[wTr])
        yield
        for cb in range(CB):
            n = n0 + cb
            tl = slice((ln0 + cb) * 64, (ln0 + cb + 1) * 64)
            es = slice(cb * HP, cb * HP + HP)
            for hp in range(HP):
                col = slice(hp * 64, (hp + 1) * 64)
                for hb in range(2):
                    ph = PH[hb]
                    S.op("pe", lambda h, hp=hp, col=col, cb=cb, ph=ph: h.matmul(Bk[3][ph, col], wT[ph, cb * HP + hp, :], Ssb[ph, hp, :], start=True, stop=True), reads=[wTr, Ssbr], writes=[Br[3]])
                    S.op("pe", lambda h, hp=hp, col=col, tl=tl, ph=ph: h.matmul(Bk[0][ph, col], qTt[ph, hp, tl], Ssb[ph, hp, :], start=True, stop=True), reads=[qTtr, Ssbr], writes=[Br[0]])
            yield
            S.op("dve", lambda h, es=es: h.tensor_tensor(vnew, ut[:, es, :], pv2(Bk[3]), ALU.subtract), reads=[utr, Br[3]], writes=[vnewr])
            yield
            for hp in range(HP):
                col = slice(hp * 64, (hp + 1) * 64)
                for hb in range(2):
                    ph = PH[hb]
                    S.op("pe", lambda h, hp=hp, col=col, cb=cb, ph=ph: h.matmul(Bk[2][ph, col], kdec[ph, cb * HP + hp, :], vnew[ph, hp, :], start=True, stop=True), reads=[kdecr, vnewr], writes=[Br[2]])
                    S.op("pe", lambda h, hp=hp, col=col, cb=cb, ph=ph: h.matmul(Bk[1][ph, col], aqk[ph, cb * HP + hp, :], vnew[ph, hp, :], start=True, stop=True), reads=[aqkr, vnewr], writes=[Br[1]])
            S.op("pool", lambda h, n=n: h.tensor_tensor(Sst, Sst, egl[:, n, :].unsqueeze(2).to_broadcast([128, HP, 64]), ALU.mult), reads=[Sstr, eglr], writes=[Sstr])
            yield
            S.op("dve", lambda h: h.tensor_tensor(Sst, Sst, pv2(Bk[2]), ALU.add), reads=[Br[2], Sstr], writes=[Sstr])
            S.op("act", lambda h: h.copy(Ssb, Sst), reads=[Sstr], writes=[Ssbr])
            S.op("dve", lambda h, n=n: h.tensor_tensor(otmp, pv2(Bk[0]), egc[:, n, :].unsqueeze(2).to_broadcast([128, HP, 64]), ALU.mult), reads=[Br[0], egcr], writes=[otmpr])
            S.op("dve", lambda h, cb=cb: h.tensor_tensor(ogt[:, ln0 + cb, :, :], otmp, pv2(Bk[1]), ALU.add), reads=[Br[1], otmpr], writes=[ogtr])
            yield
        if b == NB - 1:
            tsl = slice(grp * 512, (grp + 1) * 512)
            for hb in range(2):
                for hp in range(HP):
                    hcol = slice((2 * hp + hb) * 64, (2 * hp + hb + 1) * 64)
                    S.dma("pool", k.GO[tsl, hcol].rearrange("(n p) e -> p n e", p=64), ogt[PH[hb], :, hp, :], reads=[ogtr], sbreg=ogtr)
            if grp + 2 < NC // 8:
                group_load(grp + 2)

    run_interleaved([do_batch(gb) for gb in range(NC // CB)], NSET, 26)

    S.barrier()
    A.reset()
    Oall, Oallr = A.f32("OallG", [128, NT, 256])
    Zall, Zallr = A.f32("ZallG", [128, NT, 256])
    Osq, Osqr = A.f32("OsqG", [128, NT, 256])
    ssg, ssgr = A.f32("ssg", [128, NT * 4])
    rsg, rsgr = A.f32("rsg", [128, NT * 4])
    dnw, dnwr = A.f32("dnw", [128, 64])
    S.dma("sp", dnw, k.p["dn_norm_w"][l].partition_broadcast(128), writes=[dnwr], sbreg=dnwr)
    S.dma("sp", Oall, k.GO.rearrange("(n p) e -> p n e", p=128), writes=[Oallr], sbreg=Oallr)
    S.dma("sp", Zall, k.PTM[:, 1280:1536].rearrange("(n p) e -> p n e", p=128), writes=[Zallr], sbreg=Zallr)
    S.op("act", lambda h: h.activation(Zall, Zall, AF.Silu), reads=[Zallr], writes=[Zallr])
    S.op("pool", lambda h: h.tensor_tensor(Osq, Oall, Oall, ALU.mult), reads=[Oallr], writes=[Osqr])
    S.op("dve", lambda h: h.tensor_reduce(ssg, Osq.rearrange("p n (g v) -> p (n g) v", g=4), AX.X, ALU.add), reads=[Osqr], writes=[ssgr])
    rstd_from_ss(k, rsg, rsgr, ssg, ssgr, 64)
    O3 = Oall.rearrange("p n (g v) -> p (n g) v", g=4)
    S.op("dve", lambda h: h.tensor_tensor(O3, O3, rsg.unsqueeze(2).to_broadcast([128, NT * 4, 64]), ALU.mult), reads=[Oallr, rsgr], writes=[Oallr])
    S.op("pool", lambda h: h.tensor_tensor(O3, O3, dnw.unsqueeze(1).to_broadcast([128, NT * 4, 64]), ALU.mult), reads=[Oallr, dnwr], writes=[Oallr])
    ObG = Osq.rearrange("p n e -> p (n e)")[:, 0:4096].bitcast(BF16).rearrange("p (n e) -> p n e", n=NT)
    S.op("dve", lambda h: h.tensor_tensor(ObG, Oall, Zall, ALU.mult), reads=[Oallr, Zallr, Osqr, ssgr], writes=[Osqr])
    S.dma("pool", k.MIX[:, 768:1024].rearrange("(n p) e -> p n e", p=128), ObG, reads=[Osqr], sbreg=Osqr)


def g1_chunk(k, c, Xt, Yt, sqt, tms, cwd, cwdr, bo, bor):
    S = k.S
    X, Xr = Xt
    Y, Yr = Yt
    sq, sqr = sqt
    if True:
        S.dma("sp", X[:, 3:T + 3], k.PFMf[512 + c * 128:512 + (c + 1) * 128, :], writes=[Xr], sbreg=Xr)
        S.op("dve", lambda h, c=c: h.tensor_scalar(Y, X[:, 3:T + 3], cwd[:, 3, c:c + 1], None, ALU.mult), reads=[Xr, cwdr], writes=[Yr])
        S.op("dve", lambda h, c=c: h.scalar_tensor_tensor(Y, X[:, 2:T + 2], cwd[:, 2, c:c + 1], Y, ALU.mult, ALU.add), reads=[Xr, cwdr, Yr], writes=[Yr])
        S.op("dve", lambda h, c=c: h.scalar_tensor_tensor(Y, X[:, 1:T + 1], cwd[:, 1, c:c + 1], Y, ALU.mult, ALU.add), reads=[Xr, cwdr, Yr], writes=[Yr])
        S.op("dve", lambda h, c=c: h.scalar_tensor_tensor(Y, X[:, 0:T], cwd[:, 0, c:c + 1], Y, ALU.mult, ALU.add), reads=[Xr, cwdr, Yr], writes=[Yr])
        S.op("act", lambda h: h.activation(Y, Y, AF.Silu), reads=[Yr], writes=[Yr])
        if c < 4:
            S.op("act", lambda h: h.activation(sq, Y, AF.Square), reads=[Yr], writes=[sqr])
            for tb in range(8):
                bank = k.banks[tb % 2]
                bankr = k.bankr[tb % 2]
                S.op("pe", lambda h, bank=bank, tb=tb: h.matmul(bank[:, :], bo, sq[:, tb * 512:(tb + 1) * 512], start=True, stop=True), reads=[bor, sqr], writes=[bankr])
                S.op("act", lambda h, bank=bank, tb=tb: h.activation(X[:, 3 + tb * 512:3 + (tb + 1) * 512], bank[:, :], AF.Ln, bias=k.epst[:, 0:1]), reads=[bankr, k.epsr], writes=[Xr])
            S.op("act", lambda h: h.activation(X[:, 3:T + 3], X[:, 3:T + 3], AF.Exp, scale=-0.5), reads=[Xr], writes=[Xr])
            if c < 2:
                S.op("dve", lambda h: h.scalar_tensor_tensor(Y, Y, 0.125, X[:, 3:T + 3], ALU.mult, ALU.mult), reads=[Yr, Xr], writes=[Yr])
                S.dma("pool", k.GQT[c * 128:(c + 1) * 128, :], Y, reads=[Yr], sbreg=Yr)
            else:
                S.op("dve", lambda h: h.tensor_tensor(Y, Y, X[:, 3:T + 3], ALU.mult), reads=[Yr, Xr], writes=[Yr])
                S.dma("pool", k.GKT[(c - 2) * 128:(c - 1) * 128, :], Y, reads=[Yr], sbreg=Yr)
        if c >= 2:
            st_, str_ = tms[c % 2]
            for i in range(NT):
                bank = k.banks[2 + i % 4]
                bankr = k.bankr[2 + i % 4]
                S.op("pe", lambda h, bank=bank, i=i: h.transpose(bank[:, 0:128], Y[:, i * 128:(i + 1) * 128], k.ident), reads=[Yr, k.identr], writes=[bankr])
                if i % 2 == 0:
                    S.op("act", lambda h, bank=bank, i=i, st_=st_: h.copy(st_[:, i, :], bank[:, 0:128]), reads=[bankr], writes=[str_])
                else:
                    S.op("dve", lambda h, bank=bank, i=i, st_=st_: h.tensor_copy(st_[:, i, :], bank[:, 0:128]), reads=[bankr], writes=[str_])
            dst = k.GKtm if c < 4 else k.GVtm
            cc = (c - 2) % 2
            S.dma("pool", dst[:, cc * 128:(cc + 1) * 128].rearrange("(n p) e -> p n e", p=128), st_, reads=[str_], sbreg=str_)


def phase_outproj_ffn(k, l, src, last):
    S, A = k.S, k.A
    MIXsrc = k.mix_in if getattr(k, "mix_in", None) is not None else k.MIX
    S.barrier()
    A.reset()
    k.bankr = [S.reg("bank%d" % i) for i in range(8)]
    u2T, u2Tr = A.bf16("u2T", [128, 8, T], nreg=NT)
    mixT, mixTr = A.bf16("mixT", [128, 8, T], nreg=NT)
    Wob, Wobr = A.bf16("Wob", [128, 8, D], nreg=8)
    wst = [A.f32("wst%d" % i, [128, D]) for i in range(2)]
    for c in range(8):
        wt, wtr = wst[c % 2]
        S.dma("sp", wt, k.p["w_out"][l, c * 128:(c + 1) * 128, :], writes=[wtr], sbreg=wtr)
        S.op("pool", lambda h, wt=wt, c=c: h.tensor_copy(Wob[:, c, :], wt), reads=[wtr], writes=[Wobr[c]])
    hb = [A.f32("hb%d" % i, [128, D]) for i in range(3)]
    ub = [A.bf16("ub%d" % i, [128, D]) for i in range(2)]
    block = norm_transpose_pass(k, MIXsrc, mixT, mixTr, hb, ub, "ssO", do_norm=False, src_bf16=(getattr(k, "mix_in", None) is None))
    hres = [A.f32("hres%d" % i, [128, D]) for i in range(2)]
    hnew = [A.f32("hnew%d" % i, [128, D]) for i in range(2)]
    cnt = {"acc": 0}
    junkO, junkOr = A.f32("junkO", [128, D])
    S.op("pool", lambda h: h.memset(k.ssF, 0.0), writes=[k.ssFr])
    ub2 = [A.bf16("ub2_%d" % i, [128, D]) for i in range(2)]
    rsO, rsOr = A.f32("rsO", [128, NT])
    pend = []

    def emit_u2T(item):
        i, u2, u2r = item
        bank = k.banks[6 + i % 2]
        bankr = k.bankr[6 + i % 2]
        pt = bank[:, :].bitcast(BF16)
        for c in range(8):
            S.op("pe", lambda h, c=c, u2=u2, pt=pt: h.transpose(pt[:, c * 128:(c + 1) * 128], u2[:, c * 128:(c + 1) * 128], k.identb),
                 reads=[u2r, k.identbr], writes=[bankr], pe_accum=True)
        if i % 2 == 0:
            S.op("act", lambda h, pt=pt, i=i: h.copy(u2T[:, :, i * 128:(i + 1) * 128], pt.rearrange("p (c t) -> p c t", c=8)), reads=[bankr], writes=[u2Tr[i]])
        else:
            S.op("dve", lambda h, pt=pt, i=i: h.tensor_copy(u2T[:, :, i * 128:(i + 1) * 128], pt.rearrange("p (c t) -> p c t", c=8)), reads=[bankr], writes=[u2Tr[i]])

    def oproj(tb):
        for i in range(tb * 4, tb * 4 + 4):
            hr, hrr = hres[i % 2]
            hn, hnr = hnew[i % 2]
            S.dma("sp", hr, src[i * 128:(i + 1) * 128, :], writes=[hrr], sbreg=hrr)
            for half in range(2):
                bank = k.banks[2 + cnt["acc"] % 4]
                bankr = k.bankr[2 + cnt["acc"] % 4]
                cnt["acc"] += 1
                for c in range(8):
                    S.op("pe", lambda h, c=c, bank=bank, i=i, half=half: h.matmul(bank[:, :], mixT[:, c, i * 128:(i + 1) * 128], Wob[:, c, half * 512:(half + 1) * 512], start=(c == 0), stop=(c == 7)),
                         reads=[mixTr[i], Wobr[c]], writes=[bankr], pe_accum=(c > 0))
                S.op("dve", lambda h, hn=hn, hr=hr, bank=bank, half=half: h.tensor_tensor(hn[:, half * 512:(half + 1) * 512], bank[:, :], hr[:, half * 512:(half + 1) * 512], ALU.add),
                     reads=[bankr, hrr], writes=[hnr])
            S.op("act", lambda h, hn=hn, i=i: h.activation(junkO, hn, AF.Square, accum_out=k.ssF[:, i:i + 1]), reads=[hnr, k.ssFr], writes=[junkOr, k.ssFr])
            S.dma("pool", k.H[i * 128:(i + 1) * 128, :], hn, reads=[hnr], sbreg=hnr)
            S.op("dve", lambda h, i=i: h.tensor_scalar(rsO[:, i:i + 1], k.ssF[:, i:i + 1], 1.0 / D, EPS, ALU.mult, ALU.add), reads=[k.ssFr], writes=[rsOr])
            S.op("act", lambda h, i=i: h.activation(rsO[:, i:i + 1], rsO[:, i:i + 1], AF.Ln), reads=[rsOr], writes=[rsOr])
            S.op("act", lambda h, i=i: h.activation(rsO[:, i:i + 1], rsO[:, i:i + 1], AF.Exp, scale=-0.5), reads=[rsOr], writes=[rsOr])
            u2, u2r = ub2[i % 2]
            S.op("dve", lambda h, hn=hn, u2=u2, i=i: h.tensor_scalar(u2, hn, rsO[:, i:i + 1], None, ALU.mult), reads=[hnr, rsOr], writes=[u2r])
            pend.append((i, u2, u2r))
            if len(pend) > 1:
                emit_u2T(pend.pop(0))

    block(0)
    for tb in range(8):
        if tb + 1 < 8:
            block(tb + 1)
        oproj(tb)
    while pend:
        emit_u2T(pend.pop(0))

    S.barrier()
    A.reset()
    k.bankr = [S.reg("bank%d" % i) for i in range(8)]
    uT, uTr = A.bf16("u2T", [128, 8, T], nreg=NT)
    nw, nwr = A.f32("nw2", [128, 8])
    S.dma("sp", nw, k.p["ffn_norm_w"][l].rearrange("(c p) -> p c", p=128), writes=[nwr], sbreg=nwr, allow_slow_non_contiguous=True)
    cw, cwr = A.f32("cw", [128, 3, 44])
    cb, cbr = A.f32("cb", [128, 44])
    for j in range(3):
        S.dma("sp", cw[:, j, :], k.p["ffn_conv_w"][l, j].rearrange("(c p) -> p c", p=128), writes=[cwr], sbreg=cwr, allow_slow_non_contiguous=True)
    S.dma("sp", cb, k.p["ffn_conv_b"][l].rearrange("(c p) -> p c", p=128), writes=[cbr], sbreg=cbr, allow_slow_non_contiguous=True)
    wf = [[A.f32("wf%d_%d" % (i, j), [128, 8, 128]) for j in range(2)] for i in range(2)]
    wb = [[A.bf16("wb%d_%d" % (i, j), [128, 8, 128]) for j in range(2)] for i in range(2)]
    Xs = [[A.f32("X%d_%d" % (i, j), [128, T + 2]) for j in range(2)] for i in range(2)]
    Y = [A.f32("Y%d" % j, [128, T]) for j in range(2)]
    aout = [A.bf16("aout%d" % i, [128, T]) for i in range(1)]
    for i in range(2):
        for j in range(2):
            S.op("pool", lambda h, i=i, j=j: h.memset(Xs[i][j][0][:, 0:2], 0.0), writes=[Xs[i][j][1]])
    cnt = {"acc": 0, "ev": 0}
    def load_w(f):
        for j in range(2):
            wt, wtr = wf[f % 2][j]
            wbt, wbtr = wb[f % 2][j]
            col = j * DFF + f * 128
            S.dma("sp", wt, k.p["ffn_w_up"][l, :, col:col + 128].rearrange("(c p) e -> p c e", p=128), writes=[wtr], sbreg=wtr)
            S.op("pool", lambda h, wt=wt, wbt=wbt: h.tensor_tensor(wbt, wt, nw.unsqueeze(2).to_broadcast([128, 8, 128]), ALU.mult), reads=[wtr, nwr], writes=[wbtr])

    load_w(0)
    for f in range(22):
        X = Xs[f % 2]
        if f + 1 < 22:
            load_w(f + 1)
        for tb in range(8):
            for j in range(2):
                wbt, wbtr = wb[f % 2][j]
                bank = k.banks[cnt["acc"] % 4]
                bankr = k.bankr[cnt["acc"] % 4]
                cnt["acc"] += 1
                for c in range(8):
                    S.op("pe", lambda h, c=c, bank=bank, wbt=wbt, tb=tb: h.matmul(bank[:, :], wbt[:, c, :], uT[:, c, tb * 512:(tb + 1) * 512], start=(c == 0), stop=(c == 7)),
                         reads=[wbtr] + uTr[tb * 4:tb * 4 + 4], writes=[bankr], pe_accum=(c > 0))
                Xj, Xjr = X[j]
                S.op("act", lambda h, Xj=Xj, bank=bank, tb=tb: h.copy(Xj[:, 2 + tb * 512:2 + (tb + 1) * 512], bank[:, :]), reads=[bankr], writes=[Xjr])
        for j in range(2):
            Xj, Xjr = X[j]
            Yj, Yjr = Y[j]
            ch = j * 22 + f
            S.op("dve", lambda h, Xj=Xj, Yj=Yj, ch=ch: h.tensor_scalar(Yj, Xj[:, 2:T + 2], cw[:, 2, ch:ch + 1], cb[:, ch:ch + 1], ALU.mult, ALU.add),
                 reads=[Xjr, cwr, cbr], writes=[Yjr])
            S.op("dve", lambda h, Xj=Xj, Yj=Yj, ch=ch: h.scalar_tensor_tensor(Yj, Xj[:, 1:T + 1], cw[:, 1, ch:ch + 1], Yj, ALU.mult, ALU.add),
                 reads=[Xjr, cwr, Yjr], writes=[Yjr])
            S.op("dve", lambda h, Xj=Xj, Yj=Yj, ch=ch: h.scalar_tensor_tensor(Yj, Xj[:, 0:T], cw[:, 0, ch:ch + 1], Yj, ALU.mult, ALU.add),
                 reads=[Xjr, cwr, Yjr], writes=[Yjr])
        S.op("act", lambda h: h.activation(Y[0][0], Y[0][0], AF.Silu), reads=[Y[0][1]], writes=[Y[0][1]])
        ao, aor = aout[0]
        S.op("pool", lambda h, ao=ao: h.tensor_tensor(ao, Y[0][0], Y[1][0], ALU.mult), reads=[Y[0][1], Y[1][1]], writes=[aor])
        S.dma("pool", k.ACTS[f * 128:(f + 1) * 128, :], ao, reads=[aor], sbreg=aor)

    S.barrier()
    A.reset()
    k.bankr = [S.reg("bank%d" % i) for i in range(8)]
    Wdb, Wdbr = A.bf16("Wdb", [128, 22, D], nreg=22)
    wst = [A.f32("wst%d" % i, [128, D]) for i in range(2)]
    for f in range(22):
        wt, wtr = wst[f % 2]
        S.dma("sp", wt, k.p["ffn_w_down"][l, f * 128:(f + 1) * 128, :], writes=[wtr], sbreg=wtr)
        S.op("pool" if f % 2 else "dve", lambda h, wt=wt, f=f: h.tensor_copy(Wdb[:, f, :], wt), reads=[wtr], writes=[Wdbr[f]])
    acts = [A.bf16("acts%d" % i, [128, 22, 512]) for i in range(2)]
    hres = [A.f32("hres%d" % i, [128, D]) for i in range(2)]
    hnew = [A.f32("hnew%d" % i, [128, D]) for i in range(2)]
    if not last:
        junk2, junk2r = A.f32("junk2", [128, D])
        S.op("pool", lambda h: h.memset(k.ssA, 0.0), writes=[k.ssAr])
    if last:
        fnw, fnwr = A.f32("fnw", [128, D])
        S.dma("sp", fnw, k.p["final_norm_w"].partition_broadcast(128), writes=[fnwr], sbreg=fnwr)
        junk, junkr = A.f32("junkF", [128, D])
        ssf, ssfr = A.f32("ssf", [128, NT])
        rsf, rsfr = A.f32("rsf", [128, NT])
        S.op("dve", lambda h: h.memset(ssf, 0.0), writes=[ssfr])
    cnt = {"acc": 0}
    for tb in range(8):
        ac, acr = acts[tb % 2]
        S.dma("sp", ac, k.ACTS[:, tb * 512:(tb + 1) * 512].rearrange("(f p) t -> p f t", p=128), writes=[acr], sbreg=acr)
        for ti in range(4):
            i = tb * 4 + ti
            hr, hrr = hres[i % 2]
            hn, hnr = hnew[i % 2]
            S.dma("sp", hr, k.H[i * 128:(i + 1) * 128, :], writes=[hrr], sbreg=hrr)
            for half in range(2):
                bank = k.banks[cnt["acc"] % 4]
                bankr = k.bankr[cnt["acc"] % 4]
                cnt["acc"] += 1
                for f in range(22):
                    S.op("pe", lambda h, f=f, bank=bank, ac=ac, ti=ti, half=half: h.matmul(bank[:, :], ac[:, f, ti * 128:(ti + 1) * 128], Wdb[:, f, half * 512:(half + 1) * 512], start=(f == 0), stop=(f == 21)),
                         reads=[acr, Wdbr[f]], writes=[bankr], pe_accum=(f > 0))
                S.op("dve", lambda h, hn=hn, hr=hr, bank=bank, half=half: h.tensor_tensor(hn[:, half * 512:(half + 1) * 512], bank[:, :], hr[:, half * 512:(half + 1) * 512], ALU.add),
                     reads=[bankr, hrr], writes=[hnr])
            if not last:
                S.op("act", lambda h, hn=hn, i=i: h.activation(junk2, hn, AF.Square, accum_out=k.ssA[:, i:i + 1]), reads=[hnr, k.ssAr], writes=[junk2r, k.ssAr])
                S.dma("pool", k.H[i * 128:(i + 1) * 128, :], hn, reads=[hnr], sbreg=hnr)
            else:
                S.op("act", lambda h, hn=hn, i=i: h.activation(junk, hn, AF.Square, accum_out=ssf[:, i:i + 1]), reads=[hnr, ssfr], writes=[junkr, ssfr])
                S.op("dve", lambda h, i=i: h.tensor_scalar(rsf[:, i:i + 1], ssf[:, i:i + 1], 1.0 / D, EPS, ALU.mult, ALU.add), reads=[ssfr], writes=[rsfr])
                S.op("act", lambda h, i=i: h.activation(rsf[:, i:i + 1], rsf[:, i:i + 1], AF.Ln), reads=[rsfr], writes=[rsfr])
                S.op("act", lambda h, i=i: h.activation(rsf[:, i:i + 1], rsf[:, i:i + 1], AF.Exp, scale=-0.5), reads=[rsfr], writes=[rsfr])
                S.op("dve", lambda h, hn=hn, i=i: h.scalar_tensor_tensor(hn, hn, rsf[:, i:i + 1], fnw, ALU.mult, ALU.mult), reads=[hnr, rsfr, fnwr], writes=[hnr])
                S.dma("pool", k.out[i * 128:(i + 1) * 128, :], hn, reads=[hnr], sbreg=hnr)


_NC_CACHE = {}


def kernel(**inputs):
    x = np.ascontiguousarray(np.asarray(inputs["x"], dtype=np.float32))
    if "nc" not in _NC_CACHE:
        _NC_CACHE["nc"] = build_program(debug=False, nlayers=2)
    nc = _NC_CACHE["nc"]
    consts = make_consts()
    base = {}
    for n in PARAM_SHAPES:
        base[n] = np.ascontiguousarray(np.asarray(inputs[n], dtype=np.float32))
    for n, v in consts.items():
        base["c_" + n] = v
    in_maps = []
    for c in range(8):
        m = dict(base)
        m["x"] = np.ascontiguousarray(x[c % 4])
        in_maps.append(m)
    res = run_bass_kernel_spmd(nc, in_maps, core_ids=list(range(8)))
    out = np.stack([np.asarray(res.results[b]["out"], dtype=np.float32) for b in range(4)], axis=0)
    return out
```
